# Optimizing a Trainium2 kernel written in Bass

```python
import math
import jax, jax.numpy as jnp
from jax import lax
import numpy as np

D_MODEL = 1024
BATCH = 8
SEQ = 2048
DEPTH = 4

N_META = 16
RMS_EPS = 1e-6
L2_EPS = 1e-6
CONV_K = 4
SB_HEADS = 8
SB_HEAD_DIM = 128
SB_WIDTH = SB_HEADS * SB_HEAD_DIM
SB_BLOCK = 128
GDN_HEADS = 8
GDN_DK = 128
GDN_DV = 128
GDN_QK_WIDTH = GDN_HEADS * GDN_DK
GDN_V_WIDTH = GDN_HEADS * GDN_DV
GDN_CHUNK = 64
SSM_EXPAND = 2
SSM_INNER = SSM_EXPAND * D_MODEL
SSM_HEAD_DIM = 64
SSM_HEADS = SSM_INNER // SSM_HEAD_DIM
SSM_GROUPS = 2
SSM_STATE = 128
SSM_CHUNK = 64
SSM_CONV_CH = SSM_INNER + 2 * SSM_GROUPS * SSM_STATE
N_BRANCH = 3
IN_SPLITS = (SB_WIDTH, SB_WIDTH, SB_WIDTH, SB_WIDTH,
             GDN_QK_WIDTH, GDN_QK_WIDTH, GDN_V_WIDTH, GDN_V_WIDTH, GDN_HEADS, GDN_HEADS,
             SSM_INNER, SSM_CONV_CH, SSM_HEADS,
             N_BRANCH * D_MODEL)
D_IN = 4 * SB_WIDTH + 2 * GDN_QK_WIDTH + 2 * GDN_V_WIDTH + 2 * GDN_HEADS + SSM_INNER + SSM_CONV_CH + SSM_HEADS + N_BRANCH * D_MODEL

kernel_name = 'hybrid_stickbreak_gdn_ssd_block'


def _split(t, sizes):
    out, start = [], 0
    for s in sizes:
        out.append(t[..., start:start + s])
        start += s
    return out


def _rmsnorm(x, g):
    xf = x.astype(jnp.float32)
    y = xf * lax.rsqrt(jnp.mean(xf * xf, axis=-1, keepdims=True) + RMS_EPS)
    return (y * g.astype(jnp.float32)).astype(x.dtype)


def _l2norm(t):
    t = t.astype(jnp.float32)
    return t * lax.rsqrt(jnp.sum(t * t, axis=-1, keepdims=True) + L2_EPS)


def _front_pad(t, n):
    return jnp.pad(t, [(0, 0), (n, 0)] + [(0, 0)] * (t.ndim - 2))


def _causal_dwconv(x, w):
    k, c = w.shape
    return lax.conv_general_dilated(x, w.astype(x.dtype).reshape(k, 1, c), window_strides=(1,),
                                    padding=[(k - 1, 0)], dimension_numbers=('NWC', 'WIO', 'NWC'),
                                    feature_group_count=c)


def _stick_breaking_attention(q, k, v):
    bsz, seq, nh, dh = q.shape
    pad = SB_BLOCK - N_META
    lp = seq + pad
    nb = lp // SB_BLOCK
    scale = dh ** -0.5
    qf, kf, vf = [jnp.swapaxes(_front_pad(t.astype(jnp.float32), pad), 1, 2) for t in (q, k, v)]
    q_blocks = jnp.moveaxis(qf.reshape(bsz, nh, nb, SB_BLOCK, dh), 2, 0)
    key_pos = jnp.arange(lp)

    def one_block(args):
        qb, bi = args
        q_pos = bi * SB_BLOCK + jnp.arange(SB_BLOCK)
        mask = (key_pos[None, :] < q_pos[:, None]) & (key_pos[None, :] >= pad)
        z = jnp.einsum('bhqd,bhkd->bhqk', qb, kf) * scale
        log_keep = jnp.where(mask, jax.nn.log_sigmoid(-z), 0.0)
        later = lax.cumsum(log_keep, axis=3, reverse=True) - log_keep
        w = jnp.where(mask, jnp.exp(jax.nn.log_sigmoid(z) + later), 0.0)
        return jnp.einsum('bhqk,bhkd->bhqd', w, vf)

    out = lax.map(one_block, (q_blocks, jnp.arange(nb)))
    out = jnp.transpose(out, (1, 0, 3, 2, 4)).reshape(bsz, lp, nh, dh)
    return out[:, pad:]


def _gated_delta_rule(q, k, v, g, beta):
    pad = GDN_CHUNK - N_META
    q, k, v, g, beta = [_front_pad(t, pad) for t in (q, k, v, g, beta)]
    bsz, lp, nh, dk = q.shape
    dv = v.shape[-1]
    cl = GDN_CHUNK
    nc = lp // cl

    def chunked(t):
        return t.reshape((bsz, nc, cl) + t.shape[2:])

    q = chunked(q * dk ** -0.5)
    k, v, g, beta = chunked(k), chunked(v), chunked(g), chunked(beta)
    gc = jnp.cumsum(g, axis=2)
    gt = jnp.moveaxis(gc, 2, 3)
    seg = gt[..., :, None] - gt[..., None, :]
    idx = jnp.arange(cl)
    strict = idx[:, None] > idx[None, :]
    incl = idx[:, None] >= idx[None, :]
    dec_strict = jnp.exp(jnp.where(strict, seg, -jnp.inf))
    dec_incl = jnp.exp(jnp.where(incl, seg, -jnp.inf))
    kb = k * beta[..., None]
    m = jnp.einsum('bclhd,bcshd->bchls', kb, k) * dec_strict
    eye = jnp.eye(cl, dtype=m.dtype)
    t_inv = lax.linalg.triangular_solve(m + eye, jnp.broadcast_to(eye, m.shape), left_side=True,
                                        lower=True, unit_diagonal=True)
    u = jnp.einsum('bchls,bcshd->bclhd', t_inv, v * beta[..., None])
    w = jnp.einsum('bchls,bcshd->bclhd', t_inv, kb * jnp.exp(gc)[..., None])
    a_qk = jnp.einsum('bclhd,bcshd->bchls', q, k) * dec_incl
    g_last = gc[:, :, -1]
    q_dec = q * jnp.exp(gc)[..., None]
    k_end = k * jnp.exp(g_last[:, :, None] - gc)[..., None]

    def step(state, inp):
        qd, ke, uc, wc, aqk, gl = inp
        v_new = uc - jnp.einsum('blhk,bhkv->blhv', wc, state)
        o = jnp.einsum('blhk,bhkv->blhv', qd, state) + jnp.einsum('bhls,bshv->blhv', aqk, v_new)
        state = state * jnp.exp(gl)[..., None, None] + jnp.einsum('blhk,blhv->bhkv', ke, v_new)
        return state, o

    s0 = jnp.zeros((bsz, nh, dk, dv), q.dtype)
    xs = tuple(jnp.moveaxis(t, 1, 0) for t in (q_dec, k_end, u, w, a_qk, g_last))
    _, o = lax.scan(step, s0, xs)
    o = jnp.moveaxis(o, 0, 1).reshape(bsz, lp, nh, dv)
    return o[:, pad:]


def _ssd_scan(x, dt, a, b_in, c_in):
    pad = SSM_CHUNK - N_META
    x, dt, b_in, c_in = [_front_pad(t, pad) for t in (x, dt, b_in, c_in)]
    bsz, lp, nh, p = x.shape
    ng, n = b_in.shape[2], b_in.shape[3]
    hg = nh // ng
    cl = SSM_CHUNK
    nc = lp // cl
    xs = (x * dt[..., None]).reshape(bsz, nc, cl, ng, hg, p)
    la = (dt * a).reshape(bsz, nc, cl, ng, hg)
    bc = b_in.reshape(bsz, nc, cl, ng, n)
    cc = c_in.reshape(bsz, nc, cl, ng, n)
    cs = jnp.cumsum(la, axis=2)
    causal = jnp.tril(jnp.ones((cl, cl), dtype=bool))
    seg = cs[:, :, :, None] - cs[:, :, None, :]
    decay = jnp.exp(jnp.where(causal[:, :, None, None], seg, -jnp.inf))
    scores = jnp.einsum('bclgn,bcsgn->bclsg', cc, bc)[..., None] * decay
    y_diag = jnp.einsum('bclsgh,bcsghp->bclghp', scores, xs)
    to_end = jnp.exp(cs[:, :, -1:] - cs)
    states = jnp.einsum('bclgn,bclghp->bcghpn', bc, xs * to_end[..., None])
    chunk_decay = jnp.exp(cs[:, :, -1])

    def step(hst, inp):
        st, cd = inp
        return cd[..., None, None] * hst + st, hst

    h0 = jnp.zeros((bsz, ng, hg, p, n), xs.dtype)
    _, h_prev = lax.scan(step, h0, (jnp.moveaxis(states, 1, 0), jnp.moveaxis(chunk_decay, 1, 0)))
    h_prev = jnp.moveaxis(h_prev, 0, 1)
    y_off = jnp.einsum('bclgn,bcghpn->bclghp', cc, h_prev) * jnp.exp(cs)[..., None]
    y = (y_diag + y_off).reshape(bsz, lp, nh, p)
    return y[:, pad:]


def _hybrid_mixer(u, w_in, gdn_conv_w, gdn_a_log, gdn_dt_bias, gdn_norm_g, ssm_conv_w, ssm_conv_b,
                  ssm_a_log, ssm_dt_bias, ssm_d, ssm_norm_g, w_branch_a, w_branch_b, w_branch_c, w_out):
    f32 = jnp.float32
    dtype = u.dtype
    bsz, seq, _ = u.shape
    proj = u @ w_in
    (sb_q, sb_k, sb_v, sb_z, gd_q, gd_k, gd_v, gd_z, gd_b, gd_a,
     ss_z, ss_xbc, ss_dt, gates) = _split(proj, IN_SPLITS)

    def heads(t, nh):
        return t.reshape(bsz, seq, nh, -1)

    o_a = _stick_breaking_attention(heads(sb_q, SB_HEADS), heads(sb_k, SB_HEADS), heads(sb_v, SB_HEADS))
    o_a = o_a.reshape(bsz, seq, SB_WIDTH).astype(dtype) * jax.nn.silu(sb_z)

    qkv = jax.nn.silu(_causal_dwconv(jnp.concatenate([gd_q, gd_k, gd_v], axis=-1), gdn_conv_w))
    cq, ck, cv = _split(qkv, (GDN_QK_WIDTH, GDN_QK_WIDTH, GDN_V_WIDTH))
    beta = jax.nn.sigmoid(gd_b.astype(f32))
    g = -jnp.exp(gdn_a_log.astype(f32)) * jax.nn.softplus(gd_a.astype(f32) + gdn_dt_bias.astype(f32))
    o_b = _gated_delta_rule(_l2norm(heads(cq, GDN_HEADS)), _l2norm(heads(ck, GDN_HEADS)),
                            heads(cv, GDN_HEADS).astype(f32), g, beta)
    o_b = _rmsnorm(o_b, gdn_norm_g).reshape(bsz, seq, GDN_V_WIDTH).astype(dtype) * jax.nn.silu(gd_z)

    xbc = jax.nn.silu(_causal_dwconv(ss_xbc, ssm_conv_w) + ssm_conv_b)
    sx, sb, sc = _split(xbc, (SSM_INNER, SSM_GROUPS * SSM_STATE, SSM_GROUPS * SSM_STATE))
    dt = jax.nn.softplus(ss_dt.astype(f32) + ssm_dt_bias.astype(f32))
    a = -jnp.exp(ssm_a_log.astype(f32))
    xh = heads(sx, SSM_HEADS).astype(f32)
    y = _ssd_scan(xh, dt, a, heads(sb, SSM_GROUPS).astype(f32), heads(sc, SSM_GROUPS).astype(f32))
    y = y + ssm_d.astype(f32)[:, None] * xh
    y = y.reshape(bsz, seq, SSM_INNER) * jax.nn.silu(ss_z.astype(f32))
    o_c = _rmsnorm(y.reshape(bsz, seq, SSM_GROUPS, -1), ssm_norm_g.reshape(SSM_GROUPS, -1))
    o_c = o_c.reshape(bsz, seq, SSM_INNER).astype(dtype)

    g_a, g_b, g_c = _split(jax.nn.sigmoid(gates), (D_MODEL, D_MODEL, D_MODEL))
    merged = g_a * (o_a @ w_branch_a) + g_b * (o_b @ w_branch_b) + g_c * (o_c @ w_branch_c)
    return merged @ w_out


def setup_inputs(seed: int = 0) -> dict:
    key = jax.random.key(seed)
    ks = jax.random.split(key, 20)
    f32 = jnp.float32

    def nrm(k, shape, scale):
        return jax.random.normal(k, shape, f32) * scale

    def gain(k, shape):
        return 1.0 + 0.02 * jax.random.normal(k, shape, f32)

    def dt_bias(k, shape):
        dt = jnp.exp(jax.random.uniform(k, shape, f32, math.log(1e-3), math.log(1e-1)))
        return dt + jnp.log(-jnp.expm1(-dt))

    def a_log(k, shape):
        return jnp.log(jax.random.uniform(k, shape, f32, 1.0, 16.0))

    return {
        'x': jax.random.normal(ks[0], (BATCH, SEQ, D_MODEL), f32),
        'meta_tokens': nrm(ks[1], (N_META, D_MODEL), 1.0),
        'norm_g': gain(ks[2], (DEPTH, D_MODEL)),
        'w_in': nrm(ks[3], (DEPTH, D_MODEL, D_IN), D_MODEL ** -0.5),
        'gdn_conv_w': nrm(ks[4], (DEPTH, CONV_K, 2 * GDN_QK_WIDTH + GDN_V_WIDTH), CONV_K ** -0.5),
        'gdn_a_log': a_log(ks[5], (DEPTH, GDN_HEADS)),
        'gdn_dt_bias': dt_bias(ks[6], (DEPTH, GDN_HEADS)),
        'gdn_norm_g': gain(ks[7], (DEPTH, GDN_DV)),
        'ssm_conv_w': nrm(ks[8], (DEPTH, CONV_K, SSM_CONV_CH), CONV_K ** -0.5),
        'ssm_conv_b': nrm(ks[9], (DEPTH, SSM_CONV_CH), 0.01),
        'ssm_a_log': a_log(ks[10], (DEPTH, SSM_HEADS)),
        'ssm_dt_bias': dt_bias(ks[11], (DEPTH, SSM_HEADS)),
        'ssm_d': 1.0 + 0.1 * jax.random.normal(ks[12], (DEPTH, SSM_HEADS), f32),
        'ssm_norm_g': gain(ks[13], (DEPTH, SSM_INNER)),
        'w_branch_a': nrm(ks[14], (DEPTH, SB_WIDTH, D_MODEL), SB_WIDTH ** -0.5),
        'w_branch_b': nrm(ks[15], (DEPTH, GDN_V_WIDTH, D_MODEL), GDN_V_WIDTH ** -0.5),
        'w_branch_c': nrm(ks[16], (DEPTH, SSM_INNER, D_MODEL), SSM_INNER ** -0.5),
        'w_out': nrm(ks[17], (DEPTH, D_MODEL, D_MODEL), D_MODEL ** -0.5),
        'final_norm_g': gain(ks[18], (D_MODEL,)),
    }


def reference(x, meta_tokens, norm_g, w_in, gdn_conv_w, gdn_a_log, gdn_dt_bias, gdn_norm_g, ssm_conv_w,
              ssm_conv_b, ssm_a_log, ssm_dt_bias, ssm_d, ssm_norm_g, w_branch_a, w_branch_b, w_branch_c,
              w_out, final_norm_g):
    bsz = x.shape[0]
    meta = jnp.broadcast_to(meta_tokens.astype(x.dtype)[None], (bsz, N_META, D_MODEL))
    h = jnp.concatenate([meta, x], axis=1)
    for layer in range(DEPTH):
        h = h + _hybrid_mixer(_rmsnorm(h, norm_g[layer]), w_in[layer], gdn_conv_w[layer], gdn_a_log[layer],
                              gdn_dt_bias[layer], gdn_norm_g[layer], ssm_conv_w[layer], ssm_conv_b[layer],
                              ssm_a_log[layer], ssm_dt_bias[layer], ssm_d[layer], ssm_norm_g[layer],
                              w_branch_a[layer], w_branch_b[layer], w_branch_c[layer], w_out[layer])
    return _rmsnorm(h, final_norm_g)[:, N_META:]
```

```python
from contextlib import ExitStack
import numpy as np
import concourse.bass as bass
import concourse.mybir as mybir
from concourse.bass_utils import run_bass_kernel_spmd

F32 = mybir.dt.float32
BF16 = mybir.dt.bfloat16
AF = mybir.ActivationFunctionType
ALU = mybir.AluOpType

D = 1024
SEQ = 2048
NMETA = 16
PAD = 112
LP = SEQ + NMETA + PAD
NT = LP // 128
DIN = 15920
DEPTH = 4
TG = [(0, 512), (512, 512), (1024, 512), (1536, 512), (2048, 128)]

O_SBQ, O_SBK, O_SBV, O_SBZ = 0, 1024, 2048, 3072
O_GQ, O_GK, O_GV, O_GZ, O_GB, O_GA = 4096, 5120, 6144, 7168, 8192, 8200
O_SZ, O_SX, O_SB, O_SC, O_SDT, O_GATE = 8208, 10256, 12304, 12560, 12816, 12848


class Res:
    __slots__ = ("name", "last_w", "readers", "sem", "dcount", "excl")

    def __init__(self, name):
        self.name = name
        self.excl = False
        self.last_w = None
        self.readers = []
        self.sem = None
        self.dcount = 0


class Op:
    __slots__ = ("eng", "fn", "deps", "is_dma", "sem", "val", "signal", "phase", "pe_mm")


class Prog:
    ENGS = ("pe", "act", "dve", "pool", "sp")

    def __init__(self, nc, es):
        self.nc = nc
        self.es = es
        self.ops = []
        self.phase = 0
        self.bar_deps = []
        self.last_on_eng = {}
        self.dmas_since_bar = []
        self.phase_sems = {}
        self.nsem = 0

    def new_sem(self, name):
        self.nsem += 1
        return self.es.enter_context(self.nc.semaphore(name))

    def res(self, name, dma=False):
        r = Res(name)
        if dma:
            r.sem = self.new_sem("d_" + name)
        return r

    def new_phase(self):
        self.phase += 1

    def _track(self, op, reads, writes):
        ex = [r for r in reads if r.excl and r not in writes]
        if ex:
            reads = [r for r in reads if not r.excl]
            writes = list(writes) + ex
        deps = set(self.bar_deps)
        for r in list(reads) + list(writes):
            if r.last_w is not None:
                deps.add(r.last_w)
        for w in writes:
            for rd in w.readers:
                deps.add(rd)
        idx = len(self.ops)
        deps.discard(idx)
        op.deps = deps
        for r in reads:
            r.readers.append(idx)
        for w in writes:
            w.last_w = idx
            w.readers = []
        self.ops.append(op)
        self.last_on_eng[op.eng] = idx
        return idx

    def op(self, eng, fn, reads=(), writes=(), mm=False):
        o = Op()
        o.eng = eng
        o.fn = fn
        o.is_dma = False
        o.sem = None
        o.val = 0
        o.signal = False
        o.phase = self.phase
        o.pe_mm = mm
        return self._track(o, reads, writes)

    def dma(self, eng, out, in_, semres, reads=(), writes=(), slow=False):
        o = Op()
        o.eng = eng
        if slow:
            o.fn = lambda e: e.dma_start(out=out, in_=in_, allow_slow_non_contiguous=True)
        else:
            o.fn = lambda e: e.dma_start(out=out, in_=in_)
        o.is_dma = True
        o.sem = semres.sem
        semres.dcount += 1
        o.val = 16 * semres.dcount
        o.signal = True
        o.phase = self.phase
        o.pe_mm = False
        idx = self._track(o, reads, writes)
        self.dmas_since_bar.append(idx)
        return idx

    def barrier(self):
        self.bar_deps = list(self.last_on_eng.values()) + list(self.dmas_since_bar)
        self.dmas_since_bar = []

    def emit(self):
        nc = self.nc
        ops = self.ops
        for i, o in enumerate(ops):
            for j in o.deps:
                p = ops[j]
                if p.is_dma:
                    continue
                if p.eng == "pe" and o.eng == "pe":
                    continue
                p.signal = True
        cnt = {}
        for o in ops:
            if o.is_dma or not o.signal:
                continue
            key = (o.eng, o.phase)
            if key not in self.phase_sems:
                self.phase_sems[key] = self.new_sem("s_%s_%d" % key)
                cnt[key] = 0
            cnt[key] += 1
            o.sem = self.phase_sems[key]
            o.val = cnt[key]
        self.counts = dict(cnt)
        per_eng = {e: [] for e in self.ENGS}
        for i, o in enumerate(ops):
            per_eng[o.eng].append(i)
        final_waits = {}
        for o in ops:
            if o.is_dma:
                final_waits[id(o.sem)] = (o.sem, max(o.val, final_waits.get(id(o.sem), (None, 0))[1]))

        def run(engname, e):
            known = {}
            for i in per_eng[engname]:
                o = ops[i]
                need = {}
                for j in o.deps:
                    p = ops[j]
                    if p.sem is None:
                        continue
                    k = id(p.sem)
                    if k not in need or need[k][1] < p.val:
                        need[k] = (p.sem, p.val)
                for k, (s, v) in need.items():
                    if known.get(k, 0) < v:
                        e.wait_ge(s, v)
                        known[k] = v
                ins = o.fn(e)
                if o.is_dma:
                    ins.then_inc(o.sem, 16)
                elif o.signal:
                    ins.then_inc(o.sem, 1)
            if engname == "sp":
                for k, (s, v) in final_waits.items():
                    e.wait_ge(s, v)

        with nc.Block() as block:
            @block.tensor
            def _(e):
                run("pe", e)

            @block.scalar
            def _(e):
                run("act", e)

            @block.vector
            def _(e):
                run("dve", e)

            @block.gpsimd
            def _(e):
                run("pool", e)

            @block.sync
            def _(e):
                run("sp", e)


class Carver:
    def __init__(self, prog, ap, nwords):
        self.prog = prog
        self.ap = ap
        self.n = nwords
        self.off = 0
        self.k = 0

    def reset(self):
        self.off = 0

    def f32(self, cols, name="w"):
        a = self.ap[:, self.off:self.off + cols]
        self.off += cols
        assert self.off <= self.n, ("work overflow", self.off, self.n)
        self.k += 1
        return a, self.prog.res("%s%d" % (name, self.k))

    def bf16(self, cols, name="w"):
        words = (cols + 1) // 2
        a = self.ap[:, self.off:self.off + words].bitcast(BF16)
        self.off += words
        assert self.off <= self.n, ("work overflow", self.off, self.n)
        self.k += 1
        return a[:, 0:cols], self.prog.res("%s%d" % (name, self.k))


def build(depth=DEPTH, dbg=None, stop_after=None, GSTOP=0, skip=()):
    nc = bass.Bass("TRN2", target_bir_lowering=False)
    es = ExitStack()
    P = Prog(nc, es)

    def din(name, shape):
        return nc.dram_tensor(name, list(shape), F32, kind="ExternalInput").ap()

    x_d = din("x", (SEQ, D))
    meta_d = din("meta_tokens", (NMETA, D))
    norm_g_d = din("norm_g", (DEPTH, D))
    w_in_d = din("w_in", (DEPTH, D, DIN))
    gdn_conv_w_d = din("gdn_conv_w", (DEPTH, 4, 3072))
    gdn_a_log_d = din("gdn_a_log", (DEPTH, 8))
    gdn_dt_bias_d = din("gdn_dt_bias", (DEPTH, 8))
    gdn_norm_g_d = din("gdn_norm_g", (DEPTH, 128))
    ssm_conv_w_d = din("ssm_conv_w", (DEPTH, 4, 2560))
    ssm_conv_b_d = din("ssm_conv_b", (DEPTH, 2560))
    ssm_a_log_d = din("ssm_a_log", (DEPTH, 32))
    ssm_dt_bias_d = din("ssm_dt_bias", (DEPTH, 32))
    ssm_d_d = din("ssm_d", (DEPTH, 32))
    ssm_norm_g_d = din("ssm_norm_g", (DEPTH, 2048))
    w_br_d = [din("w_branch_a", (DEPTH, 1024, D)), din("w_branch_b", (DEPTH, 1024, D)),
              din("w_branch_c", (DEPTH, 2048, D))]
    w_out_d = din("w_out", (DEPTH, D, D))
    fin_g_d = din("final_norm_g", (1, D))
    out_d = nc.dram_tensor("out", [SEQ, D], F32, kind="ExternalOutput").ap()
    dbg_d = None
    if dbg is not None:
        dbg_d = nc.dram_tensor("dbg", list(dbg["shape"]), F32, kind="ExternalOutput").ap()

    def sb(name, shape, dt):
        return es.enter_context(nc.sbuf_tensor(name, list(shape), dt))

    def ps(name, shape, dt=F32):
        return es.enter_context(nc.psum_tensor(name, list(shape), dt))

    def OP(eng, meth, reads=(), writes=(), **kw):
        return P.op(eng, lambda e: getattr(e, meth)(**kw), reads=reads, writes=writes)

    def MM(out, lhsT, rhs, start, stop, reads, writes):
        return P.op("pe", lambda e: e.matmul(out=out, lhsT=lhsT, rhs=rhs, start=start, stop=stop),
                    reads=reads, writes=writes, mm=True)

    def TR(out, in_, ident, reads, writes):
        return P.op("pe", lambda e: e.transpose(out=out, in_=in_, identity=ident),
                    reads=reads, writes=writes, mm=True)

    h_sb = sb("h", (128, NT, D), F32)
    uT = sb("uT", (128, 8, LP), BF16)
    oT = sb("oT", (128, 8, LP), BF16)
    stage = [sb("stg%d" % i, (128, 8, 128), F32) for i in range(2)]
    wslab = [sb("wsl%d" % i, (128, 8, 128), BF16) for i in range(2)]
    WORKW = 13312
    work = sb("work", (128, WORKW), F32)
    ident_f = sb("ident_f", (128, 128), F32)
    ident_b = sb("ident_b", (128, 128), BF16)
    ones_b = sb("ones_b", (128, 128), BF16)
    ones_f = sb("ones_f", (128, 128), F32)
    uincl_f = sb("uincl_f", (128, 128), F32)
    tmat = sb("tmat", (128, 4, 128), BF16)
    mw = sb("mw", (128, 896), BF16)
    small = sb("small", (128, 64), F32)
    gfm = sb("gfm", (128, 8), F32)

    h_res = [P.res("h%d" % t) for t in range(NT)]
    hload_res = [P.res("hl%d" % i, dma=True) for i in range(5)]
    uT_res = [P.res("uT%d" % t) for t in range(NT)]
    oT_res = [P.res("oT%d" % k) for k in range(8)]
    stage_res = [P.res("stg%d" % i, dma=True) for i in range(2)]
    wslab_res = [P.res("wsl%d" % i) for i in range(2)]
    const_res = P.res("const")
    gfm_res = P.res("gfm", dma=True)
    small_res = P.res("small")
    dbg_res = P.res("dbg", dma=True)
    prm_res = [P.res("prm%d" % i, dma=True) for i in range(8)]
    out_res = [P.res("outst%d" % i, dma=True) for i in range(2)]

    banks = [ps("bank%d" % i, (128, 512), F32) for i in range(8)]
    bank_res = [P.res("bank%d" % i) for i in range(8)]
    for _r in bank_res:
        _r.excl = True
    carve = Carver(P, work, WORKW)
    bctr = [0]

    def nb(nmax=8):
        i = bctr[0] % nmax
        bctr[0] += 1
        return banks[i], bank_res[i]

    def bfv(bank):
        return bank[:].bitcast(BF16)

    cr = [const_res]
    OP("pool", "memset", writes=cr, ap=ident_f[:], constant=1.0)
    OP("pool", "affine_select", writes=cr, out=ident_f[:], in_=ident_f[:], pattern=[[-1, 128]],
       compare_op=ALU.is_equal, fill=0.0, base=0, channel_multiplier=1)
    OP("pool", "tensor_copy", reads=cr, writes=cr, out=ident_b[:], in_=ident_f[:])
    OP("pool", "memset", writes=cr, ap=ones_b[:], constant=1.0)
    OP("pool", "memset", writes=cr, ap=ones_f[:], constant=1.0)
    OP("pool", "memset", writes=cr, ap=uincl_f[:], constant=1.0)
    OP("pool", "affine_select", writes=cr, out=uincl_f[:], in_=uincl_f[:], pattern=[[1, 128]],
       compare_op=ALU.is_ge, fill=0.0, base=0, channel_multiplier=-1)
    OP("pool", "memset", writes=cr, ap=tmat[:], constant=-1.0)
    for v in (0, 2):
        OP("pool", "affine_select", writes=cr, out=tmat[:, v, :], in_=tmat[:, v, :], pattern=[[-1, 128]],
           compare_op=ALU.is_gt, fill=0.0, base=0, channel_multiplier=1)
    for v in (1, 3):
        OP("pool", "affine_select", writes=cr, out=tmat[:, v, :], in_=tmat[:, v, :], pattern=[[1, 128]],
           compare_op=ALU.is_ge, fill=0.0, base=0, channel_multiplier=-1)
    OP("pool", "memset", writes=cr, ap=tmat[0:PAD, 2:4, :], constant=0.0)
    OP("pool", "memset", writes=cr, ap=mw[:], constant=1.0)
    OP("pool", "affine_select", writes=cr, out=mw[:], in_=mw[:], pattern=[[1, 896]],
       compare_op=ALU.is_gt, fill=0.0, base=-384, channel_multiplier=-1)

    OP("pool", "memset", writes=[h_res[0]], ap=h_sb[:, 0, :], constant=0.0)
    P.dma("sp", h_sb[PAD:128, 0, :], meta_d[:, :], hload_res[4], writes=[h_res[0]])
    xv = x_d.rearrange("(t p) d -> p t d", p=128)
    for i in range(4):
        P.dma("sp", h_sb[:, 1 + 4 * i:5 + 4 * i, :], xv[:, 4 * i:4 * i + 4, :], hload_res[i],
              writes=[h_res[1 + 4 * i + j] for j in range(4)])

    wctr = [0]

    def stream_slab(src, C, dst=None, dst_res=None):
        i = wctr[0] % 2
        wctr[0] += 1
        P.dma("sp", stage[i][:, :, 0:C], src, stage_res[i], writes=[stage_res[i]])
        if dst is None:
            dst, dst_res = wslab[i][:, :, 0:C], wslab_res[i]
        OP("pool", "tensor_copy", reads=[stage_res[i]], writes=[dst_res], out=dst,
           in_=stage[i][:, :, 0:C])
        return dst, dst_res

    def win(layer, c0, C):
        return w_in_d[layer].rearrange("(k p) c -> p k c", p=128)[:, :, c0:c0 + C]

    def tiles_of(t0, n):
        return uT_res[t0 // 128:(t0 + n + 127) // 128]

    def proj_fm(layer, c0, evac, ncol=128):
        w, wr = stream_slab(win(layer, c0, ncol), ncol)
        for (t0, n) in TG:
            b, br = nb()
            for k in range(8):
                MM(b[0:ncol, 0:n], w[:, k, :], uT[:, k, t0:t0 + n], k == 0, k == 7,
                   reads=[wr] + tiles_of(t0, n), writes=[br])
            evac(b[0:ncol, 0:n], br, t0, n)

    def proj_tm(layer, c0, ncol, evac):
        w, wr = stream_slab(win(layer, c0, ncol), ncol)
        per = min(4, 512 // ncol)
        for tt0 in range(0, NT, per):
            cnt = min(per, NT - tt0)
            b, br = nb()
            for j in range(cnt):
                tt = tt0 + j
                for k in range(8):
                    MM(b[:, j * ncol:(j + 1) * ncol], uT[:, k, tt * 128:(tt + 1) * 128], w[:, k, :],
                       k == 0, k == 7, reads=[wr, uT_res[tt]], writes=[br])
            evac(b[:, 0:cnt * ncol], br, tt0, cnt)

    def load_small(dst, src, ri, slow=True):
        P.dma("sp", dst, src, prm_res[ri], writes=[prm_res[ri]], slow=slow)

    def layer_norm_T(layer):
        P.barrier()
        carve.reset()
        P.dma("sp", gfm[:, :], norm_g_d[layer].rearrange("(k p) -> p k", p=128), gfm_res,
              writes=[gfm_res], slow=True)
        junk, junk_r = carve.f32(1024, "junk")
        uns = [carve.bf16(1024, "un") for _ in range(2)]
        for tt in range(NT):
            un, un_r = uns[tt % 2]
            ss = small[:, 0:1]
            rs = small[:, 1:2]
            OP("act", "activation", reads=[h_res[tt]], writes=[junk_r, small_res], out=junk,
               in_=h_sb[:, tt, :], func=AF.Square, accum_out=ss)
            OP("act", "activation", reads=[small_res], writes=[small_res], out=rs, in_=ss,
               func=AF.Sqrt, scale=1.0 / D, bias=1e-6)
            OP("dve", "reciprocal", reads=[small_res], writes=[small_res], out=rs, in_=rs)
            OP("dve", "tensor_scalar", reads=[h_res[tt], small_res], writes=[un_r], out=un,
               in0=h_sb[:, tt, :], scalar1=rs, scalar2=None, op0=ALU.mult)
            b, br = nb()
            pT = bfv(b)
            for k in range(8):
                TR(pT[:, k * 128:(k + 1) * 128], un[:, k * 128:(k + 1) * 128], ident_b[:],
                   reads=[un_r, const_res], writes=[br])
            OP("dve", "tensor_tensor", reads=[br, gfm_res], writes=[uT_res[tt]],
               out=uT[:, :, tt * 128:(tt + 1) * 128], in0=pT.rearrange("p (k t) -> p k t", t=128),
               in1=gfm[:, :].unsqueeze(2).to_broadcast([128, 8, 128]), op=ALU.mult)

    def unit_attention(layer):
        P.barrier()
        carve.reset()
        qT, qT_r = carve.bf16(LP, "qT")
        kT, kT_r = carve.bf16(LP, "kT")
        zg, zg_r = carve.bf16(LP, "zg")
        vtm_flat, vtm_r = carve.bf16(NT * 128, "vtm")
        vtm = vtm_flat.rearrange("p (t d) -> p t d", d=128)
        spf = [carve.f32(512, "spf") for _ in range(2)]
        spb = [carve.bf16(512, "spb") for _ in range(2)]
        t2 = [carve.f32(512, "t2") for _ in range(2)]
        wb = [carve.bf16(512, "wb") for _ in range(2)]
        scale = 128.0 ** -0.5
        A_b, A_r = banks[6], bank_res[6]
        O_b, O_r = banks[7], bank_res[7]
        it = [0]
        for h in range(8):
            proj_fm(layer, O_SBQ + h * 128, lambda b, br, t0, n: OP(
                "act", "activation", reads=[br], writes=[qT_r], out=qT[:, t0:t0 + n], in_=b,
                func=AF.Copy, scale=scale))
            proj_fm(layer, O_SBK + h * 128, lambda b, br, t0, n: OP(
                "dve", "tensor_copy", reads=[br], writes=[kT_r], out=kT[:, t0:t0 + n], in_=b))
            proj_fm(layer, O_SBZ + h * 128, lambda b, br, t0, n: OP(
                "act", "activation", reads=[br], writes=[zg_r], out=zg[:, t0:t0 + n], in_=b,
                func=AF.Silu))
            proj_tm(layer, O_SBV + h * 128, 128, lambda b, br, tt0, cnt: OP(
                "dve", "tensor_copy", reads=[br], writes=[vtm_r], out=vtm[:, tt0:tt0 + cnt, :],
                in_=b.rearrange("p (t d) -> p t d", d=128)))
            for g in range(5):
                qb0 = 4 * g
                nq = 512 if g < 4 else 128
                nblk = nq // 128
                kmax = qb0 + nblk - 1
                q0 = qb0 * 128
                for idx, kb in enumerate(range(kmax, -1, -1)):
                    i2 = it[0] % 2
                    it[0] += 1
                    zb, zr = nb(6)
                    sp_a, sp_r = spf[i2]
                    spb_a, spb_r = spb[i2]
                    t2_a, t2_r = t2[i2]
                    wb_a, wb_r = wb[i2]
                    MM(zb[:, 0:nq], kT[:, kb * 128:(kb + 1) * 128], qT[:, q0:q0 + nq], True, True,
                       reads=[kT_r, qT_r], writes=[zr])
                    OP("act", "activation", reads=[zr], writes=[sp_r], out=sp_a[:, 0:nq],
                       in_=zb[:, 0:nq], func=AF.Exp)
                    OP("act", "activation", reads=[sp_r], writes=[sp_r], out=sp_a[:, 0:nq],
                       in_=sp_a[:, 0:nq], func=AF.Ln, bias=1.0)
                    diag = kb >= qb0
                    if diag:
                        moff = 384 - 128 * (kb - qb0)
                        OP("dve", "tensor_tensor", reads=[sp_r, const_res], writes=[spb_r],
                           out=spb_a[:, 0:nq], in0=sp_a[:, 0:nq], in1=mw[:, moff:moff + nq],
                           op=ALU.mult)
                    else:
                        OP("dve", "tensor_copy", reads=[sp_r], writes=[spb_r], out=spb_a[:, 0:nq],
                           in_=sp_a[:, 0:nq])
                    tv = 2 if kb == 0 else 0
                    MM(A_b[:, 0:nq], tmat[:, tv, :], spb_a[:, 0:nq], idx == 0, False,
                       reads=[spb_r, const_res], writes=[A_r])
                    OP("dve", "tensor_tensor", reads=[zr, sp_r], writes=[t2_r], out=t2_a[:, 0:nq],
                       in0=zb[:, 0:nq], in1=sp_a[:, 0:nq], op=ALU.subtract)
                    OP("dve", "tensor_tensor", reads=[A_r, t2_r], writes=[t2_r], out=t2_a[:, 0:nq],
                       in0=A_b[:, 0:nq], in1=t2_a[:, 0:nq], op=ALU.add)
                    OP("act", "activation", reads=[t2_r], writes=[wb_r], out=wb_a[:, 0:nq],
                       in_=t2_a[:, 0:nq], func=AF.Exp)
                    if diag:
                        OP("pool", "tensor_tensor", reads=[wb_r, const_res], writes=[wb_r],
                           out=wb_a[:, 0:nq], in0=wb_a[:, 0:nq], in1=mw[:, moff:moff + nq],
                           op=ALU.mult)
                    MM(A_b[:, 0:nq], tmat[:, tv + 1, :], spb_a[:, 0:nq], False, kb == 0,
                       reads=[spb_r, const_res], writes=[A_r])
                    MM(O_b[:, 0:nq], vtm[:, kb, :], wb_a[:, 0:nq], idx == 0, kb == 0,
                       reads=[vtm_r, wb_r], writes=[O_r])
                OP("dve", "tensor_tensor", reads=[O_r, zg_r], writes=[oT_res[h]],
                   out=oT[:, h, q0:q0 + nq], in0=O_b[:, 0:nq], in1=zg[:, q0:q0 + nq], op=ALU.mult)

    def unit_merge(layer, br_idx, row0, gate_c0):
        P.barrier()
        carve.reset()
        wbr_f, wbr_r = carve.bf16(8 * 1024, "wbr")
        wg_f, wg_r = carve.bf16(8 * 1024, "wg")
        wbr = wbr_f.rearrange("p (k c) -> p k c", c=1024)
        wg = wg_f.rearrange("p (k c) -> p k c", c=1024)
        sgs = [carve.bf16(1024, "sg") for _ in range(2)]
        src_br = w_br_d[br_idx][layer].rearrange("(k p) c -> p k c", p=128)
        for cb in range(8):
            stream_slab(src_br[:, row0 // 128:row0 // 128 + 8, cb * 128:(cb + 1) * 128], 128,
                        dst=wbr[:, :, cb * 128:(cb + 1) * 128], dst_res=wbr_r)
            stream_slab(win(layer, gate_c0 + cb * 128, 128), 128,
                        dst=wg[:, :, cb * 128:(cb + 1) * 128], dst_res=wg_r)
        for tt in range(NT):
            tok = slice(tt * 128, (tt + 1) * 128)
            pb = [nb(), nb()]
            gb = [nb(), nb()]
            for cb in range(8):
                b, br = pb[cb // 4]
                for fk in range(8):
                    MM(b[:, (cb % 4) * 128:(cb % 4 + 1) * 128], wbr[:, fk, cb * 128:(cb + 1) * 128],
                       oT[:, fk, tok], fk == 0, fk == 7, reads=[wbr_r, oT_res[fk]], writes=[br])
            for cb in range(8):
                b, br = gb[cb // 4]
                for k in range(8):
                    MM(b[:, (cb % 4) * 128:(cb % 4 + 1) * 128], wg[:, k, cb * 128:(cb + 1) * 128],
                       uT[:, k, tok], k == 0, k == 7, reads=[wg_r, uT_res[tt]], writes=[br])
            sg, sg_r = sgs[tt % 2]
            for hf in range(2):
                OP("act", "activation", reads=[gb[hf][1]], writes=[sg_r],
                   out=sg[:, hf * 512:(hf + 1) * 512], in_=gb[hf][0][:, :], func=AF.Sigmoid)
            for hf in range(2):
                OP("dve", "tensor_tensor", reads=[pb[hf][1], sg_r],
                   writes=[oT_res[4 * hf + j] for j in range(4)],
                   out=oT[:, 4 * hf:4 * hf + 4, tok],
                   in0=pb[hf][0][:, :].rearrange("p (c t) -> p c t", t=128),
                   in1=sg[:, hf * 512:(hf + 1) * 512].rearrange("p (c t) -> p c t", t=128),
                   op=ALU.mult)
        src_o = w_out_d[layer].rearrange("(k p) c -> p k c", p=128)
        for ds in range(8):
            w, wr = stream_slab(src_o[:, :, ds * 128:(ds + 1) * 128], 128)
            for tt0 in range(0, NT, 4):
                cnt = min(4, NT - tt0)
                b, br = nb()
                for j in range(cnt):
                    tt = tt0 + j
                    for cb in range(8):
                        MM(b[:, j * 128:(j + 1) * 128], oT[:, cb, tt * 128:(tt + 1) * 128], w[:, cb, :],
                           cb == 0, cb == 7, reads=[wr, oT_res[cb]], writes=[br])
                OP("dve", "tensor_tensor", reads=[br] + h_res[tt0:tt0 + cnt], writes=h_res[tt0:tt0 + cnt],
                   out=h_sb[:, tt0:tt0 + cnt, ds * 128:(ds + 1) * 128],
                   in0=b[:, 0:cnt * 128].rearrange("p (t d) -> p t d", d=128),
                   in1=h_sb[:, tt0:tt0 + cnt, ds * 128:(ds + 1) * 128], op=ALU.add)


    def bc_rows(src_row_ap, n):
        return src_row_ap.to_broadcast([128, n])

    def conv_silu(b, br, n, raw, raw_r, acc, acc_r, cw4, prm_r, bias, out_ap, out_res):
        OP("act", "activation", reads=[br], writes=[raw_r], out=raw[:, 3:3 + n], in_=b, func=AF.Copy)
        if bias is None:
            OP("dve", "tensor_scalar", reads=[raw_r, prm_r], writes=[acc_r], out=acc[:, 0:n],
               in0=raw[:, 3:3 + n], scalar1=cw4[:, 3:4], scalar2=None, op0=ALU.mult)
        else:
            OP("dve", "tensor_scalar", reads=[raw_r, prm_r], writes=[acc_r], out=acc[:, 0:n],
               in0=raw[:, 3:3 + n], scalar1=cw4[:, 3:4], scalar2=bias, op0=ALU.mult, op1=ALU.add)
        for j in (2, 1, 0):
            OP("dve", "scalar_tensor_tensor", reads=[raw_r, prm_r, acc_r], writes=[acc_r],
               out=acc[:, 0:n], in0=raw[:, j:j + n], scalar=cw4[:, j:j + 1], in1=acc[:, 0:n],
               op0=ALU.mult, op1=ALU.add)
        OP("act", "activation", reads=[acc_r], writes=[out_res], out=out_ap, in_=acc[:, 0:n],
           func=AF.Silu)
        OP("pool", "tensor_copy", reads=[raw_r], writes=[raw_r], out=raw[:, 0:3], in_=raw[:, n:n + 3])

    def load_conv_w(dst, dst_r, src2d, nblk, tmp, tmp_r, ri):
        P.dma("sp", tmp[0:nblk, :], src2d.rearrange("j (b p) -> b j p", p=128), prm_res[ri],
              writes=[tmp_r])
        b, br = nb()
        for j in range(4):
            TR(b[:, j * nblk:(j + 1) * nblk], tmp[0:nblk, j * 128:(j + 1) * 128], ident_f[0:nblk, 0:nblk],
               reads=[tmp_r, const_res], writes=[br])
        OP("dve", "tensor_copy", reads=[br], writes=[dst_r], out=dst, in_=b[:, 0:4 * nblk])

    def unit_gdn(layer):
        P.barrier()
        carve.reset()
        ba, ba_r = carve.f32(NT * 16, "ba")
        ba3 = ba.rearrange("p (t c) -> p t c", c=16)

        def sm(name):
            a, r = carve.f32(NT * 8, name)
            return a, a.rearrange("p (t c) -> p t c", c=8), r
        xa, xa3, xa_r = sm("xa")
        lb, lb3, lb_r = sm("lb")
        g_all, g3, g_r = sm("g")
        gc, gc3, gc_r = sm("gc")
        ngc, ngc3, ngc_r = sm("ngc")
        gb_, gb3, gb_r = sm("gb")
        egb, egb3, egb_r = sm("egb")
        beta, beta3, beta_r = sm("beta")
        alog, alog_r = carve.f32(8, "alog")
        dtb, dtb_r = carve.f32(8, "dtb")
        gng, gng_r = carve.f32(1, "gng")
        cw, cw_r = carve.f32(96, "cw")
        cw3 = cw.rearrange("p (j b) -> p j b", b=24)
        cwt, cwt_r = carve.f32(512, "cwt")
        load_small(alog, bc_rows(gdn_a_log_d[layer:layer + 1, :], 8), 0)
        load_small(dtb, bc_rows(gdn_dt_bias_d[layer:layer + 1, :], 8), 1)
        load_small(gng, gdn_norm_g_d[layer].rearrange("(p o) -> p o", o=1), 2)
        alog_r, dtb_r, gng_r = prm_res[0], prm_res[1], prm_res[2]
        load_conv_w(cw, cw_r, gdn_conv_w_d[layer], 24, cwt, cwt_r, 3)
        if GSTOP == 1:
            return
        proj_tm(layer, O_GB, 16, lambda b, br, tt0, cnt: OP(
            "dve", "tensor_copy", reads=[br], writes=[ba_r], out=ba[:, tt0 * 16:(tt0 + cnt) * 16], in_=b))
        OP("dve", "tensor_tensor", reads=[ba_r, dtb_r], writes=[xa_r], out=xa3, in0=ba3[:, :, 8:16],
           in1=dtb.unsqueeze(1).to_broadcast([128, NT, 8]), op=ALU.add)
        OP("act", "activation", reads=[xa_r], writes=[xa_r], out=xa, in_=xa, func=AF.Exp)
        OP("act", "activation", reads=[xa_r], writes=[xa_r], out=xa, in_=xa, func=AF.Ln, bias=1.0)
        OP("act", "activation", reads=[alog_r], writes=[alog_r], out=alog, in_=alog, func=AF.Exp)
        OP("dve", "scalar_tensor_tensor", reads=[xa_r, alog_r], writes=[g_r], out=g3, in0=xa3, scalar=-1.0,
           in1=alog.unsqueeze(1).to_broadcast([128, NT, 8]), op0=ALU.mult, op1=ALU.mult)
        OP("pool", "memset", reads=[], writes=[g_r], ap=g3[0:PAD, 0, :], constant=0.0)
        OP("act", "activation", reads=[ba_r], writes=[lb_r], out=lb3, in_=ba3[:, :, 0:8], func=AF.Exp,
           scale=-1.0)
        OP("act", "activation", reads=[lb_r], writes=[lb_r], out=lb, in_=lb, func=AF.Ln, bias=1.0)
        OP("act", "activation", reads=[lb_r], writes=[beta_r], out=beta, in_=lb, func=AF.Exp, scale=-1.0)
        b, br = nb()
        MM(b[:, 0:NT * 8], uincl_f[:], g_all, True, True, reads=[g_r, const_res], writes=[br])
        OP("dve", "tensor_copy", reads=[br], writes=[gc_r], out=gc, in_=b[:, 0:NT * 8])
        OP("dve", "tensor_scalar", reads=[gc_r], writes=[ngc_r], out=ngc, in0=gc, scalar1=-1.0,
           scalar2=None, op0=ALU.mult)
        OP("dve", "tensor_tensor", reads=[gc_r, lb_r], writes=[gb_r], out=gb_, in0=gc, in1=lb,
           op=ALU.subtract)
        OP("act", "activation", reads=[gb_r], writes=[egb_r], out=egb, in_=gb_, func=AF.Exp)
        if GSTOP == 2:
            return
        qT, qT_r = carve.bf16(LP, "qT")
        kT, kT_r = carve.bf16(LP, "kT")
        ktm_f, ktm_r = carve.bf16(NT * 128, "ktm")
        vtm_f, vtm_r = carve.bf16(NT * 128, "vtm")
        ktm = ktm_f.rearrange("p (t d) -> p t d", d=128)
        vtm = vtm_f.rearrange("p (t d) -> p t d", d=128)
        raw, raw_r = carve.f32(515, "raw")
        acc, acc_r = carve.f32(512, "acc")
        qs, qs_r = carve.f32(512, "qs")
        sqb, sqb_r = carve.bf16(512, "sqb")
        rn, rn_r = carve.f32(512, "rn")
        egbrow, egbrow_r = carve.f32(128, "egbrow")
        RwT, RwT_r = carve.f32(128, "RwT")
        dd, dd_r = carve.f32(128, "dd")
        vTt, vTt_r = carve.bf16(512, "vTt")
        zgt, zgt_r = carve.bf16(512, "zgt")
        dg, dg_r = carve.f32(256, "dg")
        E12, E12_r = carve.f32(256, "E12")
        eg, eg_r = carve.f32(128, "eg")
        AN = [carve.f32(256, "AN") for _ in range(2)]
        Pm = [carve.f32(128, "Pm") for _ in range(2)]
        aqkT, aqkT_r = carve.bf16(128, "aqkT")
        Rv, Rv_r = carve.f32(128, "Rv")
        vnew, vnew_r = carve.bf16(128, "vnew")
        qd, qd_r = carve.bf16(128, "qd")
        kend, kend_r = carve.bf16(128, "kend")
        S, S_r = carve.f32(128, "S")
        S_bf, Sbf_r = carve.bf16(128, "Sbf")
        glk, glk_r = carve.f32(2, "glk")
        gl = glk[:, 0:1]
        kes = glk[:, 1:2]

        def l2norm_to(dst, dst_r, t0, n, scl):
            OP("act", "activation", reads=[qs_r], writes=[sqb_r], out=sqb[:, 0:n], in_=qs[:, 0:n],
               func=AF.Square)
            b2, b2r = nb()
            MM(b2[:, 0:n], ones_b[:], sqb[:, 0:n], True, True, reads=[sqb_r, const_res], writes=[b2r])
            OP("act", "activation", reads=[b2r], writes=[rn_r], out=rn[:, 0:n], in_=b2[:, 0:n],
               func=AF.Sqrt, bias=1e-6)
            OP("dve", "reciprocal", reads=[rn_r], writes=[rn_r], out=rn[:, 0:n], in_=rn[:, 0:n])
            OP("dve", "scalar_tensor_tensor", reads=[qs_r, rn_r], writes=[dst_r], out=dst[:, t0:t0 + n],
               in0=qs[:, 0:n], scalar=scl, in1=rn[:, 0:n], op0=ALU.mult, op1=ALU.mult)

        def to_tm(srcT, src_r, c0, n, dst3, dst_r, tt0):
            b2, b2r = nb()
            v = bfv(b2)
            cnt = n // 128
            for j in range(cnt):
                TR(v[:, j * 128:(j + 1) * 128], srcT[:, c0 + j * 128:c0 + (j + 1) * 128], ident_b[:],
                   reads=[src_r, const_res], writes=[b2r])
            OP("dve", "tensor_copy", reads=[b2r], writes=[dst_r], out=dst3[:, tt0:tt0 + cnt, :],
               in_=v[:, 0:n].rearrange("p (t d) -> p t d", d=128))

        for h in range(8):
            OP("pool", "memset", writes=[raw_r], ap=raw[:, 0:3], constant=0.0)

            def ev_q(b, br, t0, n, h=h):
                conv_silu(b, br, n, raw, raw_r, acc, acc_r, cw3[:, :, h], cw_r, None, qs[:, 0:n], qs_r)
                l2norm_to(qT, qT_r, t0, n, 128.0 ** -0.5)
            proj_fm(layer, O_GQ + h * 128, ev_q)
            OP("pool", "memset", writes=[raw_r], ap=raw[:, 0:3], constant=0.0)

            def ev_k(b, br, t0, n, h=h):
                conv_silu(b, br, n, raw, raw_r, acc, acc_r, cw3[:, :, 8 + h], cw_r, None, qs[:, 0:n], qs_r)
                l2norm_to(kT, kT_r, t0, n, 1.0)
                to_tm(kT, kT_r, t0, n, ktm, ktm_r, t0 // 128)
            proj_fm(layer, O_GK + h * 128, ev_k)
            OP("pool", "memset", writes=[raw_r], ap=raw[:, 0:3], constant=0.0)

            def ev_v(b, br, t0, n, h=h):
                conv_silu(b, br, n, raw, raw_r, acc, acc_r, cw3[:, :, 16 + h], cw_r, None, vTt[:, 0:n], vTt_r)
                to_tm(vTt, vTt_r, 0, n, vtm, vtm_r, t0 // 128)
            proj_fm(layer, O_GV + h * 128, ev_v)
            OP("pool", "memset", writes=[S_r], ap=S, constant=0.0)
            OP("pool", "memset", writes=[Sbf_r], ap=S_bf, constant=0.0)
            if GSTOP == 3:
                return
            for c in range(NT):
                if GSTOP == 4 and c == 1:
                    return
                tok = slice(c * 128, (c + 1) * 128)
                gcol = gc3[:, c, h:h + 1]
                ngcol = ngc3[:, c, h:h + 1]
                gbcol = gb3[:, c, h:h + 1]
                OP("dve", "tensor_scalar", reads=[gc_r, const_res], writes=[dg_r], out=dg[:, 0:128],
                   in0=ident_f[:], scalar1=gcol, scalar2=None, op0=ALU.mult)
                OP("dve", "tensor_scalar", reads=[gb_r, const_res], writes=[dg_r], out=dg[:, 128:256],
                   in0=ident_f[:], scalar1=gbcol, scalar2=None, op0=ALU.mult)
                bG, bGr = nb()
                MM(bG[:, 0:256], ones_f[:], dg, True, True, reads=[dg_r, const_res], writes=[bGr])
                OP("act", "activation", reads=[bGr, ngc_r], writes=[E12_r], out=E12, in_=bG[:, 0:256],
                   func=AF.Exp, bias=ngcol)
                OP("pool", "affine_select", reads=[E12_r], writes=[E12_r], out=E12[:, 0:128],
                   in_=E12[:, 0:128], pattern=[[1, 128]], compare_op=ALU.is_ge, fill=0.0, base=0,
                   channel_multiplier=-1)
                OP("pool", "affine_select", reads=[E12_r], writes=[E12_r], out=E12[:, 128:256],
                   in_=E12[:, 128:256], pattern=[[1, 128]], compare_op=ALU.is_gt, fill=0.0, base=0,
                   channel_multiplier=-1)
                OP("act", "activation", reads=[bGr], writes=[eg_r], out=eg, in_=bG[:, 0:128], func=AF.Exp)
                OP("dve", "tensor_copy", reads=[bGr], writes=[glk_r], out=gl, in_=bG[:, 127:128])
                OP("act", "activation", reads=[gc_r, glk_r], writes=[glk_r], out=kes, in_=gcol,
                   func=AF.Exp, scale=-1.0, bias=gl)
                if GSTOP == 11:
                    return
                OP("act", "activation", reads=[bGr], writes=[egbrow_r], out=egbrow, in_=bG[:, 128:256],
                   func=AF.Exp)
                OP("dve", "tensor_tensor", reads=[kT_r, egbrow_r], writes=[RwT_r], out=RwT, in0=kT[:, tok],
                   in1=egbrow, op=ALU.mult)
                bK, bKr = nb()
                MM(bK[:, 0:128], kT[:, tok], qT[:, tok], True, True, reads=[kT_r, qT_r], writes=[bKr])
                MM(bK[:, 128:256], kT[:, tok], kT[:, tok], True, True, reads=[kT_r], writes=[bKr])
                OP("dve", "tensor_tensor", reads=[bKr, E12_r], writes=[aqkT_r], out=aqkT, in0=bK[:, 0:128],
                   in1=E12[:, 0:128], op=ALU.mult)
                an0, an0_r = AN[0]
                pm0, pm0_r = Pm[0]
                OP("dve", "scalar_tensor_tensor", reads=[bKr, E12_r], writes=[an0_r], out=an0[:, 0:128],
                   in0=bK[:, 128:256], scalar=-1.0, in1=E12[:, 128:256], op0=ALU.mult, op1=ALU.mult)
                OP("pool", "tensor_tensor", reads=[an0_r, const_res], writes=[pm0_r], out=pm0,
                   in0=an0[:, 0:128], in1=ident_f[:], op=ALU.add)
                bT, bTr = nb()
                TR(bT[:, 0:128], an0[:, 0:128], ident_f[:], reads=[an0_r, const_res], writes=[bTr])
                OP("act", "activation", reads=[bTr], writes=[an0_r], out=an0[:, 128:256],
                   in_=bT[:, 0:128], func=AF.Copy)
                if GSTOP == 12:
                    return
                ai, pi = 0, 0
                for p in (1, 2, 4, 8, 16, 32, 64):
                    an, an_r = AN[ai]
                    if p > 1:
                        pc, pc_r = Pm[pi]
                        pn, pn_r = Pm[1 - pi]
                        bP, bPr = nb()
                        MM(bP[:, 0:128], an[:, 128:256], pc, True, True, reads=[an_r, pc_r], writes=[bPr])
                        OP("dve", "tensor_tensor", reads=[bPr, pc_r], writes=[pn_r], out=pn,
                           in0=bP[:, 0:128], in1=pc, op=ALU.add)
                        pi = 1 - pi
                    if p < 64:
                        an2, an2_r = AN[1 - ai]
                        bX, bXr = nb()
                        MM(bX[:, 0:128], an[:, 128:256], an[:, 0:128], True, True, reads=[an_r], writes=[bXr])
                        MM(bX[:, 128:256], an[:, 0:128], an[:, 128:256], True, True, reads=[an_r],
                           writes=[bXr])
                        OP("act", "activation", reads=[bXr], writes=[an2_r], out=an2, in_=bX[:, 0:256],
                           func=AF.Copy)
                        ai = 1 - ai
                if GSTOP == 13:
                    return
                pf, pf_r = Pm[pi]
                OP("dve", "tensor_scalar", reads=[vtm_r, beta_r], writes=[Rv_r], out=Rv, in0=vtm[:, c, :],
                   scalar1=beta3[:, c, h:h + 1], scalar2=None, op0=ALU.mult)
                bS, bSr = nb()
                MM(bS[:, 0:128], RwT, S, True, True, reads=[RwT_r, S_r], writes=[bSr])
                OP("dve", "tensor_tensor", reads=[Rv_r, bSr], writes=[dd_r], out=dd, in0=Rv,
                   in1=bS[:, 0:128], op=ALU.subtract)
                bV, bVr = nb()
                MM(bV[:, 0:128], pf, dd, True, True, reads=[pf_r, dd_r], writes=[bVr])
                OP("act", "activation", reads=[bVr], writes=[vnew_r], out=vnew, in_=bV[:, 0:128], func=AF.Copy)
                OP("dve", "tensor_tensor", reads=[qT_r, eg_r], writes=[qd_r], out=qd, in0=qT[:, tok], in1=eg,
                   op=ALU.mult)
                OP("dve", "tensor_scalar", reads=[ktm_r, glk_r], writes=[kend_r], out=kend, in0=ktm[:, c, :],
                   scalar1=kes, scalar2=None, op0=ALU.mult)
                bO, bOr = nb()
                MM(bO[:, 0:128], S_bf, qd, True, False, reads=[Sbf_r, qd_r], writes=[bOr])
                MM(bO[:, 0:128], vnew, aqkT, False, True, reads=[vnew_r, aqkT_r], writes=[bOr])
                OP("act", "activation", reads=[bOr], writes=[oT_res[h]], out=oT[:, h, tok], in_=bO[:, 0:128],
                   func=AF.Copy)
                bD, bDr = nb()
                MM(bD[:, 0:128], kend, vnew, True, True, reads=[kend_r, vnew_r], writes=[bDr])
                OP("dve", "scalar_tensor_tensor", reads=[S_r, eg_r, bDr], writes=[S_r], out=S, in0=S,
                   scalar=eg[:, 127:128], in1=bD[:, 0:128], op0=ALU.mult, op1=ALU.add)
                OP("pool", "tensor_copy", reads=[S_r], writes=[Sbf_r], out=S_bf, in_=S)

            def ev_z(b, br, t0, n, h=h):
                OP("act", "activation", reads=[br], writes=[zgt_r], out=zgt[:, 0:n], in_=b, func=AF.Silu)
                OP("act", "activation", reads=[oT_res[h]], writes=[sqb_r], out=sqb[:, 0:n],
                   in_=oT[:, h, t0:t0 + n], func=AF.Square)
                b2, b2r = nb()
                MM(b2[:, 0:n], ones_b[:], sqb[:, 0:n], True, True, reads=[sqb_r, const_res], writes=[b2r])
                OP("act", "activation", reads=[b2r], writes=[rn_r], out=rn[:, 0:n], in_=b2[:, 0:n],
                   func=AF.Sqrt, scale=1.0 / 128, bias=1e-6)
                OP("dve", "reciprocal", reads=[rn_r], writes=[rn_r], out=rn[:, 0:n], in_=rn[:, 0:n])
                OP("dve", "scalar_tensor_tensor", reads=[oT_res[h], gng_r, rn_r], writes=[oT_res[h]],
                   out=oT[:, h, t0:t0 + n], in0=oT[:, h, t0:t0 + n], scalar=gng[:, 0:1], in1=rn[:, 0:n],
                   op0=ALU.mult, op1=ALU.mult)
                OP("dve", "tensor_tensor", reads=[oT_res[h], zgt_r], writes=[oT_res[h]],
                   out=oT[:, h, t0:t0 + n], in0=oT[:, h, t0:t0 + n], in1=zgt[:, 0:n], op=ALU.mult)
            proj_fm(layer, O_GZ + h * 128, ev_z)


    def unit_ssd(layer, g):
        P.barrier()
        carve.reset()
        dtr, dtr_r = carve.f32(NT * 32, "dtr")
        dt_, dt_r = carve.f32(NT * 32, "dt")
        cs, cs_r = carve.f32(NT * 32, "cs")
        ncs, ncs_r = carve.f32(NT * 32, "ncs")
        dt3 = dt_.rearrange("p (t c) -> p t c", c=32)
        dtr3 = dtr.rearrange("p (t c) -> p t c", c=32)
        cs3 = cs.rearrange("p (t c) -> p t c", c=32)
        ncs3 = ncs.rearrange("p (t c) -> p t c", c=32)
        alog, _ = carve.f32(32, "alog")
        dtb, _ = carve.f32(32, "dtb")
        dcol, _ = carve.f32(16, "dcol")
        ng, _ = carve.f32(16, "ng")
        cbt, cbt_r = carve.f32(128, "cbt")
        cw, cw_r = carve.f32(80, "cw")
        cw3 = cw.rearrange("p (j b) -> p j b", b=20)
        cbias, cbias_r = carve.f32(20, "cbias")
        cwt, cwt_r = carve.f32(512, "cwt")
        load_small(alog, bc_rows(ssm_a_log_d[layer:layer + 1, :], 32), 0)
        load_small(dtb, bc_rows(ssm_dt_bias_d[layer:layer + 1, :], 32), 1)
        dv = ssm_d_d[layer].rearrange("(b s) -> s b", s=2)
        P.dma("sp", dcol[0:64, :], dv[0:1, :].to_broadcast([64, 16]), prm_res[2], writes=[prm_res[2]], slow=True)
        P.dma("sp", dcol[64:128, :], dv[1:2, :].to_broadcast([64, 16]), prm_res[2], writes=[prm_res[2]],
              slow=True)
        load_small(ng, ssm_norm_g_d[layer].rearrange("(b p) -> p b", p=128), 4)
        alog_r, dtb_r, dcol_r, ng_r = prm_res[0], prm_res[1], prm_res[2], prm_res[4]
        load_conv_w(cw, cw_r, ssm_conv_w_d[layer], 20, cwt, cwt_r, 3)
        P.dma("sp", cbt[0:20, :], ssm_conv_b_d[layer].rearrange("(b p) -> b p", p=128), prm_res[5],
              writes=[cbt_r])
        b, br = nb()
        TR(b[:, 0:20], cbt[0:20, :], ident_f[0:20, 0:20], reads=[cbt_r, const_res], writes=[br])
        OP("dve", "tensor_copy", reads=[br], writes=[cbias_r], out=cbias, in_=b[:, 0:20])
        proj_tm(layer, O_SDT, 32, lambda b, br, tt0, cnt: OP(
            "dve", "tensor_copy", reads=[br], writes=[dtr_r], out=dtr[:, tt0 * 32:(tt0 + cnt) * 32], in_=b))
        OP("dve", "tensor_tensor", reads=[dtr_r, dtb_r], writes=[dt_r], out=dt3, in0=dtr3,
           in1=dtb.unsqueeze(1).to_broadcast([128, NT, 32]), op=ALU.add)
        OP("act", "activation", reads=[dt_r], writes=[dt_r], out=dt_, in_=dt_, func=AF.Exp)
        OP("act", "activation", reads=[dt_r], writes=[dt_r], out=dt_, in_=dt_, func=AF.Ln, bias=1.0)
        OP("pool", "memset", reads=[], writes=[dt_r], ap=dt3[0:PAD, 0, :], constant=0.0)
        OP("act", "activation", reads=[alog_r], writes=[alog_r], out=alog, in_=alog, func=AF.Exp)
        OP("dve", "scalar_tensor_tensor", reads=[dt_r, alog_r], writes=[dtr_r], out=dtr3, in0=dt3, scalar=-1.0,
           in1=alog.unsqueeze(1).to_broadcast([128, NT, 32]), op0=ALU.mult, op1=ALU.mult)
        for hf in range(2):
            w0 = hf * 272
            b, br = nb()
            MM(b[:, 0:272], uincl_f[:], dtr[:, w0:w0 + 272], True, True, reads=[dtr_r, const_res], writes=[br])
            OP("dve", "tensor_copy", reads=[br], writes=[cs_r], out=cs[:, w0:w0 + 272], in_=b[:, 0:272])
        OP("dve", "tensor_scalar", reads=[cs_r], writes=[ncs_r], out=ncs, in0=cs, scalar1=-1.0, scalar2=None,
           op0=ALU.mult)

        BT, BT_r = carve.bf16(LP, "BT")
        CT, CT_r = carve.bf16(LP, "CT")
        xT, xT_r = carve.bf16(LP, "xT")
        Btm_f, Btm_r = carve.bf16(NT * 128, "Btm")
        Btm = Btm_f.rearrange("p (t d) -> p t d", d=128)
        raw, raw_r = carve.f32(515, "raw")
        acc, acc_r = carve.f32(512, "acc")
        zgt, zgt_r = carve.bf16(512, "zgt")
        sqs = [carve.bf16(512, "sq") for _ in range(2)]
        rn, rn_r = carve.f32(512, "rn")
        dg, dg_r = carve.f32(256, "dg")
        E12, E12_r = carve.f32(256, "E12")
        eg, eg_r = carve.f32(256, "eg")
        aqkT, aqkT_r = carve.bf16(256, "aqkT")
        xs2, xs2_r = carve.bf16(256, "xs2")
        qd, qd_r = carve.bf16(256, "qd")
        kend, kend_r = carve.bf16(256, "kend")
        S, S_r = carve.f32(128, "S")
        S2, S2_r = carve.bf16(256, "S2")
        glk, glk_r = carve.f32(4, "glk")
        aqk3 = aqkT.rearrange("p (s l) -> p s l", l=128)
        xs3 = xs2.rearrange("p (s l) -> p s l", l=128)
        qd3 = qd.rearrange("p (s l) -> p s l", l=128)
        kend3 = kend.rearrange("p (s l) -> p s l", l=128)
        S23 = S2.rearrange("p (s l) -> p s l", l=128)
        E3 = E12.rearrange("p (s l) -> p s l", l=128)
        eg3 = eg.rearrange("p (s l) -> p s l", l=128)

        def to_tm(srcT, src_r, c0, n, dst3, dst_r, tt0):
            b2, b2r = nb()
            v = bfv(b2)
            cnt = n // 128
            for j in range(cnt):
                TR(v[:, j * 128:(j + 1) * 128], srcT[:, c0 + j * 128:c0 + (j + 1) * 128], ident_b[:],
                   reads=[src_r, const_res], writes=[b2r])
            OP("dve", "tensor_copy", reads=[b2r], writes=[dst_r], out=dst3[:, tt0:tt0 + cnt, :],
               in_=v[:, 0:n].rearrange("p (t d) -> p t d", d=128))

        OP("pool", "memset", writes=[raw_r], ap=raw[:, 0:3], constant=0.0)

        def ev_B(b, br, t0, n):
            conv_silu(b, br, n, raw, raw_r, acc, acc_r, cw3[:, :, 16 + g], cw_r, cbias[:, 16 + g:17 + g],
                      BT[:, t0:t0 + n], BT_r)
            to_tm(BT, BT_r, t0, n, Btm, Btm_r, t0 // 128)
        proj_fm(layer, O_SB + g * 128, ev_B)
        OP("pool", "memset", writes=[raw_r], ap=raw[:, 0:3], constant=0.0)
        proj_fm(layer, O_SC + g * 128, lambda b, br, t0, n: conv_silu(
            b, br, n, raw, raw_r, acc, acc_r, cw3[:, :, 18 + g], cw_r, cbias[:, 18 + g:19 + g],
            CT[:, t0:t0 + n], CT_r))
        OP("pool", "memset", writes=[xs2_r], ap=xs2, constant=0.0)
        OP("pool", "memset", writes=[S2_r], ap=S2, constant=0.0)
        for j in range(8):
            blk = g * 8 + j
            hh = [2 * blk, 2 * blk + 1]
            OP("pool", "memset", writes=[raw_r], ap=raw[:, 0:3], constant=0.0)
            proj_fm(layer, O_SX + blk * 128, lambda b, br, t0, n, blk=blk: conv_silu(
                b, br, n, raw, raw_r, acc, acc_r, cw3[:, :, blk], cw_r, cbias[:, blk:blk + 1],
                xT[:, t0:t0 + n], xT_r))
            OP("pool", "memset", writes=[S_r], ap=S, constant=0.0)
            for s2 in range(2):
                OP("pool", "memset", writes=[S2_r], ap=S23[:, s2, s2 * 64:(s2 + 1) * 64], constant=0.0)
            for c in range(NT):
                tok = slice(c * 128, (c + 1) * 128)
                for s2 in range(2):
                    OP("dve", "tensor_scalar", reads=[cs_r, const_res], writes=[dg_r],
                       out=dg[:, s2 * 128:(s2 + 1) * 128], in0=ident_f[:], scalar1=cs3[:, c, hh[s2]:hh[s2] + 1],
                       scalar2=None, op0=ALU.mult)
                bG, bGr = nb()
                MM(bG[:, 0:256], ones_f[:], dg, True, True, reads=[dg_r, const_res], writes=[bGr])
                for s2 in range(2):
                    OP("act", "activation", reads=[bGr, ncs_r], writes=[E12_r], out=E3[:, s2, :],
                       in_=bG[:, s2 * 128:(s2 + 1) * 128], func=AF.Exp, bias=ncs3[:, c, hh[s2]:hh[s2] + 1])
                OP("pool", "affine_select", reads=[E12_r], writes=[E12_r], out=E3, in_=E3,
                   pattern=[[0, 2], [1, 128]], compare_op=ALU.is_ge, fill=0.0, base=0, channel_multiplier=-1)
                OP("act", "activation", reads=[bGr], writes=[eg_r], out=eg, in_=bG[:, 0:256], func=AF.Exp)
                OP("dve", "tensor_copy", reads=[bGr], writes=[glk_r], out=glk[:, 0:2],
                   in_=bG[:, 0:256].rearrange("p (s l) -> p s l", l=128)[:, :, 127])
                for s2 in range(2):
                    OP("act", "activation", reads=[cs_r, glk_r], writes=[glk_r], out=glk[:, 2 + s2:3 + s2],
                       in_=cs3[:, c, hh[s2]:hh[s2] + 1], func=AF.Exp, scale=-1.0, bias=glk[:, s2:s2 + 1])
                bK, bKr = nb()
                MM(bK[:, 0:128], BT[:, tok], CT[:, tok], True, True, reads=[BT_r, CT_r], writes=[bKr])
                OP("dve", "tensor_tensor", reads=[bKr, E12_r], writes=[aqkT_r], out=aqk3,
                   in0=bK[:, 0:128].unsqueeze(1).to_broadcast([128, 2, 128]), in1=E3, op=ALU.mult)
                bT, bTr = nb()
                TR(bfv(bT)[:, 0:128], xT[:, tok], ident_b[:], reads=[xT_r, const_res], writes=[bTr])
                for s2 in range(2):
                    OP("dve", "tensor_scalar", reads=[bTr, dt_r], writes=[xs2_r],
                       out=xs3[:, s2, s2 * 64:(s2 + 1) * 64], in0=bfv(bT)[:, s2 * 64:(s2 + 1) * 64],
                       scalar1=dt3[:, c, hh[s2]:hh[s2] + 1], scalar2=None, op0=ALU.mult)
                OP("dve", "tensor_tensor", reads=[CT_r, eg_r], writes=[qd_r], out=qd3,
                   in0=CT[:, tok].unsqueeze(1).to_broadcast([128, 2, 128]), in1=eg3, op=ALU.mult)
                for s2 in range(2):
                    OP("dve", "tensor_scalar", reads=[Btm_r, glk_r], writes=[kend_r], out=kend3[:, s2, :],
                       in0=Btm[:, c, :], scalar1=glk[:, 2 + s2:3 + s2], scalar2=None, op0=ALU.mult)
                bO, bOr = nb()
                for s2 in range(2):
                    MM(bO[:, 0:128], S23[:, s2, :], qd3[:, s2, :], s2 == 0, False, reads=[S2_r, qd_r],
                       writes=[bOr])
                for s2 in range(2):
                    MM(bO[:, 0:128], xs3[:, s2, :], aqk3[:, s2, :], False, s2 == 1, reads=[xs2_r, aqkT_r],
                       writes=[bOr])
                OP("act", "activation", reads=[bOr], writes=[oT_res[j]], out=oT[:, j, tok], in_=bO[:, 0:128],
                   func=AF.Copy)
                bD, bDr = nb()
                for s2 in range(2):
                    MM(bD[:, s2 * 64:(s2 + 1) * 64], kend3[:, s2, :], xs3[:, s2, s2 * 64:(s2 + 1) * 64], True, True,
                       reads=[kend_r, xs2_r], writes=[bDr])
                for s2 in range(2):
                    cols = slice(s2 * 64, (s2 + 1) * 64)
                    OP("dve", "scalar_tensor_tensor", reads=[S_r, eg_r, bDr], writes=[S_r], out=S[:, cols],
                       in0=S[:, cols], scalar=eg3[:, s2, 127:128], in1=bD[:, cols], op0=ALU.mult, op1=ALU.add)
                for s2 in range(2):
                    cols = slice(s2 * 64, (s2 + 1) * 64)
                    OP("pool", "tensor_copy", reads=[S_r], writes=[S2_r], out=S23[:, s2, cols], in_=S[:, cols])

            def ev_z(b, br, t0, n, j=j, blk=blk):
                OP("act", "activation", reads=[br], writes=[zgt_r], out=zgt[:, 0:n], in_=b, func=AF.Silu)
                OP("dve", "scalar_tensor_tensor", reads=[xT_r, dcol_r, oT_res[j]], writes=[oT_res[j]],
                   out=oT[:, j, t0:t0 + n], in0=xT[:, t0:t0 + n], scalar=dcol[:, blk:blk + 1],
                   in1=oT[:, j, t0:t0 + n], op0=ALU.mult, op1=ALU.add)
                OP("dve", "tensor_tensor", reads=[oT_res[j], zgt_r], writes=[oT_res[j]],
                   out=oT[:, j, t0:t0 + n], in0=oT[:, j, t0:t0 + n], in1=zgt[:, 0:n], op=ALU.mult)
            proj_fm(layer, O_SZ + blk * 128, ev_z)
        for (t0, n) in TG:
            bq, bqr = nb()
            for j in range(8):
                sq, sq_r = sqs[j % 2]
                OP("dve", "tensor_tensor", reads=[oT_res[j]], writes=[sq_r], out=sq[:, 0:n],
                   in0=oT[:, j, t0:t0 + n], in1=oT[:, j, t0:t0 + n], op=ALU.mult)
                MM(bq[:, 0:n], ones_b[:], sq[:, 0:n], j == 0, j == 7, reads=[sq_r, const_res], writes=[bqr])
            OP("act", "activation", reads=[bqr], writes=[rn_r], out=rn[:, 0:n], in_=bq[:, 0:n], func=AF.Sqrt,
               scale=1.0 / 1024, bias=1e-6)
            OP("dve", "reciprocal", reads=[rn_r], writes=[rn_r], out=rn[:, 0:n], in_=rn[:, 0:n])
            for j in range(8):
                blk = g * 8 + j
                OP("dve", "scalar_tensor_tensor", reads=[oT_res[j], ng_r, rn_r], writes=[oT_res[j]],
                   out=oT[:, j, t0:t0 + n], in0=oT[:, j, t0:t0 + n], scalar=ng[:, blk:blk + 1], in1=rn[:, 0:n],
                   op0=ALU.mult, op1=ALU.mult)

    def final_out():
        P.barrier()
        carve.reset()
        fg, fg_r = carve.f32(1024, "fg")
        junk, junk_r = carve.f32(1024, "junk")
        outs = [carve.f32(1024, "ot") for _ in range(2)]
        load_small(fg, fin_g_d[0:1, :].to_broadcast([128, 1024]), 0, slow=True)
        for tt in range(1, NT):
            ot, ot_r = outs[tt % 2]
            osr = out_res[tt % 2]
            ss = small[:, 0:1]
            rs = small[:, 1:2]
            OP("act", "activation", reads=[h_res[tt]], writes=[junk_r, small_res], out=junk,
               in_=h_sb[:, tt, :], func=AF.Square, accum_out=ss)
            OP("act", "activation", reads=[small_res], writes=[small_res], out=rs, in_=ss,
               func=AF.Sqrt, scale=1.0 / D, bias=1e-6)
            OP("dve", "reciprocal", reads=[small_res], writes=[small_res], out=rs, in_=rs)
            OP("dve", "scalar_tensor_tensor", reads=[h_res[tt], small_res, prm_res[0], osr], writes=[ot_r],
               out=ot, in0=h_sb[:, tt, :], scalar=rs, in1=fg, op0=ALU.mult, op1=ALU.mult)
            P.dma("sp", out_d[(tt - 1) * 128:tt * 128, :], ot, osr, reads=[ot_r], writes=[osr])

    def dump_oT():
        P.barrier()
        carve.reset()
        t32, t32_r = carve.f32(LP, "dump")
        for k in range(8):
            OP("dve", "tensor_copy", reads=oT_res, writes=[t32_r], out=t32, in_=oT[:, k, :])
            P.dma("sp", dbg_d[k * 128:(k + 1) * 128, :], t32, dbg_res, reads=[t32_r])

    def dump_h():
        P.barrier()
        for tt in range(NT):
            P.dma("sp", dbg_d[tt * 128:(tt + 1) * 128, :], h_sb[:, tt, :], dbg_res, reads=[h_res[tt]])

    done = False
    for layer in range(depth):
        P.new_phase()
        layer_norm_T(layer)
        if "A" not in skip:
            unit_attention(layer)
        if stop_after == ("A", layer):
            dump_oT()
            done = True
            break
        if "A" not in skip:
            unit_merge(layer, 0, 0, O_GATE)
        if stop_after == ("Am", layer):
            dump_h()
            done = True
            break
        if "B" not in skip:
            unit_gdn(layer)
        if stop_after == ("B", layer):
            dump_oT()
            done = True
            break
        if "B" not in skip:
            unit_merge(layer, 1, 0, O_GATE + 1024)
        if stop_after == ("Bm", layer):
            dump_h()
            done = True
            break
        for g in range(2):
            unit_ssd(layer, g)
            if stop_after == ("C%d" % g, layer):
                dump_oT()
                done = True
                break
            unit_merge(layer, 2, g * 1024, O_GATE + 2048)
        if done:
            break
        OP("pool", "memset", reads=[], writes=[h_res[0]], ap=h_sb[0:PAD, 0, :], constant=0.0)
        if stop_after == ("L", layer):
            dump_h()
            done = True
            break
    if not done:
        final_out()

    P.emit()
    import os as _os
    if _os.environ.get("KDEBUG"):
        print("ops", len(P.ops), "sems", P.nsem, "counts", P.counts)
    es.close()
    return nc


_NAMES = ["meta_tokens", "norm_g", "w_in", "gdn_conv_w", "gdn_a_log", "gdn_dt_bias", "gdn_norm_g",
          "ssm_conv_w", "ssm_conv_b", "ssm_a_log", "ssm_dt_bias", "ssm_d", "ssm_norm_g",
          "w_branch_a", "w_branch_b", "w_branch_c", "w_out"]


def make_in_maps(inputs, ncores=8):
    shared = {n: np.ascontiguousarray(np.asarray(inputs[n], dtype=np.float32)) for n in _NAMES}
    shared["final_norm_g"] = np.ascontiguousarray(
        np.asarray(inputs["final_norm_g"], dtype=np.float32).reshape(1, D))
    x = np.asarray(inputs["x"], dtype=np.float32)
    maps = []
    for c in range(ncores):
        m = dict(shared)
        m["x"] = np.ascontiguousarray(x[c])
        maps.append(m)
    return maps


def kernel(**inputs):
    nc = build()
    in_maps = make_in_maps(inputs)
    res = run_bass_kernel_spmd(nc, in_maps, core_ids=list(range(8)))
    return np.stack([r["out"] for r in res.results], axis=0)
```

```python
from contextlib import ExitStack
import numpy as np
import concourse.bass as bass
import concourse.mybir as mybir
from concourse.bass_utils import run_bass_kernel_spmd

F32 = mybir.dt.float32
BF16 = mybir.dt.bfloat16
AF = mybir.ActivationFunctionType
ALU = mybir.AluOpType

D = 1024
SEQ = 2048
NMETA = 16
PAD = 112
LP = SEQ + NMETA + PAD
NT = LP // 128
DIN = 15920
DEPTH = 4
TG = [(0, 512), (512, 512), (1024, 512), (1536, 512), (2048, 128)]

O_SBQ, O_SBK, O_SBV, O_SBZ = 0, 1024, 2048, 3072
O_GQ, O_GK, O_GV, O_GZ, O_GB, O_GA = 4096, 5120, 6144, 7168, 8192, 8200
O_SZ, O_SX, O_SB, O_SC, O_SDT, O_GATE = 8208, 10256, 12304, 12560, 12816, 12848


class Res:
    __slots__ = ("name", "last_w", "readers", "sem", "dcount", "excl")

    def __init__(self, name):
        self.name = name
        self.excl = False
        self.last_w = None
        self.readers = []
        self.sem = None
        self.dcount = 0


class Op:
    __slots__ = ("eng", "fn", "deps", "is_dma", "sem", "val", "signal", "phase", "pe_mm")


class Prog:
    ENGS = ("pe", "act", "dve", "pool", "sp")

    def __init__(self, nc, es):
        self.nc = nc
        self.es = es
        self.ops = []
        self.phase = 0
        self.bar_deps = []
        self.last_on_eng = {}
        self.dmas_since_bar = []
        self.phase_sems = {}
        self.nsem = 0

    def new_sem(self, name):
        self.nsem += 1
        return self.es.enter_context(self.nc.semaphore(name))

    def res(self, name, dma=False):
        r = Res(name)
        if dma:
            r.sem = self.new_sem("d_" + name)
        return r

    def new_phase(self):
        self.phase += 1

    def _track(self, op, reads, writes):
        ex = [r for r in reads if r.excl and r not in writes]
        if ex:
            reads = [r for r in reads if not r.excl]
            writes = list(writes) + ex
        deps = set(self.bar_deps)
        for r in list(reads) + list(writes):
            if r.last_w is not None:
                deps.add(r.last_w)
        for w in writes:
            for rd in w.readers:
                deps.add(rd)
        idx = len(self.ops)
        deps.discard(idx)
        op.deps = deps
        for r in reads:
            r.readers.append(idx)
        for w in writes:
            w.last_w = idx
            w.readers = []
        self.ops.append(op)
        self.last_on_eng[op.eng] = idx
        return idx

    def op(self, eng, fn, reads=(), writes=(), mm=False):
        o = Op()
        o.eng = eng
        o.fn = fn
        o.is_dma = False
        o.sem = None
        o.val = 0
        o.signal = False
        o.phase = self.phase
        o.pe_mm = mm
        return self._track(o, reads, writes)

    def dma(self, eng, out, in_, semres, reads=(), writes=(), slow=False):
        o = Op()
        o.eng = eng
        if slow:
            o.fn = lambda e: e.dma_start(out=out, in_=in_, allow_slow_non_contiguous=True)
        else:
            o.fn = lambda e: e.dma_start(out=out, in_=in_)
        o.is_dma = True
        o.sem = semres.sem
        semres.dcount += 1
        o.val = 16 * semres.dcount
        o.signal = True
        o.phase = self.phase
        o.pe_mm = False
        idx = self._track(o, reads, writes)
        self.dmas_since_bar.append(idx)
        return idx

    def barrier(self):
        self.bar_deps = list(self.last_on_eng.values()) + list(self.dmas_since_bar)
        self.dmas_since_bar = []

    def emit(self):
        nc = self.nc
        ops = self.ops
        for i, o in enumerate(ops):
            for j in o.deps:
                p = ops[j]
                if p.is_dma:
                    continue
                if p.eng == "pe" and o.eng == "pe":
                    continue
                p.signal = True
        cnt = {}
        for o in ops:
            if o.is_dma or not o.signal:
                continue
            key = (o.eng, o.phase)
            if key not in self.phase_sems:
                self.phase_sems[key] = self.new_sem("s_%s_%d" % key)
                cnt[key] = 0
            cnt[key] += 1
            o.sem = self.phase_sems[key]
            o.val = cnt[key]
        self.counts = dict(cnt)
        per_eng = {e: [] for e in self.ENGS}
        for i, o in enumerate(ops):
            per_eng[o.eng].append(i)
        final_waits = {}
        for o in ops:
            if o.is_dma:
                final_waits[id(o.sem)] = (o.sem, max(o.val, final_waits.get(id(o.sem), (None, 0))[1]))

        def run(engname, e):
            known = {}
            for i in per_eng[engname]:
                o = ops[i]
                need = {}
                for j in o.deps:
                    p = ops[j]
                    if p.sem is None:
                        continue
                    k = id(p.sem)
                    if k not in need or need[k][1] < p.val:
                        need[k] = (p.sem, p.val)
                for k, (s, v) in need.items():
                    if known.get(k, 0) < v:
                        e.wait_ge(s, v)
                        known[k] = v
                ins = o.fn(e)
                if o.is_dma:
                    ins.then_inc(o.sem, 16)
                elif o.signal:
                    ins.then_inc(o.sem, 1)
            if engname == "sp":
                for k, (s, v) in final_waits.items():
                    e.wait_ge(s, v)

        with nc.Block() as block:
            @block.tensor
            def _(e):
                run("pe", e)

            @block.scalar
            def _(e):
                run("act", e)

            @block.vector
            def _(e):
                run("dve", e)

            @block.gpsimd
            def _(e):
                run("pool", e)

            @block.sync
            def _(e):
                run("sp", e)


class Carver:
    def __init__(self, prog, ap, nwords):
        self.prog = prog
        self.ap = ap
        self.n = nwords
        self.off = 0
        self.k = 0

    def reset(self):
        self.off = 0

    def f32(self, cols, name="w"):
        a = self.ap[:, self.off:self.off + cols]
        self.off += cols
        assert self.off <= self.n, ("work overflow", self.off, self.n)
        self.k += 1
        return a, self.prog.res("%s%d" % (name, self.k))

    def bf16(self, cols, name="w"):
        words = (cols + 1) // 2
        a = self.ap[:, self.off:self.off + words].bitcast(BF16)
        self.off += words
        assert self.off <= self.n, ("work overflow", self.off, self.n)
        self.k += 1
        return a[:, 0:cols], self.prog.res("%s%d" % (name, self.k))


def build(depth=DEPTH, dbg=None, stop_after=None, GSTOP=0, skip=()):
    nc = bass.Bass("TRN2", target_bir_lowering=False)
    es = ExitStack()
    P = Prog(nc, es)

    def din(name, shape):
        return nc.dram_tensor(name, list(shape), F32, kind="ExternalInput").ap()

    x_d = din("x", (SEQ, D))
    meta_d = din("meta_tokens", (NMETA, D))
    norm_g_d = din("norm_g", (DEPTH, D))
    w_in_d = din("w_in", (DEPTH, D, DIN))
    gdn_conv_w_d = din("gdn_conv_w", (DEPTH, 4, 3072))
    gdn_a_log_d = din("gdn_a_log", (DEPTH, 8))
    gdn_dt_bias_d = din("gdn_dt_bias", (DEPTH, 8))
    gdn_norm_g_d = din("gdn_norm_g", (DEPTH, 128))
    ssm_conv_w_d = din("ssm_conv_w", (DEPTH, 4, 2560))
    ssm_conv_b_d = din("ssm_conv_b", (DEPTH, 2560))
    ssm_a_log_d = din("ssm_a_log", (DEPTH, 32))
    ssm_dt_bias_d = din("ssm_dt_bias", (DEPTH, 32))
    ssm_d_d = din("ssm_d", (DEPTH, 32))
    ssm_norm_g_d = din("ssm_norm_g", (DEPTH, 2048))
    w_br_d = [din("w_branch_a", (DEPTH, 1024, D)), din("w_branch_b", (DEPTH, 1024, D)),
              din("w_branch_c", (DEPTH, 2048, D))]
    w_out_d = din("w_out", (DEPTH, D, D))
    fin_g_d = din("final_norm_g", (1, D))
    out_d = nc.dram_tensor("out", [SEQ, D], F32, kind="ExternalOutput").ap()
    dbg_d = None
    if dbg is not None:
        dbg_d = nc.dram_tensor("dbg", list(dbg["shape"]), F32, kind="ExternalOutput").ap()

    def sb(name, shape, dt):
        return es.enter_context(nc.sbuf_tensor(name, list(shape), dt))

    def ps(name, shape, dt=F32):
        return es.enter_context(nc.psum_tensor(name, list(shape), dt))

    def OP(eng, meth, reads=(), writes=(), **kw):
        return P.op(eng, lambda e: getattr(e, meth)(**kw), reads=reads, writes=writes)

    def MM(out, lhsT, rhs, start, stop, reads, writes):
        return P.op("pe", lambda e: e.matmul(out=out, lhsT=lhsT, rhs=rhs, start=start, stop=stop),
                    reads=reads, writes=writes, mm=True)

    def TR(out, in_, ident, reads, writes):
        return P.op("pe", lambda e: e.transpose(out=out, in_=in_, identity=ident),
                    reads=reads, writes=writes, mm=True)

    h_sb = sb("h", (128, NT, D), F32)
    uT = sb("uT", (128, 8, LP), BF16)
    oT = sb("oT", (128, 8, LP), BF16)
    stage = [sb("stg%d" % i, (128, 8, 128), F32) for i in range(2)]
    wslab = [sb("wsl%d" % i, (128, 8, 128), BF16) for i in range(2)]
    WORKW = 13312
    work = sb("work", (128, WORKW), F32)
    ident_f = sb("ident_f", (128, 128), F32)
    ident_b = sb("ident_b", (128, 128), BF16)
    ones_b = sb("ones_b", (128, 128), BF16)
    ones_f = sb("ones_f", (128, 128), F32)
    uincl_f = sb("uincl_f", (128, 128), F32)
    tmat = sb("tmat", (128, 4, 128), BF16)
    mw = sb("mw", (128, 896), BF16)
    small = sb("small", (128, 64), F32)
    gfm = sb("gfm", (128, 8), F32)

    h_res = [P.res("h%d" % t) for t in range(NT)]
    hload_res = [P.res("hl%d" % i, dma=True) for i in range(5)]
    uT_res = [P.res("uT%d" % t) for t in range(NT)]
    oT_res = [P.res("oT%d" % k) for k in range(8)]
    stage_res = [P.res("stg%d" % i, dma=True) for i in range(2)]
    wslab_res = [P.res("wsl%d" % i) for i in range(2)]
    const_res = P.res("const")
    gfm_res = P.res("gfm", dma=True)
    small_res = P.res("small")
    dbg_res = P.res("dbg", dma=True)
    prm_res = [P.res("prm%d" % i, dma=True) for i in range(8)]
    out_res = [P.res("outst%d" % i, dma=True) for i in range(2)]

    banks = [ps("bank%d" % i, (128, 512), F32) for i in range(8)]
    bank_res = [P.res("bank%d" % i) for i in range(8)]
    for _r in bank_res:
        _r.excl = True
    carve = Carver(P, work, WORKW)
    bctr = [0]

    def nb(nmax=8):
        i = bctr[0] % nmax
        bctr[0] += 1
        return banks[i], bank_res[i]

    def bfv(bank):
        return bank[:].bitcast(BF16)

    cr = [const_res]
    OP("pool", "memset", writes=cr, ap=ident_f[:], constant=1.0)
    OP("pool", "affine_select", writes=cr, out=ident_f[:], in_=ident_f[:], pattern=[[-1, 128]],
       compare_op=ALU.is_equal, fill=0.0, base=0, channel_multiplier=1)
    OP("pool", "tensor_copy", reads=cr, writes=cr, out=ident_b[:], in_=ident_f[:])
    OP("pool", "memset", writes=cr, ap=ones_b[:], constant=1.0)
    OP("pool", "memset", writes=cr, ap=ones_f[:], constant=1.0)
    OP("pool", "memset", writes=cr, ap=uincl_f[:], constant=1.0)
    OP("pool", "affine_select", writes=cr, out=uincl_f[:], in_=uincl_f[:], pattern=[[1, 128]],
       compare_op=ALU.is_ge, fill=0.0, base=0, channel_multiplier=-1)
    OP("pool", "memset", writes=cr, ap=tmat[:], constant=-1.0)
    for v in (0, 2):
        OP("pool", "affine_select", writes=cr, out=tmat[:, v, :], in_=tmat[:, v, :], pattern=[[-1, 128]],
           compare_op=ALU.is_gt, fill=0.0, base=0, channel_multiplier=1)
    for v in (1, 3):
        OP("pool", "affine_select", writes=cr, out=tmat[:, v, :], in_=tmat[:, v, :], pattern=[[1, 128]],
           compare_op=ALU.is_ge, fill=0.0, base=0, channel_multiplier=-1)
    OP("pool", "memset", writes=cr, ap=tmat[0:PAD, 2:4, :], constant=0.0)
    OP("pool", "memset", writes=cr, ap=mw[:], constant=1.0)
    OP("pool", "affine_select", writes=cr, out=mw[:], in_=mw[:], pattern=[[1, 896]],
       compare_op=ALU.is_gt, fill=0.0, base=-384, channel_multiplier=-1)

    OP("pool", "memset", writes=[h_res[0]], ap=h_sb[:, 0, :], constant=0.0)
    P.dma("sp", h_sb[PAD:128, 0, :], meta_d[:, :], hload_res[4], writes=[h_res[0]])
    xv = x_d.rearrange("(t p) d -> p t d", p=128)
    for i in range(4):
        P.dma("sp", h_sb[:, 1 + 4 * i:5 + 4 * i, :], xv[:, 4 * i:4 * i + 4, :], hload_res[i],
              writes=[h_res[1 + 4 * i + j] for j in range(4)])

    wctr = [0]

    def stream_slab(src, C, dst=None, dst_res=None):
        i = wctr[0] % 2
        wctr[0] += 1
        P.dma("sp", stage[i][:, :, 0:C], src, stage_res[i], writes=[stage_res[i]])
        if dst is None:
            dst, dst_res = wslab[i][:, :, 0:C], wslab_res[i]
        OP("pool", "tensor_copy", reads=[stage_res[i]], writes=[dst_res], out=dst,
           in_=stage[i][:, :, 0:C])
        return dst, dst_res

    def win(layer, c0, C):
        return w_in_d[layer].rearrange("(k p) c -> p k c", p=128)[:, :, c0:c0 + C]

    def tiles_of(t0, n):
        return uT_res[t0 // 128:(t0 + n + 127) // 128]

    def proj_fm(layer, c0, evac, ncol=128):
        w, wr = stream_slab(win(layer, c0, ncol), ncol)
        for (t0, n) in TG:
            b, br = nb()
            for k in range(8):
                MM(b[0:ncol, 0:n], w[:, k, :], uT[:, k, t0:t0 + n], k == 0, k == 7,
                   reads=[wr] + tiles_of(t0, n), writes=[br])
            evac(b[0:ncol, 0:n], br, t0, n)

    def proj_tm(layer, c0, ncol, evac):
        w, wr = stream_slab(win(layer, c0, ncol), ncol)
        per = min(4, 512 // ncol)
        for tt0 in range(0, NT, per):
            cnt = min(per, NT - tt0)
            b, br = nb()
            for j in range(cnt):
                tt = tt0 + j
                for k in range(8):
                    MM(b[:, j * ncol:(j + 1) * ncol], uT[:, k, tt * 128:(tt + 1) * 128], w[:, k, :],
                       k == 0, k == 7, reads=[wr, uT_res[tt]], writes=[br])
            evac(b[:, 0:cnt * ncol], br, tt0, cnt)

    def load_small(dst, src, ri, slow=True):
        P.dma("sp", dst, src, prm_res[ri], writes=[prm_res[ri]], slow=slow)

    def layer_norm_T(layer):
        P.barrier()
        carve.reset()
        P.dma("sp", gfm[:, :], norm_g_d[layer].rearrange("(k p) -> p k", p=128), gfm_res,
              writes=[gfm_res], slow=True)
        junk, junk_r = carve.f32(1024, "junk")
        uns = [carve.bf16(1024, "un") for _ in range(2)]
        for tt in range(NT):
            un, un_r = uns[tt % 2]
            ss = small[:, 0:1]
            rs = small[:, 1:2]
            OP("act", "activation", reads=[h_res[tt]], writes=[junk_r, small_res], out=junk,
               in_=h_sb[:, tt, :], func=AF.Square, accum_out=ss)
            OP("act", "activation", reads=[small_res], writes=[small_res], out=rs, in_=ss,
               func=AF.Sqrt, scale=1.0 / D, bias=1e-6)
            OP("dve", "reciprocal", reads=[small_res], writes=[small_res], out=rs, in_=rs)
            OP("dve", "tensor_scalar", reads=[h_res[tt], small_res], writes=[un_r], out=un,
               in0=h_sb[:, tt, :], scalar1=rs, scalar2=None, op0=ALU.mult)
            b, br = nb()
            pT = bfv(b)
            for k in range(8):
                TR(pT[:, k * 128:(k + 1) * 128], un[:, k * 128:(k + 1) * 128], ident_b[:],
                   reads=[un_r, const_res], writes=[br])
            OP("dve", "tensor_tensor", reads=[br, gfm_res], writes=[uT_res[tt]],
               out=uT[:, :, tt * 128:(tt + 1) * 128], in0=pT.rearrange("p (k t) -> p k t", t=128),
               in1=gfm[:, :].unsqueeze(2).to_broadcast([128, 8, 128]), op=ALU.mult)

    def unit_attention(layer):
        P.barrier()
        carve.reset()
        qT, qT_r = carve.bf16(LP, "qT")
        kT, kT_r = carve.bf16(LP, "kT")
        zg, zg_r = carve.bf16(LP, "zg")
        vtm_flat, vtm_r = carve.bf16(NT * 128, "vtm")
        vtm = vtm_flat.rearrange("p (t d) -> p t d", d=128)
        NBUF = 3
        AHEAD = 2
        spf = [carve.f32(512, "spf") for _ in range(NBUF)]
        spb = [carve.bf16(512, "spb") for _ in range(NBUF)]
        t2 = [carve.f32(512, "t2") for _ in range(NBUF)]
        wb = [carve.bf16(512, "wb") for _ in range(NBUF)]
        scale = 128.0 ** -0.5
        gctr = [0]
        for h in range(8):
            proj_fm(layer, O_SBQ + h * 128, lambda b, br, t0, n: OP(
                "act", "activation", reads=[br], writes=[qT_r], out=qT[:, t0:t0 + n], in_=b,
                func=AF.Copy, scale=scale))
            proj_fm(layer, O_SBK + h * 128, lambda b, br, t0, n: OP(
                "dve", "tensor_copy", reads=[br], writes=[kT_r], out=kT[:, t0:t0 + n], in_=b))
            proj_fm(layer, O_SBZ + h * 128, lambda b, br, t0, n: OP(
                "act", "activation", reads=[br], writes=[zg_r], out=zg[:, t0:t0 + n], in_=b,
                func=AF.Silu))
            proj_tm(layer, O_SBV + h * 128, 128, lambda b, br, tt0, cnt: OP(
                "dve", "tensor_copy", reads=[br], writes=[vtm_r], out=vtm[:, tt0:tt0 + cnt, :],
                in_=b.rearrange("p (t d) -> p t d", d=128)))
            its = []
            for g in range(5):
                qb0 = 4 * g
                nq = 512 if g < 4 else 128
                kmax = qb0 + nq // 128 - 1
                gi = gctr[0] % 2
                gctr[0] += 1
                for idx, kb in enumerate(range(kmax, -1, -1)):
                    its.append(dict(qb0=qb0, nq=nq, q0=qb0 * 128, idx=idx, kb=kb, gi=gi, zb=None))

            n_it = len(its)

            def PZ(j):
                d = its[j]
                nq, kb, q0 = d["nq"], d["kb"], d["q0"]
                zb, zr = nb(4)
                d["zb"] = (zb, zr)
                MM(zb[:, 0:nq], kT[:, kb * 128:(kb + 1) * 128], qT[:, q0:q0 + nq], True, True,
                   reads=[kT_r, qT_r], writes=[zr])

            def AE(j):
                d = its[j]
                nq = d["nq"]
                zb, zr = d["zb"]
                sp_a, sp_r = spf[j % NBUF]
                OP("act", "activation", reads=[zr], writes=[sp_r], out=sp_a[:, 0:nq],
                   in_=zb[:, 0:nq], func=AF.Exp)
                OP("act", "activation", reads=[sp_r], writes=[sp_r], out=sp_a[:, 0:nq],
                   in_=sp_a[:, 0:nq], func=AF.Ln, bias=1.0)

            def DS(j):
                d = its[j]
                nq, kb, qb0 = d["nq"], d["kb"], d["qb0"]
                zb, zr = d["zb"]
                sp_a, sp_r = spf[j % NBUF]
                spb_a, spb_r = spb[j % NBUF]
                t2_a, t2_r = t2[j % NBUF]
                if kb >= qb0:
                    moff = 384 - 128 * (kb - qb0)
                    OP("dve", "tensor_tensor", reads=[sp_r, const_res], writes=[spb_r],
                       out=spb_a[:, 0:nq], in0=sp_a[:, 0:nq], in1=mw[:, moff:moff + nq],
                       op=ALU.mult)
                else:
                    OP("dve", "tensor_copy", reads=[sp_r], writes=[spb_r], out=spb_a[:, 0:nq],
                       in_=sp_a[:, 0:nq])
                OP("dve", "tensor_tensor", reads=[zr, sp_r], writes=[t2_r], out=t2_a[:, 0:nq],
                   in0=zb[:, 0:nq], in1=sp_a[:, 0:nq], op=ALU.subtract)

            def banksof(d):
                gi = d["gi"]
                return banks[4 + gi], bank_res[4 + gi], banks[6 + gi], bank_res[6 + gi]

            def PT(j):
                d = its[j]
                nq, kb, idx = d["nq"], d["kb"], d["idx"]
                A_b, A_r, O_b, O_r = banksof(d)
                spb_a, spb_r = spb[j % NBUF]
                tv = 2 if kb == 0 else 0
                MM(A_b[:, 0:nq], tmat[:, tv, :], spb_a[:, 0:nq], idx == 0, False,
                   reads=[spb_r, const_res], writes=[A_r])

            def DA(j):
                d = its[j]
                nq = d["nq"]
                A_b, A_r, O_b, O_r = banksof(d)
                t2_a, t2_r = t2[j % NBUF]
                OP("dve", "tensor_tensor", reads=[A_r, t2_r], writes=[t2_r], out=t2_a[:, 0:nq],
                   in0=A_b[:, 0:nq], in1=t2_a[:, 0:nq], op=ALU.add)

            def AW(j):
                d = its[j]
                nq, kb, qb0 = d["nq"], d["kb"], d["qb0"]
                t2_a, t2_r = t2[j % NBUF]
                wb_a, wb_r = wb[j % NBUF]
                OP("act", "activation", reads=[t2_r], writes=[wb_r], out=wb_a[:, 0:nq],
                   in_=t2_a[:, 0:nq], func=AF.Exp)
                if kb >= qb0:
                    moff = 384 - 128 * (kb - qb0)
                    OP("pool", "tensor_tensor", reads=[wb_r, const_res], writes=[wb_r],
                       out=wb_a[:, 0:nq], in0=wb_a[:, 0:nq], in1=mw[:, moff:moff + nq],
                       op=ALU.mult)

            def PL(j):
                d = its[j]
                nq, kb = d["nq"], d["kb"]
                A_b, A_r, O_b, O_r = banksof(d)
                spb_a, spb_r = spb[j % NBUF]
                tv = 2 if kb == 0 else 0
                MM(A_b[:, 0:nq], tmat[:, tv + 1, :], spb_a[:, 0:nq], False, kb == 0,
                   reads=[spb_r, const_res], writes=[A_r])

            def PO(j, h=h):
                d = its[j]
                nq, kb, idx, q0 = d["nq"], d["kb"], d["idx"], d["q0"]
                A_b, A_r, O_b, O_r = banksof(d)
                wb_a, wb_r = wb[j % NBUF]
                MM(O_b[:, 0:nq], vtm[:, kb, :], wb_a[:, 0:nq], idx == 0, kb == 0,
                   reads=[vtm_r, wb_r], writes=[O_r])
                if kb == 0:
                    OP("dve", "tensor_tensor", reads=[O_r, zg_r], writes=[oT_res[h]],
                       out=oT[:, h, q0:q0 + nq], in0=O_b[:, 0:nq], in1=zg[:, q0:q0 + nq], op=ALU.mult)

            PZ(0)
            AE(0)
            PZ(1)
            AE(1)
            DS(0)
            for j in range(n_it):
                PT(j)
                if j + 2 < n_it:
                    PZ(j + 2)
                DA(j)
                if j + 1 < n_it:
                    DS(j + 1)
                AW(j)
                if j + 2 < n_it:
                    AE(j + 2)
                PL(j)
                if j >= 1:
                    PO(j - 1)
            PO(n_it - 1)

    def unit_merge(layer, br_idx, row0, gate_c0):
        P.barrier()
        carve.reset()
        wbr_f, wbr_r = carve.bf16(8 * 1024, "wbr")
        wg_f, wg_r = carve.bf16(8 * 1024, "wg")
        wbr = wbr_f.rearrange("p (k c) -> p k c", c=1024)
        wg = wg_f.rearrange("p (k c) -> p k c", c=1024)
        sgs = [carve.bf16(1024, "sg") for _ in range(2)]
        src_br = w_br_d[br_idx][layer].rearrange("(k p) c -> p k c", p=128)
        for cb in range(8):
            stream_slab(src_br[:, row0 // 128:row0 // 128 + 8, cb * 128:(cb + 1) * 128], 128,
                        dst=wbr[:, :, cb * 128:(cb + 1) * 128], dst_res=wbr_r)
            stream_slab(win(layer, gate_c0 + cb * 128, 128), 128,
                        dst=wg[:, :, cb * 128:(cb + 1) * 128], dst_res=wg_r)
        for tt in range(NT):
            tok = slice(tt * 128, (tt + 1) * 128)
            pb = [nb(), nb()]
            gb = [nb(), nb()]
            for cb in range(8):
                b, br = pb[cb // 4]
                for fk in range(8):
                    MM(b[:, (cb % 4) * 128:(cb % 4 + 1) * 128], wbr[:, fk, cb * 128:(cb + 1) * 128],
                       oT[:, fk, tok], fk == 0, fk == 7, reads=[wbr_r, oT_res[fk]], writes=[br])
            for cb in range(8):
                b, br = gb[cb // 4]
                for k in range(8):
                    MM(b[:, (cb % 4) * 128:(cb % 4 + 1) * 128], wg[:, k, cb * 128:(cb + 1) * 128],
                       uT[:, k, tok], k == 0, k == 7, reads=[wg_r, uT_res[tt]], writes=[br])
            sg, sg_r = sgs[tt % 2]
            for hf in range(2):
                OP("act", "activation", reads=[gb[hf][1]], writes=[sg_r],
                   out=sg[:, hf * 512:(hf + 1) * 512], in_=gb[hf][0][:, :], func=AF.Sigmoid)
            for hf in range(2):
                OP("dve", "tensor_tensor", reads=[pb[hf][1], sg_r],
                   writes=[oT_res[4 * hf + j] for j in range(4)],
                   out=oT[:, 4 * hf:4 * hf + 4, tok],
                   in0=pb[hf][0][:, :].rearrange("p (c t) -> p c t", t=128),
                   in1=sg[:, hf * 512:(hf + 1) * 512].rearrange("p (c t) -> p c t", t=128),
                   op=ALU.mult)
        src_o = w_out_d[layer].rearrange("(k p) c -> p k c", p=128)
        for ds in range(8):
            w, wr = stream_slab(src_o[:, :, ds * 128:(ds + 1) * 128], 128)
            for tt0 in range(0, NT, 4):
                cnt = min(4, NT - tt0)
                b, br = nb()
                for j in range(cnt):
                    tt = tt0 + j
                    for cb in range(8):
                        MM(b[:, j * 128:(j + 1) * 128], oT[:, cb, tt * 128:(tt + 1) * 128], w[:, cb, :],
                           cb == 0, cb == 7, reads=[wr, oT_res[cb]], writes=[br])
                OP("dve", "tensor_tensor", reads=[br] + h_res[tt0:tt0 + cnt], writes=h_res[tt0:tt0 + cnt],
                   out=h_sb[:, tt0:tt0 + cnt, ds * 128:(ds + 1) * 128],
                   in0=b[:, 0:cnt * 128].rearrange("p (t d) -> p t d", d=128),
                   in1=h_sb[:, tt0:tt0 + cnt, ds * 128:(ds + 1) * 128], op=ALU.add)


    def bc_rows(src_row_ap, n):
        return src_row_ap.to_broadcast([128, n])

    def conv_silu(b, br, n, raw, raw_r, acc, acc_r, cw4, prm_r, bias, out_ap, out_res):
        OP("act", "activation", reads=[br], writes=[raw_r], out=raw[:, 3:3 + n], in_=b, func=AF.Copy)
        if bias is None:
            OP("dve", "tensor_scalar", reads=[raw_r, prm_r], writes=[acc_r], out=acc[:, 0:n],
               in0=raw[:, 3:3 + n], scalar1=cw4[:, 3:4], scalar2=None, op0=ALU.mult)
        else:
            OP("dve", "tensor_scalar", reads=[raw_r, prm_r], writes=[acc_r], out=acc[:, 0:n],
               in0=raw[:, 3:3 + n], scalar1=cw4[:, 3:4], scalar2=bias, op0=ALU.mult, op1=ALU.add)
        for j in (2, 1, 0):
            OP("dve", "scalar_tensor_tensor", reads=[raw_r, prm_r, acc_r], writes=[acc_r],
               out=acc[:, 0:n], in0=raw[:, j:j + n], scalar=cw4[:, j:j + 1], in1=acc[:, 0:n],
               op0=ALU.mult, op1=ALU.add)
        OP("act", "activation", reads=[acc_r], writes=[out_res], out=out_ap, in_=acc[:, 0:n],
           func=AF.Silu)
        OP("pool", "tensor_copy", reads=[raw_r], writes=[raw_r], out=raw[:, 0:3], in_=raw[:, n:n + 3])

    def load_conv_w(dst, dst_r, src2d, nblk, tmp, tmp_r, ri):
        P.dma("sp", tmp[0:nblk, :], src2d.rearrange("j (b p) -> b j p", p=128), prm_res[ri],
              writes=[tmp_r])
        b, br = nb()
        for j in range(4):
            TR(b[:, j * nblk:(j + 1) * nblk], tmp[0:nblk, j * 128:(j + 1) * 128], ident_f[0:nblk, 0:nblk],
               reads=[tmp_r, const_res], writes=[br])
        OP("dve", "tensor_copy", reads=[br], writes=[dst_r], out=dst, in_=b[:, 0:4 * nblk])

    def unit_gdn(layer):
        P.barrier()
        carve.reset()
        ba, ba_r = carve.f32(NT * 16, "ba")
        ba3 = ba.rearrange("p (t c) -> p t c", c=16)

        def sm(name):
            a, r = carve.f32(NT * 8, name)
            return a, a.rearrange("p (t c) -> p t c", c=8), r
        xa, xa3, xa_r = sm("xa")
        lb, lb3, lb_r = sm("lb")
        g_all, g3, g_r = sm("g")
        gc, gc3, gc_r = sm("gc")
        ngc, ngc3, ngc_r = sm("ngc")
        gb_, gb3, gb_r = sm("gb")
        egb, egb3, egb_r = sm("egb")
        beta, beta3, beta_r = sm("beta")
        alog, alog_r = carve.f32(8, "alog")
        dtb, dtb_r = carve.f32(8, "dtb")
        gng, gng_r = carve.f32(1, "gng")
        cw, cw_r = carve.f32(96, "cw")
        cw3 = cw.rearrange("p (j b) -> p j b", b=24)
        cwt, cwt_r = carve.f32(512, "cwt")
        load_small(alog, bc_rows(gdn_a_log_d[layer:layer + 1, :], 8), 0)
        load_small(dtb, bc_rows(gdn_dt_bias_d[layer:layer + 1, :], 8), 1)
        load_small(gng, gdn_norm_g_d[layer].rearrange("(p o) -> p o", o=1), 2)
        alog_r, dtb_r, gng_r = prm_res[0], prm_res[1], prm_res[2]
        load_conv_w(cw, cw_r, gdn_conv_w_d[layer], 24, cwt, cwt_r, 3)
        if GSTOP == 1:
            return
        proj_tm(layer, O_GB, 16, lambda b, br, tt0, cnt: OP(
            "dve", "tensor_copy", reads=[br], writes=[ba_r], out=ba[:, tt0 * 16:(tt0 + cnt) * 16], in_=b))
        OP("dve", "tensor_tensor", reads=[ba_r, dtb_r], writes=[xa_r], out=xa3, in0=ba3[:, :, 8:16],
           in1=dtb.unsqueeze(1).to_broadcast([128, NT, 8]), op=ALU.add)
        OP("act", "activation", reads=[xa_r], writes=[xa_r], out=xa, in_=xa, func=AF.Exp)
        OP("act", "activation", reads=[xa_r], writes=[xa_r], out=xa, in_=xa, func=AF.Ln, bias=1.0)
        OP("act", "activation", reads=[alog_r], writes=[alog_r], out=alog, in_=alog, func=AF.Exp)
        OP("dve", "scalar_tensor_tensor", reads=[xa_r, alog_r], writes=[g_r], out=g3, in0=xa3, scalar=-1.0,
           in1=alog.unsqueeze(1).to_broadcast([128, NT, 8]), op0=ALU.mult, op1=ALU.mult)
        OP("pool", "memset", reads=[], writes=[g_r], ap=g3[0:PAD, 0, :], constant=0.0)
        OP("act", "activation", reads=[ba_r], writes=[lb_r], out=lb3, in_=ba3[:, :, 0:8], func=AF.Exp,
           scale=-1.0)
        OP("act", "activation", reads=[lb_r], writes=[lb_r], out=lb, in_=lb, func=AF.Ln, bias=1.0)
        OP("act", "activation", reads=[lb_r], writes=[beta_r], out=beta, in_=lb, func=AF.Exp, scale=-1.0)
        b, br = nb()
        MM(b[:, 0:NT * 8], uincl_f[:], g_all, True, True, reads=[g_r, const_res], writes=[br])
        OP("dve", "tensor_copy", reads=[br], writes=[gc_r], out=gc, in_=b[:, 0:NT * 8])
        OP("dve", "tensor_scalar", reads=[gc_r], writes=[ngc_r], out=ngc, in0=gc, scalar1=-1.0,
           scalar2=None, op0=ALU.mult)
        OP("dve", "tensor_tensor", reads=[gc_r, lb_r], writes=[gb_r], out=gb_, in0=gc, in1=lb,
           op=ALU.subtract)
        OP("act", "activation", reads=[gb_r], writes=[egb_r], out=egb, in_=gb_, func=AF.Exp)
        if GSTOP == 2:
            return
        qT, qT_r = carve.bf16(LP, "qT")
        kT, kT_r = carve.bf16(LP, "kT")
        ktm_f, ktm_r = carve.bf16(NT * 128, "ktm")
        vtm_f, vtm_r = carve.bf16(NT * 128, "vtm")
        ktm = ktm_f.rearrange("p (t d) -> p t d", d=128)
        vtm = vtm_f.rearrange("p (t d) -> p t d", d=128)
        raw, raw_r = carve.f32(515, "raw")
        acc, acc_r = carve.f32(512, "acc")
        qs, qs_r = carve.f32(512, "qs")
        sqb, sqb_r = carve.bf16(512, "sqb")
        rn, rn_r = carve.f32(512, "rn")
        egbrow, egbrow_r = carve.f32(128, "egbrow")
        RwT, RwT_r = carve.f32(128, "RwT")
        dd, dd_r = carve.f32(128, "dd")
        vTt, vTt_r = carve.bf16(512, "vTt")
        zgt, zgt_r = carve.bf16(512, "zgt")
        dg, dg_r = carve.f32(256, "dg")
        E12, E12_r = carve.f32(256, "E12")
        eg, eg_r = carve.f32(128, "eg")
        AN = [carve.f32(256, "AN") for _ in range(2)]
        Pm = [carve.f32(128, "Pm") for _ in range(2)]
        aqkT, aqkT_r = carve.bf16(128, "aqkT")
        Rv, Rv_r = carve.f32(128, "Rv")
        vnew, vnew_r = carve.bf16(128, "vnew")
        qd, qd_r = carve.bf16(128, "qd")
        kend, kend_r = carve.bf16(128, "kend")
        S, S_r = carve.f32(128, "S")
        S_bf, Sbf_r = carve.bf16(128, "Sbf")
        glk, glk_r = carve.f32(2, "glk")
        gl = glk[:, 0:1]
        kes = glk[:, 1:2]

        def l2norm_to(dst, dst_r, t0, n, scl):
            OP("act", "activation", reads=[qs_r], writes=[sqb_r], out=sqb[:, 0:n], in_=qs[:, 0:n],
               func=AF.Square)
            b2, b2r = nb()
            MM(b2[:, 0:n], ones_b[:], sqb[:, 0:n], True, True, reads=[sqb_r, const_res], writes=[b2r])
            OP("act", "activation", reads=[b2r], writes=[rn_r], out=rn[:, 0:n], in_=b2[:, 0:n],
               func=AF.Sqrt, bias=1e-6)
            OP("dve", "reciprocal", reads=[rn_r], writes=[rn_r], out=rn[:, 0:n], in_=rn[:, 0:n])
            OP("dve", "scalar_tensor_tensor", reads=[qs_r, rn_r], writes=[dst_r], out=dst[:, t0:t0 + n],
               in0=qs[:, 0:n], scalar=scl, in1=rn[:, 0:n], op0=ALU.mult, op1=ALU.mult)

        def to_tm(srcT, src_r, c0, n, dst3, dst_r, tt0):
            b2, b2r = nb()
            v = bfv(b2)
            cnt = n // 128
            for j in range(cnt):
                TR(v[:, j * 128:(j + 1) * 128], srcT[:, c0 + j * 128:c0 + (j + 1) * 128], ident_b[:],
                   reads=[src_r, const_res], writes=[b2r])
            OP("dve", "tensor_copy", reads=[b2r], writes=[dst_r], out=dst3[:, tt0:tt0 + cnt, :],
               in_=v[:, 0:n].rearrange("p (t d) -> p t d", d=128))

        for h in range(8):
            OP("pool", "memset", writes=[raw_r], ap=raw[:, 0:3], constant=0.0)

            def ev_q(b, br, t0, n, h=h):
                conv_silu(b, br, n, raw, raw_r, acc, acc_r, cw3[:, :, h], cw_r, None, qs[:, 0:n], qs_r)
                l2norm_to(qT, qT_r, t0, n, 128.0 ** -0.5)
            proj_fm(layer, O_GQ + h * 128, ev_q)
            OP("pool", "memset", writes=[raw_r], ap=raw[:, 0:3], constant=0.0)

            def ev_k(b, br, t0, n, h=h):
                conv_silu(b, br, n, raw, raw_r, acc, acc_r, cw3[:, :, 8 + h], cw_r, None, qs[:, 0:n], qs_r)
                l2norm_to(kT, kT_r, t0, n, 1.0)
                to_tm(kT, kT_r, t0, n, ktm, ktm_r, t0 // 128)
            proj_fm(layer, O_GK + h * 128, ev_k)
            OP("pool", "memset", writes=[raw_r], ap=raw[:, 0:3], constant=0.0)

            def ev_v(b, br, t0, n, h=h):
                conv_silu(b, br, n, raw, raw_r, acc, acc_r, cw3[:, :, 16 + h], cw_r, None, vTt[:, 0:n], vTt_r)
                to_tm(vTt, vTt_r, 0, n, vtm, vtm_r, t0 // 128)
            proj_fm(layer, O_GV + h * 128, ev_v)
            OP("pool", "memset", writes=[S_r], ap=S, constant=0.0)
            OP("pool", "memset", writes=[Sbf_r], ap=S_bf, constant=0.0)
            if GSTOP == 3:
                return
            for c in range(NT):
                if GSTOP == 4 and c == 1:
                    return
                tok = slice(c * 128, (c + 1) * 128)
                gcol = gc3[:, c, h:h + 1]
                ngcol = ngc3[:, c, h:h + 1]
                gbcol = gb3[:, c, h:h + 1]
                OP("dve", "tensor_scalar", reads=[gc_r, const_res], writes=[dg_r], out=dg[:, 0:128],
                   in0=ident_f[:], scalar1=gcol, scalar2=None, op0=ALU.mult)
                OP("dve", "tensor_scalar", reads=[gb_r, const_res], writes=[dg_r], out=dg[:, 128:256],
                   in0=ident_f[:], scalar1=gbcol, scalar2=None, op0=ALU.mult)
                bG, bGr = nb()
                MM(bG[:, 0:256], ones_f[:], dg, True, True, reads=[dg_r, const_res], writes=[bGr])
                OP("act", "activation", reads=[bGr, ngc_r], writes=[E12_r], out=E12, in_=bG[:, 0:256],
                   func=AF.Exp, bias=ngcol)
                OP("pool", "affine_select", reads=[E12_r], writes=[E12_r], out=E12[:, 0:128],
                   in_=E12[:, 0:128], pattern=[[1, 128]], compare_op=ALU.is_ge, fill=0.0, base=0,
                   channel_multiplier=-1)
                OP("pool", "affine_select", reads=[E12_r], writes=[E12_r], out=E12[:, 128:256],
                   in_=E12[:, 128:256], pattern=[[1, 128]], compare_op=ALU.is_gt, fill=0.0, base=0,
                   channel_multiplier=-1)
                OP("act", "activation", reads=[bGr], writes=[eg_r], out=eg, in_=bG[:, 0:128], func=AF.Exp)
                OP("dve", "tensor_copy", reads=[bGr], writes=[glk_r], out=gl, in_=bG[:, 127:128])
                OP("act", "activation", reads=[gc_r, glk_r], writes=[glk_r], out=kes, in_=gcol,
                   func=AF.Exp, scale=-1.0, bias=gl)
                if GSTOP == 11:
                    return
                OP("act", "activation", reads=[bGr], writes=[egbrow_r], out=egbrow, in_=bG[:, 128:256],
                   func=AF.Exp)
                OP("dve", "tensor_tensor", reads=[kT_r, egbrow_r], writes=[RwT_r], out=RwT, in0=kT[:, tok],
                   in1=egbrow, op=ALU.mult)
                bK, bKr = nb()
                MM(bK[:, 0:128], kT[:, tok], qT[:, tok], True, True, reads=[kT_r, qT_r], writes=[bKr])
                MM(bK[:, 128:256], kT[:, tok], kT[:, tok], True, True, reads=[kT_r], writes=[bKr])
                OP("dve", "tensor_tensor", reads=[bKr, E12_r], writes=[aqkT_r], out=aqkT, in0=bK[:, 0:128],
                   in1=E12[:, 0:128], op=ALU.mult)
                an0, an0_r = AN[0]
                pm0, pm0_r = Pm[0]
                OP("dve", "scalar_tensor_tensor", reads=[bKr, E12_r], writes=[an0_r], out=an0[:, 0:128],
                   in0=bK[:, 128:256], scalar=-1.0, in1=E12[:, 128:256], op0=ALU.mult, op1=ALU.mult)
                OP("pool", "tensor_tensor", reads=[an0_r, const_res], writes=[pm0_r], out=pm0,
                   in0=an0[:, 0:128], in1=ident_f[:], op=ALU.add)
                bT, bTr = nb()
                TR(bT[:, 0:128], an0[:, 0:128], ident_f[:], reads=[an0_r, const_res], writes=[bTr])
                OP("act", "activation", reads=[bTr], writes=[an0_r], out=an0[:, 128:256],
                   in_=bT[:, 0:128], func=AF.Copy)
                if GSTOP == 12:
                    return
                ai, pi = 0, 0
                for p in (1, 2, 4, 8, 16, 32, 64):
                    an, an_r = AN[ai]
                    if p > 1:
                        pc, pc_r = Pm[pi]
                        pn, pn_r = Pm[1 - pi]
                        bP, bPr = nb()
                        MM(bP[:, 0:128], an[:, 128:256], pc, True, True, reads=[an_r, pc_r], writes=[bPr])
                        OP("dve", "tensor_tensor", reads=[bPr, pc_r], writes=[pn_r], out=pn,
                           in0=bP[:, 0:128], in1=pc, op=ALU.add)
                        pi = 1 - pi
                    if p < 64:
                        an2, an2_r = AN[1 - ai]
                        bX, bXr = nb()
                        MM(bX[:, 0:128], an[:, 128:256], an[:, 0:128], True, True, reads=[an_r], writes=[bXr])
                        MM(bX[:, 128:256], an[:, 0:128], an[:, 128:256], True, True, reads=[an_r],
                           writes=[bXr])
                        OP("act", "activation", reads=[bXr], writes=[an2_r], out=an2, in_=bX[:, 0:256],
                           func=AF.Copy)
                        ai = 1 - ai
                if GSTOP == 13:
                    return
                pf, pf_r = Pm[pi]
                OP("dve", "tensor_scalar", reads=[vtm_r, beta_r], writes=[Rv_r], out=Rv, in0=vtm[:, c, :],
                   scalar1=beta3[:, c, h:h + 1], scalar2=None, op0=ALU.mult)
                bS, bSr = nb()
                MM(bS[:, 0:128], RwT, S, True, True, reads=[RwT_r, S_r], writes=[bSr])
                OP("dve", "tensor_tensor", reads=[Rv_r, bSr], writes=[dd_r], out=dd, in0=Rv,
                   in1=bS[:, 0:128], op=ALU.subtract)
                bV, bVr = nb()
                MM(bV[:, 0:128], pf, dd, True, True, reads=[pf_r, dd_r], writes=[bVr])
                OP("act", "activation", reads=[bVr], writes=[vnew_r], out=vnew, in_=bV[:, 0:128], func=AF.Copy)
                OP("dve", "tensor_tensor", reads=[qT_r, eg_r], writes=[qd_r], out=qd, in0=qT[:, tok], in1=eg,
                   op=ALU.mult)
                OP("dve", "tensor_scalar", reads=[ktm_r, glk_r], writes=[kend_r], out=kend, in0=ktm[:, c, :],
                   scalar1=kes, scalar2=None, op0=ALU.mult)
                bO, bOr = nb()
                MM(bO[:, 0:128], S_bf, qd, True, False, reads=[Sbf_r, qd_r], writes=[bOr])
                MM(bO[:, 0:128], vnew, aqkT, False, True, reads=[vnew_r, aqkT_r], writes=[bOr])
                OP("act", "activation", reads=[bOr], writes=[oT_res[h]], out=oT[:, h, tok], in_=bO[:, 0:128],
                   func=AF.Copy)
                bD, bDr = nb()
                MM(bD[:, 0:128], kend, vnew, True, True, reads=[kend_r, vnew_r], writes=[bDr])
                OP("dve", "scalar_tensor_tensor", reads=[S_r, eg_r, bDr], writes=[S_r], out=S, in0=S,
                   scalar=eg[:, 127:128], in1=bD[:, 0:128], op0=ALU.mult, op1=ALU.add)
                OP("pool", "tensor_copy", reads=[S_r], writes=[Sbf_r], out=S_bf, in_=S)

            def ev_z(b, br, t0, n, h=h):
                OP("act", "activation", reads=[br], writes=[zgt_r], out=zgt[:, 0:n], in_=b, func=AF.Silu)
                OP("act", "activation", reads=[oT_res[h]], writes=[sqb_r], out=sqb[:, 0:n],
                   in_=oT[:, h, t0:t0 + n], func=AF.Square)
                b2, b2r = nb()
                MM(b2[:, 0:n], ones_b[:], sqb[:, 0:n], True, True, reads=[sqb_r, const_res], writes=[b2r])
                OP("act", "activation", reads=[b2r], writes=[rn_r], out=rn[:, 0:n], in_=b2[:, 0:n],
                   func=AF.Sqrt, scale=1.0 / 128, bias=1e-6)
                OP("dve", "reciprocal", reads=[rn_r], writes=[rn_r], out=rn[:, 0:n], in_=rn[:, 0:n])
                OP("dve", "scalar_tensor_tensor", reads=[oT_res[h], gng_r, rn_r], writes=[oT_res[h]],
                   out=oT[:, h, t0:t0 + n], in0=oT[:, h, t0:t0 + n], scalar=gng[:, 0:1], in1=rn[:, 0:n],
                   op0=ALU.mult, op1=ALU.mult)
                OP("dve", "tensor_tensor", reads=[oT_res[h], zgt_r], writes=[oT_res[h]],
                   out=oT[:, h, t0:t0 + n], in0=oT[:, h, t0:t0 + n], in1=zgt[:, 0:n], op=ALU.mult)
            proj_fm(layer, O_GZ + h * 128, ev_z)


    def unit_ssd(layer, g):
        P.barrier()
        carve.reset()
        dtr, dtr_r = carve.f32(NT * 32, "dtr")
        dt_, dt_r = carve.f32(NT * 32, "dt")
        cs, cs_r = carve.f32(NT * 32, "cs")
        ncs, ncs_r = carve.f32(NT * 32, "ncs")
        dt3 = dt_.rearrange("p (t c) -> p t c", c=32)
        dtr3 = dtr.rearrange("p (t c) -> p t c", c=32)
        cs3 = cs.rearrange("p (t c) -> p t c", c=32)
        ncs3 = ncs.rearrange("p (t c) -> p t c", c=32)
        alog, _ = carve.f32(32, "alog")
        dtb, _ = carve.f32(32, "dtb")
        dcol, _ = carve.f32(16, "dcol")
        ng, _ = carve.f32(16, "ng")
        cbt, cbt_r = carve.f32(128, "cbt")
        cw, cw_r = carve.f32(80, "cw")
        cw3 = cw.rearrange("p (j b) -> p j b", b=20)
        cbias, cbias_r = carve.f32(20, "cbias")
        acc, acc_r = carve.f32(512, "acc")
        cwt, cwt_r = acc, acc_r
        load_small(alog, bc_rows(ssm_a_log_d[layer:layer + 1, :], 32), 0)
        load_small(dtb, bc_rows(ssm_dt_bias_d[layer:layer + 1, :], 32), 1)
        dv = ssm_d_d[layer].rearrange("(b s) -> s b", s=2)
        P.dma("sp", dcol[0:64, :], dv[0:1, :].to_broadcast([64, 16]), prm_res[2], writes=[prm_res[2]], slow=True)
        P.dma("sp", dcol[64:128, :], dv[1:2, :].to_broadcast([64, 16]), prm_res[2], writes=[prm_res[2]],
              slow=True)
        load_small(ng, ssm_norm_g_d[layer].rearrange("(b p) -> p b", p=128), 4)
        alog_r, dtb_r, dcol_r, ng_r = prm_res[0], prm_res[1], prm_res[2], prm_res[4]
        load_conv_w(cw, cw_r, ssm_conv_w_d[layer], 20, cwt, cwt_r, 3)
        P.dma("sp", cbt[0:20, :], ssm_conv_b_d[layer].rearrange("(b p) -> b p", p=128), prm_res[5],
              writes=[cbt_r])
        b, br = nb()
        TR(b[:, 0:20], cbt[0:20, :], ident_f[0:20, 0:20], reads=[cbt_r, const_res], writes=[br])
        OP("dve", "tensor_copy", reads=[br], writes=[cbias_r], out=cbias, in_=b[:, 0:20])
        proj_tm(layer, O_SDT, 32, lambda b, br, tt0, cnt: OP(
            "dve", "tensor_copy", reads=[br], writes=[dtr_r], out=dtr[:, tt0 * 32:(tt0 + cnt) * 32], in_=b))
        OP("dve", "tensor_tensor", reads=[dtr_r, dtb_r], writes=[dt_r], out=dt3, in0=dtr3,
           in1=dtb.unsqueeze(1).to_broadcast([128, NT, 32]), op=ALU.add)
        OP("act", "activation", reads=[dt_r], writes=[dt_r], out=dt_, in_=dt_, func=AF.Exp)
        OP("act", "activation", reads=[dt_r], writes=[dt_r], out=dt_, in_=dt_, func=AF.Ln, bias=1.0)
        OP("pool", "memset", reads=[], writes=[dt_r], ap=dt3[0:PAD, 0, :], constant=0.0)
        OP("act", "activation", reads=[alog_r], writes=[alog_r], out=alog, in_=alog, func=AF.Exp)
        OP("dve", "scalar_tensor_tensor", reads=[dt_r, alog_r], writes=[dtr_r], out=dtr3, in0=dt3, scalar=-1.0,
           in1=alog.unsqueeze(1).to_broadcast([128, NT, 32]), op0=ALU.mult, op1=ALU.mult)
        for hf in range(2):
            w0 = hf * 272
            b, br = nb()
            MM(b[:, 0:272], uincl_f[:], dtr[:, w0:w0 + 272], True, True, reads=[dtr_r, const_res], writes=[br])
            OP("dve", "tensor_copy", reads=[br], writes=[cs_r], out=cs[:, w0:w0 + 272], in_=b[:, 0:272])
        OP("dve", "tensor_scalar", reads=[cs_r], writes=[ncs_r], out=ncs, in0=cs, scalar1=-1.0, scalar2=None,
           op0=ALU.mult)

        BT, BT_r = carve.bf16(LP, "BT")
        CT, CT_r = carve.bf16(LP, "CT")
        xT, xT_r = carve.bf16(LP, "xT")
        Btm_f, Btm_r = carve.bf16(NT * 128, "Btm")
        Btm = Btm_f.rearrange("p (t d) -> p t d", d=128)
        raw, raw_r = carve.f32(515, "raw")
        zgt, zgt_r = carve.bf16(512, "zgt")
        sqs = [carve.bf16(512, "sq") for _ in range(2)]
        rn, rn_r = carve.f32(512, "rn")
        NSET = 3
        sets = []
        for _i in range(NSET):
            st = {}
            st["dg"], st["dg_r"] = carve.f32(256, "dg")
            st["E12"], st["E12_r"] = carve.f32(256, "E12")
            st["eg"], st["eg_r"] = carve.f32(256, "eg")
            st["aqkT"], st["aqkT_r"] = carve.bf16(256, "aqkT")
            st["xs2"], st["xs2_r"] = carve.bf16(256, "xs2")
            st["qd"], st["qd_r"] = carve.bf16(256, "qd")
            st["kend"], st["kend_r"] = carve.bf16(256, "kend")
            st["glk"], st["glk_r"] = carve.f32(4, "glk")
            for nm in ("aqkT", "xs2", "qd", "kend", "E12", "eg"):
                st[nm + "3"] = st[nm].rearrange("p (s l) -> p s l", l=128)
            sets.append(st)
        S, S_r = carve.f32(128, "S")
        S2, S2_r = carve.bf16(256, "S2")
        S23 = S2.rearrange("p (s l) -> p s l", l=128)

        def to_tm(srcT, src_r, c0, n, dst3, dst_r, tt0):
            b2, b2r = nb()
            v = bfv(b2)
            cnt = n // 128
            for j in range(cnt):
                TR(v[:, j * 128:(j + 1) * 128], srcT[:, c0 + j * 128:c0 + (j + 1) * 128], ident_b[:],
                   reads=[src_r, const_res], writes=[b2r])
            OP("dve", "tensor_copy", reads=[b2r], writes=[dst_r], out=dst3[:, tt0:tt0 + cnt, :],
               in_=v[:, 0:n].rearrange("p (t d) -> p t d", d=128))

        OP("pool", "memset", writes=[raw_r], ap=raw[:, 0:3], constant=0.0)

        def ev_B(b, br, t0, n):
            conv_silu(b, br, n, raw, raw_r, acc, acc_r, cw3[:, :, 16 + g], cw_r, cbias[:, 16 + g:17 + g],
                      BT[:, t0:t0 + n], BT_r)
            to_tm(BT, BT_r, t0, n, Btm, Btm_r, t0 // 128)
        proj_fm(layer, O_SB + g * 128, ev_B)
        OP("pool", "memset", writes=[raw_r], ap=raw[:, 0:3], constant=0.0)
        proj_fm(layer, O_SC + g * 128, lambda b, br, t0, n: conv_silu(
            b, br, n, raw, raw_r, acc, acc_r, cw3[:, :, 18 + g], cw_r, cbias[:, 18 + g:19 + g],
            CT[:, t0:t0 + n], CT_r))
        for st in sets:
            OP("pool", "memset", writes=[st["xs2_r"]], ap=st["xs2"], constant=0.0)
        OP("pool", "memset", writes=[S2_r], ap=S2, constant=0.0)
        for j in range(8):
            blk = g * 8 + j
            hh = [2 * blk, 2 * blk + 1]
            OP("pool", "memset", writes=[raw_r], ap=raw[:, 0:3], constant=0.0)
            proj_fm(layer, O_SX + blk * 128, lambda b, br, t0, n, blk=blk: conv_silu(
                b, br, n, raw, raw_r, acc, acc_r, cw3[:, :, blk], cw_r, cbias[:, blk:blk + 1],
                xT[:, t0:t0 + n], xT_r))
            OP("pool", "memset", writes=[S_r], ap=S, constant=0.0)
            for s2 in range(2):
                OP("pool", "memset", writes=[S2_r], ap=S23[:, s2, s2 * 64:(s2 + 1) * 64], constant=0.0)
            def prep(c, hh=hh):
                st = sets[c % NSET]
                dg, dg_r, E12_r, eg, eg_r = st["dg"], st["dg_r"], st["E12_r"], st["eg"], st["eg_r"]
                E3, eg3, aqk3, xs3, qd3, kend3 = st["E123"], st["eg3"], st["aqkT3"], st["xs23"], st["qd3"], st["kend3"]
                glk, glk_r = st["glk"], st["glk_r"]
                tok = slice(c * 128, (c + 1) * 128)
                for s2 in range(2):
                    OP("dve", "tensor_scalar", reads=[cs_r, const_res], writes=[dg_r],
                       out=dg[:, s2 * 128:(s2 + 1) * 128], in0=ident_f[:], scalar1=cs3[:, c, hh[s2]:hh[s2] + 1],
                       scalar2=None, op0=ALU.mult)
                bG, bGr = nb()
                MM(bG[:, 0:256], ones_f[:], dg, True, True, reads=[dg_r, const_res], writes=[bGr])
                bK, bKr = nb()
                MM(bK[:, 0:128], BT[:, tok], CT[:, tok], True, True, reads=[BT_r, CT_r], writes=[bKr])
                bT, bTr = nb()
                TR(bfv(bT)[:, 0:128], xT[:, tok], ident_b[:], reads=[xT_r, const_res], writes=[bTr])
                for s2 in range(2):
                    OP("act", "activation", reads=[bGr, ncs_r], writes=[E12_r], out=E3[:, s2, :],
                       in_=bG[:, s2 * 128:(s2 + 1) * 128], func=AF.Exp, bias=ncs3[:, c, hh[s2]:hh[s2] + 1])
                OP("act", "activation", reads=[bGr], writes=[eg_r], out=eg, in_=bG[:, 0:256], func=AF.Exp)
                OP("dve", "tensor_copy", reads=[bGr], writes=[glk_r], out=glk[:, 0:2],
                   in_=bG[:, 0:256].rearrange("p (s l) -> p s l", l=128)[:, :, 127])
                OP("pool", "affine_select", reads=[E12_r], writes=[E12_r], out=E3, in_=E3,
                   pattern=[[0, 2], [1, 128]], compare_op=ALU.is_ge, fill=0.0, base=0, channel_multiplier=-1)
                for s2 in range(2):
                    OP("act", "activation", reads=[cs_r, glk_r], writes=[glk_r], out=glk[:, 2 + s2:3 + s2],
                       in_=cs3[:, c, hh[s2]:hh[s2] + 1], func=AF.Exp, scale=-1.0, bias=glk[:, s2:s2 + 1])
                for s2 in range(2):
                    OP("dve", "tensor_scalar", reads=[bTr, dt_r], writes=[st["xs2_r"]],
                       out=xs3[:, s2, s2 * 64:(s2 + 1) * 64], in0=bfv(bT)[:, s2 * 64:(s2 + 1) * 64],
                       scalar1=dt3[:, c, hh[s2]:hh[s2] + 1], scalar2=None, op0=ALU.mult)
                OP("dve", "tensor_tensor", reads=[bKr, E12_r], writes=[st["aqkT_r"]], out=aqk3,
                   in0=bK[:, 0:128].unsqueeze(1).to_broadcast([128, 2, 128]), in1=E3, op=ALU.mult)
                OP("dve", "tensor_tensor", reads=[CT_r, eg_r], writes=[st["qd_r"]], out=qd3,
                   in0=CT[:, tok].unsqueeze(1).to_broadcast([128, 2, 128]), in1=eg3, op=ALU.mult)
                for s2 in range(2):
                    OP("dve", "tensor_scalar", reads=[Btm_r, glk_r], writes=[st["kend_r"]], out=kend3[:, s2, :],
                       in0=Btm[:, c, :], scalar1=glk[:, 2 + s2:3 + s2], scalar2=None, op0=ALU.mult)

            def scan(c, j=j):
                st = sets[c % NSET]
                eg_r, eg3, aqk3, xs3, qd3, kend3 = st["eg_r"], st["eg3"], st["aqkT3"], st["xs23"], st["qd3"], st["kend3"]
                tok = slice(c * 128, (c + 1) * 128)
                bD, bDr = nb()
                for s2 in range(2):
                    MM(bD[:, s2 * 64:(s2 + 1) * 64], kend3[:, s2, :], xs3[:, s2, s2 * 64:(s2 + 1) * 64], True, True,
                       reads=[st["kend_r"], st["xs2_r"]], writes=[bDr])
                bO, bOr = nb()
                for s2 in range(2):
                    MM(bO[:, 0:128], S23[:, s2, :], qd3[:, s2, :], s2 == 0, False, reads=[S2_r, st["qd_r"]],
                       writes=[bOr])
                for s2 in range(2):
                    MM(bO[:, 0:128], xs3[:, s2, :], aqk3[:, s2, :], False, s2 == 1,
                       reads=[st["xs2_r"], st["aqkT_r"]], writes=[bOr])
                for s2 in range(2):
                    cols = slice(s2 * 64, (s2 + 1) * 64)
                    OP("dve", "scalar_tensor_tensor", reads=[S_r, eg_r, bDr], writes=[S_r], out=S[:, cols],
                       in0=S[:, cols], scalar=eg3[:, s2, 127:128], in1=bD[:, cols], op0=ALU.mult, op1=ALU.add)
                for s2 in range(2):
                    cols = slice(s2 * 64, (s2 + 1) * 64)
                    OP("pool", "tensor_copy", reads=[S_r], writes=[S2_r], out=S23[:, s2, cols], in_=S[:, cols])
                OP("act", "activation", reads=[bOr], writes=[oT_res[j]], out=oT[:, j, tok], in_=bO[:, 0:128],
                   func=AF.Copy)

            prep(0)
            prep(1)
            for c in range(NT):
                scan(c)
                if c + 2 < NT:
                    prep(c + 2)

            def ev_z(b, br, t0, n, j=j, blk=blk):
                OP("act", "activation", reads=[br], writes=[zgt_r], out=zgt[:, 0:n], in_=b, func=AF.Silu)
                OP("dve", "scalar_tensor_tensor", reads=[xT_r, dcol_r, oT_res[j]], writes=[oT_res[j]],
                   out=oT[:, j, t0:t0 + n], in0=xT[:, t0:t0 + n], scalar=dcol[:, blk:blk + 1],
                   in1=oT[:, j, t0:t0 + n], op0=ALU.mult, op1=ALU.add)
                OP("dve", "tensor_tensor", reads=[oT_res[j], zgt_r], writes=[oT_res[j]],
                   out=oT[:, j, t0:t0 + n], in0=oT[:, j, t0:t0 + n], in1=zgt[:, 0:n], op=ALU.mult)
            proj_fm(layer, O_SZ + blk * 128, ev_z)
        for (t0, n) in TG:
            bq, bqr = nb()
            for j in range(8):
                sq, sq_r = sqs[j % 2]
                OP("dve", "tensor_tensor", reads=[oT_res[j]], writes=[sq_r], out=sq[:, 0:n],
                   in0=oT[:, j, t0:t0 + n], in1=oT[:, j, t0:t0 + n], op=ALU.mult)
                MM(bq[:, 0:n], ones_b[:], sq[:, 0:n], j == 0, j == 7, reads=[sq_r, const_res], writes=[bqr])
            OP("act", "activation", reads=[bqr], writes=[rn_r], out=rn[:, 0:n], in_=bq[:, 0:n], func=AF.Sqrt,
               scale=1.0 / 1024, bias=1e-6)
            OP("dve", "reciprocal", reads=[rn_r], writes=[rn_r], out=rn[:, 0:n], in_=rn[:, 0:n])
            for j in range(8):
                blk = g * 8 + j
                OP("dve", "scalar_tensor_tensor", reads=[oT_res[j], ng_r, rn_r], writes=[oT_res[j]],
                   out=oT[:, j, t0:t0 + n], in0=oT[:, j, t0:t0 + n], scalar=ng[:, blk:blk + 1], in1=rn[:, 0:n],
                   op0=ALU.mult, op1=ALU.mult)

    def final_out():
        P.barrier()
        carve.reset()
        fg, fg_r = carve.f32(1024, "fg")
        junk, junk_r = carve.f32(1024, "junk")
        outs = [carve.f32(1024, "ot") for _ in range(2)]
        load_small(fg, fin_g_d[0:1, :].to_broadcast([128, 1024]), 0, slow=True)
        for tt in range(1, NT):
            ot, ot_r = outs[tt % 2]
            osr = out_res[tt % 2]
            ss = small[:, 0:1]
            rs = small[:, 1:2]
            OP("act", "activation", reads=[h_res[tt]], writes=[junk_r, small_res], out=junk,
               in_=h_sb[:, tt, :], func=AF.Square, accum_out=ss)
            OP("act", "activation", reads=[small_res], writes=[small_res], out=rs, in_=ss,
               func=AF.Sqrt, scale=1.0 / D, bias=1e-6)
            OP("dve", "reciprocal", reads=[small_res], writes=[small_res], out=rs, in_=rs)
            OP("dve", "scalar_tensor_tensor", reads=[h_res[tt], small_res, prm_res[0], osr], writes=[ot_r],
               out=ot, in0=h_sb[:, tt, :], scalar=rs, in1=fg, op0=ALU.mult, op1=ALU.mult)
            P.dma("sp", out_d[(tt - 1) * 128:tt * 128, :], ot, osr, reads=[ot_r], writes=[osr])

    def dump_oT():
        P.barrier()
        carve.reset()
        t32, t32_r = carve.f32(LP, "dump")
        for k in range(8):
            OP("dve", "tensor_copy", reads=oT_res, writes=[t32_r], out=t32, in_=oT[:, k, :])
            P.dma("sp", dbg_d[k * 128:(k + 1) * 128, :], t32, dbg_res, reads=[t32_r])

    def dump_h():
        P.barrier()
        for tt in range(NT):
            P.dma("sp", dbg_d[tt * 128:(tt + 1) * 128, :], h_sb[:, tt, :], dbg_res, reads=[h_res[tt]])

    done = False
    for layer in range(depth):
        P.new_phase()
        layer_norm_T(layer)
        if "A" not in skip:
            unit_attention(layer)
        if stop_after == ("A", layer):
            dump_oT()
            done = True
            break
        if "A" not in skip:
            unit_merge(layer, 0, 0, O_GATE)
        if stop_after == ("Am", layer):
            dump_h()
            done = True
            break
        if "B" not in skip:
            unit_gdn(layer)
        if stop_after == ("B", layer):
            dump_oT()
            done = True
            break
        if "B" not in skip:
            unit_merge(layer, 1, 0, O_GATE + 1024)
        if stop_after == ("Bm", layer):
            dump_h()
            done = True
            break
        for g in range(2):
            if "C" in skip:
                break
            unit_ssd(layer, g)
            if stop_after == ("C%d" % g, layer):
                dump_oT()
                done = True
                break
            unit_merge(layer, 2, g * 1024, O_GATE + 2048)
        if done:
            break
        OP("pool", "memset", reads=[], writes=[h_res[0]], ap=h_sb[0:PAD, 0, :], constant=0.0)
        if stop_after == ("L", layer):
            dump_h()
            done = True
            break
    if not done:
        final_out()

    P.emit()
    import os as _os
    if _os.environ.get("KDEBUG"):
        print("ops", len(P.ops), "sems", P.nsem, "counts", P.counts)
    es.close()
    return nc


_NAMES = ["meta_tokens", "norm_g", "w_in", "gdn_conv_w", "gdn_a_log", "gdn_dt_bias", "gdn_norm_g",
          "ssm_conv_w", "ssm_conv_b", "ssm_a_log", "ssm_dt_bias", "ssm_d", "ssm_norm_g",
          "w_branch_a", "w_branch_b", "w_branch_c", "w_out"]


def make_in_maps(inputs, ncores=8):
    shared = {n: np.ascontiguousarray(np.asarray(inputs[n], dtype=np.float32)) for n in _NAMES}
    shared["final_norm_g"] = np.ascontiguousarray(
        np.asarray(inputs["final_norm_g"], dtype=np.float32).reshape(1, D))
    x = np.asarray(inputs["x"], dtype=np.float32)
    maps = []
    for c in range(ncores):
        m = dict(shared)
        m["x"] = np.ascontiguousarray(x[c])
        maps.append(m)
    return maps


def kernel(**inputs):
    nc = build()
    in_maps = make_in_maps(inputs)
    res = run_bass_kernel_spmd(nc, in_maps, core_ids=list(range(8)))
    return np.stack([r["out"] for r in res.results], axis=0)
```

```python
from contextlib import ExitStack
import numpy as np
import concourse.bass as bass
import concourse.mybir as mybir
from concourse.bass_utils import run_bass_kernel_spmd

F32 = mybir.dt.float32
BF16 = mybir.dt.bfloat16
AF = mybir.ActivationFunctionType
ALU = mybir.AluOpType

D = 1024
SEQ = 2048
NMETA = 16
PAD = 112
LP = SEQ + NMETA + PAD
NT = LP // 128
DIN = 15920
DEPTH = 4
TG = [(0, 512), (512, 512), (1024, 512), (1536, 512), (2048, 128)]

O_SBQ, O_SBK, O_SBV, O_SBZ = 0, 1024, 2048, 3072
O_GQ, O_GK, O_GV, O_GZ, O_GB, O_GA = 4096, 5120, 6144, 7168, 8192, 8200
O_SZ, O_SX, O_SB, O_SC, O_SDT, O_GATE = 8208, 10256, 12304, 12560, 12816, 12848


class Res:
    __slots__ = ("name", "last_w", "readers", "sem", "dcount", "excl")

    def __init__(self, name):
        self.name = name
        self.excl = False
        self.last_w = None
        self.readers = []
        self.sem = None
        self.dcount = 0


class Op:
    __slots__ = ("eng", "fn", "deps", "is_dma", "sem", "val", "signal", "phase", "pe_mm")


class Prog:
    ENGS = ("pe", "act", "dve", "pool", "sp")

    def __init__(self, nc, es):
        self.nc = nc
        self.es = es
        self.ops = []
        self.phase = 0
        self.bar_deps = []
        self.last_on_eng = {}
        self.dmas_since_bar = []
        self.phase_sems = {}
        self.nsem = 0

    def new_sem(self, name):
        self.nsem += 1
        return self.es.enter_context(self.nc.semaphore(name))

    def res(self, name, dma=False):
        r = Res(name)
        if dma:
            r.sem = self.new_sem("d_" + name)
        return r

    def new_phase(self):
        self.phase += 1

    def _track(self, op, reads, writes):
        ex = [r for r in reads if r.excl and r not in writes]
        if ex:
            reads = [r for r in reads if not r.excl]
            writes = list(writes) + ex
        deps = set(self.bar_deps)
        for r in list(reads) + list(writes):
            if r.last_w is not None:
                deps.add(r.last_w)
        for w in writes:
            for rd in w.readers:
                deps.add(rd)
        idx = len(self.ops)
        deps.discard(idx)
        op.deps = deps
        for r in reads:
            r.readers.append(idx)
        for w in writes:
            w.last_w = idx
            w.readers = []
        self.ops.append(op)
        self.last_on_eng[op.eng] = idx
        return idx

    def op(self, eng, fn, reads=(), writes=(), mm=False):
        o = Op()
        o.eng = eng
        o.fn = fn
        o.is_dma = False
        o.sem = None
        o.val = 0
        o.signal = False
        o.phase = self.phase
        o.pe_mm = mm
        return self._track(o, reads, writes)

    def dma(self, eng, out, in_, semres, reads=(), writes=(), slow=False):
        o = Op()
        o.eng = eng
        if slow:
            o.fn = lambda e: e.dma_start(out=out, in_=in_, allow_slow_non_contiguous=True)
        else:
            o.fn = lambda e: e.dma_start(out=out, in_=in_)
        o.is_dma = True
        o.sem = semres.sem
        semres.dcount += 1
        o.val = 16 * semres.dcount
        o.signal = True
        o.phase = self.phase
        o.pe_mm = False
        idx = self._track(o, reads, writes)
        self.dmas_since_bar.append(idx)
        return idx

    def barrier(self):
        self.bar_deps = list(self.last_on_eng.values()) + list(self.dmas_since_bar)
        self.dmas_since_bar = []

    def emit(self):
        nc = self.nc
        ops = self.ops
        for i, o in enumerate(ops):
            for j in o.deps:
                p = ops[j]
                if p.is_dma:
                    continue
                if p.eng == "pe" and o.eng == "pe":
                    continue
                p.signal = True
        cnt = {}
        for o in ops:
            if o.is_dma or not o.signal:
                continue
            key = (o.eng, o.phase)
            if key not in self.phase_sems:
                self.phase_sems[key] = self.new_sem("s_%s_%d" % key)
                cnt[key] = 0
            cnt[key] += 1
            o.sem = self.phase_sems[key]
            o.val = cnt[key]
        self.counts = dict(cnt)
        per_eng = {e: [] for e in self.ENGS}
        for i, o in enumerate(ops):
            per_eng[o.eng].append(i)
        final_waits = {}
        for o in ops:
            if o.is_dma:
                final_waits[id(o.sem)] = (o.sem, max(o.val, final_waits.get(id(o.sem), (None, 0))[1]))

        def run(engname, e):
            known = {}
            for i in per_eng[engname]:
                o = ops[i]
                need = {}
                for j in o.deps:
                    p = ops[j]
                    if p.sem is None:
                        continue
                    k = id(p.sem)
                    if k not in need or need[k][1] < p.val:
                        need[k] = (p.sem, p.val)
                for k, (s, v) in need.items():
                    if known.get(k, 0) < v:
                        e.wait_ge(s, v)
                        known[k] = v
                ins = o.fn(e)
                if o.is_dma:
                    ins.then_inc(o.sem, 16)
                elif o.signal:
                    ins.then_inc(o.sem, 1)
            if engname == "sp":
                for k, (s, v) in final_waits.items():
                    e.wait_ge(s, v)

        with nc.Block() as block:
            @block.tensor
            def _(e):
                run("pe", e)

            @block.scalar
            def _(e):
                run("act", e)

            @block.vector
            def _(e):
                run("dve", e)

            @block.gpsimd
            def _(e):
                run("pool", e)

            @block.sync
            def _(e):
                run("sp", e)


class Carver:
    def __init__(self, prog, ap, nwords):
        self.prog = prog
        self.ap = ap
        self.n = nwords
        self.off = 0
        self.k = 0

    def reset(self):
        self.off = 0

    def f32(self, cols, name="w"):
        a = self.ap[:, self.off:self.off + cols]
        self.off += cols
        assert self.off <= self.n, ("work overflow", self.off, self.n)
        self.k += 1
        return a, self.prog.res("%s%d" % (name, self.k))

    def bf16(self, cols, name="w"):
        words = (cols + 1) // 2
        a = self.ap[:, self.off:self.off + words].bitcast(BF16)
        self.off += words
        assert self.off <= self.n, ("work overflow", self.off, self.n)
        self.k += 1
        return a[:, 0:cols], self.prog.res("%s%d" % (name, self.k))


def build(depth=DEPTH, dbg=None, stop_after=None, GSTOP=0, skip=()):
    nc = bass.Bass("TRN2", target_bir_lowering=False)
    es = ExitStack()
    P = Prog(nc, es)

    def din(name, shape):
        return nc.dram_tensor(name, list(shape), F32, kind="ExternalInput").ap()

    x_d = din("x", (SEQ, D))
    meta_d = din("meta_tokens", (NMETA, D))
    norm_g_d = din("norm_g", (DEPTH, D))
    w_in_d = din("w_in", (DEPTH, D, DIN))
    gdn_conv_w_d = din("gdn_conv_w", (DEPTH, 4, 3072))
    gdn_a_log_d = din("gdn_a_log", (DEPTH, 8))
    gdn_dt_bias_d = din("gdn_dt_bias", (DEPTH, 8))
    gdn_norm_g_d = din("gdn_norm_g", (DEPTH, 128))
    ssm_conv_w_d = din("ssm_conv_w", (DEPTH, 4, 2560))
    ssm_conv_b_d = din("ssm_conv_b", (DEPTH, 2560))
    ssm_a_log_d = din("ssm_a_log", (DEPTH, 32))
    ssm_dt_bias_d = din("ssm_dt_bias", (DEPTH, 32))
    ssm_d_d = din("ssm_d", (DEPTH, 32))
    ssm_norm_g_d = din("ssm_norm_g", (DEPTH, 2048))
    w_br_d = [din("w_branch_a", (DEPTH, 1024, D)), din("w_branch_b", (DEPTH, 1024, D)),
              din("w_branch_c", (DEPTH, 2048, D))]
    w_out_d = din("w_out", (DEPTH, D, D))
    fin_g_d = din("final_norm_g", (1, D))
    out_d = nc.dram_tensor("out", [SEQ, D], F32, kind="ExternalOutput").ap()
    dbg_d = None
    if dbg is not None:
        dbg_d = nc.dram_tensor("dbg", list(dbg["shape"]), F32, kind="ExternalOutput").ap()

    def sb(name, shape, dt):
        return es.enter_context(nc.sbuf_tensor(name, list(shape), dt))

    def ps(name, shape, dt=F32):
        return es.enter_context(nc.psum_tensor(name, list(shape), dt))

    def OP(eng, meth, reads=(), writes=(), **kw):
        return P.op(eng, lambda e: getattr(e, meth)(**kw), reads=reads, writes=writes)

    def MM(out, lhsT, rhs, start, stop, reads, writes):
        return P.op("pe", lambda e: e.matmul(out=out, lhsT=lhsT, rhs=rhs, start=start, stop=stop),
                    reads=reads, writes=writes, mm=True)

    def TR(out, in_, ident, reads, writes):
        return P.op("pe", lambda e: e.transpose(out=out, in_=in_, identity=ident),
                    reads=reads, writes=writes, mm=True)

    h_sb = sb("h", (128, NT, D), F32)
    uT = sb("uT", (128, 8, LP), BF16)
    oT = sb("oT", (128, 8, LP), BF16)
    stage = [sb("stg%d" % i, (128, 8, 128), F32) for i in range(2)]
    wslab = [sb("wsl%d" % i, (128, 8, 128), BF16) for i in range(2)]
    WORKW = 13312
    work = sb("work", (128, WORKW), F32)
    ident_f = sb("ident_f", (128, 128), F32)
    ident_b = sb("ident_b", (128, 128), BF16)
    ones_b = sb("ones_b", (128, 128), BF16)
    ones_f = sb("ones_f", (128, 128), F32)
    uincl_f = sb("uincl_f", (128, 128), F32)
    tmat = sb("tmat", (128, 4, 128), BF16)
    mw = sb("mw", (128, 896), BF16)
    small = sb("small", (128, 64), F32)
    gfm = sb("gfm", (128, 8), F32)

    h_res = [P.res("h%d" % t) for t in range(NT)]
    hload_res = [P.res("hl%d" % i, dma=True) for i in range(5)]
    uT_res = [P.res("uT%d" % t) for t in range(NT)]
    oT_res = [P.res("oT%d" % k) for k in range(8)]
    stage_res = [P.res("stg%d" % i, dma=True) for i in range(2)]
    wslab_res = [P.res("wsl%d" % i) for i in range(2)]
    const_res = P.res("const")
    gfm_res = P.res("gfm", dma=True)
    small_res = P.res("small")
    dbg_res = P.res("dbg", dma=True)
    prm_res = [P.res("prm%d" % i, dma=True) for i in range(8)]
    out_res = [P.res("outst%d" % i, dma=True) for i in range(2)]

    banks = [ps("bank%d" % i, (128, 512), F32) for i in range(8)]
    bank_res = [P.res("bank%d" % i) for i in range(8)]
    for _r in bank_res:
        _r.excl = True
    carve = Carver(P, work, WORKW)
    bctr = [0]

    def nb(nmax=8):
        i = bctr[0] % nmax
        bctr[0] += 1
        return banks[i], bank_res[i]

    def bfv(bank):
        return bank[:].bitcast(BF16)

    cr = [const_res]
    OP("pool", "memset", writes=cr, ap=ident_f[:], constant=1.0)
    OP("pool", "affine_select", writes=cr, out=ident_f[:], in_=ident_f[:], pattern=[[-1, 128]],
       compare_op=ALU.is_equal, fill=0.0, base=0, channel_multiplier=1)
    OP("pool", "tensor_copy", reads=cr, writes=cr, out=ident_b[:], in_=ident_f[:])
    OP("pool", "memset", writes=cr, ap=ones_b[:], constant=1.0)
    OP("pool", "memset", writes=cr, ap=ones_f[:], constant=1.0)
    OP("pool", "memset", writes=cr, ap=uincl_f[:], constant=1.0)
    OP("pool", "affine_select", writes=cr, out=uincl_f[:], in_=uincl_f[:], pattern=[[1, 128]],
       compare_op=ALU.is_ge, fill=0.0, base=0, channel_multiplier=-1)
    OP("pool", "memset", writes=cr, ap=tmat[:], constant=-1.0)
    for v in (0, 2):
        OP("pool", "affine_select", writes=cr, out=tmat[:, v, :], in_=tmat[:, v, :], pattern=[[-1, 128]],
           compare_op=ALU.is_gt, fill=0.0, base=0, channel_multiplier=1)
    for v in (1, 3):
        OP("pool", "affine_select", writes=cr, out=tmat[:, v, :], in_=tmat[:, v, :], pattern=[[1, 128]],
           compare_op=ALU.is_ge, fill=0.0, base=0, channel_multiplier=-1)
    OP("pool", "memset", writes=cr, ap=tmat[0:PAD, 2:4, :], constant=0.0)
    OP("pool", "memset", writes=cr, ap=mw[:], constant=1.0)
    OP("pool", "affine_select", writes=cr, out=mw[:], in_=mw[:], pattern=[[1, 896]],
       compare_op=ALU.is_gt, fill=0.0, base=-384, channel_multiplier=-1)

    OP("pool", "memset", writes=[h_res[0]], ap=h_sb[:, 0, :], constant=0.0)
    P.dma("sp", h_sb[PAD:128, 0, :], meta_d[:, :], hload_res[4], writes=[h_res[0]])
    xv = x_d.rearrange("(t p) d -> p t d", p=128)
    for i in range(4):
        P.dma("sp", h_sb[:, 1 + 4 * i:5 + 4 * i, :], xv[:, 4 * i:4 * i + 4, :], hload_res[i],
              writes=[h_res[1 + 4 * i + j] for j in range(4)])

    wctr = [0]

    def stream_slab(src, C, dst=None, dst_res=None):
        i = wctr[0] % 2
        wctr[0] += 1
        P.dma("sp", stage[i][:, :, 0:C], src, stage_res[i], writes=[stage_res[i]])
        if dst is None:
            dst, dst_res = wslab[i][:, :, 0:C], wslab_res[i]
        OP("pool", "tensor_copy", reads=[stage_res[i]], writes=[dst_res], out=dst,
           in_=stage[i][:, :, 0:C])
        return dst, dst_res

    def win(layer, c0, C):
        return w_in_d[layer].rearrange("(k p) c -> p k c", p=128)[:, :, c0:c0 + C]

    def tiles_of(t0, n):
        return uT_res[t0 // 128:(t0 + n + 127) // 128]

    def proj_fm(layer, c0, evac, ncol=128):
        w, wr = stream_slab(win(layer, c0, ncol), ncol)
        for (t0, n) in TG:
            b, br = nb()
            for k in range(8):
                MM(b[0:ncol, 0:n], w[:, k, :], uT[:, k, t0:t0 + n], k == 0, k == 7,
                   reads=[wr] + tiles_of(t0, n), writes=[br])
            evac(b[0:ncol, 0:n], br, t0, n)

    def proj_tm(layer, c0, ncol, evac):
        w, wr = stream_slab(win(layer, c0, ncol), ncol)
        per = min(4, 512 // ncol)
        for tt0 in range(0, NT, per):
            cnt = min(per, NT - tt0)
            b, br = nb()
            for j in range(cnt):
                tt = tt0 + j
                for k in range(8):
                    MM(b[:, j * ncol:(j + 1) * ncol], uT[:, k, tt * 128:(tt + 1) * 128], w[:, k, :],
                       k == 0, k == 7, reads=[wr, uT_res[tt]], writes=[br])
            evac(b[:, 0:cnt * ncol], br, tt0, cnt)

    def load_small(dst, src, ri, slow=True):
        P.dma("sp", dst, src, prm_res[ri], writes=[prm_res[ri]], slow=slow)

    def layer_norm_T(layer):
        P.barrier()
        carve.reset()
        P.dma("sp", gfm[:, :], norm_g_d[layer].rearrange("(k p) -> p k", p=128), gfm_res,
              writes=[gfm_res], slow=True)
        junk, junk_r = carve.f32(1024, "junk")
        uns = [carve.bf16(1024, "un") for _ in range(2)]
        for tt in range(NT):
            un, un_r = uns[tt % 2]
            ss = small[:, 0:1]
            rs = small[:, 1:2]
            OP("act", "activation", reads=[h_res[tt]], writes=[junk_r, small_res], out=junk,
               in_=h_sb[:, tt, :], func=AF.Square, accum_out=ss)
            OP("act", "activation", reads=[small_res], writes=[small_res], out=rs, in_=ss,
               func=AF.Sqrt, scale=1.0 / D, bias=1e-6)
            OP("dve", "reciprocal", reads=[small_res], writes=[small_res], out=rs, in_=rs)
            OP("dve", "tensor_scalar", reads=[h_res[tt], small_res], writes=[un_r], out=un,
               in0=h_sb[:, tt, :], scalar1=rs, scalar2=None, op0=ALU.mult)
            b, br = nb()
            pT = bfv(b)
            for k in range(8):
                TR(pT[:, k * 128:(k + 1) * 128], un[:, k * 128:(k + 1) * 128], ident_b[:],
                   reads=[un_r, const_res], writes=[br])
            OP("dve", "tensor_tensor", reads=[br, gfm_res], writes=[uT_res[tt]],
               out=uT[:, :, tt * 128:(tt + 1) * 128], in0=pT.rearrange("p (k t) -> p k t", t=128),
               in1=gfm[:, :].unsqueeze(2).to_broadcast([128, 8, 128]), op=ALU.mult)

    def unit_attention(layer):
        P.barrier()
        carve.reset()
        qT, qT_r = carve.bf16(LP, "qT")
        kT, kT_r = carve.bf16(LP, "kT")
        zg, zg_r = carve.bf16(LP, "zg")
        vtm_flat, vtm_r = carve.bf16(NT * 128, "vtm")
        vtm = vtm_flat.rearrange("p (t d) -> p t d", d=128)
        NBUF = 3
        AHEAD = 2
        spf = [carve.f32(512, "spf") for _ in range(NBUF)]
        spb = [carve.bf16(512, "spb") for _ in range(NBUF)]
        t2 = [carve.f32(512, "t2") for _ in range(NBUF)]
        wb = [carve.bf16(512, "wb") for _ in range(NBUF)]
        scale = 128.0 ** -0.5
        gctr = [0]
        for h in range(8):
            proj_fm(layer, O_SBQ + h * 128, lambda b, br, t0, n: OP(
                "act", "activation", reads=[br], writes=[qT_r], out=qT[:, t0:t0 + n], in_=b,
                func=AF.Copy, scale=scale))
            proj_fm(layer, O_SBK + h * 128, lambda b, br, t0, n: OP(
                "dve", "tensor_copy", reads=[br], writes=[kT_r], out=kT[:, t0:t0 + n], in_=b))
            proj_fm(layer, O_SBZ + h * 128, lambda b, br, t0, n: OP(
                "act", "activation", reads=[br], writes=[zg_r], out=zg[:, t0:t0 + n], in_=b,
                func=AF.Silu))
            proj_tm(layer, O_SBV + h * 128, 128, lambda b, br, tt0, cnt: OP(
                "dve", "tensor_copy", reads=[br], writes=[vtm_r], out=vtm[:, tt0:tt0 + cnt, :],
                in_=b.rearrange("p (t d) -> p t d", d=128)))
            its = []
            for g in range(5):
                qb0 = 4 * g
                nq = 512 if g < 4 else 128
                kmax = qb0 + nq // 128 - 1
                gi = gctr[0] % 2
                gctr[0] += 1
                for idx, kb in enumerate(range(kmax, -1, -1)):
                    its.append(dict(qb0=qb0, nq=nq, q0=qb0 * 128, idx=idx, kb=kb, gi=gi, zb=None))

            n_it = len(its)

            def PZ(j):
                d = its[j]
                nq, kb, q0 = d["nq"], d["kb"], d["q0"]
                zb, zr = nb(4)
                d["zb"] = (zb, zr)
                MM(zb[:, 0:nq], kT[:, kb * 128:(kb + 1) * 128], qT[:, q0:q0 + nq], True, True,
                   reads=[kT_r, qT_r], writes=[zr])

            def AE(j):
                d = its[j]
                nq = d["nq"]
                zb, zr = d["zb"]
                sp_a, sp_r = spf[j % NBUF]
                OP("act", "activation", reads=[zr], writes=[sp_r], out=sp_a[:, 0:nq],
                   in_=zb[:, 0:nq], func=AF.Exp)
                OP("act", "activation", reads=[sp_r], writes=[sp_r], out=sp_a[:, 0:nq],
                   in_=sp_a[:, 0:nq], func=AF.Ln, bias=1.0)

            def DS(j):
                d = its[j]
                nq, kb, qb0 = d["nq"], d["kb"], d["qb0"]
                zb, zr = d["zb"]
                sp_a, sp_r = spf[j % NBUF]
                spb_a, spb_r = spb[j % NBUF]
                t2_a, t2_r = t2[j % NBUF]
                if kb >= qb0:
                    moff = 384 - 128 * (kb - qb0)
                    OP("dve", "tensor_tensor", reads=[sp_r, const_res], writes=[spb_r],
                       out=spb_a[:, 0:nq], in0=sp_a[:, 0:nq], in1=mw[:, moff:moff + nq],
                       op=ALU.mult)
                else:
                    OP("dve", "tensor_copy", reads=[sp_r], writes=[spb_r], out=spb_a[:, 0:nq],
                       in_=sp_a[:, 0:nq])
                OP("dve", "tensor_tensor", reads=[zr, sp_r], writes=[t2_r], out=t2_a[:, 0:nq],
                   in0=zb[:, 0:nq], in1=sp_a[:, 0:nq], op=ALU.subtract)

            def banksof(d):
                gi = d["gi"]
                return banks[4 + gi], bank_res[4 + gi], banks[6 + gi], bank_res[6 + gi]

            def PT(j):
                d = its[j]
                nq, kb, idx = d["nq"], d["kb"], d["idx"]
                A_b, A_r, O_b, O_r = banksof(d)
                spb_a, spb_r = spb[j % NBUF]
                tv = 2 if kb == 0 else 0
                MM(A_b[:, 0:nq], tmat[:, tv, :], spb_a[:, 0:nq], idx == 0, False,
                   reads=[spb_r, const_res], writes=[A_r])

            def DA(j):
                d = its[j]
                nq = d["nq"]
                A_b, A_r, O_b, O_r = banksof(d)
                t2_a, t2_r = t2[j % NBUF]
                OP("dve", "tensor_tensor", reads=[A_r, t2_r], writes=[t2_r], out=t2_a[:, 0:nq],
                   in0=A_b[:, 0:nq], in1=t2_a[:, 0:nq], op=ALU.add)

            def AW(j):
                d = its[j]
                nq, kb, qb0 = d["nq"], d["kb"], d["qb0"]
                t2_a, t2_r = t2[j % NBUF]
                wb_a, wb_r = wb[j % NBUF]
                OP("act", "activation", reads=[t2_r], writes=[wb_r], out=wb_a[:, 0:nq],
                   in_=t2_a[:, 0:nq], func=AF.Exp)
                if kb >= qb0:
                    moff = 384 - 128 * (kb - qb0)
                    OP("pool", "tensor_tensor", reads=[wb_r, const_res], writes=[wb_r],
                       out=wb_a[:, 0:nq], in0=wb_a[:, 0:nq], in1=mw[:, moff:moff + nq],
                       op=ALU.mult)

            def PL(j):
                d = its[j]
                nq, kb = d["nq"], d["kb"]
                A_b, A_r, O_b, O_r = banksof(d)
                spb_a, spb_r = spb[j % NBUF]
                tv = 2 if kb == 0 else 0
                MM(A_b[:, 0:nq], tmat[:, tv + 1, :], spb_a[:, 0:nq], False, kb == 0,
                   reads=[spb_r, const_res], writes=[A_r])

            def PO(j, h=h):
                d = its[j]
                nq, kb, idx, q0 = d["nq"], d["kb"], d["idx"], d["q0"]
                A_b, A_r, O_b, O_r = banksof(d)
                wb_a, wb_r = wb[j % NBUF]
                MM(O_b[:, 0:nq], vtm[:, kb, :], wb_a[:, 0:nq], idx == 0, kb == 0,
                   reads=[vtm_r, wb_r], writes=[O_r])
                if kb == 0:
                    OP("dve", "tensor_tensor", reads=[O_r, zg_r], writes=[oT_res[h]],
                       out=oT[:, h, q0:q0 + nq], in0=O_b[:, 0:nq], in1=zg[:, q0:q0 + nq], op=ALU.mult)

            PZ(0)
            AE(0)
            PZ(1)
            AE(1)
            DS(0)
            for j in range(n_it):
                PT(j)
                if j + 2 < n_it:
                    PZ(j + 2)
                DA(j)
                if j + 1 < n_it:
                    DS(j + 1)
                AW(j)
                if j + 2 < n_it:
                    AE(j + 2)
                PL(j)
                if j >= 1:
                    PO(j - 1)
            PO(n_it - 1)

    def unit_merge(layer, br_idx, row0, gate_c0):
        P.barrier()
        carve.reset()
        wbr_f, wbr_r = carve.bf16(8 * 1024, "wbr")
        wg_f, wg_r = carve.bf16(8 * 1024, "wg")
        wbr = wbr_f.rearrange("p (k c) -> p k c", c=1024)
        wg = wg_f.rearrange("p (k c) -> p k c", c=1024)
        sgs = [carve.bf16(1024, "sg") for _ in range(2)]
        src_br = w_br_d[br_idx][layer].rearrange("(k p) c -> p k c", p=128)
        for cb in range(8):
            stream_slab(src_br[:, row0 // 128:row0 // 128 + 8, cb * 128:(cb + 1) * 128], 128,
                        dst=wbr[:, :, cb * 128:(cb + 1) * 128], dst_res=wbr_r)
            stream_slab(win(layer, gate_c0 + cb * 128, 128), 128,
                        dst=wg[:, :, cb * 128:(cb + 1) * 128], dst_res=wg_r)
        for tt in range(NT):
            tok = slice(tt * 128, (tt + 1) * 128)
            pb = [nb(), nb()]
            gb = [nb(), nb()]
            for cb in range(8):
                b, br = pb[cb // 4]
                for fk in range(8):
                    MM(b[:, (cb % 4) * 128:(cb % 4 + 1) * 128], wbr[:, fk, cb * 128:(cb + 1) * 128],
                       oT[:, fk, tok], fk == 0, fk == 7, reads=[wbr_r, oT_res[fk]], writes=[br])
            for cb in range(8):
                b, br = gb[cb // 4]
                for k in range(8):
                    MM(b[:, (cb % 4) * 128:(cb % 4 + 1) * 128], wg[:, k, cb * 128:(cb + 1) * 128],
                       uT[:, k, tok], k == 0, k == 7, reads=[wg_r, uT_res[tt]], writes=[br])
            sg, sg_r = sgs[tt % 2]
            for hf in range(2):
                OP("act", "activation", reads=[gb[hf][1]], writes=[sg_r],
                   out=sg[:, hf * 512:(hf + 1) * 512], in_=gb[hf][0][:, :], func=AF.Sigmoid)
            for hf in range(2):
                OP("dve", "tensor_tensor", reads=[pb[hf][1], sg_r],
                   writes=[oT_res[4 * hf + j] for j in range(4)],
                   out=oT[:, 4 * hf:4 * hf + 4, tok],
                   in0=pb[hf][0][:, :].rearrange("p (c t) -> p c t", t=128),
                   in1=sg[:, hf * 512:(hf + 1) * 512].rearrange("p (c t) -> p c t", t=128),
                   op=ALU.mult)
        src_o = w_out_d[layer].rearrange("(k p) c -> p k c", p=128)
        for ds in range(8):
            w, wr = stream_slab(src_o[:, :, ds * 128:(ds + 1) * 128], 128)
            for tt0 in range(0, NT, 4):
                cnt = min(4, NT - tt0)
                b, br = nb()
                for j in range(cnt):
                    tt = tt0 + j
                    for cb in range(8):
                        MM(b[:, j * 128:(j + 1) * 128], oT[:, cb, tt * 128:(tt + 1) * 128], w[:, cb, :],
                           cb == 0, cb == 7, reads=[wr, oT_res[cb]], writes=[br])
                OP("dve", "tensor_tensor", reads=[br] + h_res[tt0:tt0 + cnt], writes=h_res[tt0:tt0 + cnt],
                   out=h_sb[:, tt0:tt0 + cnt, ds * 128:(ds + 1) * 128],
                   in0=b[:, 0:cnt * 128].rearrange("p (t d) -> p t d", d=128),
                   in1=h_sb[:, tt0:tt0 + cnt, ds * 128:(ds + 1) * 128], op=ALU.add)


    def bc_rows(src_row_ap, n):
        return src_row_ap.to_broadcast([128, n])

    def conv_silu(b, br, n, raw, raw_r, acc, acc_r, cw4, prm_r, bias, out_ap, out_res):
        OP("act", "activation", reads=[br], writes=[raw_r], out=raw[:, 3:3 + n], in_=b, func=AF.Copy)
        if bias is None:
            OP("dve", "tensor_scalar", reads=[raw_r, prm_r], writes=[acc_r], out=acc[:, 0:n],
               in0=raw[:, 3:3 + n], scalar1=cw4[:, 3:4], scalar2=None, op0=ALU.mult)
        else:
            OP("dve", "tensor_scalar", reads=[raw_r, prm_r], writes=[acc_r], out=acc[:, 0:n],
               in0=raw[:, 3:3 + n], scalar1=cw4[:, 3:4], scalar2=bias, op0=ALU.mult, op1=ALU.add)
        for j in (2, 1, 0):
            OP("dve", "scalar_tensor_tensor", reads=[raw_r, prm_r, acc_r], writes=[acc_r],
               out=acc[:, 0:n], in0=raw[:, j:j + n], scalar=cw4[:, j:j + 1], in1=acc[:, 0:n],
               op0=ALU.mult, op1=ALU.add)
        OP("act", "activation", reads=[acc_r], writes=[out_res], out=out_ap, in_=acc[:, 0:n],
           func=AF.Silu)
        OP("pool", "tensor_copy", reads=[raw_r], writes=[raw_r], out=raw[:, 0:3], in_=raw[:, n:n + 3])

    def load_conv_w(dst, dst_r, src2d, nblk, tmp, tmp_r, ri):
        P.dma("sp", tmp[0:nblk, :], src2d.rearrange("j (b p) -> b j p", p=128), prm_res[ri],
              writes=[tmp_r])
        b, br = nb()
        for j in range(4):
            TR(b[:, j * nblk:(j + 1) * nblk], tmp[0:nblk, j * 128:(j + 1) * 128], ident_f[0:nblk, 0:nblk],
               reads=[tmp_r, const_res], writes=[br])
        OP("dve", "tensor_copy", reads=[br], writes=[dst_r], out=dst, in_=b[:, 0:4 * nblk])

    def unit_gdn(layer):
        P.barrier()
        carve.reset()
        ba, ba_r = carve.f32(NT * 16, "ba")
        ba3 = ba.rearrange("p (t c) -> p t c", c=16)

        def sm(name):
            a, r = carve.f32(NT * 8, name)
            return a, a.rearrange("p (t c) -> p t c", c=8), r
        xa, xa3, xa_r = sm("xa")
        lb, lb3, lb_r = sm("lb")
        g_all, g3, g_r = sm("g")
        gc, gc3, gc_r = sm("gc")
        ngc, ngc3, ngc_r = sm("ngc")
        gb_, gb3, gb_r = sm("gb")
        egb, egb3, egb_r = sm("egb")
        beta, beta3, beta_r = sm("beta")
        alog, alog_r = carve.f32(8, "alog")
        dtb, dtb_r = carve.f32(8, "dtb")
        gng, gng_r = carve.f32(1, "gng")
        cw, cw_r = carve.f32(96, "cw")
        cw3 = cw.rearrange("p (j b) -> p j b", b=24)
        qs, qs_r = carve.f32(512, "qs")
        cwt, cwt_r = qs, qs_r
        load_small(alog, bc_rows(gdn_a_log_d[layer:layer + 1, :], 8), 0)
        load_small(dtb, bc_rows(gdn_dt_bias_d[layer:layer + 1, :], 8), 1)
        load_small(gng, gdn_norm_g_d[layer].rearrange("(p o) -> p o", o=1), 2)
        alog_r, dtb_r, gng_r = prm_res[0], prm_res[1], prm_res[2]
        load_conv_w(cw, cw_r, gdn_conv_w_d[layer], 24, cwt, cwt_r, 3)
        if GSTOP == 1:
            return
        proj_tm(layer, O_GB, 16, lambda b, br, tt0, cnt: OP(
            "dve", "tensor_copy", reads=[br], writes=[ba_r], out=ba[:, tt0 * 16:(tt0 + cnt) * 16], in_=b))
        OP("dve", "tensor_tensor", reads=[ba_r, dtb_r], writes=[xa_r], out=xa3, in0=ba3[:, :, 8:16],
           in1=dtb.unsqueeze(1).to_broadcast([128, NT, 8]), op=ALU.add)
        OP("act", "activation", reads=[xa_r], writes=[xa_r], out=xa, in_=xa, func=AF.Exp)
        OP("act", "activation", reads=[xa_r], writes=[xa_r], out=xa, in_=xa, func=AF.Ln, bias=1.0)
        OP("act", "activation", reads=[alog_r], writes=[alog_r], out=alog, in_=alog, func=AF.Exp)
        OP("dve", "scalar_tensor_tensor", reads=[xa_r, alog_r], writes=[g_r], out=g3, in0=xa3, scalar=-1.0,
           in1=alog.unsqueeze(1).to_broadcast([128, NT, 8]), op0=ALU.mult, op1=ALU.mult)
        OP("pool", "memset", reads=[], writes=[g_r], ap=g3[0:PAD, 0, :], constant=0.0)
        OP("act", "activation", reads=[ba_r], writes=[lb_r], out=lb3, in_=ba3[:, :, 0:8], func=AF.Exp,
           scale=-1.0)
        OP("act", "activation", reads=[lb_r], writes=[lb_r], out=lb, in_=lb, func=AF.Ln, bias=1.0)
        OP("act", "activation", reads=[lb_r], writes=[beta_r], out=beta, in_=lb, func=AF.Exp, scale=-1.0)
        b, br = nb()
        MM(b[:, 0:NT * 8], uincl_f[:], g_all, True, True, reads=[g_r, const_res], writes=[br])
        OP("dve", "tensor_copy", reads=[br], writes=[gc_r], out=gc, in_=b[:, 0:NT * 8])
        OP("dve", "tensor_scalar", reads=[gc_r], writes=[ngc_r], out=ngc, in0=gc, scalar1=-1.0,
           scalar2=None, op0=ALU.mult)
        OP("dve", "tensor_tensor", reads=[gc_r, lb_r], writes=[gb_r], out=gb_, in0=gc, in1=lb,
           op=ALU.subtract)
        OP("act", "activation", reads=[gb_r], writes=[egb_r], out=egb, in_=gb_, func=AF.Exp)
        if GSTOP == 2:
            return
        qT, qT_r = carve.bf16(LP, "qT")
        kT, kT_r = carve.bf16(LP, "kT")
        ktm_f, ktm_r = carve.bf16(NT * 128, "ktm")
        vtm_f, vtm_r = carve.bf16(NT * 128, "vtm")
        ktm = ktm_f.rearrange("p (t d) -> p t d", d=128)
        vtm = vtm_f.rearrange("p (t d) -> p t d", d=128)
        raw, raw_r = carve.f32(515, "raw")
        acc, acc_r = carve.f32(512, "acc")
        sqb, sqb_r = carve.bf16(512, "sqb")
        rn, rn_r = carve.f32(512, "rn")
        dd, dd_r = carve.f32(128, "dd")
        vTt, vTt_r = carve.bf16(512, "vTt")
        zgt, zgt_r = carve.bf16(512, "zgt")
        NSET = 2
        gsets = []
        for _i in range(NSET):
            st = {}
            for nm, kind, n in (("dg", "f", 256), ("E12", "f", 256), ("eg", "f", 128), ("egbrow", "f", 128),
                                ("RwT", "f", 128), ("AN0", "f", 256), ("AN1", "f", 256), ("Pm0", "f", 128),
                                ("Pm1", "f", 128), ("aqkT", "b", 128), ("Rv", "f", 128), ("qd", "b", 128),
                                ("kend", "b", 128), ("glk", "f", 2)):
                st[nm], st[nm + "_r"] = (carve.f32 if kind == "f" else carve.bf16)(n, nm)
            gsets.append(st)
        vnew, vnew_r = carve.bf16(128, "vnew")
        S, S_r = carve.f32(128, "S")
        S_bf, Sbf_r = carve.bf16(128, "Sbf")
        import os as _os2
        if _os2.environ.get("KDEBUG"):
            print("GDN carve off", carve.off, "of", WORKW)

        def l2norm_to(dst, dst_r, t0, n, scl):
            OP("act", "activation", reads=[qs_r], writes=[sqb_r], out=sqb[:, 0:n], in_=qs[:, 0:n],
               func=AF.Square)
            b2, b2r = nb()
            MM(b2[:, 0:n], ones_b[:], sqb[:, 0:n], True, True, reads=[sqb_r, const_res], writes=[b2r])
            OP("act", "activation", reads=[b2r], writes=[rn_r], out=rn[:, 0:n], in_=b2[:, 0:n],
               func=AF.Sqrt, bias=1e-6)
            OP("dve", "reciprocal", reads=[rn_r], writes=[rn_r], out=rn[:, 0:n], in_=rn[:, 0:n])
            OP("dve", "scalar_tensor_tensor", reads=[qs_r, rn_r], writes=[dst_r], out=dst[:, t0:t0 + n],
               in0=qs[:, 0:n], scalar=scl, in1=rn[:, 0:n], op0=ALU.mult, op1=ALU.mult)

        def to_tm(srcT, src_r, c0, n, dst3, dst_r, tt0):
            b2, b2r = nb()
            v = bfv(b2)
            cnt = n // 128
            for j in range(cnt):
                TR(v[:, j * 128:(j + 1) * 128], srcT[:, c0 + j * 128:c0 + (j + 1) * 128], ident_b[:],
                   reads=[src_r, const_res], writes=[b2r])
            OP("dve", "tensor_copy", reads=[b2r], writes=[dst_r], out=dst3[:, tt0:tt0 + cnt, :],
               in_=v[:, 0:n].rearrange("p (t d) -> p t d", d=128))

        for h in range(8):
            OP("pool", "memset", writes=[raw_r], ap=raw[:, 0:3], constant=0.0)

            def ev_q(b, br, t0, n, h=h):
                conv_silu(b, br, n, raw, raw_r, acc, acc_r, cw3[:, :, h], cw_r, None, qs[:, 0:n], qs_r)
                l2norm_to(qT, qT_r, t0, n, 128.0 ** -0.5)
            proj_fm(layer, O_GQ + h * 128, ev_q)
            OP("pool", "memset", writes=[raw_r], ap=raw[:, 0:3], constant=0.0)

            def ev_k(b, br, t0, n, h=h):
                conv_silu(b, br, n, raw, raw_r, acc, acc_r, cw3[:, :, 8 + h], cw_r, None, qs[:, 0:n], qs_r)
                l2norm_to(kT, kT_r, t0, n, 1.0)
                to_tm(kT, kT_r, t0, n, ktm, ktm_r, t0 // 128)
            proj_fm(layer, O_GK + h * 128, ev_k)
            OP("pool", "memset", writes=[raw_r], ap=raw[:, 0:3], constant=0.0)

            def ev_v(b, br, t0, n, h=h):
                conv_silu(b, br, n, raw, raw_r, acc, acc_r, cw3[:, :, 16 + h], cw_r, None, vTt[:, 0:n], vTt_r)
                to_tm(vTt, vTt_r, 0, n, vtm, vtm_r, t0 // 128)
            proj_fm(layer, O_GV + h * 128, ev_v)
            OP("pool", "memset", writes=[S_r], ap=S, constant=0.0)
            OP("pool", "memset", writes=[Sbf_r], ap=S_bf, constant=0.0)
            if GSTOP == 3:
                return
            def T_steps(c, h=h):
                st = gsets[c % NSET]
                tok = slice(c * 128, (c + 1) * 128)
                gcol = gc3[:, c, h:h + 1]
                ngcol = ngc3[:, c, h:h + 1]
                gbcol = gb3[:, c, h:h + 1]
                dg, dg_r, E12, E12_r, eg, eg_r = st["dg"], st["dg_r"], st["E12"], st["E12_r"], st["eg"], st["eg_r"]
                egbrow, egbrow_r, RwT, RwT_r = st["egbrow"], st["egbrow_r"], st["RwT"], st["RwT_r"]
                glk, glk_r = st["glk"], st["glk_r"]
                gl, kes = glk[:, 0:1], glk[:, 1:2]
                AN = [(st["AN0"], st["AN0_r"]), (st["AN1"], st["AN1_r"])]
                Pm = [(st["Pm0"], st["Pm0_r"]), (st["Pm1"], st["Pm1_r"])]
                an0, an0_r = AN[0]
                pm0, pm0_r = Pm[0]
                loc = {}
                steps = []

                def s0():
                    OP("dve", "tensor_scalar", reads=[gc_r, const_res], writes=[dg_r], out=dg[:, 0:128],
                       in0=ident_f[:], scalar1=gcol, scalar2=None, op0=ALU.mult)
                    OP("dve", "tensor_scalar", reads=[gb_r, const_res], writes=[dg_r], out=dg[:, 128:256],
                       in0=ident_f[:], scalar1=gbcol, scalar2=None, op0=ALU.mult)
                    bK, bKr = nb()
                    loc["bK"] = (bK, bKr)
                    MM(bK[:, 0:128], kT[:, tok], qT[:, tok], True, True, reads=[kT_r, qT_r], writes=[bKr])
                    MM(bK[:, 128:256], kT[:, tok], kT[:, tok], True, True, reads=[kT_r], writes=[bKr])
                    OP("dve", "tensor_scalar", reads=[vtm_r, beta_r], writes=[st["Rv_r"]], out=st["Rv"],
                       in0=vtm[:, c, :], scalar1=beta3[:, c, h:h + 1], scalar2=None, op0=ALU.mult)
                steps.append(s0)

                def s1():
                    bG, bGr = nb()
                    loc["bG"] = (bG, bGr)
                    MM(bG[:, 0:256], ones_f[:], dg, True, True, reads=[dg_r, const_res], writes=[bGr])
                steps.append(s1)

                def s2():
                    bG, bGr = loc["bG"]
                    OP("act", "activation", reads=[bGr, ngc_r], writes=[E12_r], out=E12, in_=bG[:, 0:256],
                       func=AF.Exp, bias=ngcol)
                    OP("dve", "tensor_copy", reads=[bGr], writes=[glk_r], out=gl, in_=bG[:, 127:128])
                    OP("act", "activation", reads=[bGr], writes=[eg_r], out=eg, in_=bG[:, 0:128], func=AF.Exp)
                    OP("act", "activation", reads=[bGr], writes=[egbrow_r], out=egbrow, in_=bG[:, 128:256],
                       func=AF.Exp)
                steps.append(s2)

                def s3():
                    OP("pool", "affine_select", reads=[E12_r], writes=[E12_r], out=E12[:, 0:128],
                       in_=E12[:, 0:128], pattern=[[1, 128]], compare_op=ALU.is_ge, fill=0.0, base=0,
                       channel_multiplier=-1)
                    OP("pool", "affine_select", reads=[E12_r], writes=[E12_r], out=E12[:, 128:256],
                       in_=E12[:, 128:256], pattern=[[1, 128]], compare_op=ALU.is_gt, fill=0.0, base=0,
                       channel_multiplier=-1)
                    OP("act", "activation", reads=[gc_r, glk_r], writes=[glk_r], out=kes, in_=gcol,
                       func=AF.Exp, scale=-1.0, bias=gl)
                steps.append(s3)

                def s4():
                    bK, bKr = loc["bK"]
                    OP("dve", "scalar_tensor_tensor", reads=[bKr, E12_r], writes=[an0_r], out=an0[:, 0:128],
                       in0=bK[:, 128:256], scalar=-1.0, in1=E12[:, 128:256], op0=ALU.mult, op1=ALU.mult)
                    OP("dve", "tensor_tensor", reads=[bKr, E12_r], writes=[st["aqkT_r"]], out=st["aqkT"],
                       in0=bK[:, 0:128], in1=E12[:, 0:128], op=ALU.mult)
                steps.append(s4)

                def s5():
                    bT, bTr = nb()
                    loc["bT"] = (bT, bTr)
                    TR(bT[:, 0:128], an0[:, 0:128], ident_f[:], reads=[an0_r, const_res], writes=[bTr])
                    OP("pool", "tensor_tensor", reads=[an0_r, const_res], writes=[pm0_r], out=pm0,
                       in0=an0[:, 0:128], in1=ident_f[:], op=ALU.add)
                steps.append(s5)

                def s6():
                    bT, bTr = loc["bT"]
                    OP("act", "activation", reads=[bTr], writes=[an0_r], out=an0[:, 128:256],
                       in_=bT[:, 0:128], func=AF.Copy)
                    OP("dve", "tensor_tensor", reads=[kT_r, egbrow_r], writes=[RwT_r], out=RwT, in0=kT[:, tok],
                       in1=egbrow, op=ALU.mult)
                    OP("dve", "tensor_tensor", reads=[qT_r, eg_r], writes=[st["qd_r"]], out=st["qd"],
                       in0=qT[:, tok], in1=eg, op=ALU.mult)
                    OP("dve", "tensor_scalar", reads=[ktm_r, glk_r], writes=[st["kend_r"]], out=st["kend"],
                       in0=ktm[:, c, :], scalar1=kes, scalar2=None, op0=ALU.mult)
                steps.append(s6)
                state = {"ai": 0, "pi": 0}
                for p in (1, 2, 4, 8, 16, 32, 64):
                    def lv_mm(p=p):
                        an, an_r = AN[state["ai"]]
                        if p > 1:
                            pc, pc_r = Pm[state["pi"]]
                            bP, bPr = nb()
                            loc["bP"] = (bP, bPr)
                            MM(bP[:, 0:128], an[:, 128:256], pc, True, True, reads=[an_r, pc_r], writes=[bPr])
                        if p < 64:
                            bX, bXr = nb()
                            loc["bX"] = (bX, bXr)
                            MM(bX[:, 0:128], an[:, 128:256], an[:, 0:128], True, True, reads=[an_r], writes=[bXr])
                            MM(bX[:, 128:256], an[:, 0:128], an[:, 128:256], True, True, reads=[an_r],
                               writes=[bXr])

                    def lv_ev(p=p):
                        if p < 64:
                            an2, an2_r = AN[1 - state["ai"]]
                            bX, bXr = loc["bX"]
                            OP("act", "activation", reads=[bXr], writes=[an2_r], out=an2, in_=bX[:, 0:256],
                               func=AF.Copy)
                            state["ai"] = 1 - state["ai"]
                        if p > 1:
                            pc, pc_r = Pm[state["pi"]]
                            pn, pn_r = Pm[1 - state["pi"]]
                            bP, bPr = loc["bP"]
                            OP("dve", "tensor_tensor", reads=[bPr, pc_r], writes=[pn_r], out=pn,
                               in0=bP[:, 0:128], in1=pc, op=ALU.add)
                            state["pi"] = 1 - state["pi"]
                    steps.append(lv_mm)
                    steps.append(lv_ev)
                return steps, state, Pm

            def S_phase(c, fin, h=h):
                st = gsets[c % NSET]
                tok = slice(c * 128, (c + 1) * 128)
                state, Pm = fin
                pf, pf_r = Pm[state["pi"]]
                bS, bSr = nb()
                MM(bS[:, 0:128], st["RwT"], S, True, True, reads=[st["RwT_r"], S_r], writes=[bSr])
                OP("dve", "tensor_tensor", reads=[st["Rv_r"], bSr], writes=[dd_r], out=dd, in0=st["Rv"],
                   in1=bS[:, 0:128], op=ALU.subtract)
                bV, bVr = nb()
                MM(bV[:, 0:128], pf, dd, True, True, reads=[pf_r, dd_r], writes=[bVr])
                OP("act", "activation", reads=[bVr], writes=[vnew_r], out=vnew, in_=bV[:, 0:128], func=AF.Copy)
                bD, bDr = nb()
                MM(bD[:, 0:128], st["kend"], vnew, True, True, reads=[st["kend_r"], vnew_r], writes=[bDr])
                bO, bOr = nb()
                MM(bO[:, 0:128], S_bf, st["qd"], True, False, reads=[Sbf_r, st["qd_r"]], writes=[bOr])
                MM(bO[:, 0:128], vnew, st["aqkT"], False, True, reads=[vnew_r, st["aqkT_r"]], writes=[bOr])
                OP("dve", "scalar_tensor_tensor", reads=[S_r, st["eg_r"], bDr], writes=[S_r], out=S, in0=S,
                   scalar=st["eg"][:, 127:128], in1=bD[:, 0:128], op0=ALU.mult, op1=ALU.add)
                OP("pool", "tensor_copy", reads=[S_r], writes=[Sbf_r], out=S_bf, in_=S)
                OP("act", "activation", reads=[bOr], writes=[oT_res[h]], out=oT[:, h, tok], in_=bO[:, 0:128],
                   func=AF.Copy)

            for c0 in range(0, NT, NSET):
                cs_ = list(range(c0, min(NT, c0 + NSET)))
                built = [T_steps(c) for c in cs_]
                nst = max(len(b[0]) for b in built)
                for i in range(nst):
                    for b in built:
                        if i < len(b[0]):
                            b[0][i]()
                for c, b in zip(cs_, built):
                    S_phase(c, (b[1], b[2]))

            def ev_z(b, br, t0, n, h=h):
                OP("act", "activation", reads=[br], writes=[zgt_r], out=zgt[:, 0:n], in_=b, func=AF.Silu)
                OP("act", "activation", reads=[oT_res[h]], writes=[sqb_r], out=sqb[:, 0:n],
                   in_=oT[:, h, t0:t0 + n], func=AF.Square)
                b2, b2r = nb()
                MM(b2[:, 0:n], ones_b[:], sqb[:, 0:n], True, True, reads=[sqb_r, const_res], writes=[b2r])
                OP("act", "activation", reads=[b2r], writes=[rn_r], out=rn[:, 0:n], in_=b2[:, 0:n],
                   func=AF.Sqrt, scale=1.0 / 128, bias=1e-6)
                OP("dve", "reciprocal", reads=[rn_r], writes=[rn_r], out=rn[:, 0:n], in_=rn[:, 0:n])
                OP("dve", "scalar_tensor_tensor", reads=[oT_res[h], gng_r, rn_r], writes=[oT_res[h]],
                   out=oT[:, h, t0:t0 + n], in0=oT[:, h, t0:t0 + n], scalar=gng[:, 0:1], in1=rn[:, 0:n],
                   op0=ALU.mult, op1=ALU.mult)
                OP("dve", "tensor_tensor", reads=[oT_res[h], zgt_r], writes=[oT_res[h]],
                   out=oT[:, h, t0:t0 + n], in0=oT[:, h, t0:t0 + n], in1=zgt[:, 0:n], op=ALU.mult)
            proj_fm(layer, O_GZ + h * 128, ev_z)


    def unit_ssd(layer, g):
        P.barrier()
        carve.reset()
        dtr, dtr_r = carve.f32(NT * 32, "dtr")
        dt_, dt_r = carve.f32(NT * 32, "dt")
        cs, cs_r = carve.f32(NT * 32, "cs")
        ncs, ncs_r = carve.f32(NT * 32, "ncs")
        dt3 = dt_.rearrange("p (t c) -> p t c", c=32)
        dtr3 = dtr.rearrange("p (t c) -> p t c", c=32)
        cs3 = cs.rearrange("p (t c) -> p t c", c=32)
        ncs3 = ncs.rearrange("p (t c) -> p t c", c=32)
        alog, _ = carve.f32(32, "alog")
        dtb, _ = carve.f32(32, "dtb")
        dcol, _ = carve.f32(16, "dcol")
        ng, _ = carve.f32(16, "ng")
        cbt, cbt_r = carve.f32(128, "cbt")
        cw, cw_r = carve.f32(80, "cw")
        cw3 = cw.rearrange("p (j b) -> p j b", b=20)
        cbias, cbias_r = carve.f32(20, "cbias")
        acc, acc_r = carve.f32(512, "acc")
        cwt, cwt_r = acc, acc_r
        load_small(alog, bc_rows(ssm_a_log_d[layer:layer + 1, :], 32), 0)
        load_small(dtb, bc_rows(ssm_dt_bias_d[layer:layer + 1, :], 32), 1)
        dv = ssm_d_d[layer].rearrange("(b s) -> s b", s=2)
        P.dma("sp", dcol[0:64, :], dv[0:1, :].to_broadcast([64, 16]), prm_res[2], writes=[prm_res[2]], slow=True)
        P.dma("sp", dcol[64:128, :], dv[1:2, :].to_broadcast([64, 16]), prm_res[2], writes=[prm_res[2]],
              slow=True)
        load_small(ng, ssm_norm_g_d[layer].rearrange("(b p) -> p b", p=128), 4)
        alog_r, dtb_r, dcol_r, ng_r = prm_res[0], prm_res[1], prm_res[2], prm_res[4]
        load_conv_w(cw, cw_r, ssm_conv_w_d[layer], 20, cwt, cwt_r, 3)
        P.dma("sp", cbt[0:20, :], ssm_conv_b_d[layer].rearrange("(b p) -> b p", p=128), prm_res[5],
              writes=[cbt_r])
        b, br = nb()
        TR(b[:, 0:20], cbt[0:20, :], ident_f[0:20, 0:20], reads=[cbt_r, const_res], writes=[br])
        OP("dve", "tensor_copy", reads=[br], writes=[cbias_r], out=cbias, in_=b[:, 0:20])
        proj_tm(layer, O_SDT, 32, lambda b, br, tt0, cnt: OP(
            "dve", "tensor_copy", reads=[br], writes=[dtr_r], out=dtr[:, tt0 * 32:(tt0 + cnt) * 32], in_=b))
        OP("dve", "tensor_tensor", reads=[dtr_r, dtb_r], writes=[dt_r], out=dt3, in0=dtr3,
           in1=dtb.unsqueeze(1).to_broadcast([128, NT, 32]), op=ALU.add)
        OP("act", "activation", reads=[dt_r], writes=[dt_r], out=dt_, in_=dt_, func=AF.Exp)
        OP("act", "activation", reads=[dt_r], writes=[dt_r], out=dt_, in_=dt_, func=AF.Ln, bias=1.0)
        OP("pool", "memset", reads=[], writes=[dt_r], ap=dt3[0:PAD, 0, :], constant=0.0)
        OP("act", "activation", reads=[alog_r], writes=[alog_r], out=alog, in_=alog, func=AF.Exp)
        OP("dve", "scalar_tensor_tensor", reads=[dt_r, alog_r], writes=[dtr_r], out=dtr3, in0=dt3, scalar=-1.0,
           in1=alog.unsqueeze(1).to_broadcast([128, NT, 32]), op0=ALU.mult, op1=ALU.mult)
        for hf in range(2):
            w0 = hf * 272
            b, br = nb()
            MM(b[:, 0:272], uincl_f[:], dtr[:, w0:w0 + 272], True, True, reads=[dtr_r, const_res], writes=[br])
            OP("dve", "tensor_copy", reads=[br], writes=[cs_r], out=cs[:, w0:w0 + 272], in_=b[:, 0:272])
        OP("dve", "tensor_scalar", reads=[cs_r], writes=[ncs_r], out=ncs, in0=cs, scalar1=-1.0, scalar2=None,
           op0=ALU.mult)

        BT, BT_r = carve.bf16(LP, "BT")
        CT, CT_r = carve.bf16(LP, "CT")
        xT, xT_r = carve.bf16(LP, "xT")
        Btm_f, Btm_r = carve.bf16(NT * 128, "Btm")
        Btm = Btm_f.rearrange("p (t d) -> p t d", d=128)
        raw, raw_r = carve.f32(515, "raw")
        zgt, zgt_r = carve.bf16(512, "zgt")
        sqs = [carve.bf16(512, "sq") for _ in range(2)]
        rn, rn_r = carve.f32(512, "rn")
        NSET = 3
        sets = []
        for _i in range(NSET):
            st = {}
            st["dg"], st["dg_r"] = carve.f32(256, "dg")
            st["E12"], st["E12_r"] = carve.f32(256, "E12")
            st["eg"], st["eg_r"] = carve.f32(256, "eg")
            st["aqkT"], st["aqkT_r"] = carve.bf16(256, "aqkT")
            st["xs2"], st["xs2_r"] = carve.bf16(256, "xs2")
            st["qd"], st["qd_r"] = carve.bf16(256, "qd")
            st["kend"], st["kend_r"] = carve.bf16(256, "kend")
            st["glk"], st["glk_r"] = carve.f32(4, "glk")
            for nm in ("aqkT", "xs2", "qd", "kend", "E12", "eg"):
                st[nm + "3"] = st[nm].rearrange("p (s l) -> p s l", l=128)
            sets.append(st)
        S, S_r = carve.f32(128, "S")
        S2, S2_r = carve.bf16(256, "S2")
        S23 = S2.rearrange("p (s l) -> p s l", l=128)

        def to_tm(srcT, src_r, c0, n, dst3, dst_r, tt0):
            b2, b2r = nb()
            v = bfv(b2)
            cnt = n // 128
            for j in range(cnt):
                TR(v[:, j * 128:(j + 1) * 128], srcT[:, c0 + j * 128:c0 + (j + 1) * 128], ident_b[:],
                   reads=[src_r, const_res], writes=[b2r])
            OP("dve", "tensor_copy", reads=[b2r], writes=[dst_r], out=dst3[:, tt0:tt0 + cnt, :],
               in_=v[:, 0:n].rearrange("p (t d) -> p t d", d=128))

        OP("pool", "memset", writes=[raw_r], ap=raw[:, 0:3], constant=0.0)

        def ev_B(b, br, t0, n):
            conv_silu(b, br, n, raw, raw_r, acc, acc_r, cw3[:, :, 16 + g], cw_r, cbias[:, 16 + g:17 + g],
                      BT[:, t0:t0 + n], BT_r)
            to_tm(BT, BT_r, t0, n, Btm, Btm_r, t0 // 128)
        proj_fm(layer, O_SB + g * 128, ev_B)
        OP("pool", "memset", writes=[raw_r], ap=raw[:, 0:3], constant=0.0)
        proj_fm(layer, O_SC + g * 128, lambda b, br, t0, n: conv_silu(
            b, br, n, raw, raw_r, acc, acc_r, cw3[:, :, 18 + g], cw_r, cbias[:, 18 + g:19 + g],
            CT[:, t0:t0 + n], CT_r))
        for st in sets:
            OP("pool", "memset", writes=[st["xs2_r"]], ap=st["xs2"], constant=0.0)
        OP("pool", "memset", writes=[S2_r], ap=S2, constant=0.0)
        for j in range(8):
            blk = g * 8 + j
            hh = [2 * blk, 2 * blk + 1]
            OP("pool", "memset", writes=[raw_r], ap=raw[:, 0:3], constant=0.0)
            proj_fm(layer, O_SX + blk * 128, lambda b, br, t0, n, blk=blk: conv_silu(
                b, br, n, raw, raw_r, acc, acc_r, cw3[:, :, blk], cw_r, cbias[:, blk:blk + 1],
                xT[:, t0:t0 + n], xT_r))
            OP("pool", "memset", writes=[S_r], ap=S, constant=0.0)
            for s2 in range(2):
                OP("pool", "memset", writes=[S2_r], ap=S23[:, s2, s2 * 64:(s2 + 1) * 64], constant=0.0)
            def prep(c, hh=hh):
                st = sets[c % NSET]
                dg, dg_r, E12_r, eg, eg_r = st["dg"], st["dg_r"], st["E12_r"], st["eg"], st["eg_r"]
                E3, eg3, aqk3, xs3, qd3, kend3 = st["E123"], st["eg3"], st["aqkT3"], st["xs23"], st["qd3"], st["kend3"]
                glk, glk_r = st["glk"], st["glk_r"]
                tok = slice(c * 128, (c + 1) * 128)
                for s2 in range(2):
                    OP("dve", "tensor_scalar", reads=[cs_r, const_res], writes=[dg_r],
                       out=dg[:, s2 * 128:(s2 + 1) * 128], in0=ident_f[:], scalar1=cs3[:, c, hh[s2]:hh[s2] + 1],
                       scalar2=None, op0=ALU.mult)
                bG, bGr = nb()
                MM(bG[:, 0:256], ones_f[:], dg, True, True, reads=[dg_r, const_res], writes=[bGr])
                bK, bKr = nb()
                MM(bK[:, 0:128], BT[:, tok], CT[:, tok], True, True, reads=[BT_r, CT_r], writes=[bKr])
                bT, bTr = nb()
                TR(bfv(bT)[:, 0:128], xT[:, tok], ident_b[:], reads=[xT_r, const_res], writes=[bTr])
                for s2 in range(2):
                    OP("act", "activation", reads=[bGr, ncs_r], writes=[E12_r], out=E3[:, s2, :],
                       in_=bG[:, s2 * 128:(s2 + 1) * 128], func=AF.Exp, bias=ncs3[:, c, hh[s2]:hh[s2] + 1])
                OP("act", "activation", reads=[bGr], writes=[eg_r], out=eg, in_=bG[:, 0:256], func=AF.Exp)
                OP("dve", "tensor_copy", reads=[bGr], writes=[glk_r], out=glk[:, 0:2],
                   in_=bG[:, 0:256].rearrange("p (s l) -> p s l", l=128)[:, :, 127])
                OP("pool", "affine_select", reads=[E12_r], writes=[E12_r], out=E3, in_=E3,
                   pattern=[[0, 2], [1, 128]], compare_op=ALU.is_ge, fill=0.0, base=0, channel_multiplier=-1)
                for s2 in range(2):
                    OP("act", "activation", reads=[cs_r, glk_r], writes=[glk_r], out=glk[:, 2 + s2:3 + s2],
                       in_=cs3[:, c, hh[s2]:hh[s2] + 1], func=AF.Exp, scale=-1.0, bias=glk[:, s2:s2 + 1])
                for s2 in range(2):
                    OP("dve", "tensor_scalar", reads=[bTr, dt_r], writes=[st["xs2_r"]],
                       out=xs3[:, s2, s2 * 64:(s2 + 1) * 64], in0=bfv(bT)[:, s2 * 64:(s2 + 1) * 64],
                       scalar1=dt3[:, c, hh[s2]:hh[s2] + 1], scalar2=None, op0=ALU.mult)
                OP("dve", "tensor_tensor", reads=[bKr, E12_r], writes=[st["aqkT_r"]], out=aqk3,
                   in0=bK[:, 0:128].unsqueeze(1).to_broadcast([128, 2, 128]), in1=E3, op=ALU.mult)
                OP("dve", "tensor_tensor", reads=[CT_r, eg_r], writes=[st["qd_r"]], out=qd3,
                   in0=CT[:, tok].unsqueeze(1).to_broadcast([128, 2, 128]), in1=eg3, op=ALU.mult)
                for s2 in range(2):
                    OP("dve", "tensor_scalar", reads=[Btm_r, glk_r], writes=[st["kend_r"]], out=kend3[:, s2, :],
                       in0=Btm[:, c, :], scalar1=glk[:, 2 + s2:3 + s2], scalar2=None, op0=ALU.mult)

            def scan(c, j=j):
                st = sets[c % NSET]
                eg_r, eg3, aqk3, xs3, qd3, kend3 = st["eg_r"], st["eg3"], st["aqkT3"], st["xs23"], st["qd3"], st["kend3"]
                tok = slice(c * 128, (c + 1) * 128)
                bD, bDr = nb()
                for s2 in range(2):
                    MM(bD[:, s2 * 64:(s2 + 1) * 64], kend3[:, s2, :], xs3[:, s2, s2 * 64:(s2 + 1) * 64], True, True,
                       reads=[st["kend_r"], st["xs2_r"]], writes=[bDr])
                bO, bOr = nb()
                for s2 in range(2):
                    MM(bO[:, 0:128], S23[:, s2, :], qd3[:, s2, :], s2 == 0, False, reads=[S2_r, st["qd_r"]],
                       writes=[bOr])
                for s2 in range(2):
                    MM(bO[:, 0:128], xs3[:, s2, :], aqk3[:, s2, :], False, s2 == 1,
                       reads=[st["xs2_r"], st["aqkT_r"]], writes=[bOr])
                for s2 in range(2):
                    cols = slice(s2 * 64, (s2 + 1) * 64)
                    OP("dve", "scalar_tensor_tensor", reads=[S_r, eg_r, bDr], writes=[S_r], out=S[:, cols],
                       in0=S[:, cols], scalar=eg3[:, s2, 127:128], in1=bD[:, cols], op0=ALU.mult, op1=ALU.add)
                for s2 in range(2):
                    cols = slice(s2 * 64, (s2 + 1) * 64)
                    OP("pool", "tensor_copy", reads=[S_r], writes=[S2_r], out=S23[:, s2, cols], in_=S[:, cols])
                OP("act", "activation", reads=[bOr], writes=[oT_res[j]], out=oT[:, j, tok], in_=bO[:, 0:128],
                   func=AF.Copy)

            prep(0)
            prep(1)
            for c in range(NT):
                scan(c)
                if c + 2 < NT:
                    prep(c + 2)

            def ev_z(b, br, t0, n, j=j, blk=blk):
                OP("act", "activation", reads=[br], writes=[zgt_r], out=zgt[:, 0:n], in_=b, func=AF.Silu)
                OP("dve", "scalar_tensor_tensor", reads=[xT_r, dcol_r, oT_res[j]], writes=[oT_res[j]],
                   out=oT[:, j, t0:t0 + n], in0=xT[:, t0:t0 + n], scalar=dcol[:, blk:blk + 1],
                   in1=oT[:, j, t0:t0 + n], op0=ALU.mult, op1=ALU.add)
                OP("dve", "tensor_tensor", reads=[oT_res[j], zgt_r], writes=[oT_res[j]],
                   out=oT[:, j, t0:t0 + n], in0=oT[:, j, t0:t0 + n], in1=zgt[:, 0:n], op=ALU.mult)
            proj_fm(layer, O_SZ + blk * 128, ev_z)
        for (t0, n) in TG:
            bq, bqr = nb()
            for j in range(8):
                sq, sq_r = sqs[j % 2]
                OP("dve", "tensor_tensor", reads=[oT_res[j]], writes=[sq_r], out=sq[:, 0:n],
                   in0=oT[:, j, t0:t0 + n], in1=oT[:, j, t0:t0 + n], op=ALU.mult)
                MM(bq[:, 0:n], ones_b[:], sq[:, 0:n], j == 0, j == 7, reads=[sq_r, const_res], writes=[bqr])
            OP("act", "activation", reads=[bqr], writes=[rn_r], out=rn[:, 0:n], in_=bq[:, 0:n], func=AF.Sqrt,
               scale=1.0 / 1024, bias=1e-6)
            OP("dve", "reciprocal", reads=[rn_r], writes=[rn_r], out=rn[:, 0:n], in_=rn[:, 0:n])
            for j in range(8):
                blk = g * 8 + j
                OP("dve", "scalar_tensor_tensor", reads=[oT_res[j], ng_r, rn_r], writes=[oT_res[j]],
                   out=oT[:, j, t0:t0 + n], in0=oT[:, j, t0:t0 + n], scalar=ng[:, blk:blk + 1], in1=rn[:, 0:n],
                   op0=ALU.mult, op1=ALU.mult)

    def final_out():
        P.barrier()
        carve.reset()
        fg, fg_r = carve.f32(1024, "fg")
        junk, junk_r = carve.f32(1024, "junk")
        outs = [carve.f32(1024, "ot") for _ in range(2)]
        load_small(fg, fin_g_d[0:1, :].to_broadcast([128, 1024]), 0, slow=True)
        for tt in range(1, NT):
            ot, ot_r = outs[tt % 2]
            osr = out_res[tt % 2]
            ss = small[:, 0:1]
            rs = small[:, 1:2]
            OP("act", "activation", reads=[h_res[tt]], writes=[junk_r, small_res], out=junk,
               in_=h_sb[:, tt, :], func=AF.Square, accum_out=ss)
            OP("act", "activation", reads=[small_res], writes=[small_res], out=rs, in_=ss,
               func=AF.Sqrt, scale=1.0 / D, bias=1e-6)
            OP("dve", "reciprocal", reads=[small_res], writes=[small_res], out=rs, in_=rs)
            OP("dve", "scalar_tensor_tensor", reads=[h_res[tt], small_res, prm_res[0], osr], writes=[ot_r],
               out=ot, in0=h_sb[:, tt, :], scalar=rs, in1=fg, op0=ALU.mult, op1=ALU.mult)
            P.dma("sp", out_d[(tt - 1) * 128:tt * 128, :], ot, osr, reads=[ot_r], writes=[osr])

    def dump_oT():
        P.barrier()
        carve.reset()
        t32, t32_r = carve.f32(LP, "dump")
        for k in range(8):
            OP("dve", "tensor_copy", reads=oT_res, writes=[t32_r], out=t32, in_=oT[:, k, :])
            P.dma("sp", dbg_d[k * 128:(k + 1) * 128, :], t32, dbg_res, reads=[t32_r])

    def dump_h():
        P.barrier()
        for tt in range(NT):
            P.dma("sp", dbg_d[tt * 128:(tt + 1) * 128, :], h_sb[:, tt, :], dbg_res, reads=[h_res[tt]])

    done = False
    for layer in range(depth):
        P.new_phase()
        layer_norm_T(layer)
        if "A" not in skip:
            unit_attention(layer)
        if stop_after == ("A", layer):
            dump_oT()
            done = True
            break
        if "A" not in skip:
            unit_merge(layer, 0, 0, O_GATE)
        if stop_after == ("Am", layer):
            dump_h()
            done = True
            break
        if "B" not in skip:
            unit_gdn(layer)
        if stop_after == ("B", layer):
            dump_oT()
            done = True
            break
        if "B" not in skip:
            unit_merge(layer, 1, 0, O_GATE + 1024)
        if stop_after == ("Bm", layer):
            dump_h()
            done = True
            break
        for g in range(2):
            if "C" in skip:
                break
            unit_ssd(layer, g)
            if stop_after == ("C%d" % g, layer):
                dump_oT()
                done = True
                break
            unit_merge(layer, 2, g * 1024, O_GATE + 2048)
        if done:
            break
        OP("pool", "memset", reads=[], writes=[h_res[0]], ap=h_sb[0:PAD, 0, :], constant=0.0)
        if stop_after == ("L", layer):
            dump_h()
            done = True
            break
    if not done:
        final_out()

    P.emit()
    import os as _os
    if _os.environ.get("KDEBUG"):
        print("ops", len(P.ops), "sems", P.nsem, "counts", P.counts)
    es.close()
    return nc


_NAMES = ["meta_tokens", "norm_g", "w_in", "gdn_conv_w", "gdn_a_log", "gdn_dt_bias", "gdn_norm_g",
          "ssm_conv_w", "ssm_conv_b", "ssm_a_log", "ssm_dt_bias", "ssm_d", "ssm_norm_g",
          "w_branch_a", "w_branch_b", "w_branch_c", "w_out"]


def make_in_maps(inputs, ncores=8):
    shared = {n: np.ascontiguousarray(np.asarray(inputs[n], dtype=np.float32)) for n in _NAMES}
    shared["final_norm_g"] = np.ascontiguousarray(
        np.asarray(inputs["final_norm_g"], dtype=np.float32).reshape(1, D))
    x = np.asarray(inputs["x"], dtype=np.float32)
    maps = []
    for c in range(ncores):
        m = dict(shared)
        m["x"] = np.ascontiguousarray(x[c])
        maps.append(m)
    return maps


def kernel(**inputs):
    nc = build()
    in_maps = make_in_maps(inputs)
    res = run_bass_kernel_spmd(nc, in_maps, core_ids=list(range(8)))
    return np.stack([r["out"] for r in res.results], axis=0)
```

```python
from contextlib import ExitStack
import numpy as np
import concourse.bass as bass
import concourse.mybir as mybir
from concourse.bass_utils import run_bass_kernel_spmd

F32 = mybir.dt.float32
BF16 = mybir.dt.bfloat16
AF = mybir.ActivationFunctionType
ALU = mybir.AluOpType

D = 1024
SEQ = 2048
NMETA = 16
PAD = 112
LP = SEQ + NMETA + PAD
NT = LP // 128
DIN = 15920
DEPTH = 4
TG = [(0, 512), (512, 512), (1024, 512), (1536, 512), (2048, 128)]

O_SBQ, O_SBK, O_SBV, O_SBZ = 0, 1024, 2048, 3072
O_GQ, O_GK, O_GV, O_GZ, O_GB, O_GA = 4096, 5120, 6144, 7168, 8192, 8200
O_SZ, O_SX, O_SB, O_SC, O_SDT, O_GATE = 8208, 10256, 12304, 12560, 12816, 12848


class Res:
    __slots__ = ("name", "last_w", "readers", "sem", "dcount", "excl")

    def __init__(self, name):
        self.name = name
        self.excl = False
        self.last_w = None
        self.readers = []
        self.sem = None
        self.dcount = 0


class Op:
    __slots__ = ("eng", "fn", "deps", "is_dma", "sem", "val", "signal", "phase", "pe_mm")


class Prog:
    ENGS = ("pe", "act", "dve", "pool", "sp")

    def __init__(self, nc, es):
        self.nc = nc
        self.es = es
        self.ops = []
        self.phase = 0
        self.bar_deps = []
        self.last_on_eng = {}
        self.dmas_since_bar = []
        self.phase_sems = {}
        self.nsem = 0

    def new_sem(self, name):
        self.nsem += 1
        return self.es.enter_context(self.nc.semaphore(name))

    def res(self, name, dma=False):
        r = Res(name)
        if dma:
            r.sem = self.new_sem("d_" + name)
        return r

    def new_phase(self):
        self.phase += 1

    def _track(self, op, reads, writes):
        ex = [r for r in reads if r.excl and r not in writes]
        if ex:
            reads = [r for r in reads if not r.excl]
            writes = list(writes) + ex
        deps = set(self.bar_deps)
        for r in list(reads) + list(writes):
            if r.last_w is not None:
                deps.add(r.last_w)
        for w in writes:
            for rd in w.readers:
                deps.add(rd)
        idx = len(self.ops)
        deps.discard(idx)
        op.deps = deps
        for r in reads:
            r.readers.append(idx)
        for w in writes:
            w.last_w = idx
            w.readers = []
        self.ops.append(op)
        self.last_on_eng[op.eng] = idx
        return idx

    def op(self, eng, fn, reads=(), writes=(), mm=False):
        o = Op()
        o.eng = eng
        o.fn = fn
        o.is_dma = False
        o.sem = None
        o.val = 0
        o.signal = False
        o.phase = self.phase
        o.pe_mm = mm
        return self._track(o, reads, writes)

    def dma(self, eng, out, in_, semres, reads=(), writes=(), slow=False):
        o = Op()
        o.eng = eng
        if slow:
            o.fn = lambda e: e.dma_start(out=out, in_=in_, allow_slow_non_contiguous=True)
        else:
            o.fn = lambda e: e.dma_start(out=out, in_=in_)
        o.is_dma = True
        o.sem = semres.sem
        semres.dcount += 1
        o.val = 16 * semres.dcount
        o.signal = True
        o.phase = self.phase
        o.pe_mm = False
        idx = self._track(o, reads, writes)
        self.dmas_since_bar.append(idx)
        return idx

    def barrier(self):
        self.bar_deps = list(self.last_on_eng.values()) + list(self.dmas_since_bar)
        self.dmas_since_bar = []

    def emit(self):
        nc = self.nc
        ops = self.ops
        for i, o in enumerate(ops):
            for j in o.deps:
                p = ops[j]
                if p.is_dma:
                    continue
                if p.eng == "pe" and o.eng == "pe":
                    continue
                p.signal = True
        cnt = {}
        for o in ops:
            if o.is_dma or not o.signal:
                continue
            key = (o.eng, o.phase)
            if key not in self.phase_sems:
                self.phase_sems[key] = self.new_sem("s_%s_%d" % key)
                cnt[key] = 0
            cnt[key] += 1
            o.sem = self.phase_sems[key]
            o.val = cnt[key]
        self.counts = dict(cnt)
        per_eng = {e: [] for e in self.ENGS}
        for i, o in enumerate(ops):
            per_eng[o.eng].append(i)
        final_waits = {}
        for o in ops:
            if o.is_dma:
                final_waits[id(o.sem)] = (o.sem, max(o.val, final_waits.get(id(o.sem), (None, 0))[1]))

        def run(engname, e):
            known = {}
            for i in per_eng[engname]:
                o = ops[i]
                need = {}
                for j in o.deps:
                    p = ops[j]
                    if p.sem is None:
                        continue
                    k = id(p.sem)
                    if k not in need or need[k][1] < p.val:
                        need[k] = (p.sem, p.val)
                for k, (s, v) in need.items():
                    if known.get(k, 0) < v:
                        e.wait_ge(s, v)
                        known[k] = v
                ins = o.fn(e)
                if o.is_dma:
                    ins.then_inc(o.sem, 16)
                elif o.signal:
                    ins.then_inc(o.sem, 1)
            if engname == "sp":
                for k, (s, v) in final_waits.items():
                    e.wait_ge(s, v)

        with nc.Block() as block:
            @block.tensor
            def _(e):
                run("pe", e)

            @block.scalar
            def _(e):
                run("act", e)

            @block.vector
            def _(e):
                run("dve", e)

            @block.gpsimd
            def _(e):
                run("pool", e)

            @block.sync
            def _(e):
                run("sp", e)


class Carver:
    def __init__(self, prog, ap, nwords):
        self.prog = prog
        self.ap = ap
        self.n = nwords
        self.off = 0
        self.k = 0

    def reset(self):
        self.off = 0

    def f32(self, cols, name="w"):
        a = self.ap[:, self.off:self.off + cols]
        self.off += cols
        assert self.off <= self.n, ("work overflow", self.off, self.n)
        self.k += 1
        return a, self.prog.res("%s%d" % (name, self.k))

    def bf16(self, cols, name="w"):
        words = (cols + 1) // 2
        a = self.ap[:, self.off:self.off + words].bitcast(BF16)
        self.off += words
        assert self.off <= self.n, ("work overflow", self.off, self.n)
        self.k += 1
        return a[:, 0:cols], self.prog.res("%s%d" % (name, self.k))


def build(depth=DEPTH, dbg=None, stop_after=None, GSTOP=0, skip=(), USE_F32R=False):
    F32R = mybir.dt.float32r
    nc = bass.Bass("TRN2", target_bir_lowering=False)
    es = ExitStack()
    P = Prog(nc, es)

    def din(name, shape):
        return nc.dram_tensor(name, list(shape), F32, kind="ExternalInput").ap()

    x_d = din("x", (SEQ, D))
    meta_d = din("meta_tokens", (NMETA, D))
    norm_g_d = din("norm_g", (DEPTH, D))
    w_in_d = din("w_in", (DEPTH, D, DIN))
    gdn_conv_w_d = din("gdn_conv_w", (DEPTH, 4, 3072))
    gdn_a_log_d = din("gdn_a_log", (DEPTH, 8))
    gdn_dt_bias_d = din("gdn_dt_bias", (DEPTH, 8))
    gdn_norm_g_d = din("gdn_norm_g", (DEPTH, 128))
    ssm_conv_w_d = din("ssm_conv_w", (DEPTH, 4, 2560))
    ssm_conv_b_d = din("ssm_conv_b", (DEPTH, 2560))
    ssm_a_log_d = din("ssm_a_log", (DEPTH, 32))
    ssm_dt_bias_d = din("ssm_dt_bias", (DEPTH, 32))
    ssm_d_d = din("ssm_d", (DEPTH, 32))
    ssm_norm_g_d = din("ssm_norm_g", (DEPTH, 2048))
    w_br_d = [din("w_branch_a", (DEPTH, 1024, D)), din("w_branch_b", (DEPTH, 1024, D)),
              din("w_branch_c", (DEPTH, 2048, D))]
    w_out_d = din("w_out", (DEPTH, D, D))
    fin_g_d = din("final_norm_g", (1, D))
    out_d = nc.dram_tensor("out", [SEQ, D], F32, kind="ExternalOutput").ap()
    dbg_d = None
    if dbg is not None:
        dbg_d = nc.dram_tensor("dbg", list(dbg["shape"]), F32, kind="ExternalOutput").ap()

    def sb(name, shape, dt):
        return es.enter_context(nc.sbuf_tensor(name, list(shape), dt))

    def ps(name, shape, dt=F32):
        return es.enter_context(nc.psum_tensor(name, list(shape), dt))

    def OP(eng, meth, reads=(), writes=(), **kw):
        return P.op(eng, lambda e: getattr(e, meth)(**kw), reads=reads, writes=writes)

    def MM(out, lhsT, rhs, start, stop, reads, writes):
        return P.op("pe", lambda e: e.matmul(out=out, lhsT=lhsT, rhs=rhs, start=start, stop=stop),
                    reads=reads, writes=writes, mm=True)

    def TR(out, in_, ident, reads, writes):
        return P.op("pe", lambda e: e.transpose(out=out, in_=in_, identity=ident),
                    reads=reads, writes=writes, mm=True)

    h_sb = sb("h", (128, NT, D), F32)
    uT = sb("uT", (128, 8, LP), BF16)
    oT = sb("oT", (128, 8, LP), BF16)
    stage = [sb("stg%d" % i, (128, 8, 128), F32) for i in range(2)]
    wslab = [sb("wsl%d" % i, (128, 8, 128), BF16) for i in range(2)]
    WORKW = 13312
    work = sb("work", (128, WORKW), F32)
    ident_f = sb("ident_f", (128, 128), F32)
    ident_b = sb("ident_b", (128, 128), BF16)
    ones_b = sb("ones_b", (128, 128), BF16)
    ones_f = sb("ones_f", (128, 128), F32)
    uincl_f = sb("uincl_f", (128, 128), F32)
    tmat = sb("tmat", (128, 4, 128), BF16)
    mw = sb("mw", (128, 896), BF16)
    small = sb("small", (128, 64), F32)
    gfm = sb("gfm", (128, 8), F32)

    h_res = [P.res("h%d" % t) for t in range(NT)]
    hload_res = [P.res("hl%d" % i, dma=True) for i in range(5)]
    uT_res = [P.res("uT%d" % t) for t in range(NT)]
    oT_res = [P.res("oT%d" % k) for k in range(8)]
    stage_res = [P.res("stg%d" % i, dma=True) for i in range(2)]
    wslab_res = [P.res("wsl%d" % i) for i in range(2)]
    const_res = P.res("const")
    gfm_res = P.res("gfm", dma=True)
    small_res = P.res("small")
    dbg_res = P.res("dbg", dma=True)
    prm_res = [P.res("prm%d" % i, dma=True) for i in range(8)]
    out_res = [P.res("outst%d" % i, dma=True) for i in range(2)]

    banks = [ps("bank%d" % i, (128, 512), F32) for i in range(8)]
    bank_res = [P.res("bank%d" % i) for i in range(8)]
    for _r in bank_res:
        _r.excl = True
    carve = Carver(P, work, WORKW)
    bctr = [0]

    def nb(nmax=8):
        i = bctr[0] % nmax
        bctr[0] += 1
        return banks[i], bank_res[i]

    def bfv(bank):
        return bank[:].bitcast(BF16)

    cr = [const_res]
    OP("pool", "memset", writes=cr, ap=ident_f[:], constant=1.0)
    OP("pool", "affine_select", writes=cr, out=ident_f[:], in_=ident_f[:], pattern=[[-1, 128]],
       compare_op=ALU.is_equal, fill=0.0, base=0, channel_multiplier=1)
    OP("pool", "tensor_copy", reads=cr, writes=cr, out=ident_b[:], in_=ident_f[:])
    OP("pool", "memset", writes=cr, ap=ones_b[:], constant=1.0)
    OP("pool", "memset", writes=cr, ap=ones_f[:], constant=1.0)
    OP("pool", "memset", writes=cr, ap=uincl_f[:], constant=1.0)
    OP("pool", "affine_select", writes=cr, out=uincl_f[:], in_=uincl_f[:], pattern=[[1, 128]],
       compare_op=ALU.is_ge, fill=0.0, base=0, channel_multiplier=-1)
    OP("pool", "memset", writes=cr, ap=tmat[:], constant=-1.0)
    for v in (0, 2):
        OP("pool", "affine_select", writes=cr, out=tmat[:, v, :], in_=tmat[:, v, :], pattern=[[-1, 128]],
           compare_op=ALU.is_gt, fill=0.0, base=0, channel_multiplier=1)
    for v in (1, 3):
        OP("pool", "affine_select", writes=cr, out=tmat[:, v, :], in_=tmat[:, v, :], pattern=[[1, 128]],
           compare_op=ALU.is_ge, fill=0.0, base=0, channel_multiplier=-1)
    OP("pool", "memset", writes=cr, ap=tmat[0:PAD, 2:4, :], constant=0.0)
    OP("pool", "memset", writes=cr, ap=mw[:], constant=1.0)
    OP("pool", "affine_select", writes=cr, out=mw[:], in_=mw[:], pattern=[[1, 896]],
       compare_op=ALU.is_gt, fill=0.0, base=-384, channel_multiplier=-1)

    OP("pool", "memset", writes=[h_res[0]], ap=h_sb[:, 0, :], constant=0.0)
    P.dma("sp", h_sb[PAD:128, 0, :], meta_d[:, :], hload_res[4], writes=[h_res[0]])
    xv = x_d.rearrange("(t p) d -> p t d", p=128)
    for i in range(4):
        P.dma("sp", h_sb[:, 1 + 4 * i:5 + 4 * i, :], xv[:, 4 * i:4 * i + 4, :], hload_res[i],
              writes=[h_res[1 + 4 * i + j] for j in range(4)])

    wctr = [0]

    def stream_slab(src, C, dst=None, dst_res=None):
        i = wctr[0] % 2
        wctr[0] += 1
        P.dma("sp", stage[i][:, :, 0:C], src, stage_res[i], writes=[stage_res[i]])
        if dst is None:
            dst, dst_res = wslab[i][:, :, 0:C], wslab_res[i]
        OP("pool", "tensor_copy", reads=[stage_res[i]], writes=[dst_res], out=dst,
           in_=stage[i][:, :, 0:C])
        return dst, dst_res

    def win(layer, c0, C):
        return w_in_d[layer].rearrange("(k p) c -> p k c", p=128)[:, :, c0:c0 + C]

    def tiles_of(t0, n):
        return uT_res[t0 // 128:(t0 + n + 127) // 128]

    def proj_fm(layer, c0, evac, ncol=128):
        w, wr = stream_slab(win(layer, c0, ncol), ncol)
        for (t0, n) in TG:
            b, br = nb()
            for k in range(8):
                MM(b[0:ncol, 0:n], w[:, k, :], uT[:, k, t0:t0 + n], k == 0, k == 7,
                   reads=[wr] + tiles_of(t0, n), writes=[br])
            evac(b[0:ncol, 0:n], br, t0, n)

    def proj_tm(layer, c0, ncol, evac):
        w, wr = stream_slab(win(layer, c0, ncol), ncol)
        per = min(4, 512 // ncol)
        for tt0 in range(0, NT, per):
            cnt = min(per, NT - tt0)
            b, br = nb()
            for j in range(cnt):
                tt = tt0 + j
                for k in range(8):
                    MM(b[:, j * ncol:(j + 1) * ncol], uT[:, k, tt * 128:(tt + 1) * 128], w[:, k, :],
                       k == 0, k == 7, reads=[wr, uT_res[tt]], writes=[br])
            evac(b[:, 0:cnt * ncol], br, tt0, cnt)

    def load_small(dst, src, ri, slow=True):
        P.dma("sp", dst, src, prm_res[ri], writes=[prm_res[ri]], slow=slow)

    def layer_norm_T(layer):
        P.barrier()
        carve.reset()
        P.dma("sp", gfm[:, :], norm_g_d[layer].rearrange("(k p) -> p k", p=128), gfm_res,
              writes=[gfm_res], slow=True)
        junk, junk_r = carve.f32(1024, "junk")
        uns = [carve.bf16(1024, "un") for _ in range(2)]
        for tt in range(NT):
            un, un_r = uns[tt % 2]
            ss = small[:, 0:1]
            rs = small[:, 1:2]
            OP("act", "activation", reads=[h_res[tt]], writes=[junk_r, small_res], out=junk,
               in_=h_sb[:, tt, :], func=AF.Square, accum_out=ss)
            OP("act", "activation", reads=[small_res], writes=[small_res], out=rs, in_=ss,
               func=AF.Sqrt, scale=1.0 / D, bias=1e-6)
            OP("dve", "reciprocal", reads=[small_res], writes=[small_res], out=rs, in_=rs)
            OP("dve", "tensor_scalar", reads=[h_res[tt], small_res], writes=[un_r], out=un,
               in0=h_sb[:, tt, :], scalar1=rs, scalar2=None, op0=ALU.mult)
            b, br = nb()
            pT = bfv(b)
            for k in range(8):
                TR(pT[:, k * 128:(k + 1) * 128], un[:, k * 128:(k + 1) * 128], ident_b[:],
                   reads=[un_r, const_res], writes=[br])
            OP("dve", "tensor_tensor", reads=[br, gfm_res], writes=[uT_res[tt]],
               out=uT[:, :, tt * 128:(tt + 1) * 128], in0=pT.rearrange("p (k t) -> p k t", t=128),
               in1=gfm[:, :].unsqueeze(2).to_broadcast([128, 8, 128]), op=ALU.mult)

    def unit_attention(layer):
        P.barrier()
        carve.reset()
        qT, qT_r = carve.bf16(LP, "qT")
        kT, kT_r = carve.bf16(LP, "kT")
        zg, zg_r = carve.bf16(LP, "zg")
        vtm_flat, vtm_r = carve.bf16(NT * 128, "vtm")
        vtm = vtm_flat.rearrange("p (t d) -> p t d", d=128)
        NBUF = 3
        AHEAD = 2
        spf = [carve.f32(512, "spf") for _ in range(NBUF)]
        spb = [carve.bf16(512, "spb") for _ in range(NBUF)]
        t2 = [carve.f32(512, "t2") for _ in range(NBUF)]
        wb = [carve.bf16(512, "wb") for _ in range(NBUF)]
        scale = 128.0 ** -0.5
        gctr = [0]
        for h in range(8):
            proj_fm(layer, O_SBQ + h * 128, lambda b, br, t0, n: OP(
                "act", "activation", reads=[br], writes=[qT_r], out=qT[:, t0:t0 + n], in_=b,
                func=AF.Copy, scale=scale))
            proj_fm(layer, O_SBK + h * 128, lambda b, br, t0, n: OP(
                "dve", "tensor_copy", reads=[br], writes=[kT_r], out=kT[:, t0:t0 + n], in_=b))
            proj_fm(layer, O_SBZ + h * 128, lambda b, br, t0, n: OP(
                "act", "activation", reads=[br], writes=[zg_r], out=zg[:, t0:t0 + n], in_=b,
                func=AF.Silu))
            proj_tm(layer, O_SBV + h * 128, 128, lambda b, br, tt0, cnt: OP(
                "dve", "tensor_copy", reads=[br], writes=[vtm_r], out=vtm[:, tt0:tt0 + cnt, :],
                in_=b.rearrange("p (t d) -> p t d", d=128)))
            its = []
            for g in range(5):
                qb0 = 4 * g
                nq = 512 if g < 4 else 128
                kmax = qb0 + nq // 128 - 1
                gi = gctr[0] % 2
                gctr[0] += 1
                for idx, kb in enumerate(range(kmax, -1, -1)):
                    its.append(dict(qb0=qb0, nq=nq, q0=qb0 * 128, idx=idx, kb=kb, gi=gi, zb=None))

            n_it = len(its)

            def PZ(j):
                d = its[j]
                nq, kb, q0 = d["nq"], d["kb"], d["q0"]
                zb, zr = nb(4)
                d["zb"] = (zb, zr)
                MM(zb[:, 0:nq], kT[:, kb * 128:(kb + 1) * 128], qT[:, q0:q0 + nq], True, True,
                   reads=[kT_r, qT_r], writes=[zr])

            def AE(j):
                d = its[j]
                nq = d["nq"]
                zb, zr = d["zb"]
                sp_a, sp_r = spf[j % NBUF]
                OP("act", "activation", reads=[zr], writes=[sp_r], out=sp_a[:, 0:nq],
                   in_=zb[:, 0:nq], func=AF.Exp)
                OP("act", "activation", reads=[sp_r], writes=[sp_r], out=sp_a[:, 0:nq],
                   in_=sp_a[:, 0:nq], func=AF.Ln, bias=1.0)

            def DS(j):
                d = its[j]
                nq, kb, qb0 = d["nq"], d["kb"], d["qb0"]
                zb, zr = d["zb"]
                sp_a, sp_r = spf[j % NBUF]
                spb_a, spb_r = spb[j % NBUF]
                t2_a, t2_r = t2[j % NBUF]
                if kb >= qb0:
                    moff = 384 - 128 * (kb - qb0)
                    OP("dve", "tensor_tensor", reads=[sp_r, const_res], writes=[spb_r],
                       out=spb_a[:, 0:nq], in0=sp_a[:, 0:nq], in1=mw[:, moff:moff + nq],
                       op=ALU.mult)
                else:
                    OP("dve", "tensor_copy", reads=[sp_r], writes=[spb_r], out=spb_a[:, 0:nq],
                       in_=sp_a[:, 0:nq])
                OP("dve", "tensor_tensor", reads=[zr, sp_r], writes=[t2_r], out=t2_a[:, 0:nq],
                   in0=zb[:, 0:nq], in1=sp_a[:, 0:nq], op=ALU.subtract)

            def banksof(d):
                gi = d["gi"]
                return banks[4 + gi], bank_res[4 + gi], banks[6 + gi], bank_res[6 + gi]

            def PT(j):
                d = its[j]
                nq, kb, idx = d["nq"], d["kb"], d["idx"]
                A_b, A_r, O_b, O_r = banksof(d)
                spb_a, spb_r = spb[j % NBUF]
                tv = 2 if kb == 0 else 0
                MM(A_b[:, 0:nq], tmat[:, tv, :], spb_a[:, 0:nq], idx == 0, False,
                   reads=[spb_r, const_res], writes=[A_r])

            def DA(j):
                d = its[j]
                nq = d["nq"]
                A_b, A_r, O_b, O_r = banksof(d)
                t2_a, t2_r = t2[j % NBUF]
                OP("dve", "tensor_tensor", reads=[A_r, t2_r], writes=[t2_r], out=t2_a[:, 0:nq],
                   in0=A_b[:, 0:nq], in1=t2_a[:, 0:nq], op=ALU.add)

            def AW(j):
                d = its[j]
                nq, kb, qb0 = d["nq"], d["kb"], d["qb0"]
                t2_a, t2_r = t2[j % NBUF]
                wb_a, wb_r = wb[j % NBUF]
                OP("act", "activation", reads=[t2_r], writes=[wb_r], out=wb_a[:, 0:nq],
                   in_=t2_a[:, 0:nq], func=AF.Exp)
                if kb >= qb0:
                    moff = 384 - 128 * (kb - qb0)
                    OP("pool", "tensor_tensor", reads=[wb_r, const_res], writes=[wb_r],
                       out=wb_a[:, 0:nq], in0=wb_a[:, 0:nq], in1=mw[:, moff:moff + nq],
                       op=ALU.mult)

            def PL(j):
                d = its[j]
                nq, kb = d["nq"], d["kb"]
                A_b, A_r, O_b, O_r = banksof(d)
                spb_a, spb_r = spb[j % NBUF]
                tv = 2 if kb == 0 else 0
                MM(A_b[:, 0:nq], tmat[:, tv + 1, :], spb_a[:, 0:nq], False, kb == 0,
                   reads=[spb_r, const_res], writes=[A_r])

            def PO(j, h=h):
                d = its[j]
                nq, kb, idx, q0 = d["nq"], d["kb"], d["idx"], d["q0"]
                A_b, A_r, O_b, O_r = banksof(d)
                wb_a, wb_r = wb[j % NBUF]
                MM(O_b[:, 0:nq], vtm[:, kb, :], wb_a[:, 0:nq], idx == 0, kb == 0,
                   reads=[vtm_r, wb_r], writes=[O_r])
                if kb == 0:
                    OP("dve", "tensor_tensor", reads=[O_r, zg_r], writes=[oT_res[h]],
                       out=oT[:, h, q0:q0 + nq], in0=O_b[:, 0:nq], in1=zg[:, q0:q0 + nq], op=ALU.mult)

            PZ(0)
            AE(0)
            PZ(1)
            AE(1)
            DS(0)
            for j in range(n_it):
                PT(j)
                if j + 2 < n_it:
                    PZ(j + 2)
                DA(j)
                if j + 1 < n_it:
                    DS(j + 1)
                AW(j)
                if j + 2 < n_it:
                    AE(j + 2)
                PL(j)
                if j >= 1:
                    PO(j - 1)
            PO(n_it - 1)

    def unit_merge(layer, br_idx, row0, gate_c0):
        P.barrier()
        carve.reset()
        wbr_f, wbr_r = carve.bf16(8 * 1024, "wbr")
        wg_f, wg_r = carve.bf16(8 * 1024, "wg")
        wbr = wbr_f.rearrange("p (k c) -> p k c", c=1024)
        wg = wg_f.rearrange("p (k c) -> p k c", c=1024)
        sgs = [carve.bf16(512, "sg") for _ in range(2)]
        mTs = [carve.bf16(8 * 512, "mT") for _ in range(2)]
        src_br = w_br_d[br_idx][layer].rearrange("(k p) c -> p k c", p=128)
        for cb in range(8):
            stream_slab(src_br[:, row0 // 128:row0 // 128 + 8, cb * 128:(cb + 1) * 128], 128,
                        dst=wbr[:, :, cb * 128:(cb + 1) * 128], dst_res=wbr_r)
            stream_slab(win(layer, gate_c0 + cb * 128, 128), 128,
                        dst=wg[:, :, cb * 128:(cb + 1) * 128], dst_res=wg_r)
        for gi_, (t0, n) in enumerate(TG):
            tok = slice(t0, t0 + n)
            mT_f, mT_r = mTs[gi_ % 2]
            mT = mT_f.rearrange("p (c t) -> p c t", t=512)
            for cb in range(8):
                pbk, pbr = nb()
                for fk in range(8):
                    MM(pbk[:, 0:n], wbr[:, fk, cb * 128:(cb + 1) * 128], oT[:, fk, tok], fk == 0, fk == 7,
                       reads=[wbr_r, oT_res[fk]], writes=[pbr])
                gbk, gbr = nb()
                for k in range(8):
                    MM(gbk[:, 0:n], wg[:, k, cb * 128:(cb + 1) * 128], uT[:, k, tok], k == 0, k == 7,
                       reads=[wg_r] + tiles_of(t0, n), writes=[gbr])
                sg, sg_r = sgs[cb % 2]
                OP("act", "activation", reads=[gbr], writes=[sg_r], out=sg[:, 0:n], in_=gbk[:, 0:n],
                   func=AF.Sigmoid)
                OP("dve", "tensor_tensor", reads=[pbr, sg_r], writes=[mT_r], out=mT[:, cb, 0:n],
                   in0=pbk[:, 0:n], in1=sg[:, 0:n], op=ALU.mult)
            OP("pool", "tensor_copy", reads=[mT_r], writes=list(oT_res), out=oT[:, :, tok], in_=mT[:, :, 0:n])
        src_o = w_out_d[layer].rearrange("(k p) c -> p k c", p=128)
        for ds in range(8):
            w, wr = stream_slab(src_o[:, :, ds * 128:(ds + 1) * 128], 128)
            for tt0 in range(0, NT, 4):
                cnt = min(4, NT - tt0)
                b, br = nb()
                for j in range(cnt):
                    tt = tt0 + j
                    for cb in range(8):
                        MM(b[:, j * 128:(j + 1) * 128], oT[:, cb, tt * 128:(tt + 1) * 128], w[:, cb, :],
                           cb == 0, cb == 7, reads=[wr, oT_res[cb]], writes=[br])
                OP("dve", "tensor_tensor", reads=[br] + h_res[tt0:tt0 + cnt], writes=h_res[tt0:tt0 + cnt],
                   out=h_sb[:, tt0:tt0 + cnt, ds * 128:(ds + 1) * 128],
                   in0=b[:, 0:cnt * 128].rearrange("p (t d) -> p t d", d=128),
                   in1=h_sb[:, tt0:tt0 + cnt, ds * 128:(ds + 1) * 128], op=ALU.add)


    def bc_rows(src_row_ap, n):
        return src_row_ap.to_broadcast([128, n])

    def conv_silu(b, br, n, raw, raw_r, acc, acc_r, cw4, prm_r, bias, out_ap, out_res):
        OP("act", "activation", reads=[br], writes=[raw_r], out=raw[:, 3:3 + n], in_=b, func=AF.Copy)
        if bias is None:
            OP("dve", "tensor_scalar", reads=[raw_r, prm_r], writes=[acc_r], out=acc[:, 0:n],
               in0=raw[:, 3:3 + n], scalar1=cw4[:, 3:4], scalar2=None, op0=ALU.mult)
        else:
            OP("dve", "tensor_scalar", reads=[raw_r, prm_r], writes=[acc_r], out=acc[:, 0:n],
               in0=raw[:, 3:3 + n], scalar1=cw4[:, 3:4], scalar2=bias, op0=ALU.mult, op1=ALU.add)
        for j in (2, 1, 0):
            OP("dve", "scalar_tensor_tensor", reads=[raw_r, prm_r, acc_r], writes=[acc_r],
               out=acc[:, 0:n], in0=raw[:, j:j + n], scalar=cw4[:, j:j + 1], in1=acc[:, 0:n],
               op0=ALU.mult, op1=ALU.add)
        OP("act", "activation", reads=[acc_r], writes=[out_res], out=out_ap, in_=acc[:, 0:n],
           func=AF.Silu)
        OP("pool", "tensor_copy", reads=[raw_r], writes=[raw_r], out=raw[:, 0:3], in_=raw[:, n:n + 3])

    def load_conv_w(dst, dst_r, src2d, nblk, tmp, tmp_r, ri):
        P.dma("sp", tmp[0:nblk, :], src2d.rearrange("j (b p) -> b j p", p=128), prm_res[ri],
              writes=[tmp_r])
        b, br = nb()
        for j in range(4):
            TR(b[:, j * nblk:(j + 1) * nblk], tmp[0:nblk, j * 128:(j + 1) * 128], ident_f[0:nblk, 0:nblk],
               reads=[tmp_r, const_res], writes=[br])
        OP("dve", "tensor_copy", reads=[br], writes=[dst_r], out=dst, in_=b[:, 0:4 * nblk])

    def unit_gdn(layer):
        P.barrier()
        carve.reset()
        ba, ba_r = carve.f32(NT * 16, "ba")
        ba3 = ba.rearrange("p (t c) -> p t c", c=16)

        def sm(name):
            a, r = carve.f32(NT * 8, name)
            return a, a.rearrange("p (t c) -> p t c", c=8), r
        xa, xa3, xa_r = sm("xa")
        lb, lb3, lb_r = sm("lb")
        g_all, g3, g_r = sm("g")
        gc, gc3, gc_r = sm("gc")
        ngc, ngc3, ngc_r = sm("ngc")
        gb_, gb3, gb_r = sm("gb")
        egb, egb3, egb_r = sm("egb")
        beta, beta3, beta_r = sm("beta")
        alog, alog_r = carve.f32(8, "alog")
        dtb, dtb_r = carve.f32(8, "dtb")
        gng, gng_r = carve.f32(1, "gng")
        cw, cw_r = carve.f32(96, "cw")
        cw3 = cw.rearrange("p (j b) -> p j b", b=24)
        qs, qs_r = carve.f32(512, "qs")
        cwt, cwt_r = qs, qs_r
        load_small(alog, bc_rows(gdn_a_log_d[layer:layer + 1, :], 8), 0)
        load_small(dtb, bc_rows(gdn_dt_bias_d[layer:layer + 1, :], 8), 1)
        load_small(gng, gdn_norm_g_d[layer].rearrange("(p o) -> p o", o=1), 2)
        alog_r, dtb_r, gng_r = prm_res[0], prm_res[1], prm_res[2]
        load_conv_w(cw, cw_r, gdn_conv_w_d[layer], 24, cwt, cwt_r, 3)
        if GSTOP == 1:
            return
        proj_tm(layer, O_GB, 16, lambda b, br, tt0, cnt: OP(
            "dve", "tensor_copy", reads=[br], writes=[ba_r], out=ba[:, tt0 * 16:(tt0 + cnt) * 16], in_=b))
        OP("dve", "tensor_tensor", reads=[ba_r, dtb_r], writes=[xa_r], out=xa3, in0=ba3[:, :, 8:16],
           in1=dtb.unsqueeze(1).to_broadcast([128, NT, 8]), op=ALU.add)
        OP("act", "activation", reads=[xa_r], writes=[xa_r], out=xa, in_=xa, func=AF.Exp)
        OP("act", "activation", reads=[xa_r], writes=[xa_r], out=xa, in_=xa, func=AF.Ln, bias=1.0)
        OP("act", "activation", reads=[alog_r], writes=[alog_r], out=alog, in_=alog, func=AF.Exp)
        OP("dve", "scalar_tensor_tensor", reads=[xa_r, alog_r], writes=[g_r], out=g3, in0=xa3, scalar=-1.0,
           in1=alog.unsqueeze(1).to_broadcast([128, NT, 8]), op0=ALU.mult, op1=ALU.mult)
        OP("pool", "memset", reads=[], writes=[g_r], ap=g3[0:PAD, 0, :], constant=0.0)
        OP("act", "activation", reads=[ba_r], writes=[lb_r], out=lb3, in_=ba3[:, :, 0:8], func=AF.Exp,
           scale=-1.0)
        OP("act", "activation", reads=[lb_r], writes=[lb_r], out=lb, in_=lb, func=AF.Ln, bias=1.0)
        OP("act", "activation", reads=[lb_r], writes=[beta_r], out=beta, in_=lb, func=AF.Exp, scale=-1.0)
        b, br = nb()
        MM(b[:, 0:NT * 8], uincl_f[:], g_all, True, True, reads=[g_r, const_res], writes=[br])
        OP("dve", "tensor_copy", reads=[br], writes=[gc_r], out=gc, in_=b[:, 0:NT * 8])
        OP("dve", "tensor_scalar", reads=[gc_r], writes=[ngc_r], out=ngc, in0=gc, scalar1=-1.0,
           scalar2=None, op0=ALU.mult)
        OP("dve", "tensor_tensor", reads=[gc_r, lb_r], writes=[gb_r], out=gb_, in0=gc, in1=lb,
           op=ALU.subtract)
        OP("act", "activation", reads=[gb_r], writes=[egb_r], out=egb, in_=gb_, func=AF.Exp)
        if GSTOP == 2:
            return
        qT, qT_r = carve.bf16(LP, "qT")
        kT, kT_r = carve.bf16(LP, "kT")
        ktm_f, ktm_r = carve.bf16(NT * 128, "ktm")
        vtm_f, vtm_r = carve.bf16(NT * 128, "vtm")
        ktm = ktm_f.rearrange("p (t d) -> p t d", d=128)
        vtm = vtm_f.rearrange("p (t d) -> p t d", d=128)
        raw, raw_r = carve.f32(515, "raw")
        acc, acc_r = carve.f32(512, "acc")
        sqb, sqb_r = carve.bf16(512, "sqb")
        rn, rn_r = carve.f32(512, "rn")
        dd, dd_r = carve.f32(128, "dd")
        vTt, vTt_r = carve.bf16(512, "vTt")
        zgt, zgt_r = carve.bf16(512, "zgt")
        NSET = 3
        gsets = []
        for _i in range(NSET):
            st = {}
            for nm, kind, n in (("dg", "f", 256), ("eg", "f", 128),
                                ("RwT", "f", 128), ("AN0", "f", 256), ("Pm0", "f", 128),
                                ("aqkT", "b", 128), ("Rv", "f", 128), ("qd", "b", 128),
                                ("kend", "b", 128), ("glk", "f", 2)):
                st[nm], st[nm + "_r"] = (carve.f32 if kind == "f" else carve.bf16)(n, nm)
            st["E12"], st["E12_r"] = st["dg"], st["dg_r"]
            st["egbrow"], st["egbrow_r"] = st["RwT"], st["RwT_r"]
            st["AN1"], st["AN1_r"] = st["AN0"], st["AN0_r"]
            st["Pm1"], st["Pm1_r"] = st["Pm0"], st["Pm0_r"]
            gsets.append(st)
        vnew, vnew_r = carve.bf16(128, "vnew")
        S, S_r = carve.f32(128, "S")
        S_bf, Sbf_r = carve.bf16(128, "Sbf")
        import os as _os2
        if _os2.environ.get("KDEBUG"):
            print("GDN carve off", carve.off, "of", WORKW)

        def l2norm_to(dst, dst_r, t0, n, scl):
            OP("act", "activation", reads=[qs_r], writes=[sqb_r], out=sqb[:, 0:n], in_=qs[:, 0:n],
               func=AF.Square)
            b2, b2r = nb()
            MM(b2[:, 0:n], ones_b[:], sqb[:, 0:n], True, True, reads=[sqb_r, const_res], writes=[b2r])
            OP("act", "activation", reads=[b2r], writes=[rn_r], out=rn[:, 0:n], in_=b2[:, 0:n],
               func=AF.Sqrt, bias=1e-6)
            OP("dve", "reciprocal", reads=[rn_r], writes=[rn_r], out=rn[:, 0:n], in_=rn[:, 0:n])
            OP("dve", "scalar_tensor_tensor", reads=[qs_r, rn_r], writes=[dst_r], out=dst[:, t0:t0 + n],
               in0=qs[:, 0:n], scalar=scl, in1=rn[:, 0:n], op0=ALU.mult, op1=ALU.mult)

        def to_tm(srcT, src_r, c0, n, dst3, dst_r, tt0):
            b2, b2r = nb()
            v = bfv(b2)
            cnt = n // 128
            for j in range(cnt):
                TR(v[:, j * 128:(j + 1) * 128], srcT[:, c0 + j * 128:c0 + (j + 1) * 128], ident_b[:],
                   reads=[src_r, const_res], writes=[b2r])
            OP("dve", "tensor_copy", reads=[b2r], writes=[dst_r], out=dst3[:, tt0:tt0 + cnt, :],
               in_=v[:, 0:n].rearrange("p (t d) -> p t d", d=128))

        for h in range(8):
            OP("pool", "memset", writes=[raw_r], ap=raw[:, 0:3], constant=0.0)

            def ev_q(b, br, t0, n, h=h):
                conv_silu(b, br, n, raw, raw_r, acc, acc_r, cw3[:, :, h], cw_r, None, qs[:, 0:n], qs_r)
                l2norm_to(qT, qT_r, t0, n, 128.0 ** -0.5)
            proj_fm(layer, O_GQ + h * 128, ev_q)
            OP("pool", "memset", writes=[raw_r], ap=raw[:, 0:3], constant=0.0)

            def ev_k(b, br, t0, n, h=h):
                conv_silu(b, br, n, raw, raw_r, acc, acc_r, cw3[:, :, 8 + h], cw_r, None, qs[:, 0:n], qs_r)
                l2norm_to(kT, kT_r, t0, n, 1.0)
                to_tm(kT, kT_r, t0, n, ktm, ktm_r, t0 // 128)
            proj_fm(layer, O_GK + h * 128, ev_k)
            OP("pool", "memset", writes=[raw_r], ap=raw[:, 0:3], constant=0.0)

            def ev_v(b, br, t0, n, h=h):
                conv_silu(b, br, n, raw, raw_r, acc, acc_r, cw3[:, :, 16 + h], cw_r, None, vTt[:, 0:n], vTt_r)
                to_tm(vTt, vTt_r, 0, n, vtm, vtm_r, t0 // 128)
            proj_fm(layer, O_GV + h * 128, ev_v)
            OP("pool", "memset", writes=[S_r], ap=S, constant=0.0)
            OP("pool", "memset", writes=[Sbf_r], ap=S_bf, constant=0.0)
            if GSTOP == 3:
                return
            def T_steps(c, h=h):
                st = gsets[c % NSET]
                tok = slice(c * 128, (c + 1) * 128)
                gcol = gc3[:, c, h:h + 1]
                ngcol = ngc3[:, c, h:h + 1]
                gbcol = gb3[:, c, h:h + 1]
                dg, dg_r, E12, E12_r, eg, eg_r = st["dg"], st["dg_r"], st["E12"], st["E12_r"], st["eg"], st["eg_r"]
                egbrow, egbrow_r, RwT, RwT_r = st["egbrow"], st["egbrow_r"], st["RwT"], st["RwT_r"]
                glk, glk_r = st["glk"], st["glk_r"]
                gl, kes = glk[:, 0:1], glk[:, 1:2]
                AN = [(st["AN0"], st["AN0_r"]), (st["AN1"], st["AN1_r"])]
                Pm = [(st["Pm0"], st["Pm0_r"]), (st["Pm1"], st["Pm1_r"])]
                an0, an0_r = AN[0]
                pm0, pm0_r = Pm[0]
                loc = {}
                steps = []

                def s0():
                    OP("dve", "tensor_scalar", reads=[gc_r, const_res], writes=[dg_r], out=dg[:, 0:128],
                       in0=ident_f[:], scalar1=gcol, scalar2=None, op0=ALU.mult)
                    OP("dve", "tensor_scalar", reads=[gb_r, const_res], writes=[dg_r], out=dg[:, 128:256],
                       in0=ident_f[:], scalar1=gbcol, scalar2=None, op0=ALU.mult)
                    bK, bKr = nb()
                    loc["bK"] = (bK, bKr)
                    MM(bK[:, 0:128], kT[:, tok], qT[:, tok], True, True, reads=[kT_r, qT_r], writes=[bKr])
                    MM(bK[:, 128:256], kT[:, tok], kT[:, tok], True, True, reads=[kT_r], writes=[bKr])
                    OP("dve", "tensor_scalar", reads=[vtm_r, beta_r], writes=[st["Rv_r"]], out=st["Rv"],
                       in0=vtm[:, c, :], scalar1=beta3[:, c, h:h + 1], scalar2=None, op0=ALU.mult)
                steps.append(s0)

                def s1():
                    bG, bGr = nb()
                    loc["bG"] = (bG, bGr)
                    MM(bG[:, 0:256], ones_f[:], dg, True, True, reads=[dg_r, const_res], writes=[bGr])
                steps.append(s1)

                def s2():
                    bG, bGr = loc["bG"]
                    OP("act", "activation", reads=[bGr, ngc_r], writes=[E12_r], out=E12, in_=bG[:, 0:256],
                       func=AF.Exp, bias=ngcol)
                    OP("dve", "tensor_copy", reads=[bGr], writes=[glk_r], out=gl, in_=bG[:, 127:128])
                    OP("act", "activation", reads=[bGr], writes=[eg_r], out=eg, in_=bG[:, 0:128], func=AF.Exp)
                    OP("act", "activation", reads=[bGr], writes=[egbrow_r], out=egbrow, in_=bG[:, 128:256],
                       func=AF.Exp)
                steps.append(s2)

                def s3():
                    OP("pool", "affine_select", reads=[E12_r], writes=[E12_r], out=E12[:, 0:128],
                       in_=E12[:, 0:128], pattern=[[1, 128]], compare_op=ALU.is_ge, fill=0.0, base=0,
                       channel_multiplier=-1)
                    OP("pool", "affine_select", reads=[E12_r], writes=[E12_r], out=E12[:, 128:256],
                       in_=E12[:, 128:256], pattern=[[1, 128]], compare_op=ALU.is_gt, fill=0.0, base=0,
                       channel_multiplier=-1)
                    OP("act", "activation", reads=[gc_r, glk_r], writes=[glk_r], out=kes, in_=gcol,
                       func=AF.Exp, scale=-1.0, bias=gl)
                steps.append(s3)

                def s4():
                    bK, bKr = loc["bK"]
                    OP("dve", "scalar_tensor_tensor", reads=[bKr, E12_r], writes=[an0_r], out=an0[:, 0:128],
                       in0=bK[:, 128:256], scalar=-1.0, in1=E12[:, 128:256], op0=ALU.mult, op1=ALU.mult)
                    OP("dve", "tensor_tensor", reads=[bKr, E12_r], writes=[st["aqkT_r"]], out=st["aqkT"],
                       in0=bK[:, 0:128], in1=E12[:, 0:128], op=ALU.mult)
                steps.append(s4)

                def s5():
                    bT, bTr = nb()
                    loc["bT"] = (bT, bTr)
                    TR(bT[:, 0:128], an0[:, 0:128], ident_f[:], reads=[an0_r, const_res], writes=[bTr])
                    OP("pool", "tensor_tensor", reads=[an0_r, const_res], writes=[pm0_r], out=pm0,
                       in0=an0[:, 0:128], in1=ident_f[:], op=ALU.add)
                steps.append(s5)

                def s6():
                    bT, bTr = loc["bT"]
                    OP("act", "activation", reads=[bTr], writes=[an0_r], out=an0[:, 128:256],
                       in_=bT[:, 0:128], func=AF.Copy)
                    OP("dve", "tensor_tensor", reads=[kT_r, egbrow_r], writes=[RwT_r], out=RwT, in0=kT[:, tok],
                       in1=egbrow, op=ALU.mult)
                    OP("dve", "tensor_tensor", reads=[qT_r, eg_r], writes=[st["qd_r"]], out=st["qd"],
                       in0=qT[:, tok], in1=eg, op=ALU.mult)
                    OP("dve", "tensor_scalar", reads=[ktm_r, glk_r], writes=[st["kend_r"]], out=st["kend"],
                       in0=ktm[:, c, :], scalar1=kes, scalar2=None, op0=ALU.mult)
                steps.append(s6)
                state = {"ai": 0, "pi": 0}
                for p in (1, 2, 4, 8, 16, 32, 64):
                    def lv_mm(p=p):
                        an, an_r = AN[state["ai"]]
                        anr = an.bitcast(F32R) if USE_F32R else an
                        if p > 1:
                            pc, pc_r = Pm[state["pi"]]
                            pcr = pc.bitcast(F32R) if USE_F32R else pc
                            bP, bPr = nb()
                            loc["bP"] = (bP, bPr)
                            MM(bP[:, 0:128], anr[:, 128:256], pcr, True, True, reads=[an_r, pc_r], writes=[bPr])
                        if p < 64:
                            bX, bXr = nb()
                            loc["bX"] = (bX, bXr)
                            MM(bX[:, 0:128], anr[:, 128:256], anr[:, 0:128], True, True, reads=[an_r], writes=[bXr])
                            MM(bX[:, 128:256], anr[:, 0:128], anr[:, 128:256], True, True, reads=[an_r],
                               writes=[bXr])

                    def lv_ev(p=p):
                        if p < 64:
                            an2, an2_r = AN[1 - state["ai"]]
                            bX, bXr = loc["bX"]
                            OP("act", "activation", reads=[bXr], writes=[an2_r], out=an2, in_=bX[:, 0:256],
                               func=AF.Copy)
                            state["ai"] = 1 - state["ai"]
                        if p > 1:
                            pc, pc_r = Pm[state["pi"]]
                            pn, pn_r = Pm[1 - state["pi"]]
                            bP, bPr = loc["bP"]
                            OP("dve", "tensor_tensor", reads=[bPr, pc_r], writes=[pn_r], out=pn,
                               in0=bP[:, 0:128], in1=pc, op=ALU.add)
                            state["pi"] = 1 - state["pi"]
                    steps.append(lv_mm)
                    steps.append(lv_ev)
                return steps, state, Pm

            def S_phase(c, fin, h=h):
                st = gsets[c % NSET]
                tok = slice(c * 128, (c + 1) * 128)
                state, Pm = fin
                pf, pf_r = Pm[state["pi"]]
                bS, bSr = nb()
                MM(bS[:, 0:128], st["RwT"], S, True, True, reads=[st["RwT_r"], S_r], writes=[bSr])
                OP("dve", "tensor_tensor", reads=[st["Rv_r"], bSr], writes=[dd_r], out=dd, in0=st["Rv"],
                   in1=bS[:, 0:128], op=ALU.subtract)
                bV, bVr = nb()
                MM(bV[:, 0:128], pf, dd, True, True, reads=[pf_r, dd_r], writes=[bVr])
                OP("act", "activation", reads=[bVr], writes=[vnew_r], out=vnew, in_=bV[:, 0:128], func=AF.Copy)
                bD, bDr = nb()
                MM(bD[:, 0:128], st["kend"], vnew, True, True, reads=[st["kend_r"], vnew_r], writes=[bDr])
                bO, bOr = nb()
                MM(bO[:, 0:128], S_bf, st["qd"], True, False, reads=[Sbf_r, st["qd_r"]], writes=[bOr])
                MM(bO[:, 0:128], vnew, st["aqkT"], False, True, reads=[vnew_r, st["aqkT_r"]], writes=[bOr])
                OP("dve", "scalar_tensor_tensor", reads=[S_r, st["eg_r"], bDr], writes=[S_r], out=S, in0=S,
                   scalar=st["eg"][:, 127:128], in1=bD[:, 0:128], op0=ALU.mult, op1=ALU.add)
                OP("pool", "tensor_copy", reads=[S_r], writes=[Sbf_r], out=S_bf, in_=S)
                OP("act", "activation", reads=[bOr], writes=[oT_res[h]], out=oT[:, h, tok], in_=bO[:, 0:128],
                   func=AF.Copy)

            for c0 in range(0, NT, NSET):
                cs_ = list(range(c0, min(NT, c0 + NSET)))
                built = [T_steps(c) for c in cs_]
                nst = max(len(b[0]) for b in built)
                for i in range(nst):
                    for b in built:
                        if i < len(b[0]):
                            b[0][i]()
                for c, b in zip(cs_, built):
                    S_phase(c, (b[1], b[2]))

            def ev_z(b, br, t0, n, h=h):
                OP("act", "activation", reads=[br], writes=[zgt_r], out=zgt[:, 0:n], in_=b, func=AF.Silu)
                OP("act", "activation", reads=[oT_res[h]], writes=[sqb_r], out=sqb[:, 0:n],
                   in_=oT[:, h, t0:t0 + n], func=AF.Square)
                b2, b2r = nb()
                MM(b2[:, 0:n], ones_b[:], sqb[:, 0:n], True, True, reads=[sqb_r, const_res], writes=[b2r])
                OP("act", "activation", reads=[b2r], writes=[rn_r], out=rn[:, 0:n], in_=b2[:, 0:n],
                   func=AF.Sqrt, scale=1.0 / 128, bias=1e-6)
                OP("dve", "reciprocal", reads=[rn_r], writes=[rn_r], out=rn[:, 0:n], in_=rn[:, 0:n])
                OP("dve", "scalar_tensor_tensor", reads=[oT_res[h], gng_r, rn_r], writes=[oT_res[h]],
                   out=oT[:, h, t0:t0 + n], in0=oT[:, h, t0:t0 + n], scalar=gng[:, 0:1], in1=rn[:, 0:n],
                   op0=ALU.mult, op1=ALU.mult)
                OP("dve", "tensor_tensor", reads=[oT_res[h], zgt_r], writes=[oT_res[h]],
                   out=oT[:, h, t0:t0 + n], in0=oT[:, h, t0:t0 + n], in1=zgt[:, 0:n], op=ALU.mult)
            proj_fm(layer, O_GZ + h * 128, ev_z)


    def unit_ssd(layer, g):
        P.barrier()
        carve.reset()
        dtr, dtr_r = carve.f32(NT * 32, "dtr")
        dt_, dt_r = carve.f32(NT * 32, "dt")
        cs, cs_r = carve.f32(NT * 32, "cs")
        ncs, ncs_r = carve.f32(NT * 32, "ncs")
        dt3 = dt_.rearrange("p (t c) -> p t c", c=32)
        dtr3 = dtr.rearrange("p (t c) -> p t c", c=32)
        cs3 = cs.rearrange("p (t c) -> p t c", c=32)
        ncs3 = ncs.rearrange("p (t c) -> p t c", c=32)
        alog, _ = carve.f32(32, "alog")
        dtb, _ = carve.f32(32, "dtb")
        dcol, _ = carve.f32(16, "dcol")
        ng, _ = carve.f32(16, "ng")
        cbt, cbt_r = carve.f32(128, "cbt")
        cw, cw_r = carve.f32(80, "cw")
        cw3 = cw.rearrange("p (j b) -> p j b", b=20)
        cbias, cbias_r = carve.f32(20, "cbias")
        acc, acc_r = carve.f32(512, "acc")
        cwt, cwt_r = acc, acc_r
        load_small(alog, bc_rows(ssm_a_log_d[layer:layer + 1, :], 32), 0)
        load_small(dtb, bc_rows(ssm_dt_bias_d[layer:layer + 1, :], 32), 1)
        dv = ssm_d_d[layer].rearrange("(b s) -> s b", s=2)
        P.dma("sp", dcol[0:64, :], dv[0:1, :].to_broadcast([64, 16]), prm_res[2], writes=[prm_res[2]], slow=True)
        P.dma("sp", dcol[64:128, :], dv[1:2, :].to_broadcast([64, 16]), prm_res[2], writes=[prm_res[2]],
              slow=True)
        load_small(ng, ssm_norm_g_d[layer].rearrange("(b p) -> p b", p=128), 4)
        alog_r, dtb_r, dcol_r, ng_r = prm_res[0], prm_res[1], prm_res[2], prm_res[4]
        load_conv_w(cw, cw_r, ssm_conv_w_d[layer], 20, cwt, cwt_r, 3)
        P.dma("sp", cbt[0:20, :], ssm_conv_b_d[layer].rearrange("(b p) -> b p", p=128), prm_res[5],
              writes=[cbt_r])
        b, br = nb()
        TR(b[:, 0:20], cbt[0:20, :], ident_f[0:20, 0:20], reads=[cbt_r, const_res], writes=[br])
        OP("dve", "tensor_copy", reads=[br], writes=[cbias_r], out=cbias, in_=b[:, 0:20])
        proj_tm(layer, O_SDT, 32, lambda b, br, tt0, cnt: OP(
            "dve", "tensor_copy", reads=[br], writes=[dtr_r], out=dtr[:, tt0 * 32:(tt0 + cnt) * 32], in_=b))
        OP("dve", "tensor_tensor", reads=[dtr_r, dtb_r], writes=[dt_r], out=dt3, in0=dtr3,
           in1=dtb.unsqueeze(1).to_broadcast([128, NT, 32]), op=ALU.add)
        OP("act", "activation", reads=[dt_r], writes=[dt_r], out=dt_, in_=dt_, func=AF.Exp)
        OP("act", "activation", reads=[dt_r], writes=[dt_r], out=dt_, in_=dt_, func=AF.Ln, bias=1.0)
        OP("pool", "memset", reads=[], writes=[dt_r], ap=dt3[0:PAD, 0, :], constant=0.0)
        OP("act", "activation", reads=[alog_r], writes=[alog_r], out=alog, in_=alog, func=AF.Exp)
        OP("dve", "scalar_tensor_tensor", reads=[dt_r, alog_r], writes=[dtr_r], out=dtr3, in0=dt3, scalar=-1.0,
           in1=alog.unsqueeze(1).to_broadcast([128, NT, 32]), op0=ALU.mult, op1=ALU.mult)
        for hf in range(2):
            w0 = hf * 272
            b, br = nb()
            MM(b[:, 0:272], uincl_f[:], dtr[:, w0:w0 + 272], True, True, reads=[dtr_r, const_res], writes=[br])
            OP("dve", "tensor_copy", reads=[br], writes=[cs_r], out=cs[:, w0:w0 + 272], in_=b[:, 0:272])
        OP("dve", "tensor_scalar", reads=[cs_r], writes=[ncs_r], out=ncs, in0=cs, scalar1=-1.0, scalar2=None,
           op0=ALU.mult)

        BT, BT_r = carve.bf16(LP, "BT")
        CT, CT_r = carve.bf16(LP, "CT")
        xT, xT_r = carve.bf16(LP, "xT")
        Btm_f, Btm_r = carve.bf16(NT * 128, "Btm")
        Btm = Btm_f.rearrange("p (t d) -> p t d", d=128)
        raw, raw_r = carve.f32(515, "raw")
        zgt, zgt_r = carve.bf16(512, "zgt")
        sqs = [carve.bf16(512, "sq") for _ in range(2)]
        rn, rn_r = carve.f32(512, "rn")
        NSET = 3
        sets = []
        for _i in range(NSET):
            st = {}
            st["dg"], st["dg_r"] = carve.f32(256, "dg")
            st["E12"], st["E12_r"] = carve.f32(256, "E12")
            st["eg"], st["eg_r"] = carve.f32(256, "eg")
            st["aqkT"], st["aqkT_r"] = carve.bf16(256, "aqkT")
            st["xs2"], st["xs2_r"] = carve.bf16(256, "xs2")
            st["qd"], st["qd_r"] = carve.bf16(256, "qd")
            st["kend"], st["kend_r"] = carve.bf16(256, "kend")
            st["glk"], st["glk_r"] = carve.f32(4, "glk")
            for nm in ("aqkT", "xs2", "qd", "kend", "E12", "eg"):
                st[nm + "3"] = st[nm].rearrange("p (s l) -> p s l", l=128)
            sets.append(st)
        S, S_r = carve.f32(128, "S")
        S2, S2_r = carve.bf16(256, "S2")
        S23 = S2.rearrange("p (s l) -> p s l", l=128)

        def to_tm(srcT, src_r, c0, n, dst3, dst_r, tt0):
            b2, b2r = nb()
            v = bfv(b2)
            cnt = n // 128
            for j in range(cnt):
                TR(v[:, j * 128:(j + 1) * 128], srcT[:, c0 + j * 128:c0 + (j + 1) * 128], ident_b[:],
                   reads=[src_r, const_res], writes=[b2r])
            OP("dve", "tensor_copy", reads=[b2r], writes=[dst_r], out=dst3[:, tt0:tt0 + cnt, :],
               in_=v[:, 0:n].rearrange("p (t d) -> p t d", d=128))

        OP("pool", "memset", writes=[raw_r], ap=raw[:, 0:3], constant=0.0)

        def ev_B(b, br, t0, n):
            conv_silu(b, br, n, raw, raw_r, acc, acc_r, cw3[:, :, 16 + g], cw_r, cbias[:, 16 + g:17 + g],
                      BT[:, t0:t0 + n], BT_r)
            to_tm(BT, BT_r, t0, n, Btm, Btm_r, t0 // 128)
        proj_fm(layer, O_SB + g * 128, ev_B)
        OP("pool", "memset", writes=[raw_r], ap=raw[:, 0:3], constant=0.0)
        proj_fm(layer, O_SC + g * 128, lambda b, br, t0, n: conv_silu(
            b, br, n, raw, raw_r, acc, acc_r, cw3[:, :, 18 + g], cw_r, cbias[:, 18 + g:19 + g],
            CT[:, t0:t0 + n], CT_r))
        for st in sets:
            OP("pool", "memset", writes=[st["xs2_r"]], ap=st["xs2"], constant=0.0)
        OP("pool", "memset", writes=[S2_r], ap=S2, constant=0.0)
        for j in range(8):
            blk = g * 8 + j
            hh = [2 * blk, 2 * blk + 1]
            OP("pool", "memset", writes=[raw_r], ap=raw[:, 0:3], constant=0.0)
            proj_fm(layer, O_SX + blk * 128, lambda b, br, t0, n, blk=blk: conv_silu(
                b, br, n, raw, raw_r, acc, acc_r, cw3[:, :, blk], cw_r, cbias[:, blk:blk + 1],
                xT[:, t0:t0 + n], xT_r))
            OP("pool", "memset", writes=[S_r], ap=S, constant=0.0)
            for s2 in range(2):
                OP("pool", "memset", writes=[S2_r], ap=S23[:, s2, s2 * 64:(s2 + 1) * 64], constant=0.0)
            def prep(c, hh=hh):
                st = sets[c % NSET]
                dg, dg_r, E12_r, eg, eg_r = st["dg"], st["dg_r"], st["E12_r"], st["eg"], st["eg_r"]
                E3, eg3, aqk3, xs3, qd3, kend3 = st["E123"], st["eg3"], st["aqkT3"], st["xs23"], st["qd3"], st["kend3"]
                glk, glk_r = st["glk"], st["glk_r"]
                tok = slice(c * 128, (c + 1) * 128)
                for s2 in range(2):
                    OP("dve", "tensor_scalar", reads=[cs_r, const_res], writes=[dg_r],
                       out=dg[:, s2 * 128:(s2 + 1) * 128], in0=ident_f[:], scalar1=cs3[:, c, hh[s2]:hh[s2] + 1],
                       scalar2=None, op0=ALU.mult)
                bG, bGr = nb()
                MM(bG[:, 0:256], ones_f[:], dg, True, True, reads=[dg_r, const_res], writes=[bGr])
                bK, bKr = nb()
                MM(bK[:, 0:128], BT[:, tok], CT[:, tok], True, True, reads=[BT_r, CT_r], writes=[bKr])
                bT, bTr = nb()
                TR(bfv(bT)[:, 0:128], xT[:, tok], ident_b[:], reads=[xT_r, const_res], writes=[bTr])
                for s2 in range(2):
                    OP("act", "activation", reads=[bGr, ncs_r], writes=[E12_r], out=E3[:, s2, :],
                       in_=bG[:, s2 * 128:(s2 + 1) * 128], func=AF.Exp, bias=ncs3[:, c, hh[s2]:hh[s2] + 1])
                OP("act", "activation", reads=[bGr], writes=[eg_r], out=eg, in_=bG[:, 0:256], func=AF.Exp)
                OP("dve", "tensor_copy", reads=[bGr], writes=[glk_r], out=glk[:, 0:2],
                   in_=bG[:, 0:256].rearrange("p (s l) -> p s l", l=128)[:, :, 127])
                OP("pool", "affine_select", reads=[E12_r], writes=[E12_r], out=E3, in_=E3,
                   pattern=[[0, 2], [1, 128]], compare_op=ALU.is_ge, fill=0.0, base=0, channel_multiplier=-1)
                for s2 in range(2):
                    OP("act", "activation", reads=[cs_r, glk_r], writes=[glk_r], out=glk[:, 2 + s2:3 + s2],
                       in_=cs3[:, c, hh[s2]:hh[s2] + 1], func=AF.Exp, scale=-1.0, bias=glk[:, s2:s2 + 1])
                for s2 in range(2):
                    OP("dve", "tensor_scalar", reads=[bTr, dt_r], writes=[st["xs2_r"]],
                       out=xs3[:, s2, s2 * 64:(s2 + 1) * 64], in0=bfv(bT)[:, s2 * 64:(s2 + 1) * 64],
                       scalar1=dt3[:, c, hh[s2]:hh[s2] + 1], scalar2=None, op0=ALU.mult)
                OP("dve", "tensor_tensor", reads=[bKr, E12_r], writes=[st["aqkT_r"]], out=aqk3,
                   in0=bK[:, 0:128].unsqueeze(1).to_broadcast([128, 2, 128]), in1=E3, op=ALU.mult)
                OP("dve", "tensor_tensor", reads=[CT_r, eg_r], writes=[st["qd_r"]], out=qd3,
                   in0=CT[:, tok].unsqueeze(1).to_broadcast([128, 2, 128]), in1=eg3, op=ALU.mult)
                for s2 in range(2):
                    OP("dve", "tensor_scalar", reads=[Btm_r, glk_r], writes=[st["kend_r"]], out=kend3[:, s2, :],
                       in0=Btm[:, c, :], scalar1=glk[:, 2 + s2:3 + s2], scalar2=None, op0=ALU.mult)

            def scan(c, j=j):
                st = sets[c % NSET]
                eg_r, eg3, aqk3, xs3, qd3, kend3 = st["eg_r"], st["eg3"], st["aqkT3"], st["xs23"], st["qd3"], st["kend3"]
                tok = slice(c * 128, (c + 1) * 128)
                bD, bDr = nb()
                for s2 in range(2):
                    MM(bD[:, s2 * 64:(s2 + 1) * 64], kend3[:, s2, :], xs3[:, s2, s2 * 64:(s2 + 1) * 64], True, True,
                       reads=[st["kend_r"], st["xs2_r"]], writes=[bDr])
                bO, bOr = nb()
                for s2 in range(2):
                    MM(bO[:, 0:128], S23[:, s2, :], qd3[:, s2, :], s2 == 0, False, reads=[S2_r, st["qd_r"]],
                       writes=[bOr])
                for s2 in range(2):
                    MM(bO[:, 0:128], xs3[:, s2, :], aqk3[:, s2, :], False, s2 == 1,
                       reads=[st["xs2_r"], st["aqkT_r"]], writes=[bOr])
                for s2 in range(2):
                    cols = slice(s2 * 64, (s2 + 1) * 64)
                    OP("dve", "scalar_tensor_tensor", reads=[S_r, eg_r, bDr], writes=[S_r], out=S[:, cols],
                       in0=S[:, cols], scalar=eg3[:, s2, 127:128], in1=bD[:, cols], op0=ALU.mult, op1=ALU.add)
                for s2 in range(2):
                    cols = slice(s2 * 64, (s2 + 1) * 64)
                    OP("pool", "tensor_copy", reads=[S_r], writes=[S2_r], out=S23[:, s2, cols], in_=S[:, cols])
                OP("act", "activation", reads=[bOr], writes=[oT_res[j]], out=oT[:, j, tok], in_=bO[:, 0:128],
                   func=AF.Copy)

            prep(0)
            prep(1)
            for c in range(NT):
                scan(c)
                if c + 2 < NT:
                    prep(c + 2)

            def ev_z(b, br, t0, n, j=j, blk=blk):
                OP("act", "activation", reads=[br], writes=[zgt_r], out=zgt[:, 0:n], in_=b, func=AF.Silu)
                OP("dve", "scalar_tensor_tensor", reads=[xT_r, dcol_r, oT_res[j]], writes=[oT_res[j]],
                   out=oT[:, j, t0:t0 + n], in0=xT[:, t0:t0 + n], scalar=dcol[:, blk:blk + 1],
                   in1=oT[:, j, t0:t0 + n], op0=ALU.mult, op1=ALU.add)
                OP("dve", "tensor_tensor", reads=[oT_res[j], zgt_r], writes=[oT_res[j]],
                   out=oT[:, j, t0:t0 + n], in0=oT[:, j, t0:t0 + n], in1=zgt[:, 0:n], op=ALU.mult)
            proj_fm(layer, O_SZ + blk * 128, ev_z)
        for (t0, n) in TG:
            bq, bqr = nb()
            for j in range(8):
                sq, sq_r = sqs[j % 2]
                OP("dve", "tensor_tensor", reads=[oT_res[j]], writes=[sq_r], out=sq[:, 0:n],
                   in0=oT[:, j, t0:t0 + n], in1=oT[:, j, t0:t0 + n], op=ALU.mult)
                MM(bq[:, 0:n], ones_b[:], sq[:, 0:n], j == 0, j == 7, reads=[sq_r, const_res], writes=[bqr])
            OP("act", "activation", reads=[bqr], writes=[rn_r], out=rn[:, 0:n], in_=bq[:, 0:n], func=AF.Sqrt,
               scale=1.0 / 1024, bias=1e-6)
            OP("dve", "reciprocal", reads=[rn_r], writes=[rn_r], out=rn[:, 0:n], in_=rn[:, 0:n])
            for j in range(8):
                blk = g * 8 + j
                OP("dve", "scalar_tensor_tensor", reads=[oT_res[j], ng_r, rn_r], writes=[oT_res[j]],
                   out=oT[:, j, t0:t0 + n], in0=oT[:, j, t0:t0 + n], scalar=ng[:, blk:blk + 1], in1=rn[:, 0:n],
                   op0=ALU.mult, op1=ALU.mult)

    def final_out():
        P.barrier()
        carve.reset()
        fg, fg_r = carve.f32(1024, "fg")
        junk, junk_r = carve.f32(1024, "junk")
        outs = [carve.f32(1024, "ot") for _ in range(2)]
        load_small(fg, fin_g_d[0:1, :].to_broadcast([128, 1024]), 0, slow=True)
        for tt in range(1, NT):
            ot, ot_r = outs[tt % 2]
            osr = out_res[tt % 2]
            ss = small[:, 0:1]
            rs = small[:, 1:2]
            OP("act", "activation", reads=[h_res[tt]], writes=[junk_r, small_res], out=junk,
               in_=h_sb[:, tt, :], func=AF.Square, accum_out=ss)
            OP("act", "activation", reads=[small_res], writes=[small_res], out=rs, in_=ss,
               func=AF.Sqrt, scale=1.0 / D, bias=1e-6)
            OP("dve", "reciprocal", reads=[small_res], writes=[small_res], out=rs, in_=rs)
            OP("dve", "scalar_tensor_tensor", reads=[h_res[tt], small_res, prm_res[0], osr], writes=[ot_r],
               out=ot, in0=h_sb[:, tt, :], scalar=rs, in1=fg, op0=ALU.mult, op1=ALU.mult)
            P.dma("sp", out_d[(tt - 1) * 128:tt * 128, :], ot, osr, reads=[ot_r], writes=[osr])

    def dump_oT():
        P.barrier()
        carve.reset()
        t32, t32_r = carve.f32(LP, "dump")
        for k in range(8):
            OP("dve", "tensor_copy", reads=oT_res, writes=[t32_r], out=t32, in_=oT[:, k, :])
            P.dma("sp", dbg_d[k * 128:(k + 1) * 128, :], t32, dbg_res, reads=[t32_r])

    def dump_h():
        P.barrier()
        for tt in range(NT):
            P.dma("sp", dbg_d[tt * 128:(tt + 1) * 128, :], h_sb[:, tt, :], dbg_res, reads=[h_res[tt]])

    done = False
    for layer in range(depth):
        P.new_phase()
        layer_norm_T(layer)
        if "A" not in skip:
            unit_attention(layer)
        if stop_after == ("A", layer):
            dump_oT()
            done = True
            break
        if "A" not in skip:
            unit_merge(layer, 0, 0, O_GATE)
        if stop_after == ("Am", layer):
            dump_h()
            done = True
            break
        if "B" not in skip:
            unit_gdn(layer)
        if stop_after == ("B", layer):
            dump_oT()
            done = True
            break
        if "B" not in skip:
            unit_merge(layer, 1, 0, O_GATE + 1024)
        if stop_after == ("Bm", layer):
            dump_h()
            done = True
            break
        for g in range(2):
            if "C" in skip:
                break
            unit_ssd(layer, g)
            if stop_after == ("C%d" % g, layer):
                dump_oT()
                done = True
                break
            unit_merge(layer, 2, g * 1024, O_GATE + 2048)
        if done:
            break
        OP("pool", "memset", reads=[], writes=[h_res[0]], ap=h_sb[0:PAD, 0, :], constant=0.0)
        if stop_after == ("L", layer):
            dump_h()
            done = True
            break
    if not done:
        final_out()

    P.emit()
    import os as _os
    if _os.environ.get("KDEBUG"):
        print("ops", len(P.ops), "sems", P.nsem, "counts", P.counts)
    es.close()
    return nc


_NAMES = ["meta_tokens", "norm_g", "w_in", "gdn_conv_w", "gdn_a_log", "gdn_dt_bias", "gdn_norm_g",
          "ssm_conv_w", "ssm_conv_b", "ssm_a_log", "ssm_dt_bias", "ssm_d", "ssm_norm_g",
          "w_branch_a", "w_branch_b", "w_branch_c", "w_out"]


def make_in_maps(inputs, ncores=8):
    shared = {n: np.ascontiguousarray(np.asarray(inputs[n], dtype=np.float32)) for n in _NAMES}
    shared["final_norm_g"] = np.ascontiguousarray(
        np.asarray(inputs["final_norm_g"], dtype=np.float32).reshape(1, D))
    x = np.asarray(inputs["x"], dtype=np.float32)
    maps = []
    for c in range(ncores):
        m = dict(shared)
        m["x"] = np.ascontiguousarray(x[c])
        maps.append(m)
    return maps


def kernel(**inputs):
    nc = build()
    in_maps = make_in_maps(inputs)
    res = run_bass_kernel_spmd(nc, in_maps, core_ids=list(range(8)))
    return np.stack([r["out"] for r in res.results], axis=0)
```

```python
from contextlib import ExitStack
import numpy as np
import concourse.bass as bass
import concourse.mybir as mybir
from concourse.bass_utils import run_bass_kernel_spmd

F32 = mybir.dt.float32
BF16 = mybir.dt.bfloat16
AF = mybir.ActivationFunctionType
ALU = mybir.AluOpType

D = 1024
SEQ = 2048
NMETA = 16
PAD = 112
LP = SEQ + NMETA + PAD
NT = LP // 128
DIN = 15920
DEPTH = 4
TG = [(0, 512), (512, 512), (1024, 512), (1536, 512), (2048, 128)]

O_SBQ, O_SBK, O_SBV, O_SBZ = 0, 1024, 2048, 3072
O_GQ, O_GK, O_GV, O_GZ, O_GB, O_GA = 4096, 5120, 6144, 7168, 8192, 8200
O_SZ, O_SX, O_SB, O_SC, O_SDT, O_GATE = 8208, 10256, 12304, 12560, 12816, 12848


class Res:
    __slots__ = ("name", "last_w", "readers", "sem", "dcount", "excl")

    def __init__(self, name):
        self.name = name
        self.excl = False
        self.last_w = None
        self.readers = []
        self.sem = None
        self.dcount = 0


class Op:
    __slots__ = ("eng", "fn", "deps", "is_dma", "sem", "val", "signal", "phase", "pe_mm")


class Prog:
    ENGS = ("pe", "act", "dve", "pool", "sp")

    def __init__(self, nc, es):
        self.nc = nc
        self.es = es
        self.ops = []
        self.phase = 0
        self.bar_deps = []
        self.last_on_eng = {}
        self.dmas_since_bar = []
        self.phase_sems = {}
        self.nsem = 0

    def new_sem(self, name):
        self.nsem += 1
        return self.es.enter_context(self.nc.semaphore(name))

    def res(self, name, dma=False):
        r = Res(name)
        if dma:
            r.sem = self.new_sem("d_" + name)
        return r

    def new_phase(self):
        self.phase += 1

    def _track(self, op, reads, writes):
        ex = [r for r in reads if r.excl and r not in writes]
        if ex:
            reads = [r for r in reads if not r.excl]
            writes = list(writes) + ex
        deps = set(self.bar_deps)
        for r in list(reads) + list(writes):
            if r.last_w is not None:
                deps.add(r.last_w)
        for w in writes:
            for rd in w.readers:
                deps.add(rd)
        idx = len(self.ops)
        deps.discard(idx)
        op.deps = deps
        for r in reads:
            r.readers.append(idx)
        for w in writes:
            w.last_w = idx
            w.readers = []
        self.ops.append(op)
        self.last_on_eng[op.eng] = idx
        return idx

    def op(self, eng, fn, reads=(), writes=(), mm=False):
        o = Op()
        o.eng = eng
        o.fn = fn
        o.is_dma = False
        o.sem = None
        o.val = 0
        o.signal = False
        o.phase = self.phase
        o.pe_mm = mm
        return self._track(o, reads, writes)

    def dma(self, eng, out, in_, semres, reads=(), writes=(), slow=False):
        o = Op()
        o.eng = eng
        if slow:
            o.fn = lambda e: e.dma_start(out=out, in_=in_, allow_slow_non_contiguous=True)
        else:
            o.fn = lambda e: e.dma_start(out=out, in_=in_)
        o.is_dma = True
        o.sem = semres.sem
        semres.dcount += 1
        o.val = 16 * semres.dcount
        o.signal = True
        o.phase = self.phase
        o.pe_mm = False
        idx = self._track(o, reads, writes)
        self.dmas_since_bar.append(idx)
        return idx

    def barrier(self):
        self.bar_deps = list(self.last_on_eng.values()) + list(self.dmas_since_bar)
        self.dmas_since_bar = []

    def emit(self):
        nc = self.nc
        ops = self.ops
        for i, o in enumerate(ops):
            for j in o.deps:
                p = ops[j]
                if p.is_dma:
                    continue
                if p.eng == "pe" and o.eng == "pe":
                    continue
                p.signal = True
        cnt = {}
        for o in ops:
            if o.is_dma or not o.signal:
                continue
            key = (o.eng, o.phase)
            if key not in self.phase_sems:
                self.phase_sems[key] = self.new_sem("s_%s_%d" % key)
                cnt[key] = 0
            cnt[key] += 1
            o.sem = self.phase_sems[key]
            o.val = cnt[key]
        self.counts = dict(cnt)
        per_eng = {e: [] for e in self.ENGS}
        for i, o in enumerate(ops):
            per_eng[o.eng].append(i)
        final_waits = {}
        for o in ops:
            if o.is_dma:
                final_waits[id(o.sem)] = (o.sem, max(o.val, final_waits.get(id(o.sem), (None, 0))[1]))

        def run(engname, e):
            known = {}
            for i in per_eng[engname]:
                o = ops[i]
                need = {}
                for j in o.deps:
                    p = ops[j]
                    if p.sem is None:
                        continue
                    k = id(p.sem)
                    if k not in need or need[k][1] < p.val:
                        need[k] = (p.sem, p.val)
                for k, (s, v) in need.items():
                    if known.get(k, 0) < v:
                        e.wait_ge(s, v)
                        known[k] = v
                ins = o.fn(e)
                if o.is_dma:
                    ins.then_inc(o.sem, 16)
                elif o.signal:
                    ins.then_inc(o.sem, 1)
            if engname == "sp":
                for k, (s, v) in final_waits.items():
                    e.wait_ge(s, v)

        with nc.Block() as block:
            @block.tensor
            def _(e):
                run("pe", e)

            @block.scalar
            def _(e):
                run("act", e)

            @block.vector
            def _(e):
                run("dve", e)

            @block.gpsimd
            def _(e):
                run("pool", e)

            @block.sync
            def _(e):
                run("sp", e)


class Carver:
    def __init__(self, prog, ap, nwords):
        self.prog = prog
        self.ap = ap
        self.n = nwords
        self.off = 0
        self.k = 0

    def reset(self):
        self.off = 0

    def f32(self, cols, name="w"):
        a = self.ap[:, self.off:self.off + cols]
        self.off += cols
        assert self.off <= self.n, ("work overflow", self.off, self.n)
        self.k += 1
        return a, self.prog.res("%s%d" % (name, self.k))

    def bf16(self, cols, name="w"):
        words = (cols + 1) // 2
        a = self.ap[:, self.off:self.off + words].bitcast(BF16)
        self.off += words
        assert self.off <= self.n, ("work overflow", self.off, self.n)
        self.k += 1
        return a[:, 0:cols], self.prog.res("%s%d" % (name, self.k))


def build(depth=DEPTH, dbg=None, stop_after=None, GSTOP=0, skip=(), USE_F32R=False):
    F32R = mybir.dt.float32r
    nc = bass.Bass("TRN2", target_bir_lowering=False)
    es = ExitStack()
    P = Prog(nc, es)

    def din(name, shape):
        return nc.dram_tensor(name, list(shape), F32, kind="ExternalInput").ap()

    x_d = din("x", (SEQ, D))
    meta_d = din("meta_tokens", (NMETA, D))
    norm_g_d = din("norm_g", (DEPTH, D))
    w_in_d = din("w_in", (DEPTH, D, DIN))
    gdn_conv_w_d = din("gdn_conv_w", (DEPTH, 4, 3072))
    gdn_a_log_d = din("gdn_a_log", (DEPTH, 8))
    gdn_dt_bias_d = din("gdn_dt_bias", (DEPTH, 8))
    gdn_norm_g_d = din("gdn_norm_g", (DEPTH, 128))
    ssm_conv_w_d = din("ssm_conv_w", (DEPTH, 4, 2560))
    ssm_conv_b_d = din("ssm_conv_b", (DEPTH, 2560))
    ssm_a_log_d = din("ssm_a_log", (DEPTH, 32))
    ssm_dt_bias_d = din("ssm_dt_bias", (DEPTH, 32))
    ssm_d_d = din("ssm_d", (DEPTH, 32))
    ssm_norm_g_d = din("ssm_norm_g", (DEPTH, 2048))
    w_br_d = [din("w_branch_a", (DEPTH, 1024, D)), din("w_branch_b", (DEPTH, 1024, D)),
              din("w_branch_c", (DEPTH, 2048, D))]
    w_out_d = din("w_out", (DEPTH, D, D))
    fin_g_d = din("final_norm_g", (1, D))
    out_d = nc.dram_tensor("out", [SEQ, D], F32, kind="ExternalOutput").ap()
    dbg_d = None
    if dbg is not None:
        dbg_d = nc.dram_tensor("dbg", list(dbg["shape"]), F32, kind="ExternalOutput").ap()

    def sb(name, shape, dt):
        return es.enter_context(nc.sbuf_tensor(name, list(shape), dt))

    def ps(name, shape, dt=F32):
        return es.enter_context(nc.psum_tensor(name, list(shape), dt))

    def OP(eng, meth, reads=(), writes=(), **kw):
        return P.op(eng, lambda e: getattr(e, meth)(**kw), reads=reads, writes=writes)

    def MM(out, lhsT, rhs, start, stop, reads, writes):
        return P.op("pe", lambda e: e.matmul(out=out, lhsT=lhsT, rhs=rhs, start=start, stop=stop),
                    reads=reads, writes=writes, mm=True)

    def TR(out, in_, ident, reads, writes):
        return P.op("pe", lambda e: e.transpose(out=out, in_=in_, identity=ident),
                    reads=reads, writes=writes, mm=True)

    h_sb = sb("h", (128, NT, D), F32)
    uT = sb("uT", (128, 8, LP), BF16)
    oT = sb("oT", (128, 8, LP), BF16)
    stage = [sb("stg%d" % i, (128, 8, 128), F32) for i in range(2)]
    wslab = [sb("wsl%d" % i, (128, 8, 128), BF16) for i in range(2)]
    WORKW = 13312
    work = sb("work", (128, WORKW), F32)
    ident_f = sb("ident_f", (128, 128), F32)
    ident_b = sb("ident_b", (128, 128), BF16)
    ones_b = sb("ones_b", (128, 128), BF16)
    ones_f = sb("ones_f", (128, 128), F32)
    uincl_f = sb("uincl_f", (128, 128), F32)
    tmat = sb("tmat", (128, 4, 128), BF16)
    mw = sb("mw", (128, 896), BF16)
    small = sb("small", (128, 64), F32)
    gfm = sb("gfm", (128, 8), F32)

    h_res = [P.res("h%d" % t) for t in range(NT)]
    hload_res = [P.res("hl%d" % i, dma=True) for i in range(5)]
    uT_res = [P.res("uT%d" % t) for t in range(NT)]
    oT_res = [P.res("oT%d" % k) for k in range(8)]
    stage_res = [P.res("stg%d" % i, dma=True) for i in range(2)]
    wslab_res = [P.res("wsl%d" % i) for i in range(2)]
    const_res = P.res("const")
    gfm_res = P.res("gfm", dma=True)
    small_res = P.res("small")
    dbg_res = P.res("dbg", dma=True)
    prm_res = [P.res("prm%d" % i, dma=True) for i in range(8)]
    out_res = [P.res("outst%d" % i, dma=True) for i in range(2)]

    banks = [ps("bank%d" % i, (128, 512), F32) for i in range(8)]
    bank_res = [P.res("bank%d" % i) for i in range(8)]
    for _r in bank_res:
        _r.excl = True
    carve = Carver(P, work, WORKW)
    bctr = [0]

    def nb(nmax=8):
        i = bctr[0] % nmax
        bctr[0] += 1
        return banks[i], bank_res[i]

    def bfv(bank):
        return bank[:].bitcast(BF16)

    cr = [const_res]
    OP("pool", "memset", writes=cr, ap=ident_f[:], constant=1.0)
    OP("pool", "affine_select", writes=cr, out=ident_f[:], in_=ident_f[:], pattern=[[-1, 128]],
       compare_op=ALU.is_equal, fill=0.0, base=0, channel_multiplier=1)
    OP("pool", "tensor_copy", reads=cr, writes=cr, out=ident_b[:], in_=ident_f[:])
    OP("pool", "memset", writes=cr, ap=ones_b[:], constant=1.0)
    OP("pool", "memset", writes=cr, ap=ones_f[:], constant=1.0)
    OP("pool", "memset", writes=cr, ap=uincl_f[:], constant=1.0)
    OP("pool", "affine_select", writes=cr, out=uincl_f[:], in_=uincl_f[:], pattern=[[1, 128]],
       compare_op=ALU.is_ge, fill=0.0, base=0, channel_multiplier=-1)
    OP("pool", "memset", writes=cr, ap=tmat[:], constant=-1.0)
    for v in (0, 2):
        OP("pool", "affine_select", writes=cr, out=tmat[:, v, :], in_=tmat[:, v, :], pattern=[[-1, 128]],
           compare_op=ALU.is_gt, fill=0.0, base=0, channel_multiplier=1)
    for v in (1, 3):
        OP("pool", "affine_select", writes=cr, out=tmat[:, v, :], in_=tmat[:, v, :], pattern=[[1, 128]],
           compare_op=ALU.is_ge, fill=0.0, base=0, channel_multiplier=-1)
    OP("pool", "memset", writes=cr, ap=tmat[0:PAD, 2:4, :], constant=0.0)
    OP("pool", "memset", writes=cr, ap=mw[:], constant=1.0)
    OP("pool", "affine_select", writes=cr, out=mw[:], in_=mw[:], pattern=[[1, 896]],
       compare_op=ALU.is_gt, fill=0.0, base=-384, channel_multiplier=-1)

    OP("pool", "memset", writes=[h_res[0]], ap=h_sb[:, 0, :], constant=0.0)
    P.dma("sp", h_sb[PAD:128, 0, :], meta_d[:, :], hload_res[4], writes=[h_res[0]])
    xv = x_d.rearrange("(t p) d -> p t d", p=128)
    for i in range(4):
        P.dma("sp", h_sb[:, 1 + 4 * i:5 + 4 * i, :], xv[:, 4 * i:4 * i + 4, :], hload_res[i],
              writes=[h_res[1 + 4 * i + j] for j in range(4)])

    wctr = [0]

    def stream_slab(src, C, dst=None, dst_res=None):
        i = wctr[0] % 2
        wctr[0] += 1
        P.dma("sp", stage[i][:, :, 0:C], src, stage_res[i], writes=[stage_res[i]])
        if dst is None:
            dst, dst_res = wslab[i][:, :, 0:C], wslab_res[i]
        OP("pool", "tensor_copy", reads=[stage_res[i]], writes=[dst_res], out=dst,
           in_=stage[i][:, :, 0:C])
        return dst, dst_res

    def win(layer, c0, C):
        return w_in_d[layer].rearrange("(k p) c -> p k c", p=128)[:, :, c0:c0 + C]

    def tiles_of(t0, n):
        return uT_res[t0 // 128:(t0 + n + 127) // 128]

    def proj_fm(layer, c0, evac, ncol=128):
        w, wr = stream_slab(win(layer, c0, ncol), ncol)
        for (t0, n) in TG:
            b, br = nb()
            for k in range(8):
                MM(b[0:ncol, 0:n], w[:, k, :], uT[:, k, t0:t0 + n], k == 0, k == 7,
                   reads=[wr] + tiles_of(t0, n), writes=[br])
            evac(b[0:ncol, 0:n], br, t0, n)

    def proj_tm(layer, c0, ncol, evac):
        w, wr = stream_slab(win(layer, c0, ncol), ncol)
        per = min(4, 512 // ncol)
        for tt0 in range(0, NT, per):
            cnt = min(per, NT - tt0)
            b, br = nb()
            for j in range(cnt):
                tt = tt0 + j
                for k in range(8):
                    MM(b[:, j * ncol:(j + 1) * ncol], uT[:, k, tt * 128:(tt + 1) * 128], w[:, k, :],
                       k == 0, k == 7, reads=[wr, uT_res[tt]], writes=[br])
            evac(b[:, 0:cnt * ncol], br, tt0, cnt)

    def load_small(dst, src, ri, slow=True):
        P.dma("sp", dst, src, prm_res[ri], writes=[prm_res[ri]], slow=slow)

    def layer_norm_T(layer):
        P.barrier()
        carve.reset()
        P.dma("sp", gfm[:, :], norm_g_d[layer].rearrange("(k p) -> p k", p=128), gfm_res,
              writes=[gfm_res], slow=True)
        junk, junk_r = carve.f32(1024, "junk")
        uns = [carve.bf16(1024, "un") for _ in range(2)]
        for tt in range(NT):
            un, un_r = uns[tt % 2]
            ss = small[:, 0:1]
            rs = small[:, 1:2]
            OP("act", "activation", reads=[h_res[tt]], writes=[junk_r, small_res], out=junk,
               in_=h_sb[:, tt, :], func=AF.Square, accum_out=ss)
            OP("act", "activation", reads=[small_res], writes=[small_res], out=rs, in_=ss,
               func=AF.Sqrt, scale=1.0 / D, bias=1e-6)
            OP("dve", "reciprocal", reads=[small_res], writes=[small_res], out=rs, in_=rs)
            OP("dve", "tensor_scalar", reads=[h_res[tt], small_res], writes=[un_r], out=un,
               in0=h_sb[:, tt, :], scalar1=rs, scalar2=None, op0=ALU.mult)
            b, br = nb()
            pT = bfv(b)
            for k in range(8):
                TR(pT[:, k * 128:(k + 1) * 128], un[:, k * 128:(k + 1) * 128], ident_b[:],
                   reads=[un_r, const_res], writes=[br])
            OP("dve", "tensor_tensor", reads=[br, gfm_res], writes=[uT_res[tt]],
               out=uT[:, :, tt * 128:(tt + 1) * 128], in0=pT.rearrange("p (k t) -> p k t", t=128),
               in1=gfm[:, :].unsqueeze(2).to_broadcast([128, 8, 128]), op=ALU.mult)

    def unit_attention(layer):
        P.barrier()
        carve.reset()
        qT, qT_r = carve.bf16(LP, "qT")
        kT, kT_r = carve.bf16(LP, "kT")
        zg, zg_r = carve.bf16(LP, "zg")
        vtm_flat, vtm_r = carve.bf16(NT * 128, "vtm")
        vtm = vtm_flat.rearrange("p (t d) -> p t d", d=128)
        NBUF = 3
        AHEAD = 2
        spf = [carve.f32(512, "spf") for _ in range(NBUF)]
        spb = [carve.bf16(512, "spb") for _ in range(NBUF)]
        t2 = [carve.f32(512, "t2") for _ in range(NBUF)]
        wb = [carve.bf16(512, "wb") for _ in range(NBUF)]
        scale = 128.0 ** -0.5
        gctr = [0]
        for h in range(8):
            proj_fm(layer, O_SBQ + h * 128, lambda b, br, t0, n: OP(
                "act", "activation", reads=[br], writes=[qT_r], out=qT[:, t0:t0 + n], in_=b,
                func=AF.Copy, scale=scale))
            proj_fm(layer, O_SBK + h * 128, lambda b, br, t0, n: OP(
                "dve", "tensor_copy", reads=[br], writes=[kT_r], out=kT[:, t0:t0 + n], in_=b))
            proj_fm(layer, O_SBZ + h * 128, lambda b, br, t0, n: OP(
                "act", "activation", reads=[br], writes=[zg_r], out=zg[:, t0:t0 + n], in_=b,
                func=AF.Silu))
            proj_tm(layer, O_SBV + h * 128, 128, lambda b, br, tt0, cnt: OP(
                "dve", "tensor_copy", reads=[br], writes=[vtm_r], out=vtm[:, tt0:tt0 + cnt, :],
                in_=b.rearrange("p (t d) -> p t d", d=128)))
            its = []
            for g in range(5):
                qb0 = 4 * g
                nq = 512 if g < 4 else 128
                kmax = qb0 + nq // 128 - 1
                gi = gctr[0] % 2
                gctr[0] += 1
                for idx, kb in enumerate(range(kmax, -1, -1)):
                    its.append(dict(qb0=qb0, nq=nq, q0=qb0 * 128, idx=idx, kb=kb, gi=gi, zb=None))

            n_it = len(its)

            def PZ(j):
                d = its[j]
                nq, kb, q0 = d["nq"], d["kb"], d["q0"]
                zb, zr = nb(4)
                d["zb"] = (zb, zr)
                MM(zb[:, 0:nq], kT[:, kb * 128:(kb + 1) * 128], qT[:, q0:q0 + nq], True, True,
                   reads=[kT_r, qT_r], writes=[zr])

            def AE(j):
                d = its[j]
                nq = d["nq"]
                zb, zr = d["zb"]
                sp_a, sp_r = spf[j % NBUF]
                OP("act", "activation", reads=[zr], writes=[sp_r], out=sp_a[:, 0:nq],
                   in_=zb[:, 0:nq], func=AF.Exp)
                OP("act", "activation", reads=[sp_r], writes=[sp_r], out=sp_a[:, 0:nq],
                   in_=sp_a[:, 0:nq], func=AF.Ln, bias=1.0)

            def DS(j):
                d = its[j]
                nq, kb, qb0 = d["nq"], d["kb"], d["qb0"]
                zb, zr = d["zb"]
                sp_a, sp_r = spf[j % NBUF]
                spb_a, spb_r = spb[j % NBUF]
                t2_a, t2_r = t2[j % NBUF]
                if kb >= qb0:
                    moff = 384 - 128 * (kb - qb0)
                    OP("dve", "tensor_tensor", reads=[sp_r, const_res], writes=[spb_r],
                       out=spb_a[:, 0:nq], in0=sp_a[:, 0:nq], in1=mw[:, moff:moff + nq],
                       op=ALU.mult)
                else:
                    OP("dve", "tensor_copy", reads=[sp_r], writes=[spb_r], out=spb_a[:, 0:nq],
                       in_=sp_a[:, 0:nq])
                OP("dve", "tensor_tensor", reads=[zr, sp_r], writes=[t2_r], out=t2_a[:, 0:nq],
                   in0=zb[:, 0:nq], in1=sp_a[:, 0:nq], op=ALU.subtract)

            def banksof(d):
                gi = d["gi"]
                return banks[4 + gi], bank_res[4 + gi], banks[6 + gi], bank_res[6 + gi]

            def PT(j):
                d = its[j]
                nq, kb, idx = d["nq"], d["kb"], d["idx"]
                A_b, A_r, O_b, O_r = banksof(d)
                spb_a, spb_r = spb[j % NBUF]
                tv = 2 if kb == 0 else 0
                MM(A_b[:, 0:nq], tmat[:, tv, :], spb_a[:, 0:nq], idx == 0, False,
                   reads=[spb_r, const_res], writes=[A_r])

            def DA(j):
                d = its[j]
                nq = d["nq"]
                A_b, A_r, O_b, O_r = banksof(d)
                t2_a, t2_r = t2[j % NBUF]
                OP("dve", "tensor_tensor", reads=[A_r, t2_r], writes=[t2_r], out=t2_a[:, 0:nq],
                   in0=A_b[:, 0:nq], in1=t2_a[:, 0:nq], op=ALU.add)

            def AW(j):
                d = its[j]
                nq, kb, qb0 = d["nq"], d["kb"], d["qb0"]
                t2_a, t2_r = t2[j % NBUF]
                wb_a, wb_r = wb[j % NBUF]
                OP("act", "activation", reads=[t2_r], writes=[wb_r], out=wb_a[:, 0:nq],
                   in_=t2_a[:, 0:nq], func=AF.Exp)
                if kb >= qb0:
                    moff = 384 - 128 * (kb - qb0)
                    OP("pool", "tensor_tensor", reads=[wb_r, const_res], writes=[wb_r],
                       out=wb_a[:, 0:nq], in0=wb_a[:, 0:nq], in1=mw[:, moff:moff + nq],
                       op=ALU.mult)

            def PL(j):
                d = its[j]
                nq, kb = d["nq"], d["kb"]
                A_b, A_r, O_b, O_r = banksof(d)
                spb_a, spb_r = spb[j % NBUF]
                tv = 2 if kb == 0 else 0
                MM(A_b[:, 0:nq], tmat[:, tv + 1, :], spb_a[:, 0:nq], False, kb == 0,
                   reads=[spb_r, const_res], writes=[A_r])

            def PO(j, h=h):
                d = its[j]
                nq, kb, idx, q0 = d["nq"], d["kb"], d["idx"], d["q0"]
                A_b, A_r, O_b, O_r = banksof(d)
                wb_a, wb_r = wb[j % NBUF]
                MM(O_b[:, 0:nq], vtm[:, kb, :], wb_a[:, 0:nq], idx == 0, kb == 0,
                   reads=[vtm_r, wb_r], writes=[O_r])
                if kb == 0:
                    OP("dve", "tensor_tensor", reads=[O_r, zg_r], writes=[oT_res[h]],
                       out=oT[:, h, q0:q0 + nq], in0=O_b[:, 0:nq], in1=zg[:, q0:q0 + nq], op=ALU.mult)

            PZ(0)
            AE(0)
            PZ(1)
            AE(1)
            DS(0)
            for j in range(n_it):
                PT(j)
                if j + 2 < n_it:
                    PZ(j + 2)
                DA(j)
                if j + 1 < n_it:
                    DS(j + 1)
                AW(j)
                if j + 2 < n_it:
                    AE(j + 2)
                PL(j)
                if j >= 1:
                    PO(j - 1)
            PO(n_it - 1)

    def unit_merge(layer, br_idx, row0, gate_c0):
        P.barrier()
        carve.reset()
        wbr_f, wbr_r = carve.bf16(8 * 1024, "wbr")
        wg_f, wg_r = carve.bf16(8 * 1024, "wg")
        wbr = wbr_f.rearrange("p (k c) -> p k c", c=1024)
        wg = wg_f.rearrange("p (k c) -> p k c", c=1024)
        sgs = [carve.bf16(512, "sg") for _ in range(2)]
        mTs = [carve.bf16(8 * 512, "mT") for _ in range(2)]
        src_br = w_br_d[br_idx][layer].rearrange("(k p) c -> p k c", p=128)
        for cb in range(8):
            stream_slab(src_br[:, row0 // 128:row0 // 128 + 8, cb * 128:(cb + 1) * 128], 128,
                        dst=wbr[:, :, cb * 128:(cb + 1) * 128], dst_res=wbr_r)
            stream_slab(win(layer, gate_c0 + cb * 128, 128), 128,
                        dst=wg[:, :, cb * 128:(cb + 1) * 128], dst_res=wg_r)
        for gi_, (t0, n) in enumerate(TG):
            tok = slice(t0, t0 + n)
            mT_f, mT_r = mTs[gi_ % 2]
            mT = mT_f.rearrange("p (c t) -> p c t", t=512)
            for cb in range(8):
                pbk, pbr = nb()
                for fk in range(8):
                    MM(pbk[:, 0:n], wbr[:, fk, cb * 128:(cb + 1) * 128], oT[:, fk, tok], fk == 0, fk == 7,
                       reads=[wbr_r, oT_res[fk]], writes=[pbr])
                gbk, gbr = nb()
                for k in range(8):
                    MM(gbk[:, 0:n], wg[:, k, cb * 128:(cb + 1) * 128], uT[:, k, tok], k == 0, k == 7,
                       reads=[wg_r] + tiles_of(t0, n), writes=[gbr])
                sg, sg_r = sgs[cb % 2]
                OP("act", "activation", reads=[gbr], writes=[sg_r], out=sg[:, 0:n], in_=gbk[:, 0:n],
                   func=AF.Sigmoid)
                OP("dve", "tensor_tensor", reads=[pbr, sg_r], writes=[mT_r], out=mT[:, cb, 0:n],
                   in0=pbk[:, 0:n], in1=sg[:, 0:n], op=ALU.mult)
            OP("pool", "tensor_copy", reads=[mT_r], writes=list(oT_res), out=oT[:, :, tok], in_=mT[:, :, 0:n])
        P.barrier()
        carve.reset()
        wo_f, wo_r = carve.bf16(8 * 1024, "wo")
        wo = wo_f.rearrange("p (k c) -> p k c", c=1024)
        src_o = w_out_d[layer].rearrange("(k p) c -> p k c", p=128)
        for ds in range(8):
            stream_slab(src_o[:, :, ds * 128:(ds + 1) * 128], 128, dst=wo[:, :, ds * 128:(ds + 1) * 128],
                        dst_res=wo_r)
        for tt in range(NT):
            for hf in range(2):
                b, br = nb()
                for cb in range(8):
                    MM(b[:, 0:512], oT[:, cb, tt * 128:(tt + 1) * 128], wo[:, cb, hf * 512:(hf + 1) * 512],
                       cb == 0, cb == 7, reads=[wo_r, oT_res[cb]], writes=[br])
                OP("dve", "tensor_tensor", reads=[br, h_res[tt]], writes=[h_res[tt]],
                   out=h_sb[:, tt, hf * 512:(hf + 1) * 512], in0=b[:, 0:512],
                   in1=h_sb[:, tt, hf * 512:(hf + 1) * 512], op=ALU.add)

    def bc_rows(src_row_ap, n):
        return src_row_ap.to_broadcast([128, n])

    def conv_silu(b, br, n, raw, raw_r, acc, acc_r, cw4, prm_r, bias, out_ap, out_res):
        OP("act", "activation", reads=[br], writes=[raw_r], out=raw[:, 3:3 + n], in_=b, func=AF.Copy)
        if bias is None:
            OP("dve", "tensor_scalar", reads=[raw_r, prm_r], writes=[acc_r], out=acc[:, 0:n],
               in0=raw[:, 3:3 + n], scalar1=cw4[:, 3:4], scalar2=None, op0=ALU.mult)
        else:
            OP("dve", "tensor_scalar", reads=[raw_r, prm_r], writes=[acc_r], out=acc[:, 0:n],
               in0=raw[:, 3:3 + n], scalar1=cw4[:, 3:4], scalar2=bias, op0=ALU.mult, op1=ALU.add)
        for j in (2, 1, 0):
            OP("dve", "scalar_tensor_tensor", reads=[raw_r, prm_r, acc_r], writes=[acc_r],
               out=acc[:, 0:n], in0=raw[:, j:j + n], scalar=cw4[:, j:j + 1], in1=acc[:, 0:n],
               op0=ALU.mult, op1=ALU.add)
        OP("act", "activation", reads=[acc_r], writes=[out_res], out=out_ap, in_=acc[:, 0:n],
           func=AF.Silu)
        OP("pool", "tensor_copy", reads=[raw_r], writes=[raw_r], out=raw[:, 0:3], in_=raw[:, n:n + 3])

    def load_conv_w(dst, dst_r, src2d, nblk, tmp, tmp_r, ri):
        P.dma("sp", tmp[0:nblk, :], src2d.rearrange("j (b p) -> b j p", p=128), prm_res[ri],
              writes=[tmp_r])
        b, br = nb()
        for j in range(4):
            TR(b[:, j * nblk:(j + 1) * nblk], tmp[0:nblk, j * 128:(j + 1) * 128], ident_f[0:nblk, 0:nblk],
               reads=[tmp_r, const_res], writes=[br])
        OP("dve", "tensor_copy", reads=[br], writes=[dst_r], out=dst, in_=b[:, 0:4 * nblk])

    def unit_gdn(layer):
        P.barrier()
        carve.reset()
        ba, ba_r = carve.f32(NT * 16, "ba")
        ba3 = ba.rearrange("p (t c) -> p t c", c=16)

        def sm(name):
            a, r = carve.f32(NT * 8, name)
            return a, a.rearrange("p (t c) -> p t c", c=8), r
        xa, xa3, xa_r = sm("xa")
        lb, lb3, lb_r = sm("lb")
        g_all, g3, g_r = sm("g")
        gc, gc3, gc_r = sm("gc")
        ngc, ngc3, ngc_r = sm("ngc")
        gb_, gb3, gb_r = sm("gb")
        egb, egb3, egb_r = sm("egb")
        beta, beta3, beta_r = sm("beta")
        alog, alog_r = carve.f32(8, "alog")
        dtb, dtb_r = carve.f32(8, "dtb")
        gng, gng_r = carve.f32(1, "gng")
        cw, cw_r = carve.f32(96, "cw")
        cw3 = cw.rearrange("p (j b) -> p j b", b=24)
        qs, qs_r = carve.f32(512, "qs")
        cwt, cwt_r = qs, qs_r
        load_small(alog, bc_rows(gdn_a_log_d[layer:layer + 1, :], 8), 0)
        load_small(dtb, bc_rows(gdn_dt_bias_d[layer:layer + 1, :], 8), 1)
        load_small(gng, gdn_norm_g_d[layer].rearrange("(p o) -> p o", o=1), 2)
        alog_r, dtb_r, gng_r = prm_res[0], prm_res[1], prm_res[2]
        load_conv_w(cw, cw_r, gdn_conv_w_d[layer], 24, cwt, cwt_r, 3)
        if GSTOP == 1:
            return
        proj_tm(layer, O_GB, 16, lambda b, br, tt0, cnt: OP(
            "dve", "tensor_copy", reads=[br], writes=[ba_r], out=ba[:, tt0 * 16:(tt0 + cnt) * 16], in_=b))
        OP("dve", "tensor_tensor", reads=[ba_r, dtb_r], writes=[xa_r], out=xa3, in0=ba3[:, :, 8:16],
           in1=dtb.unsqueeze(1).to_broadcast([128, NT, 8]), op=ALU.add)
        OP("act", "activation", reads=[xa_r], writes=[xa_r], out=xa, in_=xa, func=AF.Exp)
        OP("act", "activation", reads=[xa_r], writes=[xa_r], out=xa, in_=xa, func=AF.Ln, bias=1.0)
        OP("act", "activation", reads=[alog_r], writes=[alog_r], out=alog, in_=alog, func=AF.Exp)
        OP("dve", "scalar_tensor_tensor", reads=[xa_r, alog_r], writes=[g_r], out=g3, in0=xa3, scalar=-1.0,
           in1=alog.unsqueeze(1).to_broadcast([128, NT, 8]), op0=ALU.mult, op1=ALU.mult)
        OP("pool", "memset", reads=[], writes=[g_r], ap=g3[0:PAD, 0, :], constant=0.0)
        OP("act", "activation", reads=[ba_r], writes=[lb_r], out=lb3, in_=ba3[:, :, 0:8], func=AF.Exp,
           scale=-1.0)
        OP("act", "activation", reads=[lb_r], writes=[lb_r], out=lb, in_=lb, func=AF.Ln, bias=1.0)
        OP("act", "activation", reads=[lb_r], writes=[beta_r], out=beta, in_=lb, func=AF.Exp, scale=-1.0)
        b, br = nb()
        MM(b[:, 0:NT * 8], uincl_f[:], g_all, True, True, reads=[g_r, const_res], writes=[br])
        OP("dve", "tensor_copy", reads=[br], writes=[gc_r], out=gc, in_=b[:, 0:NT * 8])
        OP("dve", "tensor_scalar", reads=[gc_r], writes=[ngc_r], out=ngc, in0=gc, scalar1=-1.0,
           scalar2=None, op0=ALU.mult)
        OP("dve", "tensor_tensor", reads=[gc_r, lb_r], writes=[gb_r], out=gb_, in0=gc, in1=lb,
           op=ALU.subtract)
        OP("act", "activation", reads=[gb_r], writes=[egb_r], out=egb, in_=gb_, func=AF.Exp)
        if GSTOP == 2:
            return
        qT, qT_r = carve.bf16(LP, "qT")
        kT, kT_r = carve.bf16(LP, "kT")
        ktm_f, ktm_r = carve.bf16(NT * 128, "ktm")
        vtm_f, vtm_r = carve.bf16(NT * 128, "vtm")
        ktm = ktm_f.rearrange("p (t d) -> p t d", d=128)
        vtm = vtm_f.rearrange("p (t d) -> p t d", d=128)
        raw, raw_r = carve.f32(515, "raw")
        acc, acc_r = carve.f32(512, "acc")
        sqb, sqb_r = carve.bf16(512, "sqb")
        rn, rn_r = carve.f32(512, "rn")
        dd, dd_r = carve.f32(128, "dd")
        vTt, vTt_r = carve.bf16(512, "vTt")
        zgt, zgt_r = carve.bf16(512, "zgt")
        NSET = 3
        gsets = []
        for _i in range(NSET):
            st = {}
            for nm, kind, n in (("dg", "f", 256), ("eg", "f", 128),
                                ("RwT", "f", 128), ("AN0", "f", 256), ("Pm0", "f", 128),
                                ("aqkT", "b", 128), ("Rv", "f", 128), ("qd", "b", 128),
                                ("kend", "b", 128), ("glk", "f", 2)):
                st[nm], st[nm + "_r"] = (carve.f32 if kind == "f" else carve.bf16)(n, nm)
            st["E12"], st["E12_r"] = st["dg"], st["dg_r"]
            st["egbrow"], st["egbrow_r"] = st["RwT"], st["RwT_r"]
            st["AN1"], st["AN1_r"] = st["AN0"], st["AN0_r"]
            st["Pm1"], st["Pm1_r"] = st["Pm0"], st["Pm0_r"]
            gsets.append(st)
        vnew, vnew_r = carve.bf16(128, "vnew")
        S, S_r = carve.f32(128, "S")
        S_bf, Sbf_r = carve.bf16(128, "Sbf")
        import os as _os2
        if _os2.environ.get("KDEBUG"):
            print("GDN carve off", carve.off, "of", WORKW)

        def l2norm_to(dst, dst_r, t0, n, scl):
            OP("act", "activation", reads=[qs_r], writes=[sqb_r], out=sqb[:, 0:n], in_=qs[:, 0:n],
               func=AF.Square)
            b2, b2r = nb()
            MM(b2[:, 0:n], ones_b[:], sqb[:, 0:n], True, True, reads=[sqb_r, const_res], writes=[b2r])
            OP("act", "activation", reads=[b2r], writes=[rn_r], out=rn[:, 0:n], in_=b2[:, 0:n],
               func=AF.Sqrt, bias=1e-6)
            OP("dve", "reciprocal", reads=[rn_r], writes=[rn_r], out=rn[:, 0:n], in_=rn[:, 0:n])
            OP("dve", "scalar_tensor_tensor", reads=[qs_r, rn_r], writes=[dst_r], out=dst[:, t0:t0 + n],
               in0=qs[:, 0:n], scalar=scl, in1=rn[:, 0:n], op0=ALU.mult, op1=ALU.mult)

        def to_tm(srcT, src_r, c0, n, dst3, dst_r, tt0):
            b2, b2r = nb()
            v = bfv(b2)
            cnt = n // 128
            for j in range(cnt):
                TR(v[:, j * 128:(j + 1) * 128], srcT[:, c0 + j * 128:c0 + (j + 1) * 128], ident_b[:],
                   reads=[src_r, const_res], writes=[b2r])
            OP("dve", "tensor_copy", reads=[b2r], writes=[dst_r], out=dst3[:, tt0:tt0 + cnt, :],
               in_=v[:, 0:n].rearrange("p (t d) -> p t d", d=128))

        for h in range(8):
            OP("pool", "memset", writes=[raw_r], ap=raw[:, 0:3], constant=0.0)

            def ev_q(b, br, t0, n, h=h):
                conv_silu(b, br, n, raw, raw_r, acc, acc_r, cw3[:, :, h], cw_r, None, qs[:, 0:n], qs_r)
                l2norm_to(qT, qT_r, t0, n, 128.0 ** -0.5)
            proj_fm(layer, O_GQ + h * 128, ev_q)
            OP("pool", "memset", writes=[raw_r], ap=raw[:, 0:3], constant=0.0)

            def ev_k(b, br, t0, n, h=h):
                conv_silu(b, br, n, raw, raw_r, acc, acc_r, cw3[:, :, 8 + h], cw_r, None, qs[:, 0:n], qs_r)
                l2norm_to(kT, kT_r, t0, n, 1.0)
                to_tm(kT, kT_r, t0, n, ktm, ktm_r, t0 // 128)
            proj_fm(layer, O_GK + h * 128, ev_k)
            OP("pool", "memset", writes=[raw_r], ap=raw[:, 0:3], constant=0.0)

            def ev_v(b, br, t0, n, h=h):
                conv_silu(b, br, n, raw, raw_r, acc, acc_r, cw3[:, :, 16 + h], cw_r, None, vTt[:, 0:n], vTt_r)
                to_tm(vTt, vTt_r, 0, n, vtm, vtm_r, t0 // 128)
            proj_fm(layer, O_GV + h * 128, ev_v)
            OP("pool", "memset", writes=[S_r], ap=S, constant=0.0)
            OP("pool", "memset", writes=[Sbf_r], ap=S_bf, constant=0.0)
            if GSTOP == 3:
                return
            def T_steps(c, h=h):
                st = gsets[c % NSET]
                tok = slice(c * 128, (c + 1) * 128)
                gcol = gc3[:, c, h:h + 1]
                ngcol = ngc3[:, c, h:h + 1]
                gbcol = gb3[:, c, h:h + 1]
                dg, dg_r, E12, E12_r, eg, eg_r = st["dg"], st["dg_r"], st["E12"], st["E12_r"], st["eg"], st["eg_r"]
                egbrow, egbrow_r, RwT, RwT_r = st["egbrow"], st["egbrow_r"], st["RwT"], st["RwT_r"]
                glk, glk_r = st["glk"], st["glk_r"]
                gl, kes = glk[:, 0:1], glk[:, 1:2]
                AN = [(st["AN0"], st["AN0_r"]), (st["AN1"], st["AN1_r"])]
                Pm = [(st["Pm0"], st["Pm0_r"]), (st["Pm1"], st["Pm1_r"])]
                an0, an0_r = AN[0]
                pm0, pm0_r = Pm[0]
                loc = {}
                steps = []

                def s0():
                    OP("dve", "tensor_scalar", reads=[gc_r, const_res], writes=[dg_r], out=dg[:, 0:128],
                       in0=ident_f[:], scalar1=gcol, scalar2=None, op0=ALU.mult)
                    OP("dve", "tensor_scalar", reads=[gb_r, const_res], writes=[dg_r], out=dg[:, 128:256],
                       in0=ident_f[:], scalar1=gbcol, scalar2=None, op0=ALU.mult)
                    bK, bKr = nb()
                    loc["bK"] = (bK, bKr)
                    MM(bK[:, 0:128], kT[:, tok], qT[:, tok], True, True, reads=[kT_r, qT_r], writes=[bKr])
                    MM(bK[:, 128:256], kT[:, tok], kT[:, tok], True, True, reads=[kT_r], writes=[bKr])
                    OP("dve", "tensor_scalar", reads=[vtm_r, beta_r], writes=[st["Rv_r"]], out=st["Rv"],
                       in0=vtm[:, c, :], scalar1=beta3[:, c, h:h + 1], scalar2=None, op0=ALU.mult)
                steps.append(s0)

                def s1():
                    bG, bGr = nb()
                    loc["bG"] = (bG, bGr)
                    MM(bG[:, 0:256], ones_f[:], dg, True, True, reads=[dg_r, const_res], writes=[bGr])
                steps.append(s1)

                def s2():
                    bG, bGr = loc["bG"]
                    OP("act", "activation", reads=[bGr, ngc_r], writes=[E12_r], out=E12, in_=bG[:, 0:256],
                       func=AF.Exp, bias=ngcol)
                    OP("dve", "tensor_copy", reads=[bGr], writes=[glk_r], out=gl, in_=bG[:, 127:128])
                    OP("act", "activation", reads=[bGr], writes=[eg_r], out=eg, in_=bG[:, 0:128], func=AF.Exp)
                    OP("act", "activation", reads=[bGr], writes=[egbrow_r], out=egbrow, in_=bG[:, 128:256],
                       func=AF.Exp)
                steps.append(s2)

                def s3():
                    OP("pool", "affine_select", reads=[E12_r], writes=[E12_r], out=E12[:, 0:128],
                       in_=E12[:, 0:128], pattern=[[1, 128]], compare_op=ALU.is_ge, fill=0.0, base=0,
                       channel_multiplier=-1)
                    OP("pool", "affine_select", reads=[E12_r], writes=[E12_r], out=E12[:, 128:256],
                       in_=E12[:, 128:256], pattern=[[1, 128]], compare_op=ALU.is_gt, fill=0.0, base=0,
                       channel_multiplier=-1)
                    OP("act", "activation", reads=[gc_r, glk_r], writes=[glk_r], out=kes, in_=gcol,
                       func=AF.Exp, scale=-1.0, bias=gl)
                steps.append(s3)

                def s4():
                    bK, bKr = loc["bK"]
                    OP("dve", "scalar_tensor_tensor", reads=[bKr, E12_r], writes=[an0_r], out=an0[:, 0:128],
                       in0=bK[:, 128:256], scalar=-1.0, in1=E12[:, 128:256], op0=ALU.mult, op1=ALU.mult)
                    OP("dve", "tensor_tensor", reads=[bKr, E12_r], writes=[st["aqkT_r"]], out=st["aqkT"],
                       in0=bK[:, 0:128], in1=E12[:, 0:128], op=ALU.mult)
                steps.append(s4)

                def s5():
                    bT, bTr = nb()
                    loc["bT"] = (bT, bTr)
                    TR(bT[:, 0:128], an0[:, 0:128], ident_f[:], reads=[an0_r, const_res], writes=[bTr])
                    OP("pool", "tensor_tensor", reads=[an0_r, const_res], writes=[pm0_r], out=pm0,
                       in0=an0[:, 0:128], in1=ident_f[:], op=ALU.add)
                steps.append(s5)

                def s6():
                    bT, bTr = loc["bT"]
                    OP("act", "activation", reads=[bTr], writes=[an0_r], out=an0[:, 128:256],
                       in_=bT[:, 0:128], func=AF.Copy)
                    OP("dve", "tensor_tensor", reads=[kT_r, egbrow_r], writes=[RwT_r], out=RwT, in0=kT[:, tok],
                       in1=egbrow, op=ALU.mult)
                    OP("dve", "tensor_tensor", reads=[qT_r, eg_r], writes=[st["qd_r"]], out=st["qd"],
                       in0=qT[:, tok], in1=eg, op=ALU.mult)
                    OP("dve", "tensor_scalar", reads=[ktm_r, glk_r], writes=[st["kend_r"]], out=st["kend"],
                       in0=ktm[:, c, :], scalar1=kes, scalar2=None, op0=ALU.mult)
                steps.append(s6)
                state = {"ai": 0, "pi": 0}
                for p in (1, 2, 4, 8, 16, 32, 64):
                    def lv_mm(p=p):
                        an, an_r = AN[state["ai"]]
                        anr = an.bitcast(F32R) if USE_F32R else an
                        if p > 1:
                            pc, pc_r = Pm[state["pi"]]
                            pcr = pc.bitcast(F32R) if USE_F32R else pc
                            bP, bPr = nb()
                            loc["bP"] = (bP, bPr)
                            MM(bP[:, 0:128], anr[:, 128:256], pcr, True, True, reads=[an_r, pc_r], writes=[bPr])
                        if p < 64:
                            bX, bXr = nb()
                            loc["bX"] = (bX, bXr)
                            MM(bX[:, 0:128], anr[:, 128:256], anr[:, 0:128], True, True, reads=[an_r], writes=[bXr])
                            MM(bX[:, 128:256], anr[:, 0:128], anr[:, 128:256], True, True, reads=[an_r],
                               writes=[bXr])

                    def lv_ev(p=p):
                        if p < 64:
                            an2, an2_r = AN[1 - state["ai"]]
                            bX, bXr = loc["bX"]
                            OP("act", "activation", reads=[bXr], writes=[an2_r], out=an2, in_=bX[:, 0:256],
                               func=AF.Copy)
                            state["ai"] = 1 - state["ai"]
                        if p > 1:
                            pc, pc_r = Pm[state["pi"]]
                            pn, pn_r = Pm[1 - state["pi"]]
                            bP, bPr = loc["bP"]
                            OP("dve", "tensor_tensor", reads=[bPr, pc_r], writes=[pn_r], out=pn,
                               in0=bP[:, 0:128], in1=pc, op=ALU.add)
                            state["pi"] = 1 - state["pi"]
                    steps.append(lv_mm)
                    steps.append(lv_ev)
                return steps, state, Pm

            def S_phase(c, fin, h=h):
                st = gsets[c % NSET]
                tok = slice(c * 128, (c + 1) * 128)
                state, Pm = fin
                pf, pf_r = Pm[state["pi"]]
                bS, bSr = nb()
                MM(bS[:, 0:128], st["RwT"], S, True, True, reads=[st["RwT_r"], S_r], writes=[bSr])
                OP("dve", "tensor_tensor", reads=[st["Rv_r"], bSr], writes=[dd_r], out=dd, in0=st["Rv"],
                   in1=bS[:, 0:128], op=ALU.subtract)
                bV, bVr = nb()
                MM(bV[:, 0:128], pf, dd, True, True, reads=[pf_r, dd_r], writes=[bVr])
                OP("act", "activation", reads=[bVr], writes=[vnew_r], out=vnew, in_=bV[:, 0:128], func=AF.Copy)
                bD, bDr = nb()
                MM(bD[:, 0:128], st["kend"], vnew, True, True, reads=[st["kend_r"], vnew_r], writes=[bDr])
                bO, bOr = nb()
                MM(bO[:, 0:128], S_bf, st["qd"], True, False, reads=[Sbf_r, st["qd_r"]], writes=[bOr])
                MM(bO[:, 0:128], vnew, st["aqkT"], False, True, reads=[vnew_r, st["aqkT_r"]], writes=[bOr])
                OP("dve", "scalar_tensor_tensor", reads=[S_r, st["eg_r"], bDr], writes=[S_r], out=S, in0=S,
                   scalar=st["eg"][:, 127:128], in1=bD[:, 0:128], op0=ALU.mult, op1=ALU.add)
                OP("pool", "tensor_copy", reads=[S_r], writes=[Sbf_r], out=S_bf, in_=S)
                OP("act", "activation", reads=[bOr], writes=[oT_res[h]], out=oT[:, h, tok], in_=bO[:, 0:128],
                   func=AF.Copy)

            for c0 in range(0, NT, NSET):
                cs_ = list(range(c0, min(NT, c0 + NSET)))
                built = [T_steps(c) for c in cs_]
                nst = max(len(b[0]) for b in built)
                for i in range(nst):
                    for b in built:
                        if i < len(b[0]):
                            b[0][i]()
                for c, b in zip(cs_, built):
                    S_phase(c, (b[1], b[2]))

            def ev_z(b, br, t0, n, h=h):
                OP("act", "activation", reads=[br], writes=[zgt_r], out=zgt[:, 0:n], in_=b, func=AF.Silu)
                OP("act", "activation", reads=[oT_res[h]], writes=[sqb_r], out=sqb[:, 0:n],
                   in_=oT[:, h, t0:t0 + n], func=AF.Square)
                b2, b2r = nb()
                MM(b2[:, 0:n], ones_b[:], sqb[:, 0:n], True, True, reads=[sqb_r, const_res], writes=[b2r])
                OP("act", "activation", reads=[b2r], writes=[rn_r], out=rn[:, 0:n], in_=b2[:, 0:n],
                   func=AF.Sqrt, scale=1.0 / 128, bias=1e-6)
                OP("dve", "reciprocal", reads=[rn_r], writes=[rn_r], out=rn[:, 0:n], in_=rn[:, 0:n])
                OP("dve", "scalar_tensor_tensor", reads=[oT_res[h], gng_r, rn_r], writes=[oT_res[h]],
                   out=oT[:, h, t0:t0 + n], in0=oT[:, h, t0:t0 + n], scalar=gng[:, 0:1], in1=rn[:, 0:n],
                   op0=ALU.mult, op1=ALU.mult)
                OP("dve", "tensor_tensor", reads=[oT_res[h], zgt_r], writes=[oT_res[h]],
                   out=oT[:, h, t0:t0 + n], in0=oT[:, h, t0:t0 + n], in1=zgt[:, 0:n], op=ALU.mult)
            proj_fm(layer, O_GZ + h * 128, ev_z)


    def unit_ssd(layer, g):
        P.barrier()
        carve.reset()
        dtr, dtr_r = carve.f32(NT * 32, "dtr")
        dt_, dt_r = carve.f32(NT * 32, "dt")
        cs, cs_r = carve.f32(NT * 32, "cs")
        ncs, ncs_r = carve.f32(NT * 32, "ncs")
        dt3 = dt_.rearrange("p (t c) -> p t c", c=32)
        dtr3 = dtr.rearrange("p (t c) -> p t c", c=32)
        cs3 = cs.rearrange("p (t c) -> p t c", c=32)
        ncs3 = ncs.rearrange("p (t c) -> p t c", c=32)
        alog, _ = carve.f32(32, "alog")
        dtb, _ = carve.f32(32, "dtb")
        dcol, _ = carve.f32(16, "dcol")
        ng, _ = carve.f32(16, "ng")
        cbt, cbt_r = carve.f32(128, "cbt")
        cw, cw_r = carve.f32(80, "cw")
        cw3 = cw.rearrange("p (j b) -> p j b", b=20)
        cbias, cbias_r = carve.f32(20, "cbias")
        acc, acc_r = carve.f32(512, "acc")
        cwt, cwt_r = acc, acc_r
        load_small(alog, bc_rows(ssm_a_log_d[layer:layer + 1, :], 32), 0)
        load_small(dtb, bc_rows(ssm_dt_bias_d[layer:layer + 1, :], 32), 1)
        dv = ssm_d_d[layer].rearrange("(b s) -> s b", s=2)
        P.dma("sp", dcol[0:64, :], dv[0:1, :].to_broadcast([64, 16]), prm_res[2], writes=[prm_res[2]], slow=True)
        P.dma("sp", dcol[64:128, :], dv[1:2, :].to_broadcast([64, 16]), prm_res[2], writes=[prm_res[2]],
              slow=True)
        load_small(ng, ssm_norm_g_d[layer].rearrange("(b p) -> p b", p=128), 4)
        alog_r, dtb_r, dcol_r, ng_r = prm_res[0], prm_res[1], prm_res[2], prm_res[4]
        load_conv_w(cw, cw_r, ssm_conv_w_d[layer], 20, cwt, cwt_r, 3)
        P.dma("sp", cbt[0:20, :], ssm_conv_b_d[layer].rearrange("(b p) -> b p", p=128), prm_res[5],
              writes=[cbt_r])
        b, br = nb()
        TR(b[:, 0:20], cbt[0:20, :], ident_f[0:20, 0:20], reads=[cbt_r, const_res], writes=[br])
        OP("dve", "tensor_copy", reads=[br], writes=[cbias_r], out=cbias, in_=b[:, 0:20])
        proj_tm(layer, O_SDT, 32, lambda b, br, tt0, cnt: OP(
            "dve", "tensor_copy", reads=[br], writes=[dtr_r], out=dtr[:, tt0 * 32:(tt0 + cnt) * 32], in_=b))
        OP("dve", "tensor_tensor", reads=[dtr_r, dtb_r], writes=[dt_r], out=dt3, in0=dtr3,
           in1=dtb.unsqueeze(1).to_broadcast([128, NT, 32]), op=ALU.add)
        OP("act", "activation", reads=[dt_r], writes=[dt_r], out=dt_, in_=dt_, func=AF.Exp)
        OP("act", "activation", reads=[dt_r], writes=[dt_r], out=dt_, in_=dt_, func=AF.Ln, bias=1.0)
        OP("pool", "memset", reads=[], writes=[dt_r], ap=dt3[0:PAD, 0, :], constant=0.0)
        OP("act", "activation", reads=[alog_r], writes=[alog_r], out=alog, in_=alog, func=AF.Exp)
        OP("dve", "scalar_tensor_tensor", reads=[dt_r, alog_r], writes=[dtr_r], out=dtr3, in0=dt3, scalar=-1.0,
           in1=alog.unsqueeze(1).to_broadcast([128, NT, 32]), op0=ALU.mult, op1=ALU.mult)
        for hf in range(2):
            w0 = hf * 272
            b, br = nb()
            MM(b[:, 0:272], uincl_f[:], dtr[:, w0:w0 + 272], True, True, reads=[dtr_r, const_res], writes=[br])
            OP("dve", "tensor_copy", reads=[br], writes=[cs_r], out=cs[:, w0:w0 + 272], in_=b[:, 0:272])
        OP("dve", "tensor_scalar", reads=[cs_r], writes=[ncs_r], out=ncs, in0=cs, scalar1=-1.0, scalar2=None,
           op0=ALU.mult)

        BT, BT_r = carve.bf16(LP, "BT")
        CT, CT_r = carve.bf16(LP, "CT")
        xT, xT_r = carve.bf16(LP, "xT")
        Btm_f, Btm_r = carve.bf16(NT * 128, "Btm")
        Btm = Btm_f.rearrange("p (t d) -> p t d", d=128)
        raw, raw_r = carve.f32(515, "raw")
        zgt, zgt_r = carve.bf16(512, "zgt")
        sqs = [carve.bf16(512, "sq") for _ in range(2)]
        rn, rn_r = carve.f32(512, "rn")
        NSET = 3
        sets = []
        for _i in range(NSET):
            st = {}
            st["dg"], st["dg_r"] = carve.f32(256, "dg")
            st["E12"], st["E12_r"] = carve.f32(256, "E12")
            st["eg"], st["eg_r"] = carve.f32(256, "eg")
            st["aqkT"], st["aqkT_r"] = carve.bf16(256, "aqkT")
            st["xs2"], st["xs2_r"] = carve.bf16(256, "xs2")
            st["qd"], st["qd_r"] = carve.bf16(256, "qd")
            st["kend"], st["kend_r"] = carve.bf16(256, "kend")
            st["glk"], st["glk_r"] = carve.f32(4, "glk")
            for nm in ("aqkT", "xs2", "qd", "kend", "E12", "eg"):
                st[nm + "3"] = st[nm].rearrange("p (s l) -> p s l", l=128)
            sets.append(st)
        S, S_r = carve.f32(128, "S")
        S2, S2_r = carve.bf16(256, "S2")
        S23 = S2.rearrange("p (s l) -> p s l", l=128)

        def to_tm(srcT, src_r, c0, n, dst3, dst_r, tt0):
            b2, b2r = nb()
            v = bfv(b2)
            cnt = n // 128
            for j in range(cnt):
                TR(v[:, j * 128:(j + 1) * 128], srcT[:, c0 + j * 128:c0 + (j + 1) * 128], ident_b[:],
                   reads=[src_r, const_res], writes=[b2r])
            OP("dve", "tensor_copy", reads=[b2r], writes=[dst_r], out=dst3[:, tt0:tt0 + cnt, :],
               in_=v[:, 0:n].rearrange("p (t d) -> p t d", d=128))

        OP("pool", "memset", writes=[raw_r], ap=raw[:, 0:3], constant=0.0)

        def ev_B(b, br, t0, n):
            conv_silu(b, br, n, raw, raw_r, acc, acc_r, cw3[:, :, 16 + g], cw_r, cbias[:, 16 + g:17 + g],
                      BT[:, t0:t0 + n], BT_r)
            to_tm(BT, BT_r, t0, n, Btm, Btm_r, t0 // 128)
        proj_fm(layer, O_SB + g * 128, ev_B)
        OP("pool", "memset", writes=[raw_r], ap=raw[:, 0:3], constant=0.0)
        proj_fm(layer, O_SC + g * 128, lambda b, br, t0, n: conv_silu(
            b, br, n, raw, raw_r, acc, acc_r, cw3[:, :, 18 + g], cw_r, cbias[:, 18 + g:19 + g],
            CT[:, t0:t0 + n], CT_r))
        for st in sets:
            OP("pool", "memset", writes=[st["xs2_r"]], ap=st["xs2"], constant=0.0)
        OP("pool", "memset", writes=[S2_r], ap=S2, constant=0.0)
        for j in range(8):
            blk = g * 8 + j
            hh = [2 * blk, 2 * blk + 1]
            OP("pool", "memset", writes=[raw_r], ap=raw[:, 0:3], constant=0.0)
            proj_fm(layer, O_SX + blk * 128, lambda b, br, t0, n, blk=blk: conv_silu(
                b, br, n, raw, raw_r, acc, acc_r, cw3[:, :, blk], cw_r, cbias[:, blk:blk + 1],
                xT[:, t0:t0 + n], xT_r))
            OP("pool", "memset", writes=[S_r], ap=S, constant=0.0)
            for s2 in range(2):
                OP("pool", "memset", writes=[S2_r], ap=S23[:, s2, s2 * 64:(s2 + 1) * 64], constant=0.0)
            def prep(c, hh=hh):
                st = sets[c % NSET]
                dg, dg_r, E12_r, eg, eg_r = st["dg"], st["dg_r"], st["E12_r"], st["eg"], st["eg_r"]
                E3, eg3, aqk3, xs3, qd3, kend3 = st["E123"], st["eg3"], st["aqkT3"], st["xs23"], st["qd3"], st["kend3"]
                glk, glk_r = st["glk"], st["glk_r"]
                tok = slice(c * 128, (c + 1) * 128)
                for s2 in range(2):
                    OP("dve", "tensor_scalar", reads=[cs_r, const_res], writes=[dg_r],
                       out=dg[:, s2 * 128:(s2 + 1) * 128], in0=ident_f[:], scalar1=cs3[:, c, hh[s2]:hh[s2] + 1],
                       scalar2=None, op0=ALU.mult)
                bG, bGr = nb()
                MM(bG[:, 0:256], ones_f[:], dg, True, True, reads=[dg_r, const_res], writes=[bGr])
                bK, bKr = nb()
                MM(bK[:, 0:128], BT[:, tok], CT[:, tok], True, True, reads=[BT_r, CT_r], writes=[bKr])
                bT, bTr = nb()
                TR(bfv(bT)[:, 0:128], xT[:, tok], ident_b[:], reads=[xT_r, const_res], writes=[bTr])
                for s2 in range(2):
                    OP("act", "activation", reads=[bGr, ncs_r], writes=[E12_r], out=E3[:, s2, :],
                       in_=bG[:, s2 * 128:(s2 + 1) * 128], func=AF.Exp, bias=ncs3[:, c, hh[s2]:hh[s2] + 1])
                OP("act", "activation", reads=[bGr], writes=[eg_r], out=eg, in_=bG[:, 0:256], func=AF.Exp)
                OP("dve", "tensor_copy", reads=[bGr], writes=[glk_r], out=glk[:, 0:2],
                   in_=bG[:, 0:256].rearrange("p (s l) -> p s l", l=128)[:, :, 127])
                OP("pool", "affine_select", reads=[E12_r], writes=[E12_r], out=E3, in_=E3,
                   pattern=[[0, 2], [1, 128]], compare_op=ALU.is_ge, fill=0.0, base=0, channel_multiplier=-1)
                for s2 in range(2):
                    OP("act", "activation", reads=[cs_r, glk_r], writes=[glk_r], out=glk[:, 2 + s2:3 + s2],
                       in_=cs3[:, c, hh[s2]:hh[s2] + 1], func=AF.Exp, scale=-1.0, bias=glk[:, s2:s2 + 1])
                for s2 in range(2):
                    OP("dve", "tensor_scalar", reads=[bTr, dt_r], writes=[st["xs2_r"]],
                       out=xs3[:, s2, s2 * 64:(s2 + 1) * 64], in0=bfv(bT)[:, s2 * 64:(s2 + 1) * 64],
                       scalar1=dt3[:, c, hh[s2]:hh[s2] + 1], scalar2=None, op0=ALU.mult)
                OP("dve", "tensor_tensor", reads=[bKr, E12_r], writes=[st["aqkT_r"]], out=aqk3,
                   in0=bK[:, 0:128].unsqueeze(1).to_broadcast([128, 2, 128]), in1=E3, op=ALU.mult)
                OP("dve", "tensor_tensor", reads=[CT_r, eg_r], writes=[st["qd_r"]], out=qd3,
                   in0=CT[:, tok].unsqueeze(1).to_broadcast([128, 2, 128]), in1=eg3, op=ALU.mult)
                for s2 in range(2):
                    OP("dve", "tensor_scalar", reads=[Btm_r, glk_r], writes=[st["kend_r"]], out=kend3[:, s2, :],
                       in0=Btm[:, c, :], scalar1=glk[:, 2 + s2:3 + s2], scalar2=None, op0=ALU.mult)

            def scan(c, j=j):
                st = sets[c % NSET]
                eg_r, eg3, aqk3, xs3, qd3, kend3 = st["eg_r"], st["eg3"], st["aqkT3"], st["xs23"], st["qd3"], st["kend3"]
                tok = slice(c * 128, (c + 1) * 128)
                bD, bDr = nb()
                for s2 in range(2):
                    MM(bD[:, s2 * 64:(s2 + 1) * 64], kend3[:, s2, :], xs3[:, s2, s2 * 64:(s2 + 1) * 64], True, True,
                       reads=[st["kend_r"], st["xs2_r"]], writes=[bDr])
                bO, bOr = nb()
                for s2 in range(2):
                    MM(bO[:, 0:128], S23[:, s2, :], qd3[:, s2, :], s2 == 0, False, reads=[S2_r, st["qd_r"]],
                       writes=[bOr])
                for s2 in range(2):
                    MM(bO[:, 0:128], xs3[:, s2, :], aqk3[:, s2, :], False, s2 == 1,
                       reads=[st["xs2_r"], st["aqkT_r"]], writes=[bOr])
                for s2 in range(2):
                    cols = slice(s2 * 64, (s2 + 1) * 64)
                    OP("dve", "scalar_tensor_tensor", reads=[S_r, eg_r, bDr], writes=[S_r], out=S[:, cols],
                       in0=S[:, cols], scalar=eg3[:, s2, 127:128], in1=bD[:, cols], op0=ALU.mult, op1=ALU.add)
                for s2 in range(2):
                    cols = slice(s2 * 64, (s2 + 1) * 64)
                    OP("pool", "tensor_copy", reads=[S_r], writes=[S2_r], out=S23[:, s2, cols], in_=S[:, cols])
                OP("act", "activation", reads=[bOr], writes=[oT_res[j]], out=oT[:, j, tok], in_=bO[:, 0:128],
                   func=AF.Copy)

            prep(0)
            prep(1)
            for c in range(NT):
                scan(c)
                if c + 2 < NT:
                    prep(c + 2)

            def ev_z(b, br, t0, n, j=j, blk=blk):
                OP("act", "activation", reads=[br], writes=[zgt_r], out=zgt[:, 0:n], in_=b, func=AF.Silu)
                OP("dve", "scalar_tensor_tensor", reads=[xT_r, dcol_r, oT_res[j]], writes=[oT_res[j]],
                   out=oT[:, j, t0:t0 + n], in0=xT[:, t0:t0 + n], scalar=dcol[:, blk:blk + 1],
                   in1=oT[:, j, t0:t0 + n], op0=ALU.mult, op1=ALU.add)
                OP("dve", "tensor_tensor", reads=[oT_res[j], zgt_r], writes=[oT_res[j]],
                   out=oT[:, j, t0:t0 + n], in0=oT[:, j, t0:t0 + n], in1=zgt[:, 0:n], op=ALU.mult)
            proj_fm(layer, O_SZ + blk * 128, ev_z)
        for (t0, n) in TG:
            bq, bqr = nb()
            for j in range(8):
                sq, sq_r = sqs[j % 2]
                OP("dve", "tensor_tensor", reads=[oT_res[j]], writes=[sq_r], out=sq[:, 0:n],
                   in0=oT[:, j, t0:t0 + n], in1=oT[:, j, t0:t0 + n], op=ALU.mult)
                MM(bq[:, 0:n], ones_b[:], sq[:, 0:n], j == 0, j == 7, reads=[sq_r, const_res], writes=[bqr])
            OP("act", "activation", reads=[bqr], writes=[rn_r], out=rn[:, 0:n], in_=bq[:, 0:n], func=AF.Sqrt,
               scale=1.0 / 1024, bias=1e-6)
            OP("dve", "reciprocal", reads=[rn_r], writes=[rn_r], out=rn[:, 0:n], in_=rn[:, 0:n])
            for j in range(8):
                blk = g * 8 + j
                OP("dve", "scalar_tensor_tensor", reads=[oT_res[j], ng_r, rn_r], writes=[oT_res[j]],
                   out=oT[:, j, t0:t0 + n], in0=oT[:, j, t0:t0 + n], scalar=ng[:, blk:blk + 1], in1=rn[:, 0:n],
                   op0=ALU.mult, op1=ALU.mult)

    def final_out():
        P.barrier()
        carve.reset()
        fg, fg_r = carve.f32(1024, "fg")
        junk, junk_r = carve.f32(1024, "junk")
        outs = [carve.f32(1024, "ot") for _ in range(2)]
        load_small(fg, fin_g_d[0:1, :].to_broadcast([128, 1024]), 0, slow=True)
        for tt in range(1, NT):
            ot, ot_r = outs[tt % 2]
            osr = out_res[tt % 2]
            ss = small[:, 0:1]
            rs = small[:, 1:2]
            OP("act", "activation", reads=[h_res[tt]], writes=[junk_r, small_res], out=junk,
               in_=h_sb[:, tt, :], func=AF.Square, accum_out=ss)
            OP("act", "activation", reads=[small_res], writes=[small_res], out=rs, in_=ss,
               func=AF.Sqrt, scale=1.0 / D, bias=1e-6)
            OP("dve", "reciprocal", reads=[small_res], writes=[small_res], out=rs, in_=rs)
            OP("dve", "scalar_tensor_tensor", reads=[h_res[tt], small_res, prm_res[0], osr], writes=[ot_r],
               out=ot, in0=h_sb[:, tt, :], scalar=rs, in1=fg, op0=ALU.mult, op1=ALU.mult)
            P.dma("sp", out_d[(tt - 1) * 128:tt * 128, :], ot, osr, reads=[ot_r], writes=[osr])

    def dump_oT():
        P.barrier()
        carve.reset()
        t32, t32_r = carve.f32(LP, "dump")
        for k in range(8):
            OP("dve", "tensor_copy", reads=oT_res, writes=[t32_r], out=t32, in_=oT[:, k, :])
            P.dma("sp", dbg_d[k * 128:(k + 1) * 128, :], t32, dbg_res, reads=[t32_r])

    def dump_h():
        P.barrier()
        for tt in range(NT):
            P.dma("sp", dbg_d[tt * 128:(tt + 1) * 128, :], h_sb[:, tt, :], dbg_res, reads=[h_res[tt]])

    done = False
    for layer in range(depth):
        P.new_phase()
        layer_norm_T(layer)
        if "A" not in skip:
            unit_attention(layer)
        if stop_after == ("A", layer):
            dump_oT()
            done = True
            break
        if "A" not in skip:
            unit_merge(layer, 0, 0, O_GATE)
        if stop_after == ("Am", layer):
            dump_h()
            done = True
            break
        if "B" not in skip:
            unit_gdn(layer)
        if stop_after == ("B", layer):
            dump_oT()
            done = True
            break
        if "B" not in skip:
            unit_merge(layer, 1, 0, O_GATE + 1024)
        if stop_after == ("Bm", layer):
            dump_h()
            done = True
            break
        for g in range(2):
            if "C" in skip:
                break
            unit_ssd(layer, g)
            if stop_after == ("C%d" % g, layer):
                dump_oT()
                done = True
                break
            unit_merge(layer, 2, g * 1024, O_GATE + 2048)
        if done:
            break
        OP("pool", "memset", reads=[], writes=[h_res[0]], ap=h_sb[0:PAD, 0, :], constant=0.0)
        if stop_after == ("L", layer):
            dump_h()
            done = True
            break
    if not done:
        final_out()

    P.emit()
    import os as _os
    if _os.environ.get("KDEBUG"):
        print("ops", len(P.ops), "sems", P.nsem, "counts", P.counts)
    es.close()
    return nc


_NAMES = ["meta_tokens", "norm_g", "w_in", "gdn_conv_w", "gdn_a_log", "gdn_dt_bias", "gdn_norm_g",
          "ssm_conv_w", "ssm_conv_b", "ssm_a_log", "ssm_dt_bias", "ssm_d", "ssm_norm_g",
          "w_branch_a", "w_branch_b", "w_branch_c", "w_out"]


def make_in_maps(inputs, ncores=8):
    shared = {n: np.ascontiguousarray(np.asarray(inputs[n], dtype=np.float32)) for n in _NAMES}
    shared["final_norm_g"] = np.ascontiguousarray(
        np.asarray(inputs["final_norm_g"], dtype=np.float32).reshape(1, D))
    x = np.asarray(inputs["x"], dtype=np.float32)
    maps = []
    for c in range(ncores):
        m = dict(shared)
        m["x"] = np.ascontiguousarray(x[c])
        maps.append(m)
    return maps


def kernel(**inputs):
    nc = build()
    in_maps = make_in_maps(inputs)
    res = run_bass_kernel_spmd(nc, in_maps, core_ids=list(range(8)))
    return np.stack([r["out"] for r in res.results], axis=0)
```

```python
from contextlib import ExitStack
import numpy as np
import concourse.bass as bass
import concourse.mybir as mybir
from concourse.bass_utils import run_bass_kernel_spmd

F32 = mybir.dt.float32
BF16 = mybir.dt.bfloat16
AF = mybir.ActivationFunctionType
ALU = mybir.AluOpType

D = 1024
SEQ = 2048
NMETA = 16
PAD = 112
LP = SEQ + NMETA + PAD
NT = LP // 128
DIN = 15920
DEPTH = 4
TG = [(0, 512), (512, 512), (1024, 512), (1536, 512), (2048, 128)]

O_SBQ, O_SBK, O_SBV, O_SBZ = 0, 1024, 2048, 3072
O_GQ, O_GK, O_GV, O_GZ, O_GB, O_GA = 4096, 5120, 6144, 7168, 8192, 8200
O_SZ, O_SX, O_SB, O_SC, O_SDT, O_GATE = 8208, 10256, 12304, 12560, 12816, 12848


class Res:
    __slots__ = ("name", "last_w", "readers", "sem", "dcount", "excl")

    def __init__(self, name):
        self.name = name
        self.excl = False
        self.last_w = None
        self.readers = []
        self.sem = None
        self.dcount = 0


class Op:
    __slots__ = ("eng", "fn", "deps", "is_dma", "sem", "val", "signal", "phase", "pe_mm")


class Prog:
    ENGS = ("pe", "act", "dve", "pool", "sp")

    def __init__(self, nc, es):
        self.nc = nc
        self.es = es
        self.ops = []
        self.phase = 0
        self.bar_deps = []
        self.last_on_eng = {}
        self.dmas_since_bar = []
        self.phase_sems = {}
        self.nsem = 0

    def new_sem(self, name):
        self.nsem += 1
        return self.es.enter_context(self.nc.semaphore(name))

    def res(self, name, dma=False):
        r = Res(name)
        if dma:
            r.sem = self.new_sem("d_" + name)
        return r

    def new_phase(self):
        self.phase += 1

    def _track(self, op, reads, writes):
        ex = [r for r in reads if r.excl and r not in writes]
        if ex:
            reads = [r for r in reads if not r.excl]
            writes = list(writes) + ex
        deps = set(self.bar_deps)
        for r in list(reads) + list(writes):
            if r.last_w is not None:
                deps.add(r.last_w)
        for w in writes:
            for rd in w.readers:
                deps.add(rd)
        idx = len(self.ops)
        deps.discard(idx)
        op.deps = deps
        for r in reads:
            r.readers.append(idx)
        for w in writes:
            w.last_w = idx
            w.readers = []
        self.ops.append(op)
        self.last_on_eng[op.eng] = idx
        return idx

    def op(self, eng, fn, reads=(), writes=(), mm=False):
        o = Op()
        o.eng = eng
        o.fn = fn
        o.is_dma = False
        o.sem = None
        o.val = 0
        o.signal = False
        o.phase = self.phase
        o.pe_mm = mm
        return self._track(o, reads, writes)

    def dma(self, eng, out, in_, semres, reads=(), writes=(), slow=False):
        o = Op()
        o.eng = eng
        if slow:
            o.fn = lambda e: e.dma_start(out=out, in_=in_, allow_slow_non_contiguous=True)
        else:
            o.fn = lambda e: e.dma_start(out=out, in_=in_)
        o.is_dma = True
        o.sem = semres.sem
        semres.dcount += 1
        o.val = 16 * semres.dcount
        o.signal = True
        o.phase = self.phase
        o.pe_mm = False
        idx = self._track(o, reads, writes)
        self.dmas_since_bar.append(idx)
        return idx

    def barrier(self):
        self.bar_deps = list(self.last_on_eng.values()) + list(self.dmas_since_bar)
        self.dmas_since_bar = []

    def emit(self):
        nc = self.nc
        ops = self.ops
        for i, o in enumerate(ops):
            for j in o.deps:
                p = ops[j]
                if p.is_dma:
                    continue
                if p.eng == "pe" and o.eng == "pe":
                    continue
                p.signal = True
        cnt = {}
        for o in ops:
            if o.is_dma or not o.signal:
                continue
            key = (o.eng, o.phase)
            if key not in self.phase_sems:
                self.phase_sems[key] = self.new_sem("s_%s_%d" % key)
                cnt[key] = 0
            cnt[key] += 1
            o.sem = self.phase_sems[key]
            o.val = cnt[key]
        self.counts = dict(cnt)
        per_eng = {e: [] for e in self.ENGS}
        for i, o in enumerate(ops):
            per_eng[o.eng].append(i)
        final_waits = {}
        for o in ops:
            if o.is_dma:
                final_waits[id(o.sem)] = (o.sem, max(o.val, final_waits.get(id(o.sem), (None, 0))[1]))

        def run(engname, e):
            known = {}
            for i in per_eng[engname]:
                o = ops[i]
                need = {}
                for j in o.deps:
                    p = ops[j]
                    if p.sem is None:
                        continue
                    k = id(p.sem)
                    if k not in need or need[k][1] < p.val:
                        need[k] = (p.sem, p.val)
                for k, (s, v) in need.items():
                    if known.get(k, 0) < v:
                        e.wait_ge(s, v)
                        known[k] = v
                ins = o.fn(e)
                if o.is_dma:
                    ins.then_inc(o.sem, 16)
                elif o.signal:
                    ins.then_inc(o.sem, 1)
            if engname == "sp":
                for k, (s, v) in final_waits.items():
                    e.wait_ge(s, v)

        with nc.Block() as block:
            @block.tensor
            def _(e):
                run("pe", e)

            @block.scalar
            def _(e):
                run("act", e)

            @block.vector
            def _(e):
                run("dve", e)

            @block.gpsimd
            def _(e):
                run("pool", e)

            @block.sync
            def _(e):
                run("sp", e)


class Carver:
    def __init__(self, prog, ap, nwords):
        self.prog = prog
        self.ap = ap
        self.n = nwords
        self.off = 0
        self.k = 0

    def reset(self):
        self.off = 0

    def f32(self, cols, name="w"):
        a = self.ap[:, self.off:self.off + cols]
        self.off += cols
        assert self.off <= self.n, ("work overflow", self.off, self.n)
        self.k += 1
        return a, self.prog.res("%s%d" % (name, self.k))

    def bf16(self, cols, name="w"):
        words = (cols + 1) // 2
        a = self.ap[:, self.off:self.off + words].bitcast(BF16)
        self.off += words
        assert self.off <= self.n, ("work overflow", self.off, self.n)
        self.k += 1
        return a[:, 0:cols], self.prog.res("%s%d" % (name, self.k))


def build(depth=DEPTH, dbg=None, stop_after=None, GSTOP=0, skip=(), USE_F32R=False):
    F32R = mybir.dt.float32r
    nc = bass.Bass("TRN2", target_bir_lowering=False)
    es = ExitStack()
    P = Prog(nc, es)

    def din(name, shape):
        return nc.dram_tensor(name, list(shape), F32, kind="ExternalInput").ap()

    x_d = din("x", (SEQ, D))
    meta_d = din("meta_tokens", (NMETA, D))
    norm_g_d = din("norm_g", (DEPTH, D))
    w_in_d = din("w_in", (DEPTH, D, DIN))
    gdn_conv_w_d = din("gdn_conv_w", (DEPTH, 4, 3072))
    gdn_a_log_d = din("gdn_a_log", (DEPTH, 8))
    gdn_dt_bias_d = din("gdn_dt_bias", (DEPTH, 8))
    gdn_norm_g_d = din("gdn_norm_g", (DEPTH, 128))
    ssm_conv_w_d = din("ssm_conv_w", (DEPTH, 4, 2560))
    ssm_conv_b_d = din("ssm_conv_b", (DEPTH, 2560))
    ssm_a_log_d = din("ssm_a_log", (DEPTH, 32))
    ssm_dt_bias_d = din("ssm_dt_bias", (DEPTH, 32))
    ssm_d_d = din("ssm_d", (DEPTH, 32))
    ssm_norm_g_d = din("ssm_norm_g", (DEPTH, 2048))
    w_br_d = [din("w_branch_a", (DEPTH, 1024, D)), din("w_branch_b", (DEPTH, 1024, D)),
              din("w_branch_c", (DEPTH, 2048, D))]
    w_out_d = din("w_out", (DEPTH, D, D))
    fin_g_d = din("final_norm_g", (1, D))
    out_d = nc.dram_tensor("out", [SEQ, D], F32, kind="ExternalOutput").ap()
    dbg_d = None
    if dbg is not None:
        dbg_d = nc.dram_tensor("dbg", list(dbg["shape"]), F32, kind="ExternalOutput").ap()

    def sb(name, shape, dt):
        return es.enter_context(nc.sbuf_tensor(name, list(shape), dt))

    def ps(name, shape, dt=F32):
        return es.enter_context(nc.psum_tensor(name, list(shape), dt))

    def OP(eng, meth, reads=(), writes=(), **kw):
        return P.op(eng, lambda e: getattr(e, meth)(**kw), reads=reads, writes=writes)

    def MM(out, lhsT, rhs, start, stop, reads, writes):
        return P.op("pe", lambda e: e.matmul(out=out, lhsT=lhsT, rhs=rhs, start=start, stop=stop),
                    reads=reads, writes=writes, mm=True)

    def TR(out, in_, ident, reads, writes):
        return P.op("pe", lambda e: e.transpose(out=out, in_=in_, identity=ident),
                    reads=reads, writes=writes, mm=True)

    h_sb = sb("h", (128, NT, D), F32)
    uT = sb("uT", (128, 8, LP), BF16)
    oT = sb("oT", (128, 8, LP), BF16)
    stage = [sb("stg%d" % i, (128, 8, 128), F32) for i in range(2)]
    wslab = [sb("wsl%d" % i, (128, 8, 128), BF16) for i in range(2)]
    WORKW = 13312
    work = sb("work", (128, WORKW), F32)
    ident_f = sb("ident_f", (128, 128), F32)
    ident_b = sb("ident_b", (128, 128), BF16)
    ones_b = sb("ones_b", (128, 128), BF16)
    ones_f = sb("ones_f", (128, 128), F32)
    uincl_f = sb("uincl_f", (128, 128), F32)
    tmat = sb("tmat", (128, 4, 128), BF16)
    mw = sb("mw", (128, 896), BF16)
    small = sb("small", (128, 64), F32)
    gfm = sb("gfm", (128, 8), F32)

    h_res = [P.res("h%d" % t) for t in range(NT)]
    hload_res = [P.res("hl%d" % i, dma=True) for i in range(5)]
    uT_res = [P.res("uT%d" % t) for t in range(NT)]
    oT_res = [P.res("oT%d" % k) for k in range(8)]
    stage_res = [P.res("stg%d" % i, dma=True) for i in range(2)]
    wslab_res = [P.res("wsl%d" % i) for i in range(2)]
    const_res = P.res("const")
    gfm_res = P.res("gfm", dma=True)
    small_res = P.res("small")
    dbg_res = P.res("dbg", dma=True)
    prm_res = [P.res("prm%d" % i, dma=True) for i in range(8)]
    out_res = [P.res("outst%d" % i, dma=True) for i in range(2)]

    banks = [ps("bank%d" % i, (128, 512), F32) for i in range(8)]
    bank_res = [P.res("bank%d" % i) for i in range(8)]
    for _r in bank_res:
        _r.excl = True
    carve = Carver(P, work, WORKW)
    bctr = [0]

    def nb(nmax=8):
        i = bctr[0] % nmax
        bctr[0] += 1
        return banks[i], bank_res[i]

    def bfv(bank):
        return bank[:].bitcast(BF16)

    cr = [const_res]
    OP("pool", "memset", writes=cr, ap=ident_f[:], constant=1.0)
    OP("pool", "affine_select", writes=cr, out=ident_f[:], in_=ident_f[:], pattern=[[-1, 128]],
       compare_op=ALU.is_equal, fill=0.0, base=0, channel_multiplier=1)
    OP("pool", "tensor_copy", reads=cr, writes=cr, out=ident_b[:], in_=ident_f[:])
    OP("pool", "memset", writes=cr, ap=ones_b[:], constant=1.0)
    OP("pool", "memset", writes=cr, ap=ones_f[:], constant=1.0)
    OP("pool", "memset", writes=cr, ap=uincl_f[:], constant=1.0)
    OP("pool", "affine_select", writes=cr, out=uincl_f[:], in_=uincl_f[:], pattern=[[1, 128]],
       compare_op=ALU.is_ge, fill=0.0, base=0, channel_multiplier=-1)
    OP("pool", "memset", writes=cr, ap=tmat[:], constant=-1.0)
    for v in (0, 2):
        OP("pool", "affine_select", writes=cr, out=tmat[:, v, :], in_=tmat[:, v, :], pattern=[[-1, 128]],
           compare_op=ALU.is_gt, fill=0.0, base=0, channel_multiplier=1)
    for v in (1, 3):
        OP("pool", "affine_select", writes=cr, out=tmat[:, v, :], in_=tmat[:, v, :], pattern=[[1, 128]],
           compare_op=ALU.is_ge, fill=0.0, base=0, channel_multiplier=-1)
    OP("pool", "memset", writes=cr, ap=tmat[0:PAD, 2:4, :], constant=0.0)
    OP("pool", "memset", writes=cr, ap=mw[:], constant=1.0)
    OP("pool", "affine_select", writes=cr, out=mw[:], in_=mw[:], pattern=[[1, 896]],
       compare_op=ALU.is_gt, fill=0.0, base=-384, channel_multiplier=-1)

    OP("pool", "memset", writes=[h_res[0]], ap=h_sb[:, 0, :], constant=0.0)
    P.dma("sp", h_sb[PAD:128, 0, :], meta_d[:, :], hload_res[4], writes=[h_res[0]])
    xv = x_d.rearrange("(t p) d -> p t d", p=128)
    for i in range(4):
        P.dma("sp", h_sb[:, 1 + 4 * i:5 + 4 * i, :], xv[:, 4 * i:4 * i + 4, :], hload_res[i],
              writes=[h_res[1 + 4 * i + j] for j in range(4)])

    wctr = [0]

    def stream_slab(src, C, dst=None, dst_res=None):
        i = wctr[0] % 2
        wctr[0] += 1
        P.dma("sp", stage[i][:, :, 0:C], src, stage_res[i], writes=[stage_res[i]])
        if dst is None:
            dst, dst_res = wslab[i][:, :, 0:C], wslab_res[i]
        OP("pool", "tensor_copy", reads=[stage_res[i]], writes=[dst_res], out=dst,
           in_=stage[i][:, :, 0:C])
        return dst, dst_res

    def win(layer, c0, C):
        return w_in_d[layer].rearrange("(k p) c -> p k c", p=128)[:, :, c0:c0 + C]

    def tiles_of(t0, n):
        return uT_res[t0 // 128:(t0 + n + 127) // 128]

    def proj_fm(layer, c0, evac, ncol=128):
        w, wr = stream_slab(win(layer, c0, ncol), ncol)
        for (t0, n) in TG:
            b, br = nb()
            for k in range(8):
                MM(b[0:ncol, 0:n], w[:, k, :], uT[:, k, t0:t0 + n], k == 0, k == 7,
                   reads=[wr] + tiles_of(t0, n), writes=[br])
            evac(b[0:ncol, 0:n], br, t0, n)

    def proj_tm(layer, c0, ncol, evac):
        w, wr = stream_slab(win(layer, c0, ncol), ncol)
        per = min(4, 512 // ncol)
        for tt0 in range(0, NT, per):
            cnt = min(per, NT - tt0)
            b, br = nb()
            for j in range(cnt):
                tt = tt0 + j
                for k in range(8):
                    MM(b[:, j * ncol:(j + 1) * ncol], uT[:, k, tt * 128:(tt + 1) * 128], w[:, k, :],
                       k == 0, k == 7, reads=[wr, uT_res[tt]], writes=[br])
            evac(b[:, 0:cnt * ncol], br, tt0, cnt)

    def load_small(dst, src, ri, slow=True):
        P.dma("sp", dst, src, prm_res[ri], writes=[prm_res[ri]], slow=slow)

    def layer_norm_T(layer):
        P.barrier()
        carve.reset()
        P.dma("sp", gfm[:, :], norm_g_d[layer].rearrange("(k p) -> p k", p=128), gfm_res,
              writes=[gfm_res], slow=True)
        junk, junk_r = carve.f32(1024, "junk")
        uns = [carve.bf16(1024, "un") for _ in range(2)]
        for tt in range(NT):
            un, un_r = uns[tt % 2]
            ss = small[:, 0:1]
            rs = small[:, 1:2]
            OP("act", "activation", reads=[h_res[tt]], writes=[junk_r, small_res], out=junk,
               in_=h_sb[:, tt, :], func=AF.Square, accum_out=ss)
            OP("act", "activation", reads=[small_res], writes=[small_res], out=rs, in_=ss,
               func=AF.Sqrt, scale=1.0 / D, bias=1e-6)
            OP("dve", "reciprocal", reads=[small_res], writes=[small_res], out=rs, in_=rs)
            OP("dve", "tensor_scalar", reads=[h_res[tt], small_res], writes=[un_r], out=un,
               in0=h_sb[:, tt, :], scalar1=rs, scalar2=None, op0=ALU.mult)
            b, br = nb()
            pT = bfv(b)
            for k in range(8):
                TR(pT[:, k * 128:(k + 1) * 128], un[:, k * 128:(k + 1) * 128], ident_b[:],
                   reads=[un_r, const_res], writes=[br])
            OP("dve", "tensor_tensor", reads=[br, gfm_res], writes=[uT_res[tt]],
               out=uT[:, :, tt * 128:(tt + 1) * 128], in0=pT.rearrange("p (k t) -> p k t", t=128),
               in1=gfm[:, :].unsqueeze(2).to_broadcast([128, 8, 128]), op=ALU.mult)

    def unit_attention(layer):
        P.barrier()
        carve.reset()
        qT, qT_r = carve.bf16(LP, "qT")
        kT, kT_r = carve.bf16(LP, "kT")
        zg, zg_r = carve.bf16(LP, "zg")
        vtm_flat, vtm_r = carve.bf16(NT * 128, "vtm")
        vtm = vtm_flat.rearrange("p (t d) -> p t d", d=128)
        NBUF = 3
        AHEAD = 2
        spf = [carve.f32(512, "spf") for _ in range(NBUF)]
        spb = [carve.bf16(512, "spb") for _ in range(NBUF)]
        t2 = [carve.f32(512, "t2") for _ in range(NBUF)]
        wb = [carve.bf16(512, "wb") for _ in range(NBUF)]
        scale = 128.0 ** -0.5
        gctr = [0]
        for h in range(8):
            proj_fm(layer, O_SBQ + h * 128, lambda b, br, t0, n: OP(
                "act", "activation", reads=[br], writes=[qT_r], out=qT[:, t0:t0 + n], in_=b,
                func=AF.Copy, scale=scale))
            proj_fm(layer, O_SBK + h * 128, lambda b, br, t0, n: OP(
                "dve", "tensor_copy", reads=[br], writes=[kT_r], out=kT[:, t0:t0 + n], in_=b))
            proj_fm(layer, O_SBZ + h * 128, lambda b, br, t0, n: OP(
                "act", "activation", reads=[br], writes=[zg_r], out=zg[:, t0:t0 + n], in_=b,
                func=AF.Silu))
            proj_tm(layer, O_SBV + h * 128, 128, lambda b, br, tt0, cnt: OP(
                "dve", "tensor_copy", reads=[br], writes=[vtm_r], out=vtm[:, tt0:tt0 + cnt, :],
                in_=b.rearrange("p (t d) -> p t d", d=128)))
            its = []
            for g in range(5):
                qb0 = 4 * g
                nq = 512 if g < 4 else 128
                kmax = qb0 + nq // 128 - 1
                gi = gctr[0] % 2
                gctr[0] += 1
                for idx, kb in enumerate(range(kmax, -1, -1)):
                    its.append(dict(qb0=qb0, nq=nq, q0=qb0 * 128, idx=idx, kb=kb, gi=gi, zb=None))

            n_it = len(its)

            def PZ(j):
                d = its[j]
                nq, kb, q0 = d["nq"], d["kb"], d["q0"]
                zb, zr = nb(4)
                d["zb"] = (zb, zr)
                MM(zb[:, 0:nq], kT[:, kb * 128:(kb + 1) * 128], qT[:, q0:q0 + nq], True, True,
                   reads=[kT_r, qT_r], writes=[zr])

            def AE(j):
                d = its[j]
                nq = d["nq"]
                zb, zr = d["zb"]
                sp_a, sp_r = spf[j % NBUF]
                OP("act", "activation", reads=[zr], writes=[sp_r], out=sp_a[:, 0:nq],
                   in_=zb[:, 0:nq], func=AF.Exp)
                OP("act", "activation", reads=[sp_r], writes=[sp_r], out=sp_a[:, 0:nq],
                   in_=sp_a[:, 0:nq], func=AF.Ln, bias=1.0)

            def DS(j):
                d = its[j]
                nq, kb, qb0 = d["nq"], d["kb"], d["qb0"]
                zb, zr = d["zb"]
                sp_a, sp_r = spf[j % NBUF]
                spb_a, spb_r = spb[j % NBUF]
                t2_a, t2_r = t2[j % NBUF]
                if kb >= qb0:
                    moff = 384 - 128 * (kb - qb0)
                    OP("dve", "tensor_tensor", reads=[sp_r, const_res], writes=[spb_r],
                       out=spb_a[:, 0:nq], in0=sp_a[:, 0:nq], in1=mw[:, moff:moff + nq],
                       op=ALU.mult)
                else:
                    OP("dve", "tensor_copy", reads=[sp_r], writes=[spb_r], out=spb_a[:, 0:nq],
                       in_=sp_a[:, 0:nq])
                OP("dve", "tensor_tensor", reads=[zr, sp_r], writes=[t2_r], out=t2_a[:, 0:nq],
                   in0=zb[:, 0:nq], in1=sp_a[:, 0:nq], op=ALU.subtract)

            def banksof(d):
                gi = d["gi"]
                return banks[4 + gi], bank_res[4 + gi], banks[6 + gi], bank_res[6 + gi]

            def PT(j):
                d = its[j]
                nq, kb, idx = d["nq"], d["kb"], d["idx"]
                A_b, A_r, O_b, O_r = banksof(d)
                spb_a, spb_r = spb[j % NBUF]
                tv = 2 if kb == 0 else 0
                MM(A_b[:, 0:nq], tmat[:, tv, :], spb_a[:, 0:nq], idx == 0, False,
                   reads=[spb_r, const_res], writes=[A_r])

            def DA(j):
                d = its[j]
                nq = d["nq"]
                A_b, A_r, O_b, O_r = banksof(d)
                t2_a, t2_r = t2[j % NBUF]
                OP("dve", "tensor_tensor", reads=[A_r, t2_r], writes=[t2_r], out=t2_a[:, 0:nq],
                   in0=A_b[:, 0:nq], in1=t2_a[:, 0:nq], op=ALU.add)

            def AW(j):
                d = its[j]
                nq, kb, qb0 = d["nq"], d["kb"], d["qb0"]
                t2_a, t2_r = t2[j % NBUF]
                wb_a, wb_r = wb[j % NBUF]
                OP("act", "activation", reads=[t2_r], writes=[wb_r], out=wb_a[:, 0:nq],
                   in_=t2_a[:, 0:nq], func=AF.Exp)
                if kb >= qb0:
                    moff = 384 - 128 * (kb - qb0)
                    OP("pool", "tensor_tensor", reads=[wb_r, const_res], writes=[wb_r],
                       out=wb_a[:, 0:nq], in0=wb_a[:, 0:nq], in1=mw[:, moff:moff + nq],
                       op=ALU.mult)

            def PL(j):
                d = its[j]
                nq, kb = d["nq"], d["kb"]
                A_b, A_r, O_b, O_r = banksof(d)
                spb_a, spb_r = spb[j % NBUF]
                tv = 2 if kb == 0 else 0
                MM(A_b[:, 0:nq], tmat[:, tv + 1, :], spb_a[:, 0:nq], False, kb == 0,
                   reads=[spb_r, const_res], writes=[A_r])

            def PO(j, h=h):
                d = its[j]
                nq, kb, idx, q0 = d["nq"], d["kb"], d["idx"], d["q0"]
                A_b, A_r, O_b, O_r = banksof(d)
                wb_a, wb_r = wb[j % NBUF]
                MM(O_b[:, 0:nq], vtm[:, kb, :], wb_a[:, 0:nq], idx == 0, kb == 0,
                   reads=[vtm_r, wb_r], writes=[O_r])
                if kb == 0:
                    OP("dve", "tensor_tensor", reads=[O_r, zg_r], writes=[oT_res[h]],
                       out=oT[:, h, q0:q0 + nq], in0=O_b[:, 0:nq], in1=zg[:, q0:q0 + nq], op=ALU.mult)

            PZ(0)
            AE(0)
            PZ(1)
            AE(1)
            DS(0)
            for j in range(n_it):
                PT(j)
                if j + 2 < n_it:
                    PZ(j + 2)
                DA(j)
                if j + 1 < n_it:
                    DS(j + 1)
                AW(j)
                if j + 2 < n_it:
                    AE(j + 2)
                PL(j)
                if j >= 1:
                    PO(j - 1)
            PO(n_it - 1)

    def unit_merge(layer, br_idx, row0, gate_c0):
        P.barrier()
        carve.reset()
        wbr_f, wbr_r = carve.bf16(8 * 1024, "wbr")
        wg_f, wg_r = carve.bf16(8 * 1024, "wg")
        wbr = wbr_f.rearrange("p (k c) -> p k c", c=1024)
        wg = wg_f.rearrange("p (k c) -> p k c", c=1024)
        sgs = [carve.bf16(512, "sg") for _ in range(2)]
        mTs = [carve.bf16(8 * 512, "mT") for _ in range(2)]
        src_br = w_br_d[br_idx][layer].rearrange("(k p) c -> p k c", p=128)
        for cb in range(8):
            stream_slab(src_br[:, row0 // 128:row0 // 128 + 8, cb * 128:(cb + 1) * 128], 128,
                        dst=wbr[:, :, cb * 128:(cb + 1) * 128], dst_res=wbr_r)
            stream_slab(win(layer, gate_c0 + cb * 128, 128), 128,
                        dst=wg[:, :, cb * 128:(cb + 1) * 128], dst_res=wg_r)
        for gi_, (t0, n) in enumerate(TG):
            tok = slice(t0, t0 + n)
            mT_f, mT_r = mTs[gi_ % 2]
            mT = mT_f.rearrange("p (c t) -> p c t", t=512)
            for cb in range(8):
                pbk, pbr = nb()
                for fk in range(8):
                    MM(pbk[:, 0:n], wbr[:, fk, cb * 128:(cb + 1) * 128], oT[:, fk, tok], fk == 0, fk == 7,
                       reads=[wbr_r, oT_res[fk]], writes=[pbr])
                gbk, gbr = nb()
                for k in range(8):
                    MM(gbk[:, 0:n], wg[:, k, cb * 128:(cb + 1) * 128], uT[:, k, tok], k == 0, k == 7,
                       reads=[wg_r] + tiles_of(t0, n), writes=[gbr])
                sg, sg_r = sgs[cb % 2]
                OP("act", "activation", reads=[gbr], writes=[sg_r], out=sg[:, 0:n], in_=gbk[:, 0:n],
                   func=AF.Sigmoid)
                OP("dve", "tensor_tensor", reads=[pbr, sg_r], writes=[mT_r], out=mT[:, cb, 0:n],
                   in0=pbk[:, 0:n], in1=sg[:, 0:n], op=ALU.mult)
            OP("pool", "tensor_copy", reads=[mT_r], writes=list(oT_res), out=oT[:, :, tok], in_=mT[:, :, 0:n])
        P.barrier()
        carve.reset()
        wo_f, wo_r = carve.bf16(8 * 1024, "wo")
        wo = wo_f.rearrange("p (k c) -> p k c", c=1024)
        src_o = w_out_d[layer].rearrange("(k p) c -> p k c", p=128)
        for ds in range(8):
            stream_slab(src_o[:, :, ds * 128:(ds + 1) * 128], 128, dst=wo[:, :, ds * 128:(ds + 1) * 128],
                        dst_res=wo_r)
        for tt in range(NT):
            for hf in range(2):
                b, br = nb()
                for cb in range(8):
                    MM(b[:, 0:512], oT[:, cb, tt * 128:(tt + 1) * 128], wo[:, cb, hf * 512:(hf + 1) * 512],
                       cb == 0, cb == 7, reads=[wo_r, oT_res[cb]], writes=[br])
                OP("dve", "tensor_tensor", reads=[br, h_res[tt]], writes=[h_res[tt]],
                   out=h_sb[:, tt, hf * 512:(hf + 1) * 512], in0=b[:, 0:512],
                   in1=h_sb[:, tt, hf * 512:(hf + 1) * 512], op=ALU.add)

    def bc_rows(src_row_ap, n):
        return src_row_ap.to_broadcast([128, n])

    def conv_silu(b, br, n, raw, raw_r, acc, acc_r, cw4, prm_r, bias, out_ap, out_res):
        OP("act", "activation", reads=[br], writes=[raw_r], out=raw[:, 3:3 + n], in_=b, func=AF.Copy)
        if bias is None:
            OP("dve", "tensor_scalar", reads=[raw_r, prm_r], writes=[acc_r], out=acc[:, 0:n],
               in0=raw[:, 3:3 + n], scalar1=cw4[:, 3:4], scalar2=None, op0=ALU.mult)
        else:
            OP("dve", "tensor_scalar", reads=[raw_r, prm_r], writes=[acc_r], out=acc[:, 0:n],
               in0=raw[:, 3:3 + n], scalar1=cw4[:, 3:4], scalar2=bias, op0=ALU.mult, op1=ALU.add)
        for j in (2, 1, 0):
            OP("dve", "scalar_tensor_tensor", reads=[raw_r, prm_r, acc_r], writes=[acc_r],
               out=acc[:, 0:n], in0=raw[:, j:j + n], scalar=cw4[:, j:j + 1], in1=acc[:, 0:n],
               op0=ALU.mult, op1=ALU.add)
        OP("act", "activation", reads=[acc_r], writes=[out_res], out=out_ap, in_=acc[:, 0:n],
           func=AF.Silu)
        OP("pool", "tensor_copy", reads=[raw_r], writes=[raw_r], out=raw[:, 0:3], in_=raw[:, n:n + 3])

    def load_conv_w(dst, dst_r, src2d, nblk, tmp, tmp_r, ri):
        P.dma("sp", tmp[0:nblk, :], src2d.rearrange("j (b p) -> b j p", p=128), prm_res[ri],
              writes=[tmp_r])
        b, br = nb()
        for j in range(4):
            TR(b[:, j * nblk:(j + 1) * nblk], tmp[0:nblk, j * 128:(j + 1) * 128], ident_f[0:nblk, 0:nblk],
               reads=[tmp_r, const_res], writes=[br])
        OP("dve", "tensor_copy", reads=[br], writes=[dst_r], out=dst, in_=b[:, 0:4 * nblk])

    def unit_gdn(layer):
        P.barrier()
        carve.reset()
        ba, ba_r = carve.f32(NT * 16, "ba")
        ba3 = ba.rearrange("p (t c) -> p t c", c=16)

        def sm(name):
            a, r = carve.f32(NT * 8, name)
            return a, a.rearrange("p (t c) -> p t c", c=8), r
        xa, xa3, xa_r = sm("xa")
        lb, lb3, lb_r = sm("lb")
        g_all, g3, g_r = sm("g")
        gc, gc3, gc_r = sm("gc")
        ngc, ngc3, ngc_r = sm("ngc")
        gb_, gb3, gb_r = sm("gb")
        egb, egb3, egb_r = sm("egb")
        beta, beta3, beta_r = sm("beta")
        alog, alog_r = carve.f32(8, "alog")
        dtb, dtb_r = carve.f32(8, "dtb")
        gng, gng_r = carve.f32(1, "gng")
        cw, cw_r = carve.f32(96, "cw")
        cw3 = cw.rearrange("p (j b) -> p j b", b=24)
        qs, qs_r = carve.f32(512, "qs")
        cwt, cwt_r = qs, qs_r
        load_small(alog, bc_rows(gdn_a_log_d[layer:layer + 1, :], 8), 0)
        load_small(dtb, bc_rows(gdn_dt_bias_d[layer:layer + 1, :], 8), 1)
        load_small(gng, gdn_norm_g_d[layer].rearrange("(p o) -> p o", o=1), 2)
        alog_r, dtb_r, gng_r = prm_res[0], prm_res[1], prm_res[2]
        load_conv_w(cw, cw_r, gdn_conv_w_d[layer], 24, cwt, cwt_r, 3)
        if GSTOP == 1:
            return
        proj_tm(layer, O_GB, 16, lambda b, br, tt0, cnt: OP(
            "dve", "tensor_copy", reads=[br], writes=[ba_r], out=ba[:, tt0 * 16:(tt0 + cnt) * 16], in_=b))
        OP("dve", "tensor_tensor", reads=[ba_r, dtb_r], writes=[xa_r], out=xa3, in0=ba3[:, :, 8:16],
           in1=dtb.unsqueeze(1).to_broadcast([128, NT, 8]), op=ALU.add)
        OP("act", "activation", reads=[xa_r], writes=[xa_r], out=xa, in_=xa, func=AF.Exp)
        OP("act", "activation", reads=[xa_r], writes=[xa_r], out=xa, in_=xa, func=AF.Ln, bias=1.0)
        OP("act", "activation", reads=[alog_r], writes=[alog_r], out=alog, in_=alog, func=AF.Exp)
        OP("dve", "scalar_tensor_tensor", reads=[xa_r, alog_r], writes=[g_r], out=g3, in0=xa3, scalar=-1.0,
           in1=alog.unsqueeze(1).to_broadcast([128, NT, 8]), op0=ALU.mult, op1=ALU.mult)
        OP("pool", "memset", reads=[], writes=[g_r], ap=g3[0:PAD, 0, :], constant=0.0)
        OP("act", "activation", reads=[ba_r], writes=[lb_r], out=lb3, in_=ba3[:, :, 0:8], func=AF.Exp,
           scale=-1.0)
        OP("act", "activation", reads=[lb_r], writes=[lb_r], out=lb, in_=lb, func=AF.Ln, bias=1.0)
        OP("act", "activation", reads=[lb_r], writes=[beta_r], out=beta, in_=lb, func=AF.Exp, scale=-1.0)
        b, br = nb()
        MM(b[:, 0:NT * 8], uincl_f[:], g_all, True, True, reads=[g_r, const_res], writes=[br])
        OP("dve", "tensor_copy", reads=[br], writes=[gc_r], out=gc, in_=b[:, 0:NT * 8])
        OP("dve", "tensor_scalar", reads=[gc_r], writes=[ngc_r], out=ngc, in0=gc, scalar1=-1.0,
           scalar2=None, op0=ALU.mult)
        OP("dve", "tensor_tensor", reads=[gc_r, lb_r], writes=[gb_r], out=gb_, in0=gc, in1=lb,
           op=ALU.subtract)
        OP("act", "activation", reads=[gb_r], writes=[egb_r], out=egb, in_=gb_, func=AF.Exp)
        if GSTOP == 2:
            return
        qT, qT_r = carve.bf16(LP, "qT")
        kT, kT_r = carve.bf16(LP, "kT")
        ktm_f, ktm_r = carve.bf16(NT * 128, "ktm")
        vtm_f, vtm_r = carve.bf16(NT * 128, "vtm")
        ktm = ktm_f.rearrange("p (t d) -> p t d", d=128)
        vtm = vtm_f.rearrange("p (t d) -> p t d", d=128)
        raw, raw_r = carve.f32(515, "raw")
        acc, acc_r = carve.f32(512, "acc")
        sqb, sqb_r = carve.bf16(512, "sqb")
        rn, rn_r = carve.f32(512, "rn")
        dd, dd_r = carve.f32(128, "dd")
        vTt, vTt_r = carve.bf16(512, "vTt")
        zgt, zgt_r = carve.bf16(512, "zgt")
        NSET = 3
        gsets = []
        for _i in range(NSET):
            st = {}
            for nm, kind, n in (("dg", "f", 256), ("eg", "f", 128),
                                ("RwT", "f", 128), ("AN0", "f", 256), ("Pm0", "f", 128),
                                ("aqkT", "b", 128), ("Rv", "f", 128), ("qd", "b", 128),
                                ("kend", "b", 128), ("glk", "f", 2)):
                st[nm], st[nm + "_r"] = (carve.f32 if kind == "f" else carve.bf16)(n, nm)
            st["E12"], st["E12_r"] = st["dg"], st["dg_r"]
            st["egbrow"], st["egbrow_r"] = st["RwT"], st["RwT_r"]
            st["AN1"], st["AN1_r"] = st["AN0"], st["AN0_r"]
            st["Pm1"], st["Pm1_r"] = st["Pm0"], st["Pm0_r"]
            gsets.append(st)
        vnew, vnew_r = carve.bf16(128, "vnew")
        S, S_r = carve.f32(128, "S")
        S_bf, Sbf_r = carve.bf16(128, "Sbf")
        import os as _os2
        if _os2.environ.get("KDEBUG"):
            print("GDN carve off", carve.off, "of", WORKW)

        def l2norm_to(dst, dst_r, t0, n, scl):
            OP("act", "activation", reads=[qs_r], writes=[sqb_r], out=sqb[:, 0:n], in_=qs[:, 0:n],
               func=AF.Square)
            b2, b2r = nb()
            MM(b2[:, 0:n], ones_b[:], sqb[:, 0:n], True, True, reads=[sqb_r, const_res], writes=[b2r])
            OP("act", "activation", reads=[b2r], writes=[rn_r], out=rn[:, 0:n], in_=b2[:, 0:n],
               func=AF.Ln, bias=1e-6)
            OP("act", "activation", reads=[rn_r], writes=[rn_r], out=rn[:, 0:n], in_=rn[:, 0:n],
               func=AF.Exp, scale=-0.5)
            OP("dve", "scalar_tensor_tensor", reads=[qs_r, rn_r], writes=[dst_r], out=dst[:, t0:t0 + n],
               in0=qs[:, 0:n], scalar=scl, in1=rn[:, 0:n], op0=ALU.mult, op1=ALU.mult)

        def to_tm(srcT, src_r, c0, n, dst3, dst_r, tt0):
            b2, b2r = nb()
            v = bfv(b2)
            cnt = n // 128
            for j in range(cnt):
                TR(v[:, j * 128:(j + 1) * 128], srcT[:, c0 + j * 128:c0 + (j + 1) * 128], ident_b[:],
                   reads=[src_r, const_res], writes=[b2r])
            OP("dve", "tensor_copy", reads=[b2r], writes=[dst_r], out=dst3[:, tt0:tt0 + cnt, :],
               in_=v[:, 0:n].rearrange("p (t d) -> p t d", d=128))

        for h in range(8):
            OP("pool", "memset", writes=[raw_r], ap=raw[:, 0:3], constant=0.0)

            def ev_q(b, br, t0, n, h=h):
                conv_silu(b, br, n, raw, raw_r, acc, acc_r, cw3[:, :, h], cw_r, None, qs[:, 0:n], qs_r)
                l2norm_to(qT, qT_r, t0, n, 128.0 ** -0.5)
            proj_fm(layer, O_GQ + h * 128, ev_q)
            OP("pool", "memset", writes=[raw_r], ap=raw[:, 0:3], constant=0.0)

            def ev_k(b, br, t0, n, h=h):
                conv_silu(b, br, n, raw, raw_r, acc, acc_r, cw3[:, :, 8 + h], cw_r, None, qs[:, 0:n], qs_r)
                l2norm_to(kT, kT_r, t0, n, 1.0)
                to_tm(kT, kT_r, t0, n, ktm, ktm_r, t0 // 128)
            proj_fm(layer, O_GK + h * 128, ev_k)
            OP("pool", "memset", writes=[raw_r], ap=raw[:, 0:3], constant=0.0)

            def ev_v(b, br, t0, n, h=h):
                conv_silu(b, br, n, raw, raw_r, acc, acc_r, cw3[:, :, 16 + h], cw_r, None, vTt[:, 0:n], vTt_r)
                to_tm(vTt, vTt_r, 0, n, vtm, vtm_r, t0 // 128)
            proj_fm(layer, O_GV + h * 128, ev_v)
            OP("pool", "memset", writes=[S_r], ap=S, constant=0.0)
            OP("pool", "memset", writes=[Sbf_r], ap=S_bf, constant=0.0)
            if GSTOP == 3:
                return
            def T_steps(c, h=h):
                st = gsets[c % NSET]
                tok = slice(c * 128, (c + 1) * 128)
                gcol = gc3[:, c, h:h + 1]
                ngcol = ngc3[:, c, h:h + 1]
                gbcol = gb3[:, c, h:h + 1]
                dg, dg_r, E12, E12_r, eg, eg_r = st["dg"], st["dg_r"], st["E12"], st["E12_r"], st["eg"], st["eg_r"]
                egbrow, egbrow_r, RwT, RwT_r = st["egbrow"], st["egbrow_r"], st["RwT"], st["RwT_r"]
                glk, glk_r = st["glk"], st["glk_r"]
                gl, kes = glk[:, 0:1], glk[:, 1:2]
                AN = [(st["AN0"], st["AN0_r"]), (st["AN1"], st["AN1_r"])]
                Pm = [(st["Pm0"], st["Pm0_r"]), (st["Pm1"], st["Pm1_r"])]
                an0, an0_r = AN[0]
                pm0, pm0_r = Pm[0]
                loc = {}
                steps = []

                def s0():
                    OP("dve", "tensor_scalar", reads=[gc_r, const_res], writes=[dg_r], out=dg[:, 0:128],
                       in0=ident_f[:], scalar1=gcol, scalar2=None, op0=ALU.mult)
                    OP("dve", "tensor_scalar", reads=[gb_r, const_res], writes=[dg_r], out=dg[:, 128:256],
                       in0=ident_f[:], scalar1=gbcol, scalar2=None, op0=ALU.mult)
                    bK, bKr = nb()
                    loc["bK"] = (bK, bKr)
                    MM(bK[:, 0:128], kT[:, tok], qT[:, tok], True, True, reads=[kT_r, qT_r], writes=[bKr])
                    MM(bK[:, 128:256], kT[:, tok], kT[:, tok], True, True, reads=[kT_r], writes=[bKr])
                    OP("dve", "tensor_scalar", reads=[vtm_r, beta_r], writes=[st["Rv_r"]], out=st["Rv"],
                       in0=vtm[:, c, :], scalar1=beta3[:, c, h:h + 1], scalar2=None, op0=ALU.mult)
                steps.append(s0)

                def s1():
                    bG, bGr = nb()
                    loc["bG"] = (bG, bGr)
                    MM(bG[:, 0:256], ones_f[:], dg, True, True, reads=[dg_r, const_res], writes=[bGr])
                steps.append(s1)

                def s2():
                    bG, bGr = loc["bG"]
                    OP("act", "activation", reads=[bGr, ngc_r], writes=[E12_r], out=E12, in_=bG[:, 0:256],
                       func=AF.Exp, bias=ngcol)
                    OP("dve", "tensor_copy", reads=[bGr], writes=[glk_r], out=gl, in_=bG[:, 127:128])
                    OP("act", "activation", reads=[bGr], writes=[eg_r], out=eg, in_=bG[:, 0:128], func=AF.Exp)
                    OP("act", "activation", reads=[bGr], writes=[egbrow_r], out=egbrow, in_=bG[:, 128:256],
                       func=AF.Exp)
                steps.append(s2)

                def s3():
                    OP("pool", "affine_select", reads=[E12_r], writes=[E12_r], out=E12[:, 0:128],
                       in_=E12[:, 0:128], pattern=[[1, 128]], compare_op=ALU.is_ge, fill=0.0, base=0,
                       channel_multiplier=-1)
                    OP("pool", "affine_select", reads=[E12_r], writes=[E12_r], out=E12[:, 128:256],
                       in_=E12[:, 128:256], pattern=[[1, 128]], compare_op=ALU.is_gt, fill=0.0, base=0,
                       channel_multiplier=-1)
                    OP("act", "activation", reads=[gc_r, glk_r], writes=[glk_r], out=kes, in_=gcol,
                       func=AF.Exp, scale=-1.0, bias=gl)
                steps.append(s3)

                def s4():
                    bK, bKr = loc["bK"]
                    OP("dve", "scalar_tensor_tensor", reads=[bKr, E12_r], writes=[an0_r], out=an0[:, 0:128],
                       in0=bK[:, 128:256], scalar=-1.0, in1=E12[:, 128:256], op0=ALU.mult, op1=ALU.mult)
                    OP("dve", "tensor_tensor", reads=[bKr, E12_r], writes=[st["aqkT_r"]], out=st["aqkT"],
                       in0=bK[:, 0:128], in1=E12[:, 0:128], op=ALU.mult)
                steps.append(s4)

                def s5():
                    bT, bTr = nb()
                    loc["bT"] = (bT, bTr)
                    TR(bT[:, 0:128], an0[:, 0:128], ident_f[:], reads=[an0_r, const_res], writes=[bTr])
                    OP("pool", "tensor_tensor", reads=[an0_r, const_res], writes=[pm0_r], out=pm0,
                       in0=an0[:, 0:128], in1=ident_f[:], op=ALU.add)
                steps.append(s5)

                def s6():
                    bT, bTr = loc["bT"]
                    OP("act", "activation", reads=[bTr], writes=[an0_r], out=an0[:, 128:256],
                       in_=bT[:, 0:128], func=AF.Copy)
                    OP("dve", "tensor_tensor", reads=[kT_r, egbrow_r], writes=[RwT_r], out=RwT, in0=kT[:, tok],
                       in1=egbrow, op=ALU.mult)
                    OP("dve", "tensor_tensor", reads=[qT_r, eg_r], writes=[st["qd_r"]], out=st["qd"],
                       in0=qT[:, tok], in1=eg, op=ALU.mult)
                    OP("dve", "tensor_scalar", reads=[ktm_r, glk_r], writes=[st["kend_r"]], out=st["kend"],
                       in0=ktm[:, c, :], scalar1=kes, scalar2=None, op0=ALU.mult)
                steps.append(s6)
                state = {"ai": 0, "pi": 0}
                for p in (1, 2, 4, 8, 16, 32, 64):
                    def lv_mm(p=p):
                        an, an_r = AN[state["ai"]]
                        anr = an.bitcast(F32R) if USE_F32R else an
                        if p > 1:
                            pc, pc_r = Pm[state["pi"]]
                            pcr = pc.bitcast(F32R) if USE_F32R else pc
                            bP, bPr = nb()
                            loc["bP"] = (bP, bPr)
                            MM(bP[:, 0:128], anr[:, 128:256], pcr, True, True, reads=[an_r, pc_r], writes=[bPr])
                        if p < 64:
                            bX, bXr = nb()
                            loc["bX"] = (bX, bXr)
                            MM(bX[:, 0:128], anr[:, 128:256], anr[:, 0:128], True, True, reads=[an_r], writes=[bXr])
                            MM(bX[:, 128:256], anr[:, 0:128], anr[:, 128:256], True, True, reads=[an_r],
                               writes=[bXr])

                    def lv_ev(p=p):
                        if p < 64:
                            an2, an2_r = AN[1 - state["ai"]]
                            bX, bXr = loc["bX"]
                            OP("act", "activation", reads=[bXr], writes=[an2_r], out=an2, in_=bX[:, 0:256],
                               func=AF.Copy)
                            state["ai"] = 1 - state["ai"]
                        if p > 1:
                            pc, pc_r = Pm[state["pi"]]
                            pn, pn_r = Pm[1 - state["pi"]]
                            bP, bPr = loc["bP"]
                            OP("dve", "tensor_tensor", reads=[bPr, pc_r], writes=[pn_r], out=pn,
                               in0=bP[:, 0:128], in1=pc, op=ALU.add)
                            state["pi"] = 1 - state["pi"]
                    steps.append(lv_mm)
                    steps.append(lv_ev)
                return steps, state, Pm

            def S_phase(c, fin, h=h):
                st = gsets[c % NSET]
                tok = slice(c * 128, (c + 1) * 128)
                state, Pm = fin
                pf, pf_r = Pm[state["pi"]]
                bS, bSr = nb()
                MM(bS[:, 0:128], st["RwT"], S, True, True, reads=[st["RwT_r"], S_r], writes=[bSr])
                OP("dve", "tensor_tensor", reads=[st["Rv_r"], bSr], writes=[dd_r], out=dd, in0=st["Rv"],
                   in1=bS[:, 0:128], op=ALU.subtract)
                bV, bVr = nb()
                MM(bV[:, 0:128], pf, dd, True, True, reads=[pf_r, dd_r], writes=[bVr])
                OP("act", "activation", reads=[bVr], writes=[vnew_r], out=vnew, in_=bV[:, 0:128], func=AF.Copy)
                bD, bDr = nb()
                MM(bD[:, 0:128], st["kend"], vnew, True, True, reads=[st["kend_r"], vnew_r], writes=[bDr])
                bO, bOr = nb()
                MM(bO[:, 0:128], S_bf, st["qd"], True, False, reads=[Sbf_r, st["qd_r"]], writes=[bOr])
                MM(bO[:, 0:128], vnew, st["aqkT"], False, True, reads=[vnew_r, st["aqkT_r"]], writes=[bOr])
                OP("dve", "scalar_tensor_tensor", reads=[S_r, st["eg_r"], bDr], writes=[S_r], out=S, in0=S,
                   scalar=st["eg"][:, 127:128], in1=bD[:, 0:128], op0=ALU.mult, op1=ALU.add)
                OP("pool", "tensor_copy", reads=[S_r], writes=[Sbf_r], out=S_bf, in_=S)
                OP("act", "activation", reads=[bOr], writes=[oT_res[h]], out=oT[:, h, tok], in_=bO[:, 0:128],
                   func=AF.Copy)

            for c0 in range(0, NT, NSET):
                cs_ = list(range(c0, min(NT, c0 + NSET)))
                built = [T_steps(c) for c in cs_]
                nst = max(len(b[0]) for b in built)
                for i in range(nst):
                    for b in built:
                        if i < len(b[0]):
                            b[0][i]()
                for c, b in zip(cs_, built):
                    S_phase(c, (b[1], b[2]))

            def ev_z(b, br, t0, n, h=h):
                OP("act", "activation", reads=[br], writes=[zgt_r], out=zgt[:, 0:n], in_=b, func=AF.Silu)
                OP("act", "activation", reads=[oT_res[h]], writes=[sqb_r], out=sqb[:, 0:n],
                   in_=oT[:, h, t0:t0 + n], func=AF.Square)
                b2, b2r = nb()
                MM(b2[:, 0:n], ones_b[:], sqb[:, 0:n], True, True, reads=[sqb_r, const_res], writes=[b2r])
                OP("act", "activation", reads=[b2r], writes=[rn_r], out=rn[:, 0:n], in_=b2[:, 0:n],
                   func=AF.Ln, scale=1.0 / 128, bias=1e-6)
                OP("act", "activation", reads=[rn_r], writes=[rn_r], out=rn[:, 0:n], in_=rn[:, 0:n],
                   func=AF.Exp, scale=-0.5)
                OP("dve", "scalar_tensor_tensor", reads=[oT_res[h], gng_r, rn_r], writes=[oT_res[h]],
                   out=oT[:, h, t0:t0 + n], in0=oT[:, h, t0:t0 + n], scalar=gng[:, 0:1], in1=rn[:, 0:n],
                   op0=ALU.mult, op1=ALU.mult)
                OP("dve", "tensor_tensor", reads=[oT_res[h], zgt_r], writes=[oT_res[h]],
                   out=oT[:, h, t0:t0 + n], in0=oT[:, h, t0:t0 + n], in1=zgt[:, 0:n], op=ALU.mult)
            proj_fm(layer, O_GZ + h * 128, ev_z)


    def unit_ssd(layer, g):
        P.barrier()
        carve.reset()
        dtr, dtr_r = carve.f32(NT * 32, "dtr")
        dt_, dt_r = carve.f32(NT * 32, "dt")
        cs, cs_r = carve.f32(NT * 32, "cs")
        ncs, ncs_r = carve.f32(NT * 32, "ncs")
        dt3 = dt_.rearrange("p (t c) -> p t c", c=32)
        dtr3 = dtr.rearrange("p (t c) -> p t c", c=32)
        cs3 = cs.rearrange("p (t c) -> p t c", c=32)
        ncs3 = ncs.rearrange("p (t c) -> p t c", c=32)
        alog, _ = carve.f32(32, "alog")
        dtb, _ = carve.f32(32, "dtb")
        dcol, _ = carve.f32(16, "dcol")
        ng, _ = carve.f32(16, "ng")
        cbt, cbt_r = carve.f32(128, "cbt")
        cw, cw_r = carve.f32(80, "cw")
        cw3 = cw.rearrange("p (j b) -> p j b", b=20)
        cbias, cbias_r = carve.f32(20, "cbias")
        acc, acc_r = carve.f32(512, "acc")
        cwt, cwt_r = acc, acc_r
        load_small(alog, bc_rows(ssm_a_log_d[layer:layer + 1, :], 32), 0)
        load_small(dtb, bc_rows(ssm_dt_bias_d[layer:layer + 1, :], 32), 1)
        dv = ssm_d_d[layer].rearrange("(b s) -> s b", s=2)
        P.dma("sp", dcol[0:64, :], dv[0:1, :].to_broadcast([64, 16]), prm_res[2], writes=[prm_res[2]], slow=True)
        P.dma("sp", dcol[64:128, :], dv[1:2, :].to_broadcast([64, 16]), prm_res[2], writes=[prm_res[2]],
              slow=True)
        load_small(ng, ssm_norm_g_d[layer].rearrange("(b p) -> p b", p=128), 4)
        alog_r, dtb_r, dcol_r, ng_r = prm_res[0], prm_res[1], prm_res[2], prm_res[4]
        load_conv_w(cw, cw_r, ssm_conv_w_d[layer], 20, cwt, cwt_r, 3)
        P.dma("sp", cbt[0:20, :], ssm_conv_b_d[layer].rearrange("(b p) -> b p", p=128), prm_res[5],
              writes=[cbt_r])
        b, br = nb()
        TR(b[:, 0:20], cbt[0:20, :], ident_f[0:20, 0:20], reads=[cbt_r, const_res], writes=[br])
        OP("dve", "tensor_copy", reads=[br], writes=[cbias_r], out=cbias, in_=b[:, 0:20])
        proj_tm(layer, O_SDT, 32, lambda b, br, tt0, cnt: OP(
            "dve", "tensor_copy", reads=[br], writes=[dtr_r], out=dtr[:, tt0 * 32:(tt0 + cnt) * 32], in_=b))
        OP("dve", "tensor_tensor", reads=[dtr_r, dtb_r], writes=[dt_r], out=dt3, in0=dtr3,
           in1=dtb.unsqueeze(1).to_broadcast([128, NT, 32]), op=ALU.add)
        OP("act", "activation", reads=[dt_r], writes=[dt_r], out=dt_, in_=dt_, func=AF.Exp)
        OP("act", "activation", reads=[dt_r], writes=[dt_r], out=dt_, in_=dt_, func=AF.Ln, bias=1.0)
        OP("pool", "memset", reads=[], writes=[dt_r], ap=dt3[0:PAD, 0, :], constant=0.0)
        OP("act", "activation", reads=[alog_r], writes=[alog_r], out=alog, in_=alog, func=AF.Exp)
        OP("dve", "scalar_tensor_tensor", reads=[dt_r, alog_r], writes=[dtr_r], out=dtr3, in0=dt3, scalar=-1.0,
           in1=alog.unsqueeze(1).to_broadcast([128, NT, 32]), op0=ALU.mult, op1=ALU.mult)
        for hf in range(2):
            w0 = hf * 272
            b, br = nb()
            MM(b[:, 0:272], uincl_f[:], dtr[:, w0:w0 + 272], True, True, reads=[dtr_r, const_res], writes=[br])
            OP("dve", "tensor_copy", reads=[br], writes=[cs_r], out=cs[:, w0:w0 + 272], in_=b[:, 0:272])
        OP("dve", "tensor_scalar", reads=[cs_r], writes=[ncs_r], out=ncs, in0=cs, scalar1=-1.0, scalar2=None,
           op0=ALU.mult)

        BT, BT_r = carve.bf16(LP, "BT")
        CT, CT_r = carve.bf16(LP, "CT")
        xT, xT_r = carve.bf16(LP, "xT")
        Btm_f, Btm_r = carve.bf16(NT * 128, "Btm")
        Btm = Btm_f.rearrange("p (t d) -> p t d", d=128)
        raw, raw_r = carve.f32(515, "raw")
        zgt, zgt_r = carve.bf16(512, "zgt")
        sqs = [carve.bf16(512, "sq") for _ in range(2)]
        rn, rn_r = carve.f32(512, "rn")
        NSET = 3
        sets = []
        for _i in range(NSET):
            st = {}
            st["dg"], st["dg_r"] = carve.f32(256, "dg")
            st["E12"], st["E12_r"] = carve.f32(256, "E12")
            st["eg"], st["eg_r"] = carve.f32(256, "eg")
            st["aqkT"], st["aqkT_r"] = carve.bf16(256, "aqkT")
            st["xs2"], st["xs2_r"] = carve.bf16(256, "xs2")
            st["qd"], st["qd_r"] = carve.bf16(256, "qd")
            st["kend"], st["kend_r"] = carve.bf16(256, "kend")
            st["glk"], st["glk_r"] = carve.f32(4, "glk")
            for nm in ("aqkT", "xs2", "qd", "kend", "E12", "eg"):
                st[nm + "3"] = st[nm].rearrange("p (s l) -> p s l", l=128)
            sets.append(st)
        S, S_r = carve.f32(128, "S")
        S2, S2_r = carve.bf16(256, "S2")
        S23 = S2.rearrange("p (s l) -> p s l", l=128)

        def to_tm(srcT, src_r, c0, n, dst3, dst_r, tt0):
            b2, b2r = nb()
            v = bfv(b2)
            cnt = n // 128
            for j in range(cnt):
                TR(v[:, j * 128:(j + 1) * 128], srcT[:, c0 + j * 128:c0 + (j + 1) * 128], ident_b[:],
                   reads=[src_r, const_res], writes=[b2r])
            OP("dve", "tensor_copy", reads=[b2r], writes=[dst_r], out=dst3[:, tt0:tt0 + cnt, :],
               in_=v[:, 0:n].rearrange("p (t d) -> p t d", d=128))

        OP("pool", "memset", writes=[raw_r], ap=raw[:, 0:3], constant=0.0)

        def ev_B(b, br, t0, n):
            conv_silu(b, br, n, raw, raw_r, acc, acc_r, cw3[:, :, 16 + g], cw_r, cbias[:, 16 + g:17 + g],
                      BT[:, t0:t0 + n], BT_r)
            to_tm(BT, BT_r, t0, n, Btm, Btm_r, t0 // 128)
        proj_fm(layer, O_SB + g * 128, ev_B)
        OP("pool", "memset", writes=[raw_r], ap=raw[:, 0:3], constant=0.0)
        proj_fm(layer, O_SC + g * 128, lambda b, br, t0, n: conv_silu(
            b, br, n, raw, raw_r, acc, acc_r, cw3[:, :, 18 + g], cw_r, cbias[:, 18 + g:19 + g],
            CT[:, t0:t0 + n], CT_r))
        for st in sets:
            OP("pool", "memset", writes=[st["xs2_r"]], ap=st["xs2"], constant=0.0)
        OP("pool", "memset", writes=[S2_r], ap=S2, constant=0.0)
        for j in range(8):
            blk = g * 8 + j
            hh = [2 * blk, 2 * blk + 1]
            OP("pool", "memset", writes=[raw_r], ap=raw[:, 0:3], constant=0.0)
            proj_fm(layer, O_SX + blk * 128, lambda b, br, t0, n, blk=blk: conv_silu(
                b, br, n, raw, raw_r, acc, acc_r, cw3[:, :, blk], cw_r, cbias[:, blk:blk + 1],
                xT[:, t0:t0 + n], xT_r))
            OP("pool", "memset", writes=[S_r], ap=S, constant=0.0)
            for s2 in range(2):
                OP("pool", "memset", writes=[S2_r], ap=S23[:, s2, s2 * 64:(s2 + 1) * 64], constant=0.0)
            def prep(c, hh=hh):
                st = sets[c % NSET]
                dg, dg_r, E12_r, eg, eg_r = st["dg"], st["dg_r"], st["E12_r"], st["eg"], st["eg_r"]
                E3, eg3, aqk3, xs3, qd3, kend3 = st["E123"], st["eg3"], st["aqkT3"], st["xs23"], st["qd3"], st["kend3"]
                glk, glk_r = st["glk"], st["glk_r"]
                tok = slice(c * 128, (c + 1) * 128)
                for s2 in range(2):
                    OP("dve", "tensor_scalar", reads=[cs_r, const_res], writes=[dg_r],
                       out=dg[:, s2 * 128:(s2 + 1) * 128], in0=ident_f[:], scalar1=cs3[:, c, hh[s2]:hh[s2] + 1],
                       scalar2=None, op0=ALU.mult)
                bG, bGr = nb()
                MM(bG[:, 0:256], ones_f[:], dg, True, True, reads=[dg_r, const_res], writes=[bGr])
                bK, bKr = nb()
                MM(bK[:, 0:128], BT[:, tok], CT[:, tok], True, True, reads=[BT_r, CT_r], writes=[bKr])
                bT, bTr = nb()
                TR(bfv(bT)[:, 0:128], xT[:, tok], ident_b[:], reads=[xT_r, const_res], writes=[bTr])
                for s2 in range(2):
                    OP("act", "activation", reads=[bGr, ncs_r], writes=[E12_r], out=E3[:, s2, :],
                       in_=bG[:, s2 * 128:(s2 + 1) * 128], func=AF.Exp, bias=ncs3[:, c, hh[s2]:hh[s2] + 1])
                OP("act", "activation", reads=[bGr], writes=[eg_r], out=eg, in_=bG[:, 0:256], func=AF.Exp)
                OP("dve", "tensor_copy", reads=[bGr], writes=[glk_r], out=glk[:, 0:2],
                   in_=bG[:, 0:256].rearrange("p (s l) -> p s l", l=128)[:, :, 127])
                OP("pool", "affine_select", reads=[E12_r], writes=[E12_r], out=E3, in_=E3,
                   pattern=[[0, 2], [1, 128]], compare_op=ALU.is_ge, fill=0.0, base=0, channel_multiplier=-1)
                for s2 in range(2):
                    OP("act", "activation", reads=[cs_r, glk_r], writes=[glk_r], out=glk[:, 2 + s2:3 + s2],
                       in_=cs3[:, c, hh[s2]:hh[s2] + 1], func=AF.Exp, scale=-1.0, bias=glk[:, s2:s2 + 1])
                for s2 in range(2):
                    OP("dve", "tensor_scalar", reads=[bTr, dt_r], writes=[st["xs2_r"]],
                       out=xs3[:, s2, s2 * 64:(s2 + 1) * 64], in0=bfv(bT)[:, s2 * 64:(s2 + 1) * 64],
                       scalar1=dt3[:, c, hh[s2]:hh[s2] + 1], scalar2=None, op0=ALU.mult)
                OP("dve", "tensor_tensor", reads=[bKr, E12_r], writes=[st["aqkT_r"]], out=aqk3,
                   in0=bK[:, 0:128].unsqueeze(1).to_broadcast([128, 2, 128]), in1=E3, op=ALU.mult)
                OP("dve", "tensor_tensor", reads=[CT_r, eg_r], writes=[st["qd_r"]], out=qd3,
                   in0=CT[:, tok].unsqueeze(1).to_broadcast([128, 2, 128]), in1=eg3, op=ALU.mult)
                for s2 in range(2):
                    OP("dve", "tensor_scalar", reads=[Btm_r, glk_r], writes=[st["kend_r"]], out=kend3[:, s2, :],
                       in0=Btm[:, c, :], scalar1=glk[:, 2 + s2:3 + s2], scalar2=None, op0=ALU.mult)

            def scan(c, j=j):
                st = sets[c % NSET]
                eg_r, eg3, aqk3, xs3, qd3, kend3 = st["eg_r"], st["eg3"], st["aqkT3"], st["xs23"], st["qd3"], st["kend3"]
                tok = slice(c * 128, (c + 1) * 128)
                bD, bDr = nb()
                for s2 in range(2):
                    MM(bD[:, s2 * 64:(s2 + 1) * 64], kend3[:, s2, :], xs3[:, s2, s2 * 64:(s2 + 1) * 64], True, True,
                       reads=[st["kend_r"], st["xs2_r"]], writes=[bDr])
                bO, bOr = nb()
                for s2 in range(2):
                    MM(bO[:, 0:128], S23[:, s2, :], qd3[:, s2, :], s2 == 0, False, reads=[S2_r, st["qd_r"]],
                       writes=[bOr])
                for s2 in range(2):
                    MM(bO[:, 0:128], xs3[:, s2, :], aqk3[:, s2, :], False, s2 == 1,
                       reads=[st["xs2_r"], st["aqkT_r"]], writes=[bOr])
                for s2 in range(2):
                    cols = slice(s2 * 64, (s2 + 1) * 64)
                    OP("dve", "scalar_tensor_tensor", reads=[S_r, eg_r, bDr], writes=[S_r], out=S[:, cols],
                       in0=S[:, cols], scalar=eg3[:, s2, 127:128], in1=bD[:, cols], op0=ALU.mult, op1=ALU.add)
                for s2 in range(2):
                    cols = slice(s2 * 64, (s2 + 1) * 64)
                    OP("pool", "tensor_copy", reads=[S_r], writes=[S2_r], out=S23[:, s2, cols], in_=S[:, cols])
                OP("act", "activation", reads=[bOr], writes=[oT_res[j]], out=oT[:, j, tok], in_=bO[:, 0:128],
                   func=AF.Copy)

            prep(0)
            prep(1)
            for c in range(NT):
                scan(c)
                if c + 2 < NT:
                    prep(c + 2)

            def ev_z(b, br, t0, n, j=j, blk=blk):
                OP("act", "activation", reads=[br], writes=[zgt_r], out=zgt[:, 0:n], in_=b, func=AF.Silu)
                OP("dve", "scalar_tensor_tensor", reads=[xT_r, dcol_r, oT_res[j]], writes=[oT_res[j]],
                   out=oT[:, j, t0:t0 + n], in0=xT[:, t0:t0 + n], scalar=dcol[:, blk:blk + 1],
                   in1=oT[:, j, t0:t0 + n], op0=ALU.mult, op1=ALU.add)
                OP("dve", "tensor_tensor", reads=[oT_res[j], zgt_r], writes=[oT_res[j]],
                   out=oT[:, j, t0:t0 + n], in0=oT[:, j, t0:t0 + n], in1=zgt[:, 0:n], op=ALU.mult)
            proj_fm(layer, O_SZ + blk * 128, ev_z)
        for (t0, n) in TG:
            bq, bqr = nb()
            for j in range(8):
                sq, sq_r = sqs[j % 2]
                OP("dve", "tensor_tensor", reads=[oT_res[j]], writes=[sq_r], out=sq[:, 0:n],
                   in0=oT[:, j, t0:t0 + n], in1=oT[:, j, t0:t0 + n], op=ALU.mult)
                MM(bq[:, 0:n], ones_b[:], sq[:, 0:n], j == 0, j == 7, reads=[sq_r, const_res], writes=[bqr])
            OP("act", "activation", reads=[bqr], writes=[rn_r], out=rn[:, 0:n], in_=bq[:, 0:n], func=AF.Ln,
               scale=1.0 / 1024, bias=1e-6)
            OP("act", "activation", reads=[rn_r], writes=[rn_r], out=rn[:, 0:n], in_=rn[:, 0:n], func=AF.Exp,
               scale=-0.5)
            for j in range(8):
                blk = g * 8 + j
                OP("dve", "scalar_tensor_tensor", reads=[oT_res[j], ng_r, rn_r], writes=[oT_res[j]],
                   out=oT[:, j, t0:t0 + n], in0=oT[:, j, t0:t0 + n], scalar=ng[:, blk:blk + 1], in1=rn[:, 0:n],
                   op0=ALU.mult, op1=ALU.mult)

    def final_out():
        P.barrier()
        carve.reset()
        fg, fg_r = carve.f32(1024, "fg")
        junk, junk_r = carve.f32(1024, "junk")
        outs = [carve.f32(1024, "ot") for _ in range(2)]
        load_small(fg, fin_g_d[0:1, :].to_broadcast([128, 1024]), 0, slow=True)
        for tt in range(1, NT):
            ot, ot_r = outs[tt % 2]
            osr = out_res[tt % 2]
            ss = small[:, 0:1]
            rs = small[:, 1:2]
            OP("act", "activation", reads=[h_res[tt]], writes=[junk_r, small_res], out=junk,
               in_=h_sb[:, tt, :], func=AF.Square, accum_out=ss)
            OP("act", "activation", reads=[small_res], writes=[small_res], out=rs, in_=ss,
               func=AF.Sqrt, scale=1.0 / D, bias=1e-6)
            OP("dve", "reciprocal", reads=[small_res], writes=[small_res], out=rs, in_=rs)
            OP("dve", "scalar_tensor_tensor", reads=[h_res[tt], small_res, prm_res[0], osr], writes=[ot_r],
               out=ot, in0=h_sb[:, tt, :], scalar=rs, in1=fg, op0=ALU.mult, op1=ALU.mult)
            P.dma("sp", out_d[(tt - 1) * 128:tt * 128, :], ot, osr, reads=[ot_r], writes=[osr])

    def dump_oT():
        P.barrier()
        carve.reset()
        t32, t32_r = carve.f32(LP, "dump")
        for k in range(8):
            OP("dve", "tensor_copy", reads=oT_res, writes=[t32_r], out=t32, in_=oT[:, k, :])
            P.dma("sp", dbg_d[k * 128:(k + 1) * 128, :], t32, dbg_res, reads=[t32_r])

    def dump_h():
        P.barrier()
        for tt in range(NT):
            P.dma("sp", dbg_d[tt * 128:(tt + 1) * 128, :], h_sb[:, tt, :], dbg_res, reads=[h_res[tt]])

    done = False
    for layer in range(depth):
        P.new_phase()
        layer_norm_T(layer)
        if "A" not in skip:
            unit_attention(layer)
        if stop_after == ("A", layer):
            dump_oT()
            done = True
            break
        if "A" not in skip:
            unit_merge(layer, 0, 0, O_GATE)
        if stop_after == ("Am", layer):
            dump_h()
            done = True
            break
        if "B" not in skip:
            unit_gdn(layer)
        if stop_after == ("B", layer):
            dump_oT()
            done = True
            break
        if "B" not in skip:
            unit_merge(layer, 1, 0, O_GATE + 1024)
        if stop_after == ("Bm", layer):
            dump_h()
            done = True
            break
        for g in range(2):
            if "C" in skip:
                break
            unit_ssd(layer, g)
            if stop_after == ("C%d" % g, layer):
                dump_oT()
                done = True
                break
            unit_merge(layer, 2, g * 1024, O_GATE + 2048)
        if done:
            break
        OP("pool", "memset", reads=[], writes=[h_res[0]], ap=h_sb[0:PAD, 0, :], constant=0.0)
        if stop_after == ("L", layer):
            dump_h()
            done = True
            break
    if not done:
        final_out()

    P.emit()
    import os as _os
    if _os.environ.get("KDEBUG"):
        print("ops", len(P.ops), "sems", P.nsem, "counts", P.counts)
    es.close()
    return nc


_NAMES = ["meta_tokens", "norm_g", "w_in", "gdn_conv_w", "gdn_a_log", "gdn_dt_bias", "gdn_norm_g",
          "ssm_conv_w", "ssm_conv_b", "ssm_a_log", "ssm_dt_bias", "ssm_d", "ssm_norm_g",
          "w_branch_a", "w_branch_b", "w_branch_c", "w_out"]


def make_in_maps(inputs, ncores=8):
    shared = {n: np.ascontiguousarray(np.asarray(inputs[n], dtype=np.float32)) for n in _NAMES}
    shared["final_norm_g"] = np.ascontiguousarray(
        np.asarray(inputs["final_norm_g"], dtype=np.float32).reshape(1, D))
    x = np.asarray(inputs["x"], dtype=np.float32)
    maps = []
    for c in range(ncores):
        m = dict(shared)
        m["x"] = np.ascontiguousarray(x[c])
        maps.append(m)
    return maps


def kernel(**inputs):
    nc = build()
    in_maps = make_in_maps(inputs)
    res = run_bass_kernel_spmd(nc, in_maps, core_ids=list(range(8)))
    return np.stack([r["out"] for r in res.results], axis=0)
```
